# Optimizing a Trainium2 kernel written in Bass

```python
import math
import jax, jax.numpy as jnp
from jax import lax
import numpy as np

D_MODEL = 2048
BATCH = 2
SEQ = 8192
DEPTH = 4

N_MIXERS = 3
PLE_DIM = 256
D_FF = 4 * D_MODEL
CHUNK = 64
EPS = 1e-6

GLA_HEADS = 4
GLA_DK = D_MODEL // 2 // GLA_HEADS
GLA_DV = D_MODEL // GLA_HEADS
GLA_GATE_RANK = 16
GLA_GATE_NORM = 16.0
HGRN_EXPAND = 128
HGRN_HEADS = D_MODEL // HGRN_EXPAND
HGRN_DK = HGRN_EXPAND
HGRN_DV = D_MODEL // HGRN_HEADS
SSM_EXPAND = 2
SSM_DINNER = SSM_EXPAND * D_MODEL
SSM_HEADDIM = 64
SSM_HEADS = SSM_DINNER // SSM_HEADDIM
SSM_GROUPS = 8
SSM_STATE = 128
SSM_CONV = 4
SSM_CONV_DIM = SSM_DINNER + 2 * SSM_GROUPS * SSM_STATE

N_GLA = len(range(0, DEPTH, N_MIXERS))
N_HGRN = len(range(1, DEPTH, N_MIXERS))
N_SSM = len(range(2, DEPTH, N_MIXERS))

kernel_name = 'hybrid_gla_hgrn2_mamba2_trunk'


def rmsnorm(x, w):
    xf = x.astype(jnp.float32)
    y = xf * lax.rsqrt(jnp.mean(xf * xf, axis=-1, keepdims=True) + EPS)
    return (y * w.astype(jnp.float32)).astype(x.dtype)


def to_chunks(t):
    b, l = t.shape[0], t.shape[1]
    t = t.reshape((b, l // CHUNK, CHUNK) + t.shape[2:])
    return jnp.moveaxis(t, 1, 0)


def from_chunks(t):
    t = jnp.moveaxis(t, 0, 1)
    return t.reshape((t.shape[0], t.shape[1] * t.shape[2]) + t.shape[3:])


def chunk_gla(q, k, v, g):
    bsz, _, h, dk = q.shape
    dv = v.shape[-1]
    tri = jnp.tril(jnp.ones((CHUNK, CHUNK), dtype=bool))

    def step(s, inp):
        qc, kc, vc, gc = inp
        b = jnp.cumsum(gc, axis=1)
        b_last = b[:, -1]
        o_inter = jnp.einsum('bthd,bhdv->bthv', qc * jnp.exp(b), s)
        rel = b[:, :, None] - b[:, None, :]
        decay = jnp.exp(jnp.where(tri[None, :, :, None, None], rel, -jnp.inf))
        att = jnp.einsum('bthd,bshd,btshd->bhts', qc, kc, decay)
        o_intra = jnp.einsum('bhts,bshv->bthv', att, vc)
        k_dec = kc * jnp.exp(b_last[:, None] - b)
        s = jnp.exp(b_last)[..., None] * s + jnp.einsum('bshd,bshv->bhdv', k_dec, vc)
        return s, o_inter + o_intra

    s0 = jnp.zeros((bsz, h, dk, dv), jnp.float32)
    _, o = lax.scan(step, s0, (to_chunks(q), to_chunks(k), to_chunks(v), to_chunks(g)))
    return from_chunks(o)


def chunk_ssd(x, dt, a, bm, cm):
    bsz, l, h, p = x.shape
    g, n = bm.shape[2], bm.shape[3]
    hpg = h // g
    tri = jnp.tril(jnp.ones((CHUNK, CHUNK), dtype=bool))
    xdt = (x * dt[..., None]).reshape(bsz, l, g, hpg, p)
    la = (dt * a).reshape(bsz, l, g, hpg)

    def step(s, inp):
        xc, lac, bc, cc = inp
        b = jnp.cumsum(lac, axis=1)
        b_last = b[:, -1]
        y_inter = jnp.einsum('btgn,btgh,bghnp->btghp', cc, jnp.exp(b), s)
        rel = b[:, :, None] - b[:, None, :]
        decay = jnp.exp(jnp.where(tri[None, :, :, None, None], rel, -jnp.inf))
        cb = jnp.einsum('btgn,bsgn->btsg', cc, bc)
        y_intra = jnp.einsum('btsg,btsgh,bsghp->btghp', cb, decay, xc)
        w_end = jnp.exp(b_last[:, None] - b)
        s = jnp.exp(b_last)[..., None, None] * s + jnp.einsum('bsgn,bsgh,bsghp->bghnp', bc, w_end, xc)
        return s, y_inter + y_intra

    s0 = jnp.zeros((bsz, g, hpg, n, p), jnp.float32)
    _, y = lax.scan(step, s0, (to_chunks(xdt), to_chunks(la), to_chunks(bm), to_chunks(cm)))
    return from_chunks(y).reshape(bsz, l, h, p)


def gla_mixer(u, w_in, w_gk2, b_gk, gn_w, w_out):
    bsz, l, _ = u.shape
    f32 = jnp.float32
    kd, vd = GLA_HEADS * GLA_DK, GLA_HEADS * GLA_DV
    q, k, v, og, gk_lr = jnp.split(u @ w_in, [kd, 2 * kd, 2 * kd + vd, 2 * kd + 2 * vd], axis=-1)
    gk = jax.nn.log_sigmoid((gk_lr @ w_gk2 + b_gk).astype(f32)) / GLA_GATE_NORM
    q = q.reshape(bsz, l, GLA_HEADS, GLA_DK).astype(f32) * (GLA_DK ** -0.5)
    k = k.reshape(bsz, l, GLA_HEADS, GLA_DK).astype(f32)
    v = v.reshape(bsz, l, GLA_HEADS, GLA_DV).astype(f32)
    gk = gk.reshape(bsz, l, GLA_HEADS, GLA_DK)
    o = chunk_gla(q, k, v, gk)
    o = rmsnorm(o, gn_w) * jax.nn.silu(og.reshape(bsz, l, GLA_HEADS, GLA_DV).astype(f32))
    return o.reshape(bsz, l, vd).astype(u.dtype) @ w_out


def hgrn2_mixer(u, lb, w_in, gn_w, w_out):
    bsz, l, _ = u.shape
    f32 = jnp.float32
    fd, vd = HGRN_HEADS * HGRN_DK, HGRN_HEADS * HGRN_DV
    q, fz, i, og = jnp.split(u @ w_in, [fd, 2 * fd, 2 * fd + vd], axis=-1)
    f = lb + (1.0 - lb) * jax.nn.sigmoid(fz.astype(f32))
    k = (1.0 - f).reshape(bsz, l, HGRN_HEADS, HGRN_DK)
    g = jnp.log(f).reshape(bsz, l, HGRN_HEADS, HGRN_DK)
    q = jax.nn.silu(q.astype(f32)).reshape(bsz, l, HGRN_HEADS, HGRN_DK) * (HGRN_DK ** -0.5)
    v = i.astype(f32).reshape(bsz, l, HGRN_HEADS, HGRN_DV)
    o = chunk_gla(q, k, v, g)
    o = rmsnorm(o, gn_w) * jax.nn.sigmoid(og.reshape(bsz, l, HGRN_HEADS, HGRN_DV).astype(f32))
    return o.reshape(bsz, l, vd).astype(u.dtype) @ w_out


def mamba2_mixer(u, w_in, conv_w, conv_b, dt_bias, a_log, d_skip, norm_w, w_out):
    bsz, l, _ = u.shape
    f32 = jnp.float32
    z, xbc, dt = jnp.split(u @ w_in, [SSM_DINNER, SSM_DINNER + SSM_CONV_DIM], axis=-1)
    xbc = lax.conv_general_dilated(
        xbc, conv_w[:, None, :], window_strides=(1,), padding=[(SSM_CONV - 1, 0)],
        dimension_numbers=('NWC', 'WIO', 'NWC'), feature_group_count=SSM_CONV_DIM) + conv_b
    xbc = jax.nn.silu(xbc)
    xs, bm, cm = jnp.split(xbc, [SSM_DINNER, SSM_DINNER + SSM_GROUPS * SSM_STATE], axis=-1)
    dt = jax.nn.softplus(dt.astype(f32) + dt_bias.astype(f32))
    a = -jnp.exp(a_log.astype(f32))
    x4 = xs.astype(f32).reshape(bsz, l, SSM_HEADS, SSM_HEADDIM)
    bm = bm.astype(f32).reshape(bsz, l, SSM_GROUPS, SSM_STATE)
    cm = cm.astype(f32).reshape(bsz, l, SSM_GROUPS, SSM_STATE)
    y = chunk_ssd(x4, dt, a, bm, cm) + d_skip.astype(f32)[:, None] * x4
    y = y.reshape(bsz, l, SSM_DINNER) * jax.nn.silu(z.astype(f32))
    y = y.reshape(bsz, l, SSM_GROUPS, SSM_DINNER // SSM_GROUPS)
    y = rmsnorm(y, norm_w.reshape(SSM_GROUPS, SSM_DINNER // SSM_GROUPS)).reshape(bsz, l, SSM_DINNER)
    return y.astype(u.dtype) @ w_out


def sq_relu_mlp(u, w_up, w_down):
    return jnp.square(jax.nn.relu(u @ w_up)) @ w_down


def setup_inputs(seed: int = 0) -> dict:
    key = jax.random.key(seed)
    ks = jax.random.split(key, 32)
    f32 = jnp.float32

    def nrm(k, shape, scale):
        return jax.random.normal(k, shape, f32) * scale

    def gain(k, shape):
        return 1.0 + 0.02 * jax.random.normal(k, shape, f32)

    res = (2.0 * DEPTH) ** -0.5
    gla_in = 2 * GLA_HEADS * GLA_DK + 2 * GLA_HEADS * GLA_DV + GLA_GATE_RANK
    hgrn_in = 2 * HGRN_HEADS * HGRN_DK + 2 * HGRN_HEADS * HGRN_DV
    ssm_in = 2 * SSM_DINNER + 2 * SSM_GROUPS * SSM_STATE + SSM_HEADS
    dt0 = jnp.exp(jax.random.uniform(ks[26], (N_SSM, SSM_HEADS), f32, math.log(1e-3), math.log(1e-1)))
    return {
        'x': nrm(ks[0], (BATCH, SEQ, D_MODEL), 1.0),
        'p': nrm(ks[1], (DEPTH, BATCH, SEQ, PLE_DIM), 1.0),
        'norm_mix': gain(ks[2], (DEPTH, D_MODEL)),
        'norm_mlp': gain(ks[3], (DEPTH, D_MODEL)),
        'norm_ple': gain(ks[4], (DEPTH, D_MODEL)),
        'norm_final': gain(ks[5], (D_MODEL,)),
        'w_up': nrm(ks[6], (DEPTH, D_MODEL, D_FF), D_MODEL ** -0.5),
        'w_down': nrm(ks[7], (DEPTH, D_FF, D_MODEL), res * D_FF ** -0.5),
        'w_ple_proj': nrm(ks[8], (DEPTH, PLE_DIM, D_MODEL), res * PLE_DIM ** -0.5),
        'w_ple_gate': nrm(ks[9], (DEPTH, D_MODEL, D_MODEL), D_MODEL ** -0.5),
        'gla_w_in': nrm(ks[10], (N_GLA, D_MODEL, gla_in), D_MODEL ** -0.5),
        'gla_w_gk2': nrm(ks[11], (N_GLA, GLA_GATE_RANK, GLA_HEADS * GLA_DK), GLA_GATE_RANK ** -0.5),
        'gla_b_gk': nrm(ks[12], (N_GLA, GLA_HEADS * GLA_DK), 0.01),
        'gla_gn': gain(ks[13], (N_GLA, GLA_DV)),
        'gla_w_out': nrm(ks[14], (N_GLA, GLA_HEADS * GLA_DV, D_MODEL), res * (GLA_HEADS * GLA_DV) ** -0.5),
        'hgrn_lb_logits': nrm(ks[15], (DEPTH, HGRN_HEADS * HGRN_DK), 0.5),
        'hgrn_w_in': nrm(ks[16], (N_HGRN, D_MODEL, hgrn_in), D_MODEL ** -0.5),
        'hgrn_gn': gain(ks[17], (N_HGRN, HGRN_DV)),
        'hgrn_w_out': nrm(ks[18], (N_HGRN, HGRN_HEADS * HGRN_DV, D_MODEL), res * (HGRN_HEADS * HGRN_DV) ** -0.5),
        'ssm_w_in': nrm(ks[19], (N_SSM, D_MODEL, ssm_in), D_MODEL ** -0.5),
        'ssm_conv_w': nrm(ks[20], (N_SSM, SSM_CONV, SSM_CONV_DIM), SSM_CONV ** -0.5),
        'ssm_conv_b': nrm(ks[21], (N_SSM, SSM_CONV_DIM), 0.01),
        'ssm_dt_bias': dt0 + jnp.log(-jnp.expm1(-dt0)),
        'ssm_a_log': jnp.log(jax.random.uniform(ks[22], (N_SSM, SSM_HEADS), f32, 1.0, 16.0)),
        'ssm_d': gain(ks[23], (N_SSM, SSM_HEADS)),
        'ssm_norm': gain(ks[24], (N_SSM, SSM_DINNER)),
        'ssm_w_out': nrm(ks[25], (N_SSM, SSM_DINNER, D_MODEL), res * SSM_DINNER ** -0.5),
    }


def reference(x, p, norm_mix, norm_mlp, norm_ple, norm_final, w_up, w_down, w_ple_proj, w_ple_gate,
              gla_w_in, gla_w_gk2, gla_b_gk, gla_gn, gla_w_out,
              hgrn_lb_logits, hgrn_w_in, hgrn_gn, hgrn_w_out,
              ssm_w_in, ssm_conv_w, ssm_conv_b, ssm_dt_bias, ssm_a_log, ssm_d, ssm_norm, ssm_w_out):
    gamma = jnp.cumsum(jax.nn.softmax(hgrn_lb_logits.astype(jnp.float32), axis=0), axis=0)
    lower_bounds = gamma - gamma[0]
    h = x
    for i in range(DEPTH):
        kind, j = i % N_MIXERS, i // N_MIXERS
        u = rmsnorm(h, norm_mix[i])
        if kind == 0:
            mix = gla_mixer(u, gla_w_in[j], gla_w_gk2[j], gla_b_gk[j], gla_gn[j], gla_w_out[j])
        elif kind == 1:
            mix = hgrn2_mixer(u, lower_bounds[i], hgrn_w_in[j], hgrn_gn[j], hgrn_w_out[j])
        else:
            mix = mamba2_mixer(u, ssm_w_in[j], ssm_conv_w[j], ssm_conv_b[j], ssm_dt_bias[j],
                               ssm_a_log[j], ssm_d[j], ssm_norm[j], ssm_w_out[j])
        h = h + mix.astype(h.dtype)
        h = h + sq_relu_mlp(rmsnorm(h, norm_mlp[i]), w_up[i], w_down[i])
        gate = jax.nn.sigmoid(rmsnorm(h, norm_ple[i]) @ w_ple_gate[i])
        h = h + gate * (p[i] @ w_ple_proj[i])
    return rmsnorm(h, norm_final)
```

```python
import numpy as np
from contextlib import ExitStack
import concourse.bass as bass
import concourse.mybir as mybir
from concourse.bass_utils import run_bass_kernel_spmd

F32 = mybir.dt.float32
BF16 = mybir.dt.bfloat16
ALU = mybir.AluOpType
AF = mybir.ActivationFunctionType

NCORES = 2
USE_CC = False
D = 2048
KC = D // 128
TT = 512
EPS = 1e-6
PLE = 256
DFF = 8192
N_MIX = 3
DEBUG = False
SSM_STAGE = 99
KINDS = None


def kind_of(i):
    if KINDS is None:
        return i % N_MIX, i // N_MIX
    k = KINDS[i]
    return k, sum(1 for x in KINDS[:i] if x == k)


class Trk:
    __slots__ = ("name", "w", "r", "dsem", "dcnt", "excl")

    def __init__(self, name="", excl=False):
        self.name = name
        self.excl = excl
        self.w = None
        self.r = {}
        self.dsem = None
        self.dcnt = 0


class Ctx:
    def __init__(self, nc, es):
        self.nc = nc
        self.es = es
        self.sems = {}
        self.engs = {}
        for nm, h in (("pe", nc.tensor), ("act", nc.scalar), ("dve", nc.vector),
                      ("pool", nc.gpsimd), ("sp", nc.sync)):
            self.sems[nm] = es.enter_context(nc.semaphore("sem_" + nm))
            self.engs[nm] = {"h": h, "cnt": 0, "known": {}}
        self.ndsem = 0
        self.phase_trks = []
        self.phase_evs = {}
        self.free_dsems = []
        self.uid = 0

    def name(self, p):
        self.uid += 1
        return "%s_%d" % (p, self.uid)

    def _dsem(self, t):
        if t.dsem is None:
            if self.free_dsems:
                t.dsem, t.dcnt = self.free_dsems.pop()
            else:
                t.dsem = "d%d" % self.ndsem
                self.ndsem += 1
                self.sems[t.dsem] = self.es.enter_context(self.nc.semaphore("dsem_%s" % t.dsem))
        return t.dsem

    def _wait(self, eng, k, v):
        e = self.engs[eng]
        if v > 0 and e["known"].get(k, 0) < v:
            e["h"].wait_ge(self.sems[k], v)
            e["known"][k] = v

    def _waits(self, eng, reads, writes):
        need = {}
        for t in reads:
            if t.w is not None:
                k, v = t.w
                if need.get(k, 0) < v:
                    need[k] = v
            if t.excl:
                for k, v in t.r.items():
                    if k != eng and need.get(k, 0) < v:
                        need[k] = v
        for t in writes:
            if t.w is not None:
                k, v = t.w
                if need.get(k, 0) < v:
                    need[k] = v
            for k, v in t.r.items():
                if need.get(k, 0) < v:
                    need[k] = v
        for k, v in need.items():
            self._wait(eng, k, v)

    def op(self, eng, fn, reads=(), writes=()):
        self._waits(eng, reads, writes)
        e = self.engs[eng]
        ins = fn(e["h"])
        e["cnt"] += 1
        ins.then_inc(self.sems[eng], 1)
        c = e["cnt"]
        for t in reads:
            if t.r.get(eng, 0) < c:
                t.r[eng] = c
        for t in writes:
            t.w = (eng, c)
            t.r = {}
        return ins

    def dma(self, q, out_ap, in_ap, out_t, in_t, **kw):
        reads = [in_t] if in_t is not None else []
        k = self._dsem(out_t)
        saved = out_t.w
        if saved is not None and saved[0] == k:
            out_t.w = None
        self._waits(q, reads, [out_t])
        out_t.w = saved
        e = self.engs[q]
        ins = e["h"].dma_start(out=out_ap, in_=in_ap, **kw)
        out_t.dcnt += 16
        ins.then_inc(self.sems[k], 16)
        if q != "sp" or in_t is not None:
            self.phase_evs[k] = out_t.dcnt
        if in_t is not None and in_t.r.get(k, 0) < out_t.dcnt:
            in_t.r[k] = out_t.dcnt
        out_t.w = (k, out_t.dcnt)
        out_t.r = {}
        return ins

    def allgather(self, out_ap, in_ap, out_t, in_t, groups):
        q = "pool"
        self._waits(q, [in_t], [out_t])
        e = self.engs[q]
        k = self._dsem(out_t)
        ins = e["h"].collective_compute("AllGather", ALU.bypass, replica_groups=groups,
                                        ins=[in_ap], outs=[out_ap])
        out_t.dcnt += 1
        ins.then_inc(self.sems[k], 1)
        if in_t.r.get(k, 0) < out_t.dcnt:
            in_t.r[k] = out_t.dcnt
        out_t.w = (k, out_t.dcnt)
        out_t.r = {}
        return ins

    def finish(self, eng, trks):
        self._waits(eng, trks, [])

    def barrier(self, extra=(), end=True):
        evs = {}
        for nm in ("pe", "act", "dve", "pool"):
            evs[nm] = self.engs[nm]["cnt"]
        for t in list(self.phase_trks) + list(extra):
            if t.dsem is not None and t.dcnt > 0:
                evs[t.dsem] = t.dcnt
        evs.update(self.phase_evs)
        self.phase_evs = {}
        for nm in ("pe", "act", "dve", "pool"):
            for k, v in evs.items():
                self._wait(nm, k, v)
        if end:
            for t in self.phase_trks:
                if t.dsem is not None:
                    self.free_dsems.append((t.dsem, t.dcnt))
                    t.dsem = None
            self.phase_trks = []


class Buf:
    def __init__(self, cx, es, name, shape, dt, psum=False, phase=True):
        nm = cx.name(name)
        if psum:
            self.t = es.enter_context(cx.nc.psum_tensor(nm, list(shape), dt))
        else:
            self.t = es.enter_context(cx.nc.sbuf_tensor(nm, list(shape), dt))
        self.k = Trk(nm, excl=psum)
        if phase:
            cx.phase_trks.append(self.k)

    def __getitem__(self, key):
        return self.t[key]


def dram(nc, cx, name, shape, dt, kind=None):
    if kind is None:
        t = nc.dram_tensor(name, list(shape), dt)
    else:
        t = nc.dram_tensor(name, list(shape), dt, kind=kind)
    return t


GLA_IN = 6160
HGRN_IN = 8192
SSM_IN = 10304


def big_weights(depth):
    out = []
    for i in range(depth):
        kind, j = kind_of(i)
        if kind == 0:
            out.append(("gla_w_in_%d" % j, D, GLA_IN))
            out.append(("gla_w_out_%d" % j, D, D))
        elif kind == 1:
            out.append(("hgrn_w_in_%d" % j, D, HGRN_IN))
            out.append(("hgrn_w_out_%d" % j, D, D))
        else:
            out.append(("ssm_w_in_%d" % j, D, SSM_IN))
            out.append(("ssm_w_out_%d" % j, 2 * D, D))
        out.append(("w_up_%d" % i, D, DFF))
        out.append(("w_down_%d" % i, DFF, D))
        out.append(("w_ple_gate_%d" % i, D, D))
        out.append(("w_ple_proj_%d" % i, PLE, D))
    return out


def col_layout(depth):
    items = []
    for i in range(depth):
        items += [("norm_mix_%d" % i, KC), ("norm_mlp_%d" % i, KC), ("norm_ple_%d" % i, KC)]
    items.append(("norm_final", KC))
    n_norm = sum(n for _, n in items)
    for i in range(depth):
        kind, j = kind_of(i)
        if kind == 0:
            items += [("gla_b_gk_%d" % j, 8), ("gla_gn_%d" % j, 4)]
        elif kind == 1:
            items += [("hgrn_gn_%d" % j, 1)]
            for l in range(depth):
                items.append(("hgrn_lb_%d" % l, KC))
        else:
            for t in range(4):
                items.append(("ssm_conv_w_%d_%d" % (j, t), 48))
            items += [("ssm_conv_b_%d" % j, 48), ("ssm_norm_%d" % j, 32)]
    lay = {}
    off = 0
    for nm, n in items:
        if nm not in lay:
            lay[nm] = (off, n)
            off += n
    return lay, off, n_norm


def to_cols(v):
    v = np.asarray(v, np.float32).reshape(-1, 128)
    return np.ascontiguousarray(v.T)


class Prog:
    def __init__(self, T, depth, enable_mix=True):
        self.T = T
        self.depth = depth
        self.NT = T // TT
        self.enable_mix = enable_mix
        self.nc = bass.Bass("TRN2", target_bir_lowering=False)
        self.lay, self.ncol, self.n_norm = col_layout(depth)
        self.dbg = {}
        self.debug = DEBUG

    def declare(self):
        nc, T, depth = self.nc, self.T, self.depth
        self.xT = nc.dram_tensor("xT", [D, T], F32, kind="ExternalInput")
        self.pT = nc.dram_tensor("pT", [depth, PLE, T], F32, kind="ExternalInput")
        self.cols_d = nc.dram_tensor("cols", [128, self.ncol], F32, kind="ExternalInput")
        self.consts_d = nc.dram_tensor("consts", [128, 4, 128], F32, kind="ExternalInput")
        self.cmask_d = nc.dram_tensor("cmask", [128, 16], F32, kind="ExternalInput")
        self.outT = nc.dram_tensor("outT", [D, T], F32, kind="ExternalOutput")
        self.wsh, self.wshb, self.wg, self.wg_k = {}, {}, {}, {}
        for nm, K, N in big_weights(depth):
            rows = K // NCORES if USE_CC else K
            self.wsh[nm] = nc.dram_tensor(nm, [rows, N], F32, kind="ExternalInput")
            if USE_CC:
                self.wshb[nm] = nc.dram_tensor(nm + "_sb", [rows, N], BF16)
            self.wg[nm] = nc.dram_tensor(nm + "_g", [K, N], BF16)
            self.wg_k[nm] = Trk(nm + "_g")
        self.hT = nc.dram_tensor("hT_scr", [D, T], F32)
        self.hT_k = [Trk("hT%d" % i) for i in range(self.NT)]
        self.out_k = [Trk("out%d" % i) for i in range(self.NT)]
        self.oloc = nc.dram_tensor("oloc_scr", [2 * D, T], F32)
        self.oloc_k = [Trk("oloc%d" % i) for i in range(self.NT)]
        self.qp = nc.dram_tensor("qp_scr", [D, T], BF16)
        self.qp_k = [Trk("qp%d" % i) for i in range(self.NT)]
        self.rows_d = {}
        self.small_d = {}
        for i in range(depth):
            kind, j = kind_of(i)
            if kind == 0:
                self.small_d["gla_w_gk2_%d" % j] = nc.dram_tensor(
                    "gla_w_gk2_%d" % j, [16, 1024], F32, kind="ExternalInput")
            if kind == 2:
                for nm in ("ssm_dt_bias", "ssm_a_log", "ssm_d"):
                    self.rows_d["%s_%d" % (nm, j)] = nc.dram_tensor(
                        "%s_%d" % (nm, j), [1, 64], F32, kind="ExternalInput")

    def dump_sb(self, name, buf, shape, dt=F32):
        if not getattr(self, "debug", False) or name in self.dbg:
            return
        d = self.nc.dram_tensor("dbg_" + name, list(shape), dt, kind="ExternalOutput")
        k = Trk(name)
        self.dbg[name] = k
        self.cx.dma("pool", d.ap(), buf[:], k, buf.k)

    def dump_dram(self, name, dt_, shape, trks, dt=F32):
        if not getattr(self, "debug", False) or name in self.dbg:
            return
        d = self.nc.dram_tensor("dbg_" + name, list(shape), dt, kind="ExternalOutput")
        k = Trk(name)
        self.dbg[name] = k
        self.cx._waits("pool", trks, [])
        self.cx.dma("pool", d.ap(), dt_.ap(), k, None)

    def col(self, name, i=0, n=1):
        off, cnt = self.lay[name]
        return self.cols[:, off + i:off + i + n]

    def build(self):
        nc = self.nc
        self.declare()
        with ExitStack() as es:
            cx = self.cx = Ctx(nc, es)
            self.cols_b = Buf(cx, es, "cols", [128, self.ncol], F32, phase=False)
            self.cols = self.cols_b.t
            self.consts_b = Buf(cx, es, "consts", [128, 4, 128], F32, phase=False)
            self.cmask_b = Buf(cx, es, "cmask", [128, 16], F32, phase=False)
            self.ones_b = Buf(cx, es, "ones", [128, 128], BF16, phase=False)
            self.ident_b = Buf(cx, es, "ident", [128, 128], BF16, phase=False)
            self.cmsk_b = Buf(cx, es, "chunkmask", [128, TT], F32, phase=False)
            self.NSLAB = 2
            self.slabs = [Buf(cx, es, "slab%d" % i, [128, 16, 512], BF16, phase=False)
                          for i in range(self.NSLAB)]
            self.slab_i = 0
            self.psum = [Buf(cx, es, "ps%d" % i, [128, 512], F32, psum=True, phase=False)
                         for i in range(8)]
            self.ps_i = 0
            self.held = set()
            self.rr = 0
            cx.dma("pool", self.cols[:], self.cols_d.ap(), self.cols_b.k, None)
            cx.dma("pool", self.consts_b[:], self.consts_d.ap(), self.consts_b.k, None)
            cx.dma("pool", self.cmask_b[:], self.cmask_d.ap(), self.cmask_b.k, None)
            cx.op("pool", lambda e: e.memset(self.ones_b[:], 1.0), [], [self.ones_b.k])
            cx.op("pool", lambda e: e.memset(self.ident_b[:], 0.0), [], [self.ident_b.k])
            cx.op("pool", lambda e: e.affine_select(
                out=self.ident_b[:], in_=self.ident_b[:], pattern=[[-1, 128]],
                compare_op=ALU.not_equal, fill=1.0, base=0, channel_multiplier=1),
                [self.ident_b.k], [self.ident_b.k])
            cx.op("pool", lambda e: e.memset(self.cmsk_b[:], 1.0), [], [self.cmsk_b.k])
            cx.op("pool", lambda e: e.memset(self.cmsk_b[:, 0:TT:64], 0.0), [], [self.cmsk_b.k])

            self.phase_weights()
            self.hsrc, self.hsrc_k = self.xT, [None] * self.NT
            for i in range(self.depth):
                kind, j = kind_of(i)
                if self.enable_mix:
                    if kind == 0:
                        self.phase_lin_mixer(i, j, "gla")
                    elif kind == 1:
                        self.phase_lin_mixer(i, j, "hgrn")
                    else:
                        self.phase_ssm(i, j)
                self.phase_mlp(i)
                self.phase_ple(i)
            self.phase_out()
            for q in ("pool", "sp", "act", "dve", "pe"):
                cx.finish(q, self.out_k + list(self.dbg.values()))
        return nc

    def next_ps(self, hold=False):
        for _ in range(16):
            b = self.psum[self.ps_i % 8]
            self.ps_i += 1
            if id(b) not in self.held:
                if hold:
                    self.held.add(id(b))
                return b
        raise RuntimeError("all PSUM banks held")

    def release(self, b):
        self.held.discard(id(b))

    def ew(self):
        self.rr += 1
        return "dve" if self.rr % 2 else "pool"

    def phase_weights(self):
        cx = self.cx
        PIECE = 4096
        with ExitStack() as es:
            NB = 3
            fin = [Buf(cx, es, "wc_in%d" % i, [128, PIECE], F32) for i in range(NB)]
            fout = [Buf(cx, es, "wc_out%d" % i, [128, PIECE], BF16) for i in range(NB)]
            n = 0
            for nm, K, N in big_weights(self.depth):
                rows = K // NCORES if USE_CC else K
                tot = rows * N
                per = tot // 128
                assert per * 128 == tot
                src = self.wsh[nm].ap().rearrange("r n -> (r n)").rearrange("(p f) -> p f", p=128)
                dstt = self.wshb[nm] if USE_CC else self.wg[nm]
                dst = dstt.ap().rearrange("r n -> (r n)").rearrange("(p f) -> p f", p=128)
                shk = Trk(nm + "_sb") if USE_CC else self.wg_k[nm]
                off = 0
                while off < per:
                    w = min(PIECE, per - off)
                    a, b = fin[n % NB], fout[n % NB]
                    cx.dma("sp", a[:, 0:w], src[:, off:off + w], a.k, None)
                    eng = ("dve", "act", "pool")[n % 3]
                    if eng == "act":
                        cx.op("act", lambda e: e.copy(out=b[:, 0:w], in_=a[:, 0:w]), [a.k], [b.k])
                    else:
                        cx.op(eng, lambda e: e.tensor_copy(out=b[:, 0:w], in_=a[:, 0:w]), [a.k], [b.k])
                    cx.dma("pool", dst[:, off:off + w], b[:, 0:w], shk, b.k)
                    off += w
                    n += 1
                if USE_CC:
                    cx.allgather(self.wg[nm].ap(), self.wshb[nm].ap(), self.wg_k[nm], shk,
                                 [list(range(NCORES))])
            cx.barrier()

    def phase_copy_in(self):
        cx = self.cx
        for it in range(self.NT):
            cx.dma("pool", self.hT.ap()[:, it * TT:(it + 1) * TT],
                   self.xT.ap()[:, it * TT:(it + 1) * TT], self.hT_k[it], None)

    def load_h(self, buf, it):
        self.load_tile(buf, self.hsrc, 0, KC, it, self.hsrc_k[it])

    def h_stored(self):
        self.hsrc, self.hsrc_k = self.hT, self.hT_k

    def load_tile(self, buf, dram_t, row0, nblk, it, trk, blk0=0):
        cx = self.cx
        src = dram_t.ap()[row0:row0 + nblk * 128, it * TT:(it + 1) * TT].rearrange(
            "(kc p) t -> p kc t", p=128)
        step = 4
        for q in range(0, nblk, step):
            n = min(step, nblk - q)
            cx.dma("pool", buf[:, blk0 + q:blk0 + q + n, :], src[:, q:q + n, :], buf.k, trk)

    def store_tile(self, buf, dram_t, row0, nblk, it, trk, blk0=0):
        cx = self.cx
        dst = dram_t.ap()[row0:row0 + nblk * 128, it * TT:(it + 1) * TT].rearrange(
            "(kc p) t -> p kc t", p=128)
        step = 4
        for q in range(0, nblk, step):
            n = min(step, nblk - q)
            cx.dma("pool", dst[:, q:q + n, :], buf[:, blk0 + q:blk0 + q + n, :], trk, buf.k)

    def rstd(self, src, blk0, nblk, n_feat, sqs, rs):
        cx = self.cx
        ps = self.next_ps()
        for b in range(nblk):
            sq = sqs[b % len(sqs)]
            cx.op("act", lambda e: e.activation(out=sq[:], in_=src[:, blk0 + b, :], func=AF.Square),
                  [src.k], [sq.k])
            cx.op("pe", lambda e: e.matmul(ps[:], self.ones_b[:], sq[:], start=(b == 0),
                                           stop=(b == nblk - 1)), [self.ones_b.k, sq.k], [ps.k])
        cx.op("act", lambda e: e.activation(out=rs[:], in_=ps[:], func=AF.Sqrt, bias=EPS,
                                            scale=1.0 / n_feat), [ps.k], [rs.k])
        cx.op("dve", lambda e: e.reciprocal(out=rs[:], in_=rs[:]), [rs.k], [rs.k])

    def norm_u(self, h, gain, u, sqs, rs):
        cx = self.cx
        self.rstd(h, 0, KC, D, sqs, rs)
        for kc in range(KC):
            cx.op("dve", lambda e: e.scalar_tensor_tensor(
                out=u[:, kc, :], in0=h[:, kc, :], scalar=self.col(gain, kc), in1=rs[:],
                op0=ALU.mult, op1=ALU.mult), [h.k, rs.k, self.cols_b.k], [u.k])

    def load_slab(self, wname, k0, nk, c0, w):
        cx = self.cx
        slab = self.slabs[self.slab_i % self.NSLAB]
        self.slab_i += 1
        src = self.wg[wname].ap()[k0 * 128:(k0 + nk) * 128, c0:c0 + w].rearrange(
            "(kc p) n -> p kc n", p=128)
        step = 4
        for q in range(0, nk, step):
            n = min(step, nk - q)
            cx.dma("sp", slab[:, q:q + n, 0:w], src[:, q:q + n, :], slab.k, self.wg_k[wname])
        return slab

    def linear_fm(self, src, nkc, wname, col0, ncols, epi, src_blk0=0):
        cx = self.cx
        ng = (ncols + 511) // 512
        nks = (nkc + 15) // 16
        for g in range(ng):
            gw = min(512, ncols - g * 512)
            nnb = (gw + 127) // 128
            pss = [self.next_ps(hold=True) for _ in range(nnb)]
            for ks in range(nks):
                nk = min(16, nkc - ks * 16)
                slab = self.load_slab(wname, ks * 16, nk, col0 + g * 512, gw)
                for nb in range(nnb):
                    bw = min(128, gw - nb * 128)
                    for kc in range(nk):
                        kk = ks * 16 + kc
                        cx.op("pe", lambda e: e.matmul(
                            pss[nb][0:bw, :], slab[:, kc, nb * 128:nb * 128 + bw],
                            src[:, src_blk0 + kk, :], start=(kk == 0), stop=(kk == nkc - 1)),
                            [slab.k, src.k], [pss[nb].k])
            for nb in range(nnb):
                bw = min(128, gw - nb * 128)
                epi(g * 4 + nb, bw, pss[nb])
                self.release(pss[nb])

    def linear_tm(self, src, wname, col0, gw, epi):
        cx = self.cx
        slab = self.load_slab(wname, 0, KC, col0, gw)
        for ts in range(TT // 128):
            ps = self.next_ps()
            for kc in range(KC):
                cx.op("pe", lambda e: e.matmul(
                    ps[:, 0:gw], src[:, kc, ts * 128:(ts + 1) * 128], slab[:, kc, 0:gw],
                    start=(kc == 0), stop=(kc == KC - 1)), [slab.k, src.k], [ps.k])
            epi(ts, ps)

    def phase_mlp(self, i):
        cx = self.cx
        with ExitStack() as es:
            h = Buf(cx, es, "h", [128, KC, TT], F32)
            u = Buf(cx, es, "u", [128, KC, TT], BF16)
            hid = Buf(cx, es, "hid", [128, DFF // 128, TT], BF16)
            sqs = [Buf(cx, es, "sq", [128, TT], BF16) for _ in range(2)]
            rs = Buf(cx, es, "rs", [128, TT], F32)
            tmps = [Buf(cx, es, "tmp", [128, TT], F32) for _ in range(3)]
            cnt = [0]
            for it in range(self.NT):
                self.load_h(h, it)
                self.norm_u(h, "norm_mlp_%d" % i, u, sqs, rs)

                def epi_up(blk, bw, ps):
                    t = tmps[cnt[0] % 3]
                    cnt[0] += 1
                    cx.op("act", lambda e: e.activation(out=t[:], in_=ps[:], func=AF.Relu), [ps.k], [t.k])
                    cx.op(self.ew(), lambda e: e.tensor_tensor(out=hid[:, blk, :], in0=t[:], in1=t[:],
                                                               op=ALU.mult), [t.k], [hid.k])
                self.linear_fm(u, KC, "w_up_%d" % i, 0, DFF, epi_up)

                def epi_down(blk, bw, ps):
                    cx.op("dve", lambda e: e.tensor_tensor(out=h[:, blk, :], in0=ps[:], in1=h[:, blk, :],
                                                           op=ALU.add), [ps.k, h.k], [h.k])
                self.linear_fm(hid, DFF // 128, "w_down_%d" % i, 0, D, epi_down)
                self.store_tile(h, self.hT, 0, KC, it, self.hT_k[it])
            self.h_stored()
            cx.barrier()

    def phase_ple(self, i):
        cx = self.cx
        with ExitStack() as es:
            h = Buf(cx, es, "h", [128, KC, TT], F32)
            u = Buf(cx, es, "u", [128, KC, TT], BF16)
            pp = Buf(cx, es, "pp", [128, KC, TT], F32)
            pf = Buf(cx, es, "pf", [128, 2, TT], F32)
            pb = Buf(cx, es, "pb", [128, 2, TT], BF16)
            sqs = [Buf(cx, es, "sq", [128, TT], BF16) for _ in range(2)]
            rs = Buf(cx, es, "rs", [128, TT], F32)
            tmps = [Buf(cx, es, "tmp", [128, TT], F32) for _ in range(3)]
            cnt = [0]
            for it in range(self.NT):
                self.load_h(h, it)
                src = self.pT.ap()[i, :, it * TT:(it + 1) * TT].rearrange("(kc p) t -> p kc t", p=128)
                cx.dma("pool", pf[:], src, pf.k, None)
                cx.op("dve", lambda e: e.tensor_copy(out=pb[:], in_=pf[:]), [pf.k], [pb.k])

                def epi_pp(blk, bw, ps):
                    cx.op("act", lambda e: e.copy(out=pp[:, blk, :], in_=ps[:]), [ps.k], [pp.k])
                self.linear_fm(pb, 2, "w_ple_proj_%d" % i, 0, D, epi_pp)
                self.norm_u(h, "norm_ple_%d" % i, u, sqs, rs)

                def epi_gate(blk, bw, ps):
                    t = tmps[cnt[0] % 3]
                    cnt[0] += 1
                    cx.op("act", lambda e: e.activation(out=t[:], in_=ps[:], func=AF.Sigmoid), [ps.k], [t.k])
                    cx.op("pool", lambda e: e.tensor_tensor(out=t[:], in0=t[:], in1=pp[:, blk, :],
                                                            op=ALU.mult), [t.k, pp.k], [t.k])
                    cx.op("dve", lambda e: e.tensor_tensor(out=h[:, blk, :], in0=t[:], in1=h[:, blk, :],
                                                           op=ALU.add), [t.k, h.k], [h.k])
                self.linear_fm(u, KC, "w_ple_gate_%d" % i, 0, D, epi_gate)
                self.store_tile(h, self.hT, 0, KC, it, self.hT_k[it])
            self.h_stored()
            cx.barrier()

    def phase_out(self):
        cx = self.cx
        with ExitStack() as es:
            h = Buf(cx, es, "h", [128, KC, TT], F32)
            o = Buf(cx, es, "o", [128, KC, TT], F32)
            sqs = [Buf(cx, es, "sq", [128, TT], BF16) for _ in range(2)]
            rs = Buf(cx, es, "rs", [128, TT], F32)
            for it in range(self.NT):
                self.load_h(h, it)
                self.norm_u(h, "norm_final", o, sqs, rs)
                self.store_tile(o, self.outT, 0, KC, it, self.out_k[it])
            cx.barrier(extra=self.out_k)

    def phase_lin_mixer(self, i, j, kind):
        cx, nc = self.cx, self.nc
        if kind == "gla":
            w_in, w_out = "gla_w_in_%d" % j, "gla_w_out_%d" % j
            NG, Hg, dkb, dvb, dv = 2, 2, 2, 4, 512
            qcol = lambda g: g * 512
            kcol = lambda g: 1024 + g * 512
            vcol = lambda g: 2048 + g * 1024
            ogcol = 4096
            qscale = 256 ** -0.5
            gn = "gla_gn_%d" % j
        else:
            w_in, w_out = "hgrn_w_in_%d" % j, "hgrn_w_out_%d" % j
            NG, Hg, dkb, dvb, dv = 4, 4, 1, 1, 128
            qcol = lambda g: g * 512
            kcol = lambda g: 2048 + g * 512
            vcol = lambda g: 4096 + g * 512
            ogcol = 6144
            qscale = 128 ** -0.5
            gn = "hgrn_gn_%d" % j
        QG = 4
        VW = Hg * dv
        NQ = NG * QG
        NS0 = NQ * dv
        NS = NS0 + NQ
        sloc = nc.dram_tensor("sloc_%d" % i, [128, NS], F32)
        sg = nc.dram_tensor("sg_%d" % i, [NCORES * 128, NS], F32)
        sloc_k, sg_k = Trk("sloc"), Trk("sg")
        mask2 = self.consts_b[:, 0, :]
        NP = TT // 128
        seqpar = USE_CC

        with ExitStack() as es:
            S32 = [[Buf(cx, es, "S32", [128, dkb, dv], F32) for _ in range(Hg)] for _ in range(NG)]
            Sbf = [[Buf(cx, es, "Sbf", [128, dkb, dv], BF16) for _ in range(Hg)] for _ in range(NG)]
            eo = [Buf(cx, es, "eo", [128, QG, 9], F32) for _ in range(NG)]
            for g in range(NG):
                for hh in range(Hg):
                    cx.op("pool", lambda e: e.memset(S32[g][hh][:], 0.0), [], [S32[g][hh].k])
                    cx.op("pool", lambda e: e.memset(Sbf[g][hh][:], 0.0), [], [Sbf[g][hh].k])
                cx.op("pool", lambda e: e.memset(eo[g][:], 1.0), [], [eo[g].k])
            if kind == "gla":
                wgk2 = Buf(cx, es, "wgk2", [16, 1024], F32)
                wgk2b = Buf(cx, es, "wgk2b", [16, 1024], BF16)
                negb = Buf(cx, es, "negb", [128, 8], F32)
                glr = Buf(cx, es, "glr", [16, TT], BF16)
                cx.dma("pool", wgk2[:], self.small_d["gla_w_gk2_%d" % j].ap(), wgk2.k, None)
                cx.op("dve", lambda e: e.tensor_copy(out=wgk2b[:], in_=wgk2[:]), [wgk2.k], [wgk2b.k])
                cx.op("dve", lambda e: e.tensor_scalar(out=negb[:], in0=self.col("gla_b_gk_%d" % j, 0, 8),
                                                       scalar1=-1.0, scalar2=None, op0=ALU.mult),
                      [self.cols_b.k], [negb.k])
            else:
                lb = Buf(cx, es, "lb", [128, KC], F32)
                oml = Buf(cx, es, "oml", [128, KC], F32)
                ex = Buf(cx, es, "ex", [128, self.depth, KC], F32)
                mx = Buf(cx, es, "mx", [128, KC], F32)
                sm = Buf(cx, es, "sm", [128, KC], F32)
                lg = lambda l: self.col("hgrn_lb_%d" % l, 0, KC)
                cx.op("dve", lambda e: e.tensor_copy(out=mx[:], in_=lg(0)), [self.cols_b.k], [mx.k])
                for l in range(1, self.depth):
                    cx.op("dve", lambda e: e.tensor_tensor(out=mx[:], in0=mx[:], in1=lg(l), op=ALU.max),
                          [mx.k, self.cols_b.k], [mx.k])
                for l in range(self.depth):
                    cx.op("dve", lambda e: e.tensor_tensor(out=ex[:, l, :], in0=lg(l), in1=mx[:], op=ALU.subtract),
                          [mx.k, self.cols_b.k], [ex.k])
                cx.op("act", lambda e: e.activation(out=ex[:], in_=ex[:], func=AF.Exp), [ex.k], [ex.k])
                cx.op("dve", lambda e: e.tensor_copy(out=sm[:], in_=ex[:, 0, :]), [ex.k], [sm.k])
                cx.op("dve", lambda e: e.memset(lb[:], 0.0), [], [lb.k])
                for l in range(1, self.depth):
                    cx.op("dve", lambda e: e.tensor_tensor(out=sm[:], in0=sm[:], in1=ex[:, l, :], op=ALU.add),
                          [sm.k, ex.k], [sm.k])
                    if l <= i:
                        cx.op("dve", lambda e: e.tensor_tensor(out=lb[:], in0=lb[:], in1=ex[:, l, :], op=ALU.add),
                              [lb.k, ex.k], [lb.k])
                cx.op("dve", lambda e: e.reciprocal(out=sm[:], in_=sm[:]), [sm.k], [sm.k])
                cx.op("dve", lambda e: e.tensor_tensor(out=lb[:], in0=lb[:], in1=sm[:], op=ALU.mult),
                      [lb.k, sm.k], [lb.k])
                cx.op("dve", lambda e: e.tensor_scalar(out=oml[:], in0=lb[:], scalar1=-1.0, scalar2=1.0,
                                                       op0=ALU.mult, op1=ALU.add), [lb.k], [oml.k])
            h = Buf(cx, es, "h", [128, KC, TT], F32)
            u = Buf(cx, es, "u", [128, KC, TT], BF16)
            sqs = [Buf(cx, es, "sq", [128, TT], BF16) for _ in range(2)]
            rs = Buf(cx, es, "rs", [128, TT], F32)
            tmps = [Buf(cx, es, "tmp", [128, TT], F32) for _ in range(6)]
            bb = Buf(cx, es, "bb", [128, QG, TT], F32)
            dl = Buf(cx, es, "dl", [128, QG, 8], F32)
            kt = Buf(cx, es, "kt", [128, QG, TT], BF16)
            kdT = Buf(cx, es, "kdT", [128, QG, TT], BF16)
            qt = Buf(cx, es, "qt", [128, QG, TT], BF16)
            qpb = Buf(cx, es, "qpb", [128, QG, TT], BF16)
            kd_tok = Buf(cx, es, "kd_tok", [128, NP, QG * 128], BF16)
            v_tok = Buf(cx, es, "v_tok", [128, NP, VW], BF16)
            atts = [Buf(cx, es, "att", [128, 128], BF16) for _ in range(4)]
            osts = [Buf(cx, es, "ost", [128, 4, 128], F32) for _ in range(2)]
            tc = [0]
            ac = [0]
            oc = [0]

            def tmp():
                tc[0] += 1
                return tmps[tc[0] % len(tmps)]

            def k_finish(g, qb, src_ap, src_k):
                te = tmp()
                cx.op("act", lambda e: e.activation(out=te[:], in_=bb[:, qb, :], func=AF.Exp, scale=-1.0),
                      [bb.k], [te.k])
                cx.op("act", lambda e: e.activation(out=dl[:, qb, :], in_=bb[:, qb, 63:TT:64], func=AF.Exp),
                      [bb.k], [dl.k])
                cx.op("dve", lambda e: e.tensor_tensor(out=te[:], in0=src_ap, in1=te[:], op=ALU.mult),
                      [src_k, te.k], [te.k])
                cx.op("pool", lambda e: e.tensor_copy(out=kt[:, qb, :], in_=te[:]), [te.k], [kt.k])
                cx.op("pool", lambda e: e.tensor_tensor(
                    out=kdT[:, qb, :].rearrange("p (c t) -> p c t", t=64),
                    in0=te[:].rearrange("p (c t) -> p c t", t=64),
                    in1=dl[:, qb, :].unsqueeze(2).to_broadcast([128, 8, 64]), op=ALU.mult),
                    [te.k, dl.k], [kdT.k])

            for it in range(self.NT):
                self.load_h(h, it)
                self.norm_u(h, "norm_mix_%d" % i, u, sqs, rs)
                if kind == "gla":
                    def epi_glr(blk, bw, ps):
                        cx.op("act", lambda e: e.copy(out=glr[:], in_=ps[0:16, :]), [ps.k], [glr.k])
                    self.linear_fm(u, KC, w_in, 6144, 16, epi_glr)
                for g in range(NG):
                    if kind == "gla":
                        for qb in range(QG):
                            gq = g * QG + qb
                            ps = self.next_ps()
                            cx.op("pe", lambda e: e.matmul(ps[:], wgk2b[0:16, gq * 128:(gq + 1) * 128], glr[:],
                                                           start=True, stop=True), [wgk2b.k, glr.k], [ps.k])
                            t1 = tmp()
                            cx.op("act", lambda e: e.activation(out=t1[:], in_=ps[:], func=AF.Exp, scale=-1.0,
                                                                bias=negb[:, gq:gq + 1]), [ps.k, negb.k], [t1.k])
                            cx.op("act", lambda e: e.activation(out=t1[:], in_=t1[:], func=AF.Ln, bias=1.0),
                                  [t1.k], [t1.k])
                            cx.op("pool", lambda e: e.tensor_scalar(out=t1[:], in0=t1[:], scalar1=-1.0 / 16.0,
                                                                    scalar2=None, op0=ALU.mult), [t1.k], [t1.k])
                            cx.op("dve", lambda e: e.tensor_tensor_scan(
                                out=bb[:, qb, :], data0=self.cmsk_b[:], data1=t1[:], initial=0.0,
                                op0=ALU.mult, op1=ALU.add), [t1.k, self.cmsk_b.k], [bb.k])

                        def epi_k(blk, bw, ps):
                            k_finish(g, blk, ps[:], ps.k)
                        self.linear_fm(u, KC, w_in, kcol(g), 512, epi_k)
                    else:
                        def epi_f(blk, bw, ps):
                            gq = g * QG + blk
                            t1, t2 = tmp(), tmp()
                            cx.op("act", lambda e: e.activation(out=t1[:], in_=ps[:], func=AF.Sigmoid), [ps.k], [t1.k])
                            cx.op("dve", lambda e: e.tensor_scalar(
                                out=t1[:], in0=t1[:], scalar1=oml[:, gq:gq + 1], scalar2=lb[:, gq:gq + 1],
                                op0=ALU.mult, op1=ALU.add), [t1.k, oml.k, lb.k], [t1.k])
                            cx.op("act", lambda e: e.activation(out=t2[:], in_=t1[:], func=AF.Ln), [t1.k], [t2.k])
                            cx.op("dve", lambda e: e.tensor_tensor_scan(
                                out=bb[:, blk, :], data0=self.cmsk_b[:], data1=t2[:], initial=0.0,
                                op0=ALU.mult, op1=ALU.add), [t2.k, self.cmsk_b.k], [bb.k])
                            cx.op("pool", lambda e: e.tensor_scalar(out=t1[:], in0=t1[:], scalar1=-1.0, scalar2=1.0,
                                                                    op0=ALU.mult, op1=ALU.add), [t1.k], [t1.k])
                            k_finish(g, blk, t1[:], t1.k)
                        self.linear_fm(u, KC, w_in, kcol(g), 512, epi_f)

                    def epi_q(blk, bw, ps):
                        te = tmp()
                        cx.op("act", lambda e: e.activation(out=te[:], in_=bb[:, blk, :], func=AF.Exp), [bb.k], [te.k])
                        if kind == "gla":
                            cx.op("dve", lambda e: e.scalar_tensor_tensor(
                                out=qt[:, blk, :], in0=ps[:], scalar=qscale, in1=te[:], op0=ALU.mult, op1=ALU.mult),
                                [ps.k, te.k], [qt.k])
                        else:
                            t2 = tmp()
                            cx.op("act", lambda e: e.activation(out=t2[:], in_=ps[:], func=AF.Silu), [ps.k], [t2.k])
                            cx.op("dve", lambda e: e.scalar_tensor_tensor(
                                out=qt[:, blk, :], in0=t2[:], scalar=qscale, in1=te[:], op0=ALU.mult, op1=ALU.mult),
                                [t2.k, te.k], [qt.k])
                    self.linear_fm(u, KC, w_in, qcol(g), 512, epi_q)

                    for c in (range(8) if seqpar else []):
                        cx.op("dve", lambda e: e.tensor_tensor(out=eo[g][:, :, c + 1], in0=eo[g][:, :, c],
                                                               in1=dl[:, :, c], op=ALU.mult), [eo[g].k, dl.k], [eo[g].k])
                    for qb in (range(QG) if seqpar else []):
                        cx.op("pool", lambda e: e.tensor_tensor(
                            out=qpb[:, qb, :].rearrange("p (c t) -> p c t", t=64),
                            in0=qt[:, qb, :].rearrange("p (c t) -> p c t", t=64),
                            in1=eo[g][:, qb, 0:8].unsqueeze(2).to_broadcast([128, 8, 64]), op=ALU.mult),
                            [qt.k, eo[g].k], [qpb.k])
                    if seqpar:
                        self.store_tile(qpb, self.qp, g * QG * 128, QG, it, self.qp_k[it])
                        cx.op("dve", lambda e: e.tensor_copy(out=eo[g][:, :, 0], in_=eo[g][:, :, 8]), [eo[g].k], [eo[g].k])

                    if it == 0 and g == 0:
                        self.dump_sb("u", u, [128, KC, TT], BF16)
                        self.dump_sb("bb", bb, [128, QG, TT])
                        self.dump_sb("kt", kt, [128, QG, TT], BF16)
                        self.dump_sb("qt", qt, [128, QG, TT], BF16)
                        self.dump_sb("kdT", kdT, [128, QG, TT], BF16)
                    for s in range(VW // 512):
                        def epi_v(ts, ps):
                            cx.op("act", lambda e: e.copy(out=v_tok[:, ts, s * 512:(s + 1) * 512], in_=ps[:]),
                                  [ps.k], [v_tok.k])
                        self.linear_tm(u, w_in, vcol(g) + s * 512, 512, epi_v)

                    for ts in range(NP):
                        ps = self.next_ps()
                        pv = ps[:, 0:256].bitcast(BF16).rearrange("p (a b) -> p a b", b=128)
                        for qb in range(QG):
                            cx.op("pe", lambda e: e.transpose(pv[:, qb, :], kdT[:, qb, ts * 128:(ts + 1) * 128],
                                                              self.ident_b[:]), [kdT.k, self.ident_b.k], [ps.k])
                        cx.op("act", lambda e: e.copy(out=kd_tok[:, ts, :].rearrange("p (a b) -> p a b", b=128),
                                                      in_=pv), [ps.k], [kd_tok.k])

                    if it == 0 and g == 0:
                        self.dump_sb("v_tok", v_tok, [128, NP, VW], BF16)
                        self.dump_sb("kd_tok", kd_tok, [128, NP, QG * 128], BF16)
                    if kind == "gla":
                        obanks = [[(hh, vb) for vb in range(4)] for hh in range(Hg)]
                    else:
                        obanks = [[(hh, 0) for hh in range(Hg)]]
                    for pr in range(NP):
                        tsl = slice(pr * 128, (pr + 1) * 128)
                        attm = {}
                        for hh in range(Hg):
                            ps = self.next_ps()
                            for jj in range(dkb):
                                qb = hh * dkb + jj
                                cx.op("pe", lambda e: e.matmul(ps[:, 0:128], kt[:, qb, tsl], qt[:, qb, tsl],
                                                               start=(jj == 0), stop=(jj == dkb - 1)),
                                      [kt.k, qt.k], [ps.k])
                            a = atts[ac[0] % 4]
                            ac[0] += 1
                            cx.op("dve", lambda e: e.tensor_tensor(out=a[:], in0=ps[:, 0:128], in1=mask2, op=ALU.mult),
                                  [ps.k, self.consts_b.k], [a.k])
                            attm[hh] = a
                        ops = []
                        for bank in obanks:
                            ps = self.next_ps(hold=True)
                            ops.append(ps)
                            for slot, (hh, vb) in enumerate(bank):
                                osl = slice(slot * 128, (slot + 1) * 128)
                                cx.op("pe", lambda e: e.matmul(
                                    ps[:, osl], v_tok[:, pr, hh * dv + vb * 128:hh * dv + (vb + 1) * 128],
                                    attm[hh][:], start=(slot == 0), stop=False, skip_group_check=True),
                                    [v_tok.k, attm[hh].k], [ps.k])
                        for c2 in range(2):
                            c = pr * 2 + c2
                            csl = slice(pr * 128 + c2 * 64, pr * 128 + (c2 + 1) * 64)
                            rows = slice(c2 * 64, (c2 + 1) * 64)
                            for bi, bank in enumerate(obanks):
                                ps = ops[bi]
                                for slot, (hh, vb) in enumerate(bank):
                                    for jj in range(dkb):
                                        qb = hh * dkb + jj
                                        cx.op("pe", lambda e: e.matmul(
                                            ps[:, slot * 128 + c2 * 64:slot * 128 + (c2 + 1) * 64],
                                            Sbf[g][hh][:, jj, vb * 128:(vb + 1) * 128], qt[:, qb, csl],
                                            start=False, stop=(jj == dkb - 1), skip_group_check=True),
                                            [Sbf[g][hh].k, qt.k], [ps.k])
                            if kind == "gla":
                                for hh in range(Hg):
                                    for jj in range(dkb):
                                        qb = hh * dkb + jj
                                        ps = self.next_ps()
                                        cx.op("pe", lambda e: e.matmul(
                                            ps[:, 0:dv], kd_tok[rows, pr, qb * 128:(qb + 1) * 128],
                                            v_tok[rows, pr, hh * dv:(hh + 1) * dv], start=True, stop=True),
                                            [kd_tok.k, v_tok.k], [ps.k])
                                        cx.op("dve", lambda e: e.scalar_tensor_tensor(
                                            out=S32[g][hh][:, jj, :], in0=S32[g][hh][:, jj, :], scalar=dl[:, qb, c:c + 1],
                                            in1=ps[:, 0:dv], op0=ALU.mult, op1=ALU.add),
                                            [S32[g][hh].k, dl.k, ps.k], [S32[g][hh].k])
                                    cx.op("act", lambda e: e.copy(out=Sbf[g][hh][:], in_=S32[g][hh][:]),
                                          [S32[g][hh].k], [Sbf[g][hh].k])
                            else:
                                ps = self.next_ps()
                                for hh in range(Hg):
                                    cx.op("pe", lambda e: e.matmul(
                                        ps[:, hh * 128:(hh + 1) * 128], kd_tok[rows, pr, hh * 128:(hh + 1) * 128],
                                        v_tok[rows, pr, hh * dv:(hh + 1) * dv], start=True, stop=True),
                                        [kd_tok.k, v_tok.k], [ps.k])
                                for hh in range(Hg):
                                    cx.op("dve", lambda e: e.scalar_tensor_tensor(
                                        out=S32[g][hh][:, 0, :], in0=S32[g][hh][:, 0, :], scalar=dl[:, hh, c:c + 1],
                                        in1=ps[:, hh * 128:(hh + 1) * 128], op0=ALU.mult, op1=ALU.add),
                                        [S32[g][hh].k, dl.k, ps.k], [S32[g][hh].k])
                                    cx.op("act", lambda e: e.copy(out=Sbf[g][hh][:], in_=S32[g][hh][:]),
                                          [S32[g][hh].k], [Sbf[g][hh].k])
                        for bi, bank in enumerate(obanks):
                            ost = osts[oc[0] % 2]
                            oc[0] += 1
                            cx.op("act", lambda e: e.copy(out=ost[:], in_=ops[bi][:].rearrange("p (a b) -> p a b", b=128)),
                                  [ops[bi].k], [ost.k])
                            hh0, vb0 = bank[0]
                            vblk0 = (g * Hg + hh0) * dvb + vb0
                            dst = self.oloc.ap()[vblk0 * 128:(vblk0 + 4) * 128,
                                                 it * TT + pr * 128:it * TT + (pr + 1) * 128].rearrange(
                                "(a p) t -> p a t", p=128)
                            cx.dma("pool", dst, ost[:], self.oloc_k[it], ost.k)
                            self.release(ops[bi])
            for g in (range(NG) if seqpar else []):
                for hh in range(Hg):
                    c0 = ((g * Hg + hh) * dkb) * dv
                    cx.dma("pool", sloc.ap()[:, c0:c0 + dkb * dv].rearrange("p (a b) -> p a b", b=dv),
                           S32[g][hh][:], sloc_k, S32[g][hh].k)
                eoc = Buf(cx, es, "eoc", [128, QG], F32)
                cx.op("dve", lambda e: e.tensor_copy(out=eoc[:], in_=eo[g][:, :, 0]), [eo[g].k], [eoc.k])
                cx.dma("pool", sloc.ap()[:, NS0 + g * QG:NS0 + (g + 1) * QG], eoc[:], sloc_k, eoc.k)
            if seqpar:
                cx.allgather(sg.ap(), sloc.ap(), sg_k, sloc_k, [list(range(NCORES))])
            self.dump_dram("oloc", self.oloc, [2 * D, self.T], self.oloc_k)
            self.dump_dram("sloc", sloc, [128, NS], [sloc_k])
            cx.barrier(extra=self.oloc_k)

        with ExitStack() as es:
            Sin = Buf(cx, es, "Sin", [128, NQ, dv], F32)
            Sinb = Buf(cx, es, "Sinb", [128, NQ, dv], BF16)
            with ExitStack() as es2:
                if not seqpar:
                    NR = 0
                else:
                    NR = NCORES
                stg = [Buf(cx, es2, "stg", [128, NS], F32) for _ in range(2)]
                dd = [Buf(cx, es2, "dd", [128, NQ], F32) for _ in range(2)]
                cx.op("pool", lambda e: e.memset(Sin[:], 0.0), [], [Sin.k])
                for r in range(NR):
                    st, d1 = stg[r % 2], dd[r % 2]
                    m = self.cmask_b[:, r:r + 1]
                    cx.dma("pool", st[:], sg.ap()[r * 128:(r + 1) * 128, :], st.k, sg_k)
                    cx.op("dve", lambda e: e.tensor_scalar(out=d1[:], in0=st[:, NS0:NS], scalar1=-1.0, scalar2=m,
                                                           op0=ALU.add, op1=ALU.mult), [st.k, self.cmask_b.k], [d1.k])
                    cx.op("dve", lambda e: e.tensor_scalar(out=d1[:], in0=d1[:], scalar1=1.0, scalar2=None,
                                                           op0=ALU.add), [d1.k], [d1.k])
                    for q in range(NQ):
                        cx.op("dve", lambda e: e.tensor_scalar(out=Sin[:, q, :], in0=Sin[:, q, :], scalar1=d1[:, q:q + 1],
                                                               scalar2=None, op0=ALU.mult), [Sin.k, d1.k], [Sin.k])
                        cx.op("dve", lambda e: e.scalar_tensor_tensor(
                            out=Sin[:, q, :], in0=st[:, q * dv:(q + 1) * dv], scalar=m, in1=Sin[:, q, :],
                            op0=ALU.mult, op1=ALU.add), [st.k, Sin.k, self.cmask_b.k], [Sin.k])
                cx.op("act", lambda e: e.copy(out=Sinb[:], in_=Sin[:]), [Sin.k], [Sinb.k])
                cx.barrier(end=False)
            h = Buf(cx, es, "h", [128, KC, TT], F32)
            u = Buf(cx, es, "u", [128, KC, TT], BF16)
            o = Buf(cx, es, "o", [128, KC, TT], F32)
            qpl = Buf(cx, es, "qpl", [128, NQ, TT], BF16)
            ogb = Buf(cx, es, "ogb", [128, KC, TT], BF16)
            sqs = [Buf(cx, es, "sq", [128, TT], BF16) for _ in range(2)]
            rss = [Buf(cx, es, "rs", [128, TT], F32) for _ in range(2)]
            tmps = [Buf(cx, es, "tmp", [128, TT], F32) for _ in range(4)]
            tc = [0]
            hpb = dv // 128
            for it in range(self.NT):
                self.load_h(h, it)
                self.load_tile(o, self.oloc, 0, KC, it, self.oloc_k[it])
                if seqpar:
                    self.load_tile(qpl, self.qp, 0, NQ, it, self.qp_k[it])
                self.norm_u(h, "norm_mix_%d" % i, u, sqs, rss[0])
                for hd in (range(NG * Hg) if seqpar else []):
                    for vb in range(hpb):
                        ps = self.next_ps()
                        for jj in range(dkb):
                            q = hd * dkb + jj
                            cx.op("pe", lambda e: e.matmul(ps[:], Sinb[:, q, vb * 128:(vb + 1) * 128], qpl[:, q, :],
                                                           start=(jj == 0), stop=(jj == dkb - 1)), [Sinb.k, qpl.k], [ps.k])
                        blk = hd * hpb + vb
                        cx.op("dve", lambda e: e.tensor_tensor(out=o[:, blk, :], in0=ps[:], in1=o[:, blk, :], op=ALU.add),
                              [ps.k, o.k], [o.k])
                cur = {"hd": -1, "rs": None}

                def epi_og(blk, bw, ps):
                    hd = blk // hpb
                    if hd != cur["hd"]:
                        cur["hd"] = hd
                        cur["rs"] = rss[hd % 2]
                        self.rstd(o, hd * hpb, hpb, dv, sqs, cur["rs"])
                    r = cur["rs"]
                    t1, t2 = tmps[tc[0] % 4], tmps[(tc[0] + 1) % 4]
                    tc[0] += 2
                    cx.op("act", lambda e: e.activation(out=t1[:], in_=ps[:],
                                                        func=(AF.Silu if kind == "gla" else AF.Sigmoid)), [ps.k], [t1.k])
                    cx.op("dve", lambda e: e.scalar_tensor_tensor(
                        out=t2[:], in0=o[:, blk, :], scalar=self.col(gn, blk % hpb), in1=r[:],
                        op0=ALU.mult, op1=ALU.mult), [o.k, r.k, self.cols_b.k], [t2.k])
                    cx.op("pool", lambda e: e.tensor_tensor(out=ogb[:, blk, :], in0=t2[:], in1=t1[:], op=ALU.mult),
                          [t1.k, t2.k], [ogb.k])
                self.linear_fm(u, KC, w_in, ogcol, D, epi_og)

                def epi_out(blk, bw, ps):
                    cx.op("dve", lambda e: e.tensor_tensor(out=h[:, blk, :], in0=ps[:], in1=h[:, blk, :], op=ALU.add),
                          [ps.k, h.k], [h.k])
                self.linear_fm(ogb, KC, w_out, 0, D, epi_out)
                self.store_tile(h, self.hT, 0, KC, it, self.hT_k[it])
            self.h_stored()
            cx.barrier()

    def phase_ssm(self, i, j):
        cx, nc = self.cx, self.nc
        w_in, w_out = "ssm_w_in_%d" % j, "ssm_w_out_%d" % j
        mask2 = self.consts_b[:, 0, :]
        selpair = self.consts_b[:, 1, :]
        sel63 = self.consts_b[:, 2, :]
        sel127 = self.consts_b[:, 3, :]
        NP = TT // 128
        G, HG, P, NST = 8, 8, 64, 128
        GI = 1

        class View:
            def __init__(self, ap, k):
                self.ap, self.k = ap, k

            def __getitem__(self, key):
                return self.ap[key]

        def bc(ap2, n):
            return ap2.unsqueeze(2).to_broadcast([128, ap2.shape[1], n])

        with ExitStack() as es:
            big1 = Buf(cx, es, "big1", [128, KC * TT], F32)
            big2 = Buf(cx, es, "big2", [128, KC * TT], F32)
            h1 = View(big1.t[:].rearrange("p (a b) -> p a b", b=TT), big1.k)
            yT = View(big1.t[:].bitcast(BF16).rearrange("p (a b) -> p a b", b=TT), big1.k)
            u = View(big2.t[:, 0:4096].bitcast(BF16).rearrange("p (a b) -> p a b", b=TT), Trk("u"))
            BT = View(big2.t[:, 4096:6144].bitcast(BF16).rearrange("p (a b) -> p a b", b=TT), Trk("BT"))
            CT = View(big2.t[:, 6144:8192].bitcast(BF16).rearrange("p (a b) -> p a b", b=TT), Trk("CT"))
            h2 = View(big2.t[:].rearrange("p (a b) -> p a b", b=TT), big2.k)
            S32 = [Buf(cx, es, "S32", [128, HG * P], F32) for _ in range(G)]
            Sbf = [Buf(cx, es, "Sbf", [128, HG * P], BF16) for _ in range(G)]
            halo = Buf(cx, es, "halo", [128, 48, 3], F32)
            identf = Buf(cx, es, "identf", [128, 128], F32)
            onesf = Buf(cx, es, "onesf", [128, 128], F32)
            dtb = Buf(cx, es, "dtb", [128, 64], F32)
            arow = Buf(cx, es, "arow", [128, 64], F32)
            ddr = Buf(cx, es, "ddr", [128, 64], F32)
            dt_tok = Buf(cx, es, "dt_tok", [128, NP, 64], F32)
            la_tok = Buf(cx, es, "la_tok", [128, NP, 64], F32)
            b_tok = Buf(cx, es, "b_tok", [128, NP, 64], F32)
            eb_tok = Buf(cx, es, "eb_tok", [128, NP, 64], F32)
            we_tok = Buf(cx, es, "we_tok", [128, NP, 64], F32)
            Dc = Buf(cx, es, "Dc", [128, NP, 2, 64], F32)
            sqs = [Buf(cx, es, "sq", [128, TT], BF16) for _ in range(1)]
            rs = Buf(cx, es, "rs", [128, TT], F32)
            tmpc = [Buf(cx, es, "tmpc", [128, TT + 3], F32) for _ in range(2)]
            acc = Buf(cx, es, "acc", [128, TT], F32)
            xTg = Buf(cx, es, "xTg", [128, 4, TT], BF16)
            x_tok = [Buf(cx, es, "x_tok", [128, NP, 512], BF16) for _ in range(GI)]
            zg = [Buf(cx, es, "zg", [128, NP, 512], F32) for _ in range(GI)]
            rel = [Buf(cx, es, "rel", [128, 8, 128], F32) for _ in range(GI)]
            Mb = [Buf(cx, es, "Mb", [128, 8, 128], BF16) for _ in range(GI)]
            xdt = [Buf(cx, es, "xdt", [128, 512], BF16) for _ in range(GI)]
            xw = [Buf(cx, es, "xw", [128, 512], BF16) for _ in range(GI)]
            ytmp = [Buf(cx, es, "ytmp", [128, 512], F32) for _ in range(GI)]
            yn = [Buf(cx, es, "yn", [128, 512], BF16) for _ in range(GI)]
            cbm = [Buf(cx, es, "cbm", [128, 128], F32) for _ in range(GI)]
            btok = [Buf(cx, es, "btok", [128, 128], BF16) for _ in range(GI)]
            ss = [Buf(cx, es, "ss", [128, 1], F32) for _ in range(GI)]
            for g in range(G):
                cx.op("pool", lambda e: e.memset(S32[g][:], 0.0), [], [S32[g].k])
                cx.op("pool", lambda e: e.memset(Sbf[g][:], 0.0), [], [Sbf[g].k])
            cx.op("pool", lambda e: e.memset(halo[:], 0.0), [], [halo.k])
            cx.op("pool", lambda e: e.memset(onesf[:], 1.0), [], [onesf.k])
            cx.op("pool", lambda e: e.memset(identf[:], 0.0), [], [identf.k])
            cx.op("pool", lambda e: e.affine_select(
                out=identf[:], in_=identf[:], pattern=[[-1, 128]], compare_op=ALU.not_equal, fill=1.0,
                base=0, channel_multiplier=1), [identf.k], [identf.k])
            cx.dma("pool", dtb[:], self.rows_d["ssm_dt_bias_%d" % j].ap().partition_broadcast(128), dtb.k, None)
            cx.dma("pool", arow[:], self.rows_d["ssm_a_log_%d" % j].ap().partition_broadcast(128), arow.k, None)
            cx.dma("pool", ddr[:], self.rows_d["ssm_d_%d" % j].ap().partition_broadcast(128), ddr.k, None)
            cx.op("act", lambda e: e.activation(out=arow[:], in_=arow[:], func=AF.Exp), [arow.k], [arow.k])
            cx.op("dve", lambda e: e.tensor_scalar(out=arow[:], in0=arow[:], scalar1=-1.0, scalar2=None, op0=ALU.mult),
                  [arow.k], [arow.k])
            cb16 = Buf(cx, es, "cb16", [128, 4, 128], BF16)
            onesb = self.ones_b
            cx.op("dve", lambda e: e.tensor_copy(out=cb16[:], in_=self.consts_b[:]), [self.consts_b.k], [cb16.k])
            la_hi = Buf(cx, es, "la_hi", [128, NP, 64], BF16)
            la_lo = Buf(cx, es, "la_lo", [128, NP, 64], BF16)
            b_hi = Buf(cx, es, "b_hi", [128, NP, 64], BF16)
            b_lo = Buf(cx, es, "b_lo", [128, NP, 64], BF16)
            hl_f = Buf(cx, es, "hl_f", [128, NP, 64], F32)
            dg_lo = Buf(cx, es, "dg_lo", [128, 8, 128], BF16)
            dg_hi = Buf(cx, es, "dg_hi", [128, 8, 128], BF16)

            def split(src, hi, lo):
                cx.op("dve", lambda e: e.tensor_copy(out=hi[:], in_=src[:]), [src.k], [hi.k])
                cx.op("dve", lambda e: e.tensor_copy(out=hl_f[:], in_=hi[:]), [hi.k], [hl_f.k])
                cx.op("dve", lambda e: e.tensor_tensor(out=lo[:], in0=src[:], in1=hl_f[:], op=ALU.subtract),
                      [src.k, hl_f.k], [lo.k])
            cc = [0]

            def conv_epi(cb, ps, dst_ap, dst_k):
                tcv = tmpc[cc[0] % 2]
                cc[0] += 1
                cx.op("pool", lambda e: e.tensor_copy(out=tcv[:, 0:3], in_=halo[:, cb, :]), [halo.k], [tcv.k])
                cx.op("act", lambda e: e.copy(out=tcv[:, 3:TT + 3], in_=ps[:]), [ps.k], [tcv.k])
                cx.op("pool", lambda e: e.tensor_copy(out=halo[:, cb, :], in_=tcv[:, TT:TT + 3]), [tcv.k], [halo.k])
                wc = lambda t: self.col("ssm_conv_w_%d_%d" % (j, t), cb)
                cx.op("dve", lambda e: e.tensor_scalar(out=acc[:], in0=tcv[:, 0:TT], scalar1=wc(0),
                                                       scalar2=self.col("ssm_conv_b_%d" % j, cb),
                                                       op0=ALU.mult, op1=ALU.add), [tcv.k, self.cols_b.k], [acc.k])
                for t in range(1, 4):
                    cx.op("dve", lambda e: e.scalar_tensor_tensor(out=acc[:], in0=tcv[:, t:t + TT], scalar=wc(t),
                                                                  in1=acc[:], op0=ALU.mult, op1=ALU.add),
                          [tcv.k, acc.k, self.cols_b.k], [acc.k])
                cx.op("act", lambda e: e.activation(out=dst_ap, in_=acc[:], func=AF.Silu), [acc.k], [dst_k])

            for it in range(self.NT):
                self.load_h(h1, it)
                self.norm_u(h1, "norm_mix_%d" % i, u, sqs, rs)

                if SSM_STAGE < 1:
                    cx.barrier(end=False)
                    continue
                def epi_bc(blk, bw, ps):
                    if blk < 8:
                        conv_epi(32 + blk, ps, BT[:, blk, :], BT.k)
                    else:
                        conv_epi(32 + blk, ps, CT[:, blk - 8, :], CT.k)
                self.linear_fm(u, KC, w_in, 8192, 2048, epi_bc)

                if SSM_STAGE < 2:
                    cx.barrier(end=False)
                    continue
                def epi_dt(ts, ps):
                    cx.op("dve", lambda e: e.tensor_tensor(out=dt_tok[:, ts, :], in0=ps[:, 0:64], in1=dtb[:], op=ALU.add),
                          [ps.k, dtb.k], [dt_tok.k])
                self.linear_tm(u, w_in, 10240, 64, epi_dt)
                cx.op("act", lambda e: e.activation(out=dt_tok[:], in_=dt_tok[:], func=AF.Exp), [dt_tok.k], [dt_tok.k])
                cx.op("act", lambda e: e.activation(out=dt_tok[:], in_=dt_tok[:], func=AF.Ln, bias=1.0), [dt_tok.k], [dt_tok.k])
                for ts in range(NP):
                    cx.op("dve", lambda e: e.tensor_tensor(out=la_tok[:, ts, :], in0=dt_tok[:, ts, :], in1=arow[:], op=ALU.mult),
                          [dt_tok.k, arow.k], [la_tok.k])
                split(la_tok, la_hi, la_lo)
                for ts in range(NP):
                    ps = self.next_ps()
                    cx.op("pe", lambda e: e.matmul(ps[:, 0:64], cb16[:, 0, :], la_hi[:, ts, :], start=True, stop=False),
                          [cb16.k, la_hi.k], [ps.k])
                    cx.op("pe", lambda e: e.matmul(ps[:, 0:64], cb16[:, 0, :], la_lo[:, ts, :], start=False, stop=True),
                          [cb16.k, la_lo.k], [ps.k])
                    cx.op("act", lambda e: e.copy(out=b_tok[:, ts, :], in_=ps[:, 0:64]), [ps.k], [b_tok.k])
                split(b_tok, b_hi, b_lo)
                cx.op("act", lambda e: e.activation(out=eb_tok[:], in_=b_tok[:], func=AF.Exp), [b_tok.k], [eb_tok.k])
                for ts in range(NP):
                    ps = self.next_ps()
                    for si in range(3):
                        for hl, src in enumerate((b_hi, b_lo)):
                            cx.op("pe", lambda e: e.matmul(ps[:, si * 64:(si + 1) * 64], cb16[:, 1 + si, :], src[:, ts, :],
                                                           start=(si == 0 and hl == 0), stop=(hl == 1),
                                                           skip_group_check=True), [cb16.k, src.k], [ps.k])
                    cx.op("dve", lambda e: e.tensor_tensor(out=we_tok[:, ts, :], in0=ps[:, 0:64], in1=b_tok[:, ts, :],
                                                           op=ALU.subtract), [ps.k, b_tok.k], [we_tok.k])
                    cx.op("act", lambda e: e.activation(out=Dc[:, ts, :, :], in_=ps[:, 64:192].rearrange("p (a b) -> p a b", b=64),
                                                        func=AF.Exp), [ps.k], [Dc.k])
                cx.op("act", lambda e: e.activation(out=we_tok[:], in_=we_tok[:], func=AF.Exp), [we_tok.k], [we_tok.k])

                if SSM_STAGE < 3:
                    cx.barrier(end=False)
                    continue
                for gp in range(G // GI):
                    gs = tuple(range(gp * GI, (gp + 1) * GI))
                    for g in gs:
                        sl = g % GI
                        def epi_x(blk, bw, ps):
                            conv_epi(g * 4 + blk, ps, xTg[:, blk, :], xTg.k)
                        self.linear_fm(u, KC, w_in, 4096 + g * 512, 512, epi_x)
                        for ts in range(NP):
                            ps = self.next_ps()
                            pv = ps[:, 0:256].bitcast(BF16).rearrange("p (a b) -> p a b", b=128)
                            for a in range(4):
                                cx.op("pe", lambda e: e.transpose(pv[:, a, :], xTg[:, a, ts * 128:(ts + 1) * 128],
                                                                  self.ident_b[:]), [xTg.k, self.ident_b.k], [ps.k])
                            cx.op("act", lambda e: e.copy(out=x_tok[sl][:, ts, :].rearrange("p (a b) -> p a b", b=128),
                                                          in_=pv), [ps.k], [x_tok[sl].k])
                        def epi_z(ts, ps):
                            cx.op("act", lambda e: e.activation(out=zg[sl][:, ts, :], in_=ps[:], func=AF.Silu),
                                  [ps.k], [zg[sl].k])
                        self.linear_tm(u, w_in, g * 512, 512, epi_z)

                    for pr in (range(NP) if SSM_STAGE >= 4 else []):
                        tsl = slice(pr * 128, (pr + 1) * 128)
                        psy, psi = {}, {}
                        for g in gs:
                            sl = g % GI
                            hs = slice(g * 8, (g + 1) * 8)
                            ps = self.next_ps()
                            pvb = ps[:, 0:64].bitcast(BF16)
                            cx.op("pe", lambda e: e.transpose(pvb, BT[:, g, tsl], self.ident_b[:]),
                                  [BT.k, self.ident_b.k], [ps.k])
                            cx.op("act", lambda e: e.copy(out=btok[sl][:], in_=pvb), [ps.k], [btok[sl].k])
                            ps = self.next_ps()
                            cx.op("pe", lambda e: e.matmul(ps[:, 0:128], BT[:, g, tsl], CT[:, g, tsl], start=True, stop=True),
                                  [BT.k, CT.k], [ps.k])
                            cx.op("dve", lambda e: e.tensor_tensor(out=cbm[sl][:], in0=ps[:, 0:128], in1=mask2, op=ALU.mult),
                                  [ps.k, self.consts_b.k], [cbm[sl].k])
                            for dgx, bx in ((dg_hi, b_hi), (dg_lo, b_lo)):
                                cx.op("dve", lambda e: e.tensor_tensor(
                                    out=dgx[:], in0=identf[:].unsqueeze(1).to_broadcast([128, 8, 128]),
                                    in1=bc(bx[:, pr, hs], 128), op=ALU.mult), [identf.k, bx.k], [dgx.k])
                            pb0, pb1 = self.next_ps(hold=True), self.next_ps(hold=True)
                            for pbx, lo4 in ((pb0, 0), (pb1, 4)):
                                for hl, dgx in enumerate((dg_hi, dg_lo)):
                                    cx.op("pe", lambda e: e.matmul(
                                        pbx[:], onesb[:], dgx[:, lo4:lo4 + 4, :].rearrange("p a b -> p (a b)"),
                                        start=(hl == 0), stop=(hl == 1)), [onesb.k, dgx.k], [pbx.k])
                            cx.op("dve", lambda e: e.tensor_tensor(
                                out=rel[sl][:, 0:4, :], in0=pb0[:].rearrange("p (a b) -> p a b", b=128),
                                in1=bc(b_tok[:, pr, g * 8:g * 8 + 4], 128), op=ALU.subtract), [pb0.k, b_tok.k], [rel[sl].k])
                            cx.op("dve", lambda e: e.tensor_tensor(
                                out=rel[sl][:, 4:8, :], in0=pb1[:].rearrange("p (a b) -> p a b", b=128),
                                in1=bc(b_tok[:, pr, g * 8 + 4:g * 8 + 8], 128), op=ALU.subtract), [pb1.k, b_tok.k], [rel[sl].k])
                            self.release(pb0)
                            self.release(pb1)
                            cx.op("pool", lambda e: e.tensor_scalar(out=rel[sl][:], in0=rel[sl][:], scalar1=0.0, scalar2=None,
                                                                    op0=ALU.min), [rel[sl].k], [rel[sl].k])
                            cx.op("act", lambda e: e.activation(out=rel[sl][:], in_=rel[sl][:], func=AF.Exp), [rel[sl].k], [rel[sl].k])
                            cx.op("pool", lambda e: e.tensor_tensor(
                                out=Mb[sl][:], in0=rel[sl][:], in1=cbm[sl][:].unsqueeze(1).to_broadcast([128, 8, 128]),
                                op=ALU.mult), [rel[sl].k, cbm[sl].k], [Mb[sl].k])
                            cx.op("pool", lambda e: e.tensor_tensor(
                                out=xdt[sl][:].rearrange("p (a b) -> p a b", b=64),
                                in0=x_tok[sl][:, pr, :].rearrange("p (a b) -> p a b", b=64),
                                in1=bc(dt_tok[:, pr, hs], 64), op=ALU.mult), [x_tok[sl].k, dt_tok.k], [xdt[sl].k])
                            cx.op("pool", lambda e: e.tensor_tensor(
                                out=xw[sl][:].rearrange("p (a b) -> p a b", b=64),
                                in0=xdt[sl][:].rearrange("p (a b) -> p a b", b=64),
                                in1=bc(we_tok[:, pr, hs], 64), op=ALU.mult), [xdt[sl].k, we_tok.k], [xw[sl].k])
                            psy[g] = self.next_ps(hold=True)
                            for h8 in range(8):
                                cx.op("pe", lambda e: e.matmul(
                                    psy[g][:, h8 * 64:(h8 + 1) * 64], Mb[sl][:, h8, :], xdt[sl][:, h8 * 64:(h8 + 1) * 64],
                                    start=(h8 == 0), stop=True, skip_group_check=True), [Mb[sl].k, xdt[sl].k], [psy[g].k])
                            psi[g] = self.next_ps(hold=True)
                        for c2 in range(2):
                            rows = slice(c2 * 64, (c2 + 1) * 64)
                            tcs = slice(pr * 128 + c2 * 64, pr * 128 + (c2 + 1) * 64)
                            for g in gs:
                                kw = {"tile_position": (0, 64)} if c2 == 1 else {}
                                cx.op("pe", lambda e: e.matmul(psi[g][rows, :], CT[:, g, tcs], Sbf[g][:], start=True, stop=True,
                                                               skip_group_check=True, **kw), [CT.k, Sbf[g].k], [psi[g].k])
                            for g in gs:
                                sl = g % GI
                                hs = slice(g * 8, (g + 1) * 8)
                                ps = self.next_ps()
                                cx.op("pe", lambda e: e.matmul(ps[:], btok[sl][rows, :], xw[sl][rows, :], start=True, stop=True),
                                      [btok[sl].k, xw[sl].k], [ps.k])
                                cx.op("dve", lambda e: e.tensor_tensor(
                                    out=S32[g][:].rearrange("p (a b) -> p a b", b=64),
                                    in0=S32[g][:].rearrange("p (a b) -> p a b", b=64),
                                    in1=bc(Dc[:, pr, c2, hs], 64), op=ALU.mult), [S32[g].k, Dc.k], [S32[g].k])
                                cx.op("dve", lambda e: e.tensor_tensor(out=S32[g][:], in0=ps[:], in1=S32[g][:], op=ALU.add),
                                      [ps.k, S32[g].k], [S32[g].k])
                                cx.op("act", lambda e: e.copy(out=Sbf[g][:], in_=S32[g][:]), [S32[g].k], [Sbf[g].k])
                        for g in gs:
                            sl = g % GI
                            hs = slice(g * 8, (g + 1) * 8)
                            y = ytmp[sl]
                            tt = View(rel[sl].t[:, 0:4, :].rearrange("p a b -> p (a b)"), rel[sl].k)
                            cx.op("dve", lambda e: e.tensor_tensor(
                                out=y[:].rearrange("p (a b) -> p a b", b=64),
                                in0=psi[g][:].rearrange("p (a b) -> p a b", b=64),
                                in1=bc(eb_tok[:, pr, hs], 64), op=ALU.mult), [psi[g].k, eb_tok.k], [y.k])
                            cx.op("dve", lambda e: e.tensor_tensor(out=y[:], in0=psy[g][:], in1=y[:], op=ALU.add),
                                  [psy[g].k, y.k], [y.k])
                            self.release(psy[g])
                            self.release(psi[g])
                            cx.op("pool", lambda e: e.tensor_tensor(
                                out=tt[:].rearrange("p (a b) -> p a b", b=64),
                                in0=x_tok[sl][:, pr, :].rearrange("p (a b) -> p a b", b=64),
                                in1=bc(ddr[:, hs], 64), op=ALU.mult), [x_tok[sl].k, ddr.k], [tt.k])
                            cx.op("pool", lambda e: e.tensor_tensor(out=y[:], in0=y[:], in1=tt[:], op=ALU.add), [y.k, tt.k], [y.k])
                            cx.op("pool", lambda e: e.tensor_tensor(out=y[:], in0=y[:], in1=zg[sl][:, pr, :], op=ALU.mult),
                                  [y.k, zg[sl].k], [y.k])
                            cx.op("pool", lambda e: e.tensor_tensor(out=tt[:], in0=y[:], in1=y[:], op=ALU.mult), [y.k], [tt.k])
                            cx.op("dve", lambda e: e.reduce_sum(out=ss[sl][:], in_=tt[:], axis=mybir.AxisListType.X),
                                  [tt.k], [ss[sl].k])
                            cx.op("act", lambda e: e.activation(out=ss[sl][:], in_=ss[sl][:], func=AF.Sqrt, bias=EPS,
                                                                scale=1.0 / 512.0), [ss[sl].k], [ss[sl].k])
                            cx.op("dve", lambda e: e.reciprocal(out=ss[sl][:], in_=ss[sl][:]), [ss[sl].k], [ss[sl].k])
                            cx.op("dve", lambda e: e.tensor_scalar(out=yn[sl][:], in0=y[:], scalar1=ss[sl][:, 0:1], scalar2=None,
                                                                   op0=ALU.mult), [y.k, ss[sl].k], [yn[sl].k])
                            ps = self.next_ps()
                            pv = ps[:, 0:256].bitcast(BF16).rearrange("p (a b) -> p a b", b=128)
                            for a in range(4):
                                cx.op("pe", lambda e: e.transpose(pv[:, a, :], yn[sl][:, a * 128:(a + 1) * 128],
                                                                  self.ident_b[:]), [yn[sl].k, self.ident_b.k], [ps.k])
                            for a in range(4):
                                cx.op("act", lambda e: e.activation(
                                    out=yT[:, g * 4 + a, tsl], in_=pv[:, a, :], func=AF.Copy,
                                    scale=self.col("ssm_norm_%d" % j, g * 4 + a)), [ps.k, self.cols_b.k], [yT.k])
                cx.barrier(end=False)
                if SSM_STAGE < 5:
                    continue
                self.load_h(h2, it)

                def epi_out(blk, bw, ps):
                    cx.op("dve", lambda e: e.tensor_tensor(out=h2[:, blk, :], in0=ps[:], in1=h2[:, blk, :], op=ALU.add),
                          [ps.k, h2.k], [h2.k])
                self.linear_fm(yT, 32, w_out, 0, D, epi_out)
                self.store_tile(h2, self.hT, 0, KC, it, self.hT_k[it])
                cx.barrier(end=False)
            self.h_stored()
            cx.barrier()


def _weight(inputs, nm):
    base, idx = nm.rsplit("_", 1)
    table = {
        "w_up": inputs["w_up"], "w_down": inputs["w_down"],
        "w_ple_proj": inputs["w_ple_proj"], "w_ple_gate": inputs["w_ple_gate"],
        "gla_w_in": inputs["gla_w_in"], "gla_w_out": inputs["gla_w_out"],
        "hgrn_w_in": inputs["hgrn_w_in"], "hgrn_w_out": inputs["hgrn_w_out"],
        "ssm_w_in": inputs["ssm_w_in"], "ssm_w_out": inputs["ssm_w_out"],
    }
    return table[base][int(idx)]


def make_consts():
    c = np.zeros((128, 4, 128), np.float32)
    s = np.arange(128)[:, None]
    t = np.arange(128)[None, :]
    c[:, 0, :] = ((s // 64 == t // 64) & (s <= t)).astype(np.float32)
    c[:, 1, :] = (((s == 63) & (t < 64)) | ((s == 127) & (t >= 64))).astype(np.float32)
    c[:, 2, :] = (s == 63).astype(np.float32) * np.ones_like(t)
    c[:, 3, :] = (s == 127).astype(np.float32) * np.ones_like(t)
    return c


def run(inputs, depth, T, enable_mix=True, trace=False):
    x = np.asarray(inputs["x"])
    p = np.asarray(inputs["p"])
    B, L, _ = x.shape
    segs = NCORES // B
    assert L == segs * T
    prog = Prog(T, depth, enable_mix)
    nc = prog.build()
    lay, ncol, _ = col_layout(depth)
    cols = np.zeros((128, ncol), np.float32)

    def put(nm, v):
        off, n = lay[nm]
        cols[:, off:off + n] = to_cols(v)
    for i in range(depth):
        put("norm_mix_%d" % i, inputs["norm_mix"][i])
        put("norm_mlp_%d" % i, inputs["norm_mlp"][i])
        put("norm_ple_%d" % i, inputs["norm_ple"][i])
        kind, j = kind_of(i)
        if kind == 0:
            put("gla_b_gk_%d" % j, inputs["gla_b_gk"][j])
            put("gla_gn_%d" % j, inputs["gla_gn"][j])
        elif kind == 1:
            put("hgrn_gn_%d" % j, inputs["hgrn_gn"][j])
            for l in range(depth):
                put("hgrn_lb_%d" % l, inputs["hgrn_lb_logits"][l])
        else:
            for t in range(4):
                put("ssm_conv_w_%d_%d" % (j, t), inputs["ssm_conv_w"][j][t])
            put("ssm_conv_b_%d" % j, inputs["ssm_conv_b"][j])
            put("ssm_norm_%d" % j, inputs["ssm_norm"][j])
    put("norm_final", inputs["norm_final"])
    consts = make_consts()
    shared = {"cols": cols, "consts": consts}
    for i in range(depth):
        kind, j = kind_of(i)
        if kind == 0:
            shared["gla_w_gk2_%d" % j] = np.ascontiguousarray(inputs["gla_w_gk2"][j], np.float32)
        if kind == 2:
            for nm in ("ssm_dt_bias", "ssm_a_log", "ssm_d"):
                shared["%s_%d" % (nm, j)] = np.ascontiguousarray(
                    np.asarray(inputs[nm][j], np.float32).reshape(1, 64))
    in_maps = []
    for c in range(NCORES):
        b, s = c // segs, c % segs
        m = dict(shared)
        m["xT"] = np.ascontiguousarray(x[b, s * T:(s + 1) * T, :].T)
        m["pT"] = np.ascontiguousarray(np.transpose(p[:depth, b, s * T:(s + 1) * T, :], (0, 2, 1)))
        cm = np.zeros((128, 16), np.float32)
        for r in range(NCORES):
            rb, rs = r // segs, r % segs
            if rb == b and rs < s:
                cm[:, r] = 1.0
            if rb == b and rs == s - 1:
                cm[:, 8 + r] = 1.0
        m["cmask"] = cm
        for nm, K, N in big_weights(depth):
            W = _weight(inputs, nm)
            if USE_CC:
                r = K // NCORES
                m[nm] = np.ascontiguousarray(W[c * r:(c + 1) * r, :], np.float32)
            else:
                m[nm] = np.ascontiguousarray(W, np.float32)
        in_maps.append(m)
    res = run_bass_kernel_spmd(nc, in_maps, core_ids=list(range(NCORES)), trace=trace)
    out = np.empty((B, L, D), np.float32)
    for c in range(NCORES):
        b, s = c // segs, c % segs
        out[b, s * T:(s + 1) * T, :] = res.results[c]["outT"].T
    if DEBUG:
        return out, res
    if trace:
        return out, res
    return out


def kernel(**inputs):
    depth = int(np.asarray(inputs["p"]).shape[0])
    B, L, _ = np.asarray(inputs["x"]).shape
    return run(inputs, depth, L * B // NCORES)
```

```python
import numpy as np
from contextlib import ExitStack
import concourse.bass as bass
import concourse.mybir as mybir
from concourse.bass_utils import run_bass_kernel_spmd

F32 = mybir.dt.float32
BF16 = mybir.dt.bfloat16
ALU = mybir.AluOpType
AF = mybir.ActivationFunctionType

NCORES = 2
USE_CC = False
D = 2048
KC = D // 128
TT = 512
EPS = 1e-6
PLE = 256
DFF = 8192
N_MIX = 3
DEBUG = False
SSM_STAGE = 99
KINDS = None


def kind_of(i):
    if KINDS is None:
        return i % N_MIX, i // N_MIX
    k = KINDS[i]
    return k, sum(1 for x in KINDS[:i] if x == k)


class Trk:
    __slots__ = ("name", "w", "r", "dsem", "dcnt", "excl")

    def __init__(self, name="", excl=False):
        self.name = name
        self.excl = excl
        self.w = None
        self.r = {}
        self.dsem = None
        self.dcnt = 0


class Ctx:
    def __init__(self, nc, es):
        self.nc = nc
        self.es = es
        self.sems = {}
        self.engs = {}
        for nm, h in (("pe", nc.tensor), ("act", nc.scalar), ("dve", nc.vector),
                      ("pool", nc.gpsimd), ("sp", nc.sync)):
            self.sems[nm] = es.enter_context(nc.semaphore("sem_" + nm))
            self.engs[nm] = {"h": h, "cnt": 0, "known": {}}
        self.ndsem = 0
        self.phase_trks = []
        self.phase_evs = {}
        self.free_dsems = []
        self.uid = 0

    def name(self, p):
        self.uid += 1
        return "%s_%d" % (p, self.uid)

    def _dsem(self, t):
        if t.dsem is None:
            if self.free_dsems:
                t.dsem, t.dcnt = self.free_dsems.pop()
            else:
                t.dsem = "d%d" % self.ndsem
                self.ndsem += 1
                self.sems[t.dsem] = self.es.enter_context(self.nc.semaphore("dsem_%s" % t.dsem))
        return t.dsem

    def _wait(self, eng, k, v):
        e = self.engs[eng]
        if k == eng and v > e["cnt"]:
            return
        if v > 0 and e["known"].get(k, 0) < v:
            e["h"].wait_ge(self.sems[k], v)
            e["known"][k] = v

    def _waits(self, eng, reads, writes):
        need = {}
        for t in reads:
            if t.w is not None:
                k, v = t.w
                if need.get(k, 0) < v:
                    need[k] = v
            if t.excl:
                for k, v in t.r.items():
                    if k != eng and need.get(k, 0) < v:
                        need[k] = v
        for t in writes:
            if t.w is not None:
                k, v = t.w
                if need.get(k, 0) < v:
                    need[k] = v
            for k, v in t.r.items():
                if need.get(k, 0) < v:
                    need[k] = v
        for k, v in need.items():
            self._wait(eng, k, v)

    def op(self, eng, fn, reads=(), writes=(), inc=True):
        self._waits(eng, reads, writes)
        e = self.engs[eng]
        ins = fn(e["h"])
        if inc:
            e["cnt"] += 1
            ins.then_inc(self.sems[eng], 1)
            c = e["cnt"]
        else:
            c = e["cnt"] + 1
        for t in reads:
            if t.r.get(eng, 0) < c:
                t.r[eng] = c
        for t in writes:
            t.w = (eng, c)
            t.r = {}
        return ins

    def dma(self, q, out_ap, in_ap, out_t, in_t, **kw):
        reads = [in_t] if in_t is not None else []
        k = self._dsem(out_t)
        saved = out_t.w
        if saved is not None and saved[0] == k:
            out_t.w = None
        self._waits(q, reads, [out_t])
        out_t.w = saved
        e = self.engs[q]
        ins = e["h"].dma_start(out=out_ap, in_=in_ap, **kw)
        out_t.dcnt += 16
        ins.then_inc(self.sems[k], 16)
        if q != "sp" or in_t is not None:
            self.phase_evs[k] = out_t.dcnt
        if in_t is not None and in_t.r.get(k, 0) < out_t.dcnt:
            in_t.r[k] = out_t.dcnt
        out_t.w = (k, out_t.dcnt)
        out_t.r = {}
        return ins

    def allgather(self, out_ap, in_ap, out_t, in_t, groups):
        q = "pool"
        self._waits(q, [in_t], [out_t])
        e = self.engs[q]
        k = self._dsem(out_t)
        ins = e["h"].collective_compute("AllGather", ALU.bypass, replica_groups=groups,
                                        ins=[in_ap], outs=[out_ap])
        out_t.dcnt += 1
        ins.then_inc(self.sems[k], 1)
        if in_t.r.get(k, 0) < out_t.dcnt:
            in_t.r[k] = out_t.dcnt
        out_t.w = (k, out_t.dcnt)
        out_t.r = {}
        return ins

    def finish(self, eng, trks):
        self._waits(eng, trks, [])

    def barrier(self, extra=(), end=True):
        evs = {}
        for nm in ("pe", "act", "dve", "pool"):
            evs[nm] = self.engs[nm]["cnt"]
        for t in list(self.phase_trks) + list(extra):
            if t.dsem is not None and t.dcnt > 0:
                evs[t.dsem] = t.dcnt
        evs.update(self.phase_evs)
        self.phase_evs = {}
        for nm in ("pe", "act", "dve", "pool"):
            for k, v in evs.items():
                self._wait(nm, k, v)
        if end:
            for t in self.phase_trks:
                if t.dsem is not None:
                    self.free_dsems.append((t.dsem, t.dcnt))
                    t.dsem = None
            self.phase_trks = []


class Buf:
    def __init__(self, cx, es, name, shape, dt, psum=False, phase=True):
        nm = cx.name(name)
        if psum:
            self.t = es.enter_context(cx.nc.psum_tensor(nm, list(shape), dt))
        else:
            self.t = es.enter_context(cx.nc.sbuf_tensor(nm, list(shape), dt))
        self.k = Trk(nm, excl=psum)
        if phase:
            cx.phase_trks.append(self.k)

    def __getitem__(self, key):
        return self.t[key]


def dram(nc, cx, name, shape, dt, kind=None):
    if kind is None:
        t = nc.dram_tensor(name, list(shape), dt)
    else:
        t = nc.dram_tensor(name, list(shape), dt, kind=kind)
    return t


GLA_IN = 6160
HGRN_IN = 8192
SSM_IN = 10304


def big_weights(depth):
    out = []
    for i in range(depth):
        kind, j = kind_of(i)
        if kind == 0:
            out.append(("gla_w_in_%d" % j, D, GLA_IN))
            out.append(("gla_w_out_%d" % j, D, D))
        elif kind == 1:
            out.append(("hgrn_w_in_%d" % j, D, HGRN_IN))
            out.append(("hgrn_w_out_%d" % j, D, D))
        else:
            out.append(("ssm_w_in_%d" % j, D, SSM_IN))
            out.append(("ssm_w_out_%d" % j, 2 * D, D))
        out.append(("w_up_%d" % i, D, DFF))
        out.append(("w_down_%d" % i, DFF, D))
        out.append(("w_ple_gate_%d" % i, D, D))
        out.append(("w_ple_proj_%d" % i, PLE, D))
    return out


def col_layout(depth):
    items = []
    for i in range(depth):
        items += [("norm_mix_%d" % i, KC), ("norm_mlp_%d" % i, KC), ("norm_ple_%d" % i, KC)]
    items.append(("norm_final", KC))
    n_norm = sum(n for _, n in items)
    for i in range(depth):
        kind, j = kind_of(i)
        if kind == 0:
            items += [("gla_b_gk_%d" % j, 8), ("gla_gn_%d" % j, 4)]
        elif kind == 1:
            items += [("hgrn_gn_%d" % j, 1)]
            for l in range(depth):
                items.append(("hgrn_lb_%d" % l, KC))
        else:
            for t in range(4):
                items.append(("ssm_conv_w_%d_%d" % (j, t), 48))
            items += [("ssm_conv_b_%d" % j, 48), ("ssm_norm_%d" % j, 32)]
    lay = {}
    off = 0
    for nm, n in items:
        if nm not in lay:
            lay[nm] = (off, n)
            off += n
    return lay, off, n_norm


def to_cols(v):
    v = np.asarray(v, np.float32).reshape(-1, 128)
    return np.ascontiguousarray(v.T)


class Prog:
    def __init__(self, T, depth, enable_mix=True):
        self.T = T
        self.depth = depth
        self.NT = T // TT
        self.enable_mix = enable_mix
        self.nc = bass.Bass("TRN2", target_bir_lowering=False)
        self.lay, self.ncol, self.n_norm = col_layout(depth)
        self.dbg = {}
        self.debug = DEBUG

    def declare(self):
        nc, T, depth = self.nc, self.T, self.depth
        self.xT = nc.dram_tensor("xT", [D, T], F32, kind="ExternalInput")
        self.pT = nc.dram_tensor("pT", [depth, PLE, T], F32, kind="ExternalInput")
        self.cols_d = nc.dram_tensor("cols", [128, self.ncol], F32, kind="ExternalInput")
        self.consts_d = nc.dram_tensor("consts", [128, 4, 128], F32, kind="ExternalInput")
        self.cmask_d = nc.dram_tensor("cmask", [128, 16], F32, kind="ExternalInput")
        self.outT = nc.dram_tensor("outT", [D, T], F32, kind="ExternalOutput")
        self.wsh, self.wshb, self.wg, self.wg_k = {}, {}, {}, {}
        for nm, K, N in big_weights(depth):
            rows = K // NCORES if USE_CC else K
            self.wsh[nm] = nc.dram_tensor(nm, [rows, N], F32, kind="ExternalInput")
            if USE_CC:
                self.wshb[nm] = nc.dram_tensor(nm + "_sb", [rows, N], BF16)
            self.wg[nm] = nc.dram_tensor(nm + "_g", [K, N], BF16)
            self.wg_k[nm] = Trk(nm + "_g")
        self.hT = nc.dram_tensor("hT_scr", [D, T], F32)
        self.hT_k = [Trk("hT%d" % i) for i in range(self.NT)]
        self.out_k = [Trk("out%d" % i) for i in range(self.NT)]
        self.oloc = nc.dram_tensor("oloc_scr", [2 * D, T], F32)
        self.oloc_k = [Trk("oloc%d" % i) for i in range(self.NT)]
        self.qp = nc.dram_tensor("qp_scr", [D, T], BF16)
        self.qp_k = [Trk("qp%d" % i) for i in range(self.NT)]
        self.rows_d = {}
        self.small_d = {}
        for i in range(depth):
            kind, j = kind_of(i)
            if kind == 0:
                self.small_d["gla_w_gk2_%d" % j] = nc.dram_tensor(
                    "gla_w_gk2_%d" % j, [16, 1024], F32, kind="ExternalInput")
            if kind == 2:
                for nm in ("ssm_dt_bias", "ssm_a_log", "ssm_d"):
                    self.rows_d["%s_%d" % (nm, j)] = nc.dram_tensor(
                        "%s_%d" % (nm, j), [1, 64], F32, kind="ExternalInput")

    def dump_sb(self, name, buf, shape, dt=F32):
        if not getattr(self, "debug", False) or name in self.dbg:
            return
        d = self.nc.dram_tensor("dbg_" + name, list(shape), dt, kind="ExternalOutput")
        k = Trk(name)
        self.dbg[name] = k
        self.cx.dma("pool", d.ap(), buf[:], k, buf.k)

    def dump_dram(self, name, dt_, shape, trks, dt=F32):
        if not getattr(self, "debug", False) or name in self.dbg:
            return
        d = self.nc.dram_tensor("dbg_" + name, list(shape), dt, kind="ExternalOutput")
        k = Trk(name)
        self.dbg[name] = k
        self.cx._waits("pool", trks, [])
        self.cx.dma("pool", d.ap(), dt_.ap(), k, None)

    def col(self, name, i=0, n=1):
        off, cnt = self.lay[name]
        return self.cols[:, off + i:off + i + n]

    def build(self):
        nc = self.nc
        self.declare()
        with ExitStack() as es:
            cx = self.cx = Ctx(nc, es)
            self.cols_b = Buf(cx, es, "cols", [128, self.ncol], F32, phase=False)
            self.cols = self.cols_b.t
            self.consts_b = Buf(cx, es, "consts", [128, 4, 128], F32, phase=False)
            self.cmask_b = Buf(cx, es, "cmask", [128, 16], F32, phase=False)
            self.ones_b = Buf(cx, es, "ones", [128, 128], BF16, phase=False)
            self.ident_b = Buf(cx, es, "ident", [128, 128], BF16, phase=False)
            self.cmsk_b = Buf(cx, es, "chunkmask", [128, TT], F32, phase=False)
            self.NSLAB = 2
            self.slabs = [Buf(cx, es, "slab%d" % i, [128, 16, 512], BF16, phase=False)
                          for i in range(self.NSLAB)]
            self.slab_i = 0
            self.psum = [Buf(cx, es, "ps%d" % i, [128, 512], F32, psum=True, phase=False)
                         for i in range(8)]
            self.ps_i = 0
            self.held = set()
            self.rr = 0
            cx.dma("pool", self.cols[:], self.cols_d.ap(), self.cols_b.k, None)
            cx.dma("pool", self.consts_b[:], self.consts_d.ap(), self.consts_b.k, None)
            cx.dma("pool", self.cmask_b[:], self.cmask_d.ap(), self.cmask_b.k, None)
            cx.op("pool", lambda e: e.memset(self.ones_b[:], 1.0), [], [self.ones_b.k])
            cx.op("pool", lambda e: e.memset(self.ident_b[:], 0.0), [], [self.ident_b.k])
            cx.op("pool", lambda e: e.affine_select(
                out=self.ident_b[:], in_=self.ident_b[:], pattern=[[-1, 128]],
                compare_op=ALU.not_equal, fill=1.0, base=0, channel_multiplier=1),
                [self.ident_b.k], [self.ident_b.k])
            cx.op("pool", lambda e: e.memset(self.cmsk_b[:], 1.0), [], [self.cmsk_b.k])
            cx.op("pool", lambda e: e.memset(self.cmsk_b[:, 0:TT:64], 0.0), [], [self.cmsk_b.k])

            self.phase_weights()
            self.hsrc, self.hsrc_k = self.xT, [None] * self.NT
            for i in range(self.depth):
                kind, j = kind_of(i)
                if self.enable_mix:
                    if kind == 0:
                        self.phase_lin_mixer(i, j, "gla")
                    elif kind == 1:
                        self.phase_lin_mixer(i, j, "hgrn")
                    else:
                        self.phase_ssm(i, j)
                self.phase_mlp(i)
                self.phase_ple(i)
            self.phase_out()
            for q in ("pool", "sp", "act", "dve", "pe"):
                cx.finish(q, self.out_k + list(self.dbg.values()))
        return nc

    def next_ps(self, hold=False):
        for _ in range(16):
            b = self.psum[self.ps_i % 8]
            self.ps_i += 1
            if id(b) not in self.held:
                if hold:
                    self.held.add(id(b))
                return b
        raise RuntimeError("all PSUM banks held")

    def release(self, b):
        self.held.discard(id(b))

    def ew(self):
        self.rr += 1
        return "dve" if self.rr % 2 else "pool"

    def phase_weights(self):
        cx = self.cx
        PIECE = 4096
        with ExitStack() as es:
            NB = 3
            fin = [Buf(cx, es, "wc_in%d" % i, [128, PIECE], F32) for i in range(NB)]
            fout = [Buf(cx, es, "wc_out%d" % i, [128, PIECE], BF16) for i in range(NB)]
            n = 0
            for nm, K, N in big_weights(self.depth):
                rows = K // NCORES if USE_CC else K
                tot = rows * N
                per = tot // 128
                assert per * 128 == tot
                src = self.wsh[nm].ap().rearrange("r n -> (r n)").rearrange("(p f) -> p f", p=128)
                dstt = self.wshb[nm] if USE_CC else self.wg[nm]
                dst = dstt.ap().rearrange("r n -> (r n)").rearrange("(p f) -> p f", p=128)
                shk = Trk(nm + "_sb") if USE_CC else self.wg_k[nm]
                off = 0
                while off < per:
                    w = min(PIECE, per - off)
                    a, b = fin[n % NB], fout[n % NB]
                    cx.dma("sp", a[:, 0:w], src[:, off:off + w], a.k, None)
                    eng = ("dve", "act", "pool")[n % 3]
                    if eng == "act":
                        cx.op("act", lambda e: e.copy(out=b[:, 0:w], in_=a[:, 0:w]), [a.k], [b.k])
                    else:
                        cx.op(eng, lambda e: e.tensor_copy(out=b[:, 0:w], in_=a[:, 0:w]), [a.k], [b.k])
                    cx.dma("pool", dst[:, off:off + w], b[:, 0:w], shk, b.k)
                    off += w
                    n += 1
                if USE_CC:
                    cx.allgather(self.wg[nm].ap(), self.wshb[nm].ap(), self.wg_k[nm], shk,
                                 [list(range(NCORES))])
            cx.barrier()

    def phase_copy_in(self):
        cx = self.cx
        for it in range(self.NT):
            cx.dma("pool", self.hT.ap()[:, it * TT:(it + 1) * TT],
                   self.xT.ap()[:, it * TT:(it + 1) * TT], self.hT_k[it], None)

    def load_h(self, buf, it):
        self.load_tile(buf, self.hsrc, 0, KC, it, self.hsrc_k[it])

    def h_stored(self):
        self.hsrc, self.hsrc_k = self.hT, self.hT_k

    def load_tile(self, buf, dram_t, row0, nblk, it, trk, blk0=0):
        cx = self.cx
        src = dram_t.ap()[row0:row0 + nblk * 128, it * TT:(it + 1) * TT].rearrange(
            "(kc p) t -> p kc t", p=128)
        step = 4
        for q in range(0, nblk, step):
            n = min(step, nblk - q)
            cx.dma("pool", buf[:, blk0 + q:blk0 + q + n, :], src[:, q:q + n, :], buf.k, trk)

    def store_tile(self, buf, dram_t, row0, nblk, it, trk, blk0=0):
        cx = self.cx
        dst = dram_t.ap()[row0:row0 + nblk * 128, it * TT:(it + 1) * TT].rearrange(
            "(kc p) t -> p kc t", p=128)
        step = 4
        for q in range(0, nblk, step):
            n = min(step, nblk - q)
            cx.dma("pool", dst[:, q:q + n, :], buf[:, blk0 + q:blk0 + q + n, :], trk, buf.k)

    def rstd(self, src, blk0, nblk, n_feat, sqs, rs):
        cx = self.cx
        ps = self.next_ps()
        for b in range(nblk):
            sq = sqs[b % len(sqs)]
            cx.op("act", lambda e: e.activation(out=sq[:], in_=src[:, blk0 + b, :], func=AF.Square),
                  [src.k], [sq.k])
            cx.op("pe", lambda e: e.matmul(ps[:], self.ones_b[:], sq[:], start=(b == 0),
                                           stop=(b == nblk - 1)), [self.ones_b.k, sq.k], [ps.k])
        cx.op("act", lambda e: e.activation(out=rs[:], in_=ps[:], func=AF.Sqrt, bias=EPS,
                                            scale=1.0 / n_feat), [ps.k], [rs.k])
        cx.op("dve", lambda e: e.reciprocal(out=rs[:], in_=rs[:]), [rs.k], [rs.k])

    def norm_u(self, h, gain, u, sqs, rs):
        cx = self.cx
        self.rstd(h, 0, KC, D, sqs, rs)
        for kc in range(KC):
            cx.op("dve", lambda e: e.scalar_tensor_tensor(
                out=u[:, kc, :], in0=h[:, kc, :], scalar=self.col(gain, kc), in1=rs[:],
                op0=ALU.mult, op1=ALU.mult), [h.k, rs.k, self.cols_b.k], [u.k])

    def load_slab(self, wname, k0, nk, c0, w):
        cx = self.cx
        slab = self.slabs[self.slab_i % self.NSLAB]
        self.slab_i += 1
        src = self.wg[wname].ap()[k0 * 128:(k0 + nk) * 128, c0:c0 + w].rearrange(
            "(kc p) n -> p kc n", p=128)
        step = 4
        for q in range(0, nk, step):
            n = min(step, nk - q)
            cx.dma("sp", slab[:, q:q + n, 0:w], src[:, q:q + n, :], slab.k, self.wg_k[wname])
        return slab

    def linear_fm(self, src, nkc, wname, col0, ncols, epi, src_blk0=0):
        cx = self.cx
        ng = (ncols + 511) // 512
        nks = (nkc + 15) // 16
        for g in range(ng):
            gw = min(512, ncols - g * 512)
            nnb = (gw + 127) // 128
            pss = [self.next_ps(hold=True) for _ in range(nnb)]
            for ks in range(nks):
                nk = min(16, nkc - ks * 16)
                slab = self.load_slab(wname, ks * 16, nk, col0 + g * 512, gw)
                for nb in range(nnb):
                    bw = min(128, gw - nb * 128)
                    for kc in range(nk):
                        kk = ks * 16 + kc
                        cx.op("pe", lambda e: e.matmul(
                            pss[nb][0:bw, :], slab[:, kc, nb * 128:nb * 128 + bw],
                            src[:, src_blk0 + kk, :], start=(kk == 0), stop=(kk == nkc - 1)),
                            [slab.k, src.k], [pss[nb].k], inc=(kc == nk - 1))
            for nb in range(nnb):
                bw = min(128, gw - nb * 128)
                epi(g * 4 + nb, bw, pss[nb])
                self.release(pss[nb])

    def linear_tm(self, src, wname, col0, gw, epi):
        cx = self.cx
        slab = self.load_slab(wname, 0, KC, col0, gw)
        for ts in range(TT // 128):
            ps = self.next_ps()
            for kc in range(KC):
                cx.op("pe", lambda e: e.matmul(
                    ps[:, 0:gw], src[:, kc, ts * 128:(ts + 1) * 128], slab[:, kc, 0:gw],
                    start=(kc == 0), stop=(kc == KC - 1)), [slab.k, src.k], [ps.k], inc=(kc == KC - 1))
            epi(ts, ps)

    def phase_mlp(self, i):
        cx = self.cx
        with ExitStack() as es:
            h = Buf(cx, es, "h", [128, KC, TT], F32)
            u = Buf(cx, es, "u", [128, KC, TT], BF16)
            hid = Buf(cx, es, "hid", [128, DFF // 128, TT], BF16)
            sqs = [Buf(cx, es, "sq", [128, TT], BF16) for _ in range(2)]
            rs = Buf(cx, es, "rs", [128, TT], F32)
            tmps = [Buf(cx, es, "tmp", [128, TT], F32) for _ in range(3)]
            cnt = [0]
            for it in range(self.NT):
                self.load_h(h, it)
                self.norm_u(h, "norm_mlp_%d" % i, u, sqs, rs)

                def epi_up(blk, bw, ps):
                    t = tmps[cnt[0] % 3]
                    cnt[0] += 1
                    cx.op("act", lambda e: e.activation(out=t[:], in_=ps[:], func=AF.Relu), [ps.k], [t.k])
                    cx.op(self.ew(), lambda e: e.tensor_tensor(out=hid[:, blk, :], in0=t[:], in1=t[:],
                                                               op=ALU.mult), [t.k], [hid.k])
                self.linear_fm(u, KC, "w_up_%d" % i, 0, DFF, epi_up)

                def epi_down(blk, bw, ps):
                    cx.op("dve", lambda e: e.tensor_tensor(out=h[:, blk, :], in0=ps[:], in1=h[:, blk, :],
                                                           op=ALU.add), [ps.k, h.k], [h.k])
                self.linear_fm(hid, DFF // 128, "w_down_%d" % i, 0, D, epi_down)
                self.store_tile(h, self.hT, 0, KC, it, self.hT_k[it])
            self.h_stored()
            cx.barrier()

    def phase_ple(self, i):
        cx = self.cx
        with ExitStack() as es:
            h = Buf(cx, es, "h", [128, KC, TT], F32)
            u = Buf(cx, es, "u", [128, KC, TT], BF16)
            pp = Buf(cx, es, "pp", [128, KC, TT], F32)
            pf = Buf(cx, es, "pf", [128, 2, TT], F32)
            pb = Buf(cx, es, "pb", [128, 2, TT], BF16)
            sqs = [Buf(cx, es, "sq", [128, TT], BF16) for _ in range(2)]
            rs = Buf(cx, es, "rs", [128, TT], F32)
            tmps = [Buf(cx, es, "tmp", [128, TT], F32) for _ in range(3)]
            cnt = [0]
            for it in range(self.NT):
                self.load_h(h, it)
                src = self.pT.ap()[i, :, it * TT:(it + 1) * TT].rearrange("(kc p) t -> p kc t", p=128)
                cx.dma("pool", pf[:], src, pf.k, None)
                cx.op("dve", lambda e: e.tensor_copy(out=pb[:], in_=pf[:]), [pf.k], [pb.k])

                def epi_pp(blk, bw, ps):
                    cx.op("act", lambda e: e.copy(out=pp[:, blk, :], in_=ps[:]), [ps.k], [pp.k])
                self.linear_fm(pb, 2, "w_ple_proj_%d" % i, 0, D, epi_pp)
                self.norm_u(h, "norm_ple_%d" % i, u, sqs, rs)

                def epi_gate(blk, bw, ps):
                    t = tmps[cnt[0] % 3]
                    cnt[0] += 1
                    cx.op("act", lambda e: e.activation(out=t[:], in_=ps[:], func=AF.Sigmoid), [ps.k], [t.k])
                    cx.op("pool", lambda e: e.tensor_tensor(out=t[:], in0=t[:], in1=pp[:, blk, :],
                                                            op=ALU.mult), [t.k, pp.k], [t.k])
                    cx.op("dve", lambda e: e.tensor_tensor(out=h[:, blk, :], in0=t[:], in1=h[:, blk, :],
                                                           op=ALU.add), [t.k, h.k], [h.k])
                self.linear_fm(u, KC, "w_ple_gate_%d" % i, 0, D, epi_gate)
                self.store_tile(h, self.hT, 0, KC, it, self.hT_k[it])
            self.h_stored()
            cx.barrier()

    def phase_out(self):
        cx = self.cx
        with ExitStack() as es:
            h = Buf(cx, es, "h", [128, KC, TT], F32)
            o = Buf(cx, es, "o", [128, KC, TT], F32)
            sqs = [Buf(cx, es, "sq", [128, TT], BF16) for _ in range(2)]
            rs = Buf(cx, es, "rs", [128, TT], F32)
            for it in range(self.NT):
                self.load_h(h, it)
                self.norm_u(h, "norm_final", o, sqs, rs)
                self.store_tile(o, self.outT, 0, KC, it, self.out_k[it])
            cx.barrier(extra=self.out_k)

    def phase_lin_mixer(self, i, j, kind):
        cx, nc = self.cx, self.nc
        if kind == "gla":
            w_in, w_out = "gla_w_in_%d" % j, "gla_w_out_%d" % j
            NG, Hg, dkb, dvb, dv = 2, 2, 2, 4, 512
            qcol = lambda g: g * 512
            kcol = lambda g: 1024 + g * 512
            vcol = lambda g: 2048 + g * 1024
            ogcol = 4096
            qscale = 256 ** -0.5
            gn = "gla_gn_%d" % j
        else:
            w_in, w_out = "hgrn_w_in_%d" % j, "hgrn_w_out_%d" % j
            NG, Hg, dkb, dvb, dv = 4, 4, 1, 1, 128
            qcol = lambda g: g * 512
            kcol = lambda g: 2048 + g * 512
            vcol = lambda g: 4096 + g * 512
            ogcol = 6144
            qscale = 128 ** -0.5
            gn = "hgrn_gn_%d" % j
        QG = 4
        VW = Hg * dv
        NQ = NG * QG
        NS0 = NQ * dv
        NS = NS0 + NQ
        sloc = nc.dram_tensor("sloc_%d" % i, [128, NS], F32)
        sg = nc.dram_tensor("sg_%d" % i, [NCORES * 128, NS], F32)
        sloc_k, sg_k = Trk("sloc"), Trk("sg")
        mask2 = self.consts_b[:, 0, :]
        NP = TT // 128
        seqpar = USE_CC

        with ExitStack() as es:
            S32 = [[Buf(cx, es, "S32", [128, dkb, dv], F32) for _ in range(Hg)] for _ in range(NG)]
            Sbf = [[Buf(cx, es, "Sbf", [128, dkb, dv], BF16) for _ in range(Hg)] for _ in range(NG)]
            eo = [Buf(cx, es, "eo", [128, QG, 9], F32) for _ in range(NG)]
            for g in range(NG):
                for hh in range(Hg):
                    cx.op("pool", lambda e: e.memset(S32[g][hh][:], 0.0), [], [S32[g][hh].k])
                    cx.op("pool", lambda e: e.memset(Sbf[g][hh][:], 0.0), [], [Sbf[g][hh].k])
                cx.op("pool", lambda e: e.memset(eo[g][:], 1.0), [], [eo[g].k])
            if kind == "gla":
                wgk2 = Buf(cx, es, "wgk2", [16, 1024], F32)
                wgk2b = Buf(cx, es, "wgk2b", [16, 1024], BF16)
                negb = Buf(cx, es, "negb", [128, 8], F32)
                glr = Buf(cx, es, "glr", [16, TT], BF16)
                cx.dma("pool", wgk2[:], self.small_d["gla_w_gk2_%d" % j].ap(), wgk2.k, None)
                cx.op("dve", lambda e: e.tensor_copy(out=wgk2b[:], in_=wgk2[:]), [wgk2.k], [wgk2b.k])
                cx.op("dve", lambda e: e.tensor_scalar(out=negb[:], in0=self.col("gla_b_gk_%d" % j, 0, 8),
                                                       scalar1=-1.0, scalar2=None, op0=ALU.mult),
                      [self.cols_b.k], [negb.k])
            else:
                lb = Buf(cx, es, "lb", [128, KC], F32)
                oml = Buf(cx, es, "oml", [128, KC], F32)
                ex = Buf(cx, es, "ex", [128, self.depth, KC], F32)
                mx = Buf(cx, es, "mx", [128, KC], F32)
                sm = Buf(cx, es, "sm", [128, KC], F32)
                lg = lambda l: self.col("hgrn_lb_%d" % l, 0, KC)
                cx.op("dve", lambda e: e.tensor_copy(out=mx[:], in_=lg(0)), [self.cols_b.k], [mx.k])
                for l in range(1, self.depth):
                    cx.op("dve", lambda e: e.tensor_tensor(out=mx[:], in0=mx[:], in1=lg(l), op=ALU.max),
                          [mx.k, self.cols_b.k], [mx.k])
                for l in range(self.depth):
                    cx.op("dve", lambda e: e.tensor_tensor(out=ex[:, l, :], in0=lg(l), in1=mx[:], op=ALU.subtract),
                          [mx.k, self.cols_b.k], [ex.k])
                cx.op("act", lambda e: e.activation(out=ex[:], in_=ex[:], func=AF.Exp), [ex.k], [ex.k])
                cx.op("dve", lambda e: e.tensor_copy(out=sm[:], in_=ex[:, 0, :]), [ex.k], [sm.k])
                cx.op("dve", lambda e: e.memset(lb[:], 0.0), [], [lb.k])
                for l in range(1, self.depth):
                    cx.op("dve", lambda e: e.tensor_tensor(out=sm[:], in0=sm[:], in1=ex[:, l, :], op=ALU.add),
                          [sm.k, ex.k], [sm.k])
                    if l <= i:
                        cx.op("dve", lambda e: e.tensor_tensor(out=lb[:], in0=lb[:], in1=ex[:, l, :], op=ALU.add),
                              [lb.k, ex.k], [lb.k])
                cx.op("dve", lambda e: e.reciprocal(out=sm[:], in_=sm[:]), [sm.k], [sm.k])
                cx.op("dve", lambda e: e.tensor_tensor(out=lb[:], in0=lb[:], in1=sm[:], op=ALU.mult),
                      [lb.k, sm.k], [lb.k])
                cx.op("dve", lambda e: e.tensor_scalar(out=oml[:], in0=lb[:], scalar1=-1.0, scalar2=1.0,
                                                       op0=ALU.mult, op1=ALU.add), [lb.k], [oml.k])
            h = Buf(cx, es, "h", [128, KC, TT], F32)
            u = Buf(cx, es, "u", [128, KC, TT], BF16)
            sqs = [Buf(cx, es, "sq", [128, TT], BF16) for _ in range(2)]
            rs = Buf(cx, es, "rs", [128, TT], F32)
            tmps = [Buf(cx, es, "tmp", [128, TT], F32) for _ in range(6)]
            bb = Buf(cx, es, "bb", [128, QG, TT], F32)
            dl = Buf(cx, es, "dl", [128, QG, 8], F32)
            kt = Buf(cx, es, "kt", [128, QG, TT], BF16)
            kdT = Buf(cx, es, "kdT", [128, QG, TT], BF16)
            qt = Buf(cx, es, "qt", [128, QG, TT], BF16)
            qpb = Buf(cx, es, "qpb", [128, QG, TT], BF16)
            kd_tok = Buf(cx, es, "kd_tok", [128, NP, QG * 128], BF16)
            v_tok = Buf(cx, es, "v_tok", [128, NP, VW], BF16)
            atts = [Buf(cx, es, "att", [128, 128], BF16) for _ in range(4)]
            osts = [Buf(cx, es, "ost", [128, 4, 128], F32) for _ in range(2)]
            tc = [0]
            ac = [0]
            oc = [0]

            def tmp():
                tc[0] += 1
                return tmps[tc[0] % len(tmps)]

            def k_finish(g, qb, src_ap, src_k):
                te = tmp()
                cx.op("act", lambda e: e.activation(out=te[:], in_=bb[:, qb, :], func=AF.Exp, scale=-1.0),
                      [bb.k], [te.k])
                cx.op("act", lambda e: e.activation(out=dl[:, qb, :], in_=bb[:, qb, 63:TT:64], func=AF.Exp),
                      [bb.k], [dl.k])
                cx.op("dve", lambda e: e.tensor_tensor(out=te[:], in0=src_ap, in1=te[:], op=ALU.mult),
                      [src_k, te.k], [te.k])
                cx.op("pool", lambda e: e.tensor_copy(out=kt[:, qb, :], in_=te[:]), [te.k], [kt.k])
                cx.op("pool", lambda e: e.tensor_tensor(
                    out=kdT[:, qb, :].rearrange("p (c t) -> p c t", t=64),
                    in0=te[:].rearrange("p (c t) -> p c t", t=64),
                    in1=dl[:, qb, :].unsqueeze(2).to_broadcast([128, 8, 64]), op=ALU.mult),
                    [te.k, dl.k], [kdT.k])

            for it in range(self.NT):
                self.load_h(h, it)
                self.norm_u(h, "norm_mix_%d" % i, u, sqs, rs)
                if kind == "gla":
                    def epi_glr(blk, bw, ps):
                        cx.op("act", lambda e: e.copy(out=glr[:], in_=ps[0:16, :]), [ps.k], [glr.k])
                    self.linear_fm(u, KC, w_in, 6144, 16, epi_glr)
                for g in range(NG):
                    if kind == "gla":
                        for qb in range(QG):
                            gq = g * QG + qb
                            ps = self.next_ps()
                            cx.op("pe", lambda e: e.matmul(ps[:], wgk2b[0:16, gq * 128:(gq + 1) * 128], glr[:],
                                                           start=True, stop=True), [wgk2b.k, glr.k], [ps.k])
                            t1 = tmp()
                            cx.op("act", lambda e: e.activation(out=t1[:], in_=ps[:], func=AF.Exp, scale=-1.0,
                                                                bias=negb[:, gq:gq + 1]), [ps.k, negb.k], [t1.k])
                            cx.op("act", lambda e: e.activation(out=t1[:], in_=t1[:], func=AF.Ln, bias=1.0),
                                  [t1.k], [t1.k])
                            cx.op("pool", lambda e: e.tensor_scalar(out=t1[:], in0=t1[:], scalar1=-1.0 / 16.0,
                                                                    scalar2=None, op0=ALU.mult), [t1.k], [t1.k])
                            cx.op("dve", lambda e: e.tensor_tensor_scan(
                                out=bb[:, qb, :], data0=self.cmsk_b[:], data1=t1[:], initial=0.0,
                                op0=ALU.mult, op1=ALU.add), [t1.k, self.cmsk_b.k], [bb.k])

                        def epi_k(blk, bw, ps):
                            k_finish(g, blk, ps[:], ps.k)
                        self.linear_fm(u, KC, w_in, kcol(g), 512, epi_k)
                    else:
                        def epi_f(blk, bw, ps):
                            gq = g * QG + blk
                            t1, t2 = tmp(), tmp()
                            cx.op("act", lambda e: e.activation(out=t1[:], in_=ps[:], func=AF.Sigmoid), [ps.k], [t1.k])
                            cx.op("dve", lambda e: e.tensor_scalar(
                                out=t1[:], in0=t1[:], scalar1=oml[:, gq:gq + 1], scalar2=lb[:, gq:gq + 1],
                                op0=ALU.mult, op1=ALU.add), [t1.k, oml.k, lb.k], [t1.k])
                            cx.op("act", lambda e: e.activation(out=t2[:], in_=t1[:], func=AF.Ln), [t1.k], [t2.k])
                            cx.op("dve", lambda e: e.tensor_tensor_scan(
                                out=bb[:, blk, :], data0=self.cmsk_b[:], data1=t2[:], initial=0.0,
                                op0=ALU.mult, op1=ALU.add), [t2.k, self.cmsk_b.k], [bb.k])
                            cx.op("pool", lambda e: e.tensor_scalar(out=t1[:], in0=t1[:], scalar1=-1.0, scalar2=1.0,
                                                                    op0=ALU.mult, op1=ALU.add), [t1.k], [t1.k])
                            k_finish(g, blk, t1[:], t1.k)
                        self.linear_fm(u, KC, w_in, kcol(g), 512, epi_f)

                    def epi_q(blk, bw, ps):
                        te = tmp()
                        cx.op("act", lambda e: e.activation(out=te[:], in_=bb[:, blk, :], func=AF.Exp), [bb.k], [te.k])
                        if kind == "gla":
                            cx.op("dve", lambda e: e.scalar_tensor_tensor(
                                out=qt[:, blk, :], in0=ps[:], scalar=qscale, in1=te[:], op0=ALU.mult, op1=ALU.mult),
                                [ps.k, te.k], [qt.k])
                        else:
                            t2 = tmp()
                            cx.op("act", lambda e: e.activation(out=t2[:], in_=ps[:], func=AF.Silu), [ps.k], [t2.k])
                            cx.op("dve", lambda e: e.scalar_tensor_tensor(
                                out=qt[:, blk, :], in0=t2[:], scalar=qscale, in1=te[:], op0=ALU.mult, op1=ALU.mult),
                                [t2.k, te.k], [qt.k])
                    self.linear_fm(u, KC, w_in, qcol(g), 512, epi_q)

                    for c in (range(8) if seqpar else []):
                        cx.op("dve", lambda e: e.tensor_tensor(out=eo[g][:, :, c + 1], in0=eo[g][:, :, c],
                                                               in1=dl[:, :, c], op=ALU.mult), [eo[g].k, dl.k], [eo[g].k])
                    for qb in (range(QG) if seqpar else []):
                        cx.op("pool", lambda e: e.tensor_tensor(
                            out=qpb[:, qb, :].rearrange("p (c t) -> p c t", t=64),
                            in0=qt[:, qb, :].rearrange("p (c t) -> p c t", t=64),
                            in1=eo[g][:, qb, 0:8].unsqueeze(2).to_broadcast([128, 8, 64]), op=ALU.mult),
                            [qt.k, eo[g].k], [qpb.k])
                    if seqpar:
                        self.store_tile(qpb, self.qp, g * QG * 128, QG, it, self.qp_k[it])
                        cx.op("dve", lambda e: e.tensor_copy(out=eo[g][:, :, 0], in_=eo[g][:, :, 8]), [eo[g].k], [eo[g].k])

                    if it == 0 and g == 0:
                        self.dump_sb("u", u, [128, KC, TT], BF16)
                        self.dump_sb("bb", bb, [128, QG, TT])
                        self.dump_sb("kt", kt, [128, QG, TT], BF16)
                        self.dump_sb("qt", qt, [128, QG, TT], BF16)
                        self.dump_sb("kdT", kdT, [128, QG, TT], BF16)
                    for s in range(VW // 512):
                        def epi_v(ts, ps):
                            cx.op("act", lambda e: e.copy(out=v_tok[:, ts, s * 512:(s + 1) * 512], in_=ps[:]),
                                  [ps.k], [v_tok.k])
                        self.linear_tm(u, w_in, vcol(g) + s * 512, 512, epi_v)

                    for ts in range(NP):
                        ps = self.next_ps()
                        pv = ps[:, 0:256].bitcast(BF16).rearrange("p (a b) -> p a b", b=128)
                        for qb in range(QG):
                            cx.op("pe", lambda e: e.transpose(pv[:, qb, :], kdT[:, qb, ts * 128:(ts + 1) * 128],
                                                              self.ident_b[:]), [kdT.k, self.ident_b.k], [ps.k])
                        cx.op("act", lambda e: e.copy(out=kd_tok[:, ts, :].rearrange("p (a b) -> p a b", b=128),
                                                      in_=pv), [ps.k], [kd_tok.k])

                    if it == 0 and g == 0:
                        self.dump_sb("v_tok", v_tok, [128, NP, VW], BF16)
                        self.dump_sb("kd_tok", kd_tok, [128, NP, QG * 128], BF16)
                    if kind == "gla":
                        obanks = [[(hh, vb) for vb in range(4)] for hh in range(Hg)]
                    else:
                        obanks = [[(hh, 0) for hh in range(Hg)]]
                    for pr in range(NP):
                        tsl = slice(pr * 128, (pr + 1) * 128)
                        attm = {}
                        for hh in range(Hg):
                            ps = self.next_ps()
                            for jj in range(dkb):
                                qb = hh * dkb + jj
                                cx.op("pe", lambda e: e.matmul(ps[:, 0:128], kt[:, qb, tsl], qt[:, qb, tsl],
                                                               start=(jj == 0), stop=(jj == dkb - 1)),
                                      [kt.k, qt.k], [ps.k])
                            a = atts[ac[0] % 4]
                            ac[0] += 1
                            cx.op("dve", lambda e: e.tensor_tensor(out=a[:], in0=ps[:, 0:128], in1=mask2, op=ALU.mult),
                                  [ps.k, self.consts_b.k], [a.k])
                            attm[hh] = a
                        ops = []
                        for bank in obanks:
                            ps = self.next_ps(hold=True)
                            ops.append(ps)
                            for slot, (hh, vb) in enumerate(bank):
                                osl = slice(slot * 128, (slot + 1) * 128)
                                cx.op("pe", lambda e: e.matmul(
                                    ps[:, osl], v_tok[:, pr, hh * dv + vb * 128:hh * dv + (vb + 1) * 128],
                                    attm[hh][:], start=(slot == 0), stop=False, skip_group_check=True),
                                    [v_tok.k, attm[hh].k], [ps.k])
                        for c2 in range(2):
                            c = pr * 2 + c2
                            csl = slice(pr * 128 + c2 * 64, pr * 128 + (c2 + 1) * 64)
                            rows = slice(c2 * 64, (c2 + 1) * 64)
                            for bi, bank in enumerate(obanks):
                                ps = ops[bi]
                                for slot, (hh, vb) in enumerate(bank):
                                    for jj in range(dkb):
                                        qb = hh * dkb + jj
                                        cx.op("pe", lambda e: e.matmul(
                                            ps[:, slot * 128 + c2 * 64:slot * 128 + (c2 + 1) * 64],
                                            Sbf[g][hh][:, jj, vb * 128:(vb + 1) * 128], qt[:, qb, csl],
                                            start=False, stop=(jj == dkb - 1), skip_group_check=True),
                                            [Sbf[g][hh].k, qt.k], [ps.k])
                            if kind == "gla":
                                for hh in range(Hg):
                                    for jj in range(dkb):
                                        qb = hh * dkb + jj
                                        ps = self.next_ps()
                                        cx.op("pe", lambda e: e.matmul(
                                            ps[:, 0:dv], kd_tok[rows, pr, qb * 128:(qb + 1) * 128],
                                            v_tok[rows, pr, hh * dv:(hh + 1) * dv], start=True, stop=True),
                                            [kd_tok.k, v_tok.k], [ps.k])
                                        cx.op("dve", lambda e: e.scalar_tensor_tensor(
                                            out=S32[g][hh][:, jj, :], in0=S32[g][hh][:, jj, :], scalar=dl[:, qb, c:c + 1],
                                            in1=ps[:, 0:dv], op0=ALU.mult, op1=ALU.add),
                                            [S32[g][hh].k, dl.k, ps.k], [S32[g][hh].k])
                                    cx.op("act", lambda e: e.copy(out=Sbf[g][hh][:], in_=S32[g][hh][:]),
                                          [S32[g][hh].k], [Sbf[g][hh].k])
                            else:
                                ps = self.next_ps()
                                for hh in range(Hg):
                                    cx.op("pe", lambda e: e.matmul(
                                        ps[:, hh * 128:(hh + 1) * 128], kd_tok[rows, pr, hh * 128:(hh + 1) * 128],
                                        v_tok[rows, pr, hh * dv:(hh + 1) * dv], start=True, stop=True),
                                        [kd_tok.k, v_tok.k], [ps.k])
                                for hh in range(Hg):
                                    cx.op("dve", lambda e: e.scalar_tensor_tensor(
                                        out=S32[g][hh][:, 0, :], in0=S32[g][hh][:, 0, :], scalar=dl[:, hh, c:c + 1],
                                        in1=ps[:, hh * 128:(hh + 1) * 128], op0=ALU.mult, op1=ALU.add),
                                        [S32[g][hh].k, dl.k, ps.k], [S32[g][hh].k])
                                    cx.op("act", lambda e: e.copy(out=Sbf[g][hh][:], in_=S32[g][hh][:]),
                                          [S32[g][hh].k], [Sbf[g][hh].k])
                        for bi, bank in enumerate(obanks):
                            ost = osts[oc[0] % 2]
                            oc[0] += 1
                            cx.op("act", lambda e: e.copy(out=ost[:], in_=ops[bi][:].rearrange("p (a b) -> p a b", b=128)),
                                  [ops[bi].k], [ost.k])
                            hh0, vb0 = bank[0]
                            vblk0 = (g * Hg + hh0) * dvb + vb0
                            dst = self.oloc.ap()[vblk0 * 128:(vblk0 + 4) * 128,
                                                 it * TT + pr * 128:it * TT + (pr + 1) * 128].rearrange(
                                "(a p) t -> p a t", p=128)
                            cx.dma("pool", dst, ost[:], self.oloc_k[it], ost.k)
                            self.release(ops[bi])
            for g in (range(NG) if seqpar else []):
                for hh in range(Hg):
                    c0 = ((g * Hg + hh) * dkb) * dv
                    cx.dma("pool", sloc.ap()[:, c0:c0 + dkb * dv].rearrange("p (a b) -> p a b", b=dv),
                           S32[g][hh][:], sloc_k, S32[g][hh].k)
                eoc = Buf(cx, es, "eoc", [128, QG], F32)
                cx.op("dve", lambda e: e.tensor_copy(out=eoc[:], in_=eo[g][:, :, 0]), [eo[g].k], [eoc.k])
                cx.dma("pool", sloc.ap()[:, NS0 + g * QG:NS0 + (g + 1) * QG], eoc[:], sloc_k, eoc.k)
            if seqpar:
                cx.allgather(sg.ap(), sloc.ap(), sg_k, sloc_k, [list(range(NCORES))])
            self.dump_dram("oloc", self.oloc, [2 * D, self.T], self.oloc_k)
            self.dump_dram("sloc", sloc, [128, NS], [sloc_k])
            cx.barrier(extra=self.oloc_k)

        with ExitStack() as es:
            Sin = Buf(cx, es, "Sin", [128, NQ, dv], F32)
            Sinb = Buf(cx, es, "Sinb", [128, NQ, dv], BF16)
            with ExitStack() as es2:
                if not seqpar:
                    NR = 0
                else:
                    NR = NCORES
                stg = [Buf(cx, es2, "stg", [128, NS], F32) for _ in range(2)]
                dd = [Buf(cx, es2, "dd", [128, NQ], F32) for _ in range(2)]
                cx.op("pool", lambda e: e.memset(Sin[:], 0.0), [], [Sin.k])
                for r in range(NR):
                    st, d1 = stg[r % 2], dd[r % 2]
                    m = self.cmask_b[:, r:r + 1]
                    cx.dma("pool", st[:], sg.ap()[r * 128:(r + 1) * 128, :], st.k, sg_k)
                    cx.op("dve", lambda e: e.tensor_scalar(out=d1[:], in0=st[:, NS0:NS], scalar1=-1.0, scalar2=m,
                                                           op0=ALU.add, op1=ALU.mult), [st.k, self.cmask_b.k], [d1.k])
                    cx.op("dve", lambda e: e.tensor_scalar(out=d1[:], in0=d1[:], scalar1=1.0, scalar2=None,
                                                           op0=ALU.add), [d1.k], [d1.k])
                    for q in range(NQ):
                        cx.op("dve", lambda e: e.tensor_scalar(out=Sin[:, q, :], in0=Sin[:, q, :], scalar1=d1[:, q:q + 1],
                                                               scalar2=None, op0=ALU.mult), [Sin.k, d1.k], [Sin.k])
                        cx.op("dve", lambda e: e.scalar_tensor_tensor(
                            out=Sin[:, q, :], in0=st[:, q * dv:(q + 1) * dv], scalar=m, in1=Sin[:, q, :],
                            op0=ALU.mult, op1=ALU.add), [st.k, Sin.k, self.cmask_b.k], [Sin.k])
                cx.op("act", lambda e: e.copy(out=Sinb[:], in_=Sin[:]), [Sin.k], [Sinb.k])
                cx.barrier(end=False)
            h = Buf(cx, es, "h", [128, KC, TT], F32)
            u = Buf(cx, es, "u", [128, KC, TT], BF16)
            o = Buf(cx, es, "o", [128, KC, TT], F32)
            qpl = Buf(cx, es, "qpl", [128, NQ, TT], BF16)
            ogb = Buf(cx, es, "ogb", [128, KC, TT], BF16)
            sqs = [Buf(cx, es, "sq", [128, TT], BF16) for _ in range(2)]
            rss = [Buf(cx, es, "rs", [128, TT], F32) for _ in range(2)]
            tmps = [Buf(cx, es, "tmp", [128, TT], F32) for _ in range(4)]
            tc = [0]
            hpb = dv // 128
            for it in range(self.NT):
                self.load_h(h, it)
                self.load_tile(o, self.oloc, 0, KC, it, self.oloc_k[it])
                if seqpar:
                    self.load_tile(qpl, self.qp, 0, NQ, it, self.qp_k[it])
                self.norm_u(h, "norm_mix_%d" % i, u, sqs, rss[0])
                for hd in (range(NG * Hg) if seqpar else []):
                    for vb in range(hpb):
                        ps = self.next_ps()
                        for jj in range(dkb):
                            q = hd * dkb + jj
                            cx.op("pe", lambda e: e.matmul(ps[:], Sinb[:, q, vb * 128:(vb + 1) * 128], qpl[:, q, :],
                                                           start=(jj == 0), stop=(jj == dkb - 1)), [Sinb.k, qpl.k], [ps.k])
                        blk = hd * hpb + vb
                        cx.op("dve", lambda e: e.tensor_tensor(out=o[:, blk, :], in0=ps[:], in1=o[:, blk, :], op=ALU.add),
                              [ps.k, o.k], [o.k])
                cur = {"hd": -1, "rs": None}

                def epi_og(blk, bw, ps):
                    hd = blk // hpb
                    if hd != cur["hd"]:
                        cur["hd"] = hd
                        cur["rs"] = rss[hd % 2]
                        self.rstd(o, hd * hpb, hpb, dv, sqs, cur["rs"])
                    r = cur["rs"]
                    t1, t2 = tmps[tc[0] % 4], tmps[(tc[0] + 1) % 4]
                    tc[0] += 2
                    cx.op("act", lambda e: e.activation(out=t1[:], in_=ps[:],
                                                        func=(AF.Silu if kind == "gla" else AF.Sigmoid)), [ps.k], [t1.k])
                    cx.op("dve", lambda e: e.scalar_tensor_tensor(
                        out=t2[:], in0=o[:, blk, :], scalar=self.col(gn, blk % hpb), in1=r[:],
                        op0=ALU.mult, op1=ALU.mult), [o.k, r.k, self.cols_b.k], [t2.k])
                    cx.op("pool", lambda e: e.tensor_tensor(out=ogb[:, blk, :], in0=t2[:], in1=t1[:], op=ALU.mult),
                          [t1.k, t2.k], [ogb.k])
                self.linear_fm(u, KC, w_in, ogcol, D, epi_og)

                def epi_out(blk, bw, ps):
                    cx.op("dve", lambda e: e.tensor_tensor(out=h[:, blk, :], in0=ps[:], in1=h[:, blk, :], op=ALU.add),
                          [ps.k, h.k], [h.k])
                self.linear_fm(ogb, KC, w_out, 0, D, epi_out)
                self.store_tile(h, self.hT, 0, KC, it, self.hT_k[it])
            self.h_stored()
            cx.barrier()

    def phase_ssm(self, i, j):
        cx, nc = self.cx, self.nc
        w_in, w_out = "ssm_w_in_%d" % j, "ssm_w_out_%d" % j
        mask2 = self.consts_b[:, 0, :]
        selpair = self.consts_b[:, 1, :]
        sel63 = self.consts_b[:, 2, :]
        sel127 = self.consts_b[:, 3, :]
        NP = TT // 128
        G, HG, P, NST = 8, 8, 64, 128
        GI = 1

        class View:
            def __init__(self, ap, k):
                self.ap, self.k = ap, k

            def __getitem__(self, key):
                return self.ap[key]

        def bc(ap2, n):
            return ap2.unsqueeze(2).to_broadcast([128, ap2.shape[1], n])

        with ExitStack() as es:
            big1 = Buf(cx, es, "big1", [128, KC * TT], F32)
            big2 = Buf(cx, es, "big2", [128, KC * TT], F32)
            h1 = View(big1.t[:].rearrange("p (a b) -> p a b", b=TT), big1.k)
            yT = View(big1.t[:].bitcast(BF16).rearrange("p (a b) -> p a b", b=TT), big1.k)
            u = View(big2.t[:, 0:4096].bitcast(BF16).rearrange("p (a b) -> p a b", b=TT), Trk("u"))
            BT = View(big2.t[:, 4096:6144].bitcast(BF16).rearrange("p (a b) -> p a b", b=TT), Trk("BT"))
            CT = View(big2.t[:, 6144:8192].bitcast(BF16).rearrange("p (a b) -> p a b", b=TT), Trk("CT"))
            h2 = View(big2.t[:].rearrange("p (a b) -> p a b", b=TT), big2.k)
            S32 = [Buf(cx, es, "S32", [128, HG * P], F32) for _ in range(G)]
            Sbf = [Buf(cx, es, "Sbf", [128, HG * P], BF16) for _ in range(G)]
            halo = Buf(cx, es, "halo", [128, 48, 3], F32)
            identf = Buf(cx, es, "identf", [128, 128], F32)
            onesf = Buf(cx, es, "onesf", [128, 128], F32)
            dtb = Buf(cx, es, "dtb", [128, 64], F32)
            arow = Buf(cx, es, "arow", [128, 64], F32)
            ddr = Buf(cx, es, "ddr", [128, 64], F32)
            dt_tok = Buf(cx, es, "dt_tok", [128, NP, 64], F32)
            la_tok = Buf(cx, es, "la_tok", [128, NP, 64], F32)
            b_tok = Buf(cx, es, "b_tok", [128, NP, 64], F32)
            eb_tok = Buf(cx, es, "eb_tok", [128, NP, 64], F32)
            we_tok = Buf(cx, es, "we_tok", [128, NP, 64], F32)
            Dc = Buf(cx, es, "Dc", [128, NP, 2, 64], F32)
            sqs = [Buf(cx, es, "sq", [128, TT], BF16) for _ in range(1)]
            rs = Buf(cx, es, "rs", [128, TT], F32)
            tmpc = [Buf(cx, es, "tmpc", [128, TT + 3], F32) for _ in range(2)]
            acc = Buf(cx, es, "acc", [128, TT], F32)
            xTg = Buf(cx, es, "xTg", [128, 4, TT], BF16)
            x_tok = [Buf(cx, es, "x_tok", [128, NP, 512], BF16) for _ in range(GI)]
            zg = [Buf(cx, es, "zg", [128, NP, 512], F32) for _ in range(GI)]
            rel = [Buf(cx, es, "rel", [128, 8, 128], F32) for _ in range(GI)]
            Mb = [Buf(cx, es, "Mb", [128, 8, 128], BF16) for _ in range(GI)]
            xdt = [Buf(cx, es, "xdt", [128, 512], BF16) for _ in range(GI)]
            xw = [Buf(cx, es, "xw", [128, 512], BF16) for _ in range(GI)]
            ytmp = [Buf(cx, es, "ytmp", [128, 512], F32) for _ in range(GI)]
            yn = [Buf(cx, es, "yn", [128, 512], BF16) for _ in range(GI)]
            cbm = [Buf(cx, es, "cbm", [128, 128], F32) for _ in range(GI)]
            btok = [Buf(cx, es, "btok", [128, 128], BF16) for _ in range(GI)]
            ss = [Buf(cx, es, "ss", [128, 1], F32) for _ in range(GI)]
            for g in range(G):
                cx.op("pool", lambda e: e.memset(S32[g][:], 0.0), [], [S32[g].k])
                cx.op("pool", lambda e: e.memset(Sbf[g][:], 0.0), [], [Sbf[g].k])
            cx.op("pool", lambda e: e.memset(halo[:], 0.0), [], [halo.k])
            cx.op("pool", lambda e: e.memset(onesf[:], 1.0), [], [onesf.k])
            cx.op("pool", lambda e: e.memset(identf[:], 0.0), [], [identf.k])
            cx.op("pool", lambda e: e.affine_select(
                out=identf[:], in_=identf[:], pattern=[[-1, 128]], compare_op=ALU.not_equal, fill=1.0,
                base=0, channel_multiplier=1), [identf.k], [identf.k])
            cx.dma("pool", dtb[:], self.rows_d["ssm_dt_bias_%d" % j].ap().partition_broadcast(128), dtb.k, None)
            cx.dma("pool", arow[:], self.rows_d["ssm_a_log_%d" % j].ap().partition_broadcast(128), arow.k, None)
            cx.dma("pool", ddr[:], self.rows_d["ssm_d_%d" % j].ap().partition_broadcast(128), ddr.k, None)
            cx.op("act", lambda e: e.activation(out=arow[:], in_=arow[:], func=AF.Exp), [arow.k], [arow.k])
            cx.op("dve", lambda e: e.tensor_scalar(out=arow[:], in0=arow[:], scalar1=-1.0, scalar2=None, op0=ALU.mult),
                  [arow.k], [arow.k])
            cb16 = Buf(cx, es, "cb16", [128, 4, 128], BF16)
            onesb = self.ones_b
            cx.op("dve", lambda e: e.tensor_copy(out=cb16[:], in_=self.consts_b[:]), [self.consts_b.k], [cb16.k])
            la_hi = Buf(cx, es, "la_hi", [128, NP, 64], BF16)
            la_lo = Buf(cx, es, "la_lo", [128, NP, 64], BF16)
            b_hi = Buf(cx, es, "b_hi", [128, NP, 64], BF16)
            b_lo = Buf(cx, es, "b_lo", [128, NP, 64], BF16)
            hl_f = Buf(cx, es, "hl_f", [128, NP, 64], F32)
            dg_lo = Buf(cx, es, "dg_lo", [128, 8, 128], BF16)
            dg_hi = Buf(cx, es, "dg_hi", [128, 8, 128], BF16)

            def split(src, hi, lo):
                cx.op("dve", lambda e: e.tensor_copy(out=hi[:], in_=src[:]), [src.k], [hi.k])
                cx.op("dve", lambda e: e.tensor_copy(out=hl_f[:], in_=hi[:]), [hi.k], [hl_f.k])
                cx.op("dve", lambda e: e.tensor_tensor(out=lo[:], in0=src[:], in1=hl_f[:], op=ALU.subtract),
                      [src.k, hl_f.k], [lo.k])
            cc = [0]

            def conv_epi(cb, ps, dst_ap, dst_k):
                tcv = tmpc[cc[0] % 2]
                cc[0] += 1
                cx.op("pool", lambda e: e.tensor_copy(out=tcv[:, 0:3], in_=halo[:, cb, :]), [halo.k], [tcv.k])
                cx.op("act", lambda e: e.copy(out=tcv[:, 3:TT + 3], in_=ps[:]), [ps.k], [tcv.k])
                cx.op("pool", lambda e: e.tensor_copy(out=halo[:, cb, :], in_=tcv[:, TT:TT + 3]), [tcv.k], [halo.k])
                wc = lambda t: self.col("ssm_conv_w_%d_%d" % (j, t), cb)
                cx.op("dve", lambda e: e.tensor_scalar(out=acc[:], in0=tcv[:, 0:TT], scalar1=wc(0),
                                                       scalar2=self.col("ssm_conv_b_%d" % j, cb),
                                                       op0=ALU.mult, op1=ALU.add), [tcv.k, self.cols_b.k], [acc.k])
                for t in range(1, 4):
                    cx.op("dve", lambda e: e.scalar_tensor_tensor(out=acc[:], in0=tcv[:, t:t + TT], scalar=wc(t),
                                                                  in1=acc[:], op0=ALU.mult, op1=ALU.add),
                          [tcv.k, acc.k, self.cols_b.k], [acc.k])
                cx.op("act", lambda e: e.activation(out=dst_ap, in_=acc[:], func=AF.Silu), [acc.k], [dst_k])

            for it in range(self.NT):
                self.load_h(h1, it)
                self.norm_u(h1, "norm_mix_%d" % i, u, sqs, rs)

                if SSM_STAGE < 1:
                    cx.barrier(end=False)
                    continue
                def epi_bc(blk, bw, ps):
                    if blk < 8:
                        conv_epi(32 + blk, ps, BT[:, blk, :], BT.k)
                    else:
                        conv_epi(32 + blk, ps, CT[:, blk - 8, :], CT.k)
                self.linear_fm(u, KC, w_in, 8192, 2048, epi_bc)

                if SSM_STAGE < 2:
                    cx.barrier(end=False)
                    continue
                def epi_dt(ts, ps):
                    cx.op("dve", lambda e: e.tensor_tensor(out=dt_tok[:, ts, :], in0=ps[:, 0:64], in1=dtb[:], op=ALU.add),
                          [ps.k, dtb.k], [dt_tok.k])
                self.linear_tm(u, w_in, 10240, 64, epi_dt)
                cx.op("act", lambda e: e.activation(out=dt_tok[:], in_=dt_tok[:], func=AF.Exp), [dt_tok.k], [dt_tok.k])
                cx.op("act", lambda e: e.activation(out=dt_tok[:], in_=dt_tok[:], func=AF.Ln, bias=1.0), [dt_tok.k], [dt_tok.k])
                for ts in range(NP):
                    cx.op("dve", lambda e: e.tensor_tensor(out=la_tok[:, ts, :], in0=dt_tok[:, ts, :], in1=arow[:], op=ALU.mult),
                          [dt_tok.k, arow.k], [la_tok.k])
                split(la_tok, la_hi, la_lo)
                for ts in range(NP):
                    ps = self.next_ps()
                    cx.op("pe", lambda e: e.matmul(ps[:, 0:64], cb16[:, 0, :], la_hi[:, ts, :], start=True, stop=False),
                          [cb16.k, la_hi.k], [ps.k])
                    cx.op("pe", lambda e: e.matmul(ps[:, 0:64], cb16[:, 0, :], la_lo[:, ts, :], start=False, stop=True),
                          [cb16.k, la_lo.k], [ps.k])
                    cx.op("act", lambda e: e.copy(out=b_tok[:, ts, :], in_=ps[:, 0:64]), [ps.k], [b_tok.k])
                split(b_tok, b_hi, b_lo)
                cx.op("act", lambda e: e.activation(out=eb_tok[:], in_=b_tok[:], func=AF.Exp), [b_tok.k], [eb_tok.k])
                for ts in range(NP):
                    ps = self.next_ps()
                    for si in range(3):
                        for hl, src in enumerate((b_hi, b_lo)):
                            cx.op("pe", lambda e: e.matmul(ps[:, si * 64:(si + 1) * 64], cb16[:, 1 + si, :], src[:, ts, :],
                                                           start=(si == 0 and hl == 0), stop=(hl == 1),
                                                           skip_group_check=True), [cb16.k, src.k], [ps.k])
                    cx.op("dve", lambda e: e.tensor_tensor(out=we_tok[:, ts, :], in0=ps[:, 0:64], in1=b_tok[:, ts, :],
                                                           op=ALU.subtract), [ps.k, b_tok.k], [we_tok.k])
                    cx.op("act", lambda e: e.activation(out=Dc[:, ts, :, :], in_=ps[:, 64:192].rearrange("p (a b) -> p a b", b=64),
                                                        func=AF.Exp), [ps.k], [Dc.k])
                cx.op("act", lambda e: e.activation(out=we_tok[:], in_=we_tok[:], func=AF.Exp), [we_tok.k], [we_tok.k])

                if SSM_STAGE < 3:
                    cx.barrier(end=False)
                    continue
                for gp in range(G // GI):
                    gs = tuple(range(gp * GI, (gp + 1) * GI))
                    for g in gs:
                        sl = g % GI
                        def epi_x(blk, bw, ps):
                            conv_epi(g * 4 + blk, ps, xTg[:, blk, :], xTg.k)
                        self.linear_fm(u, KC, w_in, 4096 + g * 512, 512, epi_x)
                        for ts in range(NP):
                            ps = self.next_ps()
                            pv = ps[:, 0:256].bitcast(BF16).rearrange("p (a b) -> p a b", b=128)
                            for a in range(4):
                                cx.op("pe", lambda e: e.transpose(pv[:, a, :], xTg[:, a, ts * 128:(ts + 1) * 128],
                                                                  self.ident_b[:]), [xTg.k, self.ident_b.k], [ps.k])
                            cx.op("act", lambda e: e.copy(out=x_tok[sl][:, ts, :].rearrange("p (a b) -> p a b", b=128),
                                                          in_=pv), [ps.k], [x_tok[sl].k])
                        def epi_z(ts, ps):
                            cx.op("act", lambda e: e.activation(out=zg[sl][:, ts, :], in_=ps[:], func=AF.Silu),
                                  [ps.k], [zg[sl].k])
                        self.linear_tm(u, w_in, g * 512, 512, epi_z)

                    for pr in (range(NP) if SSM_STAGE >= 4 else []):
                        tsl = slice(pr * 128, (pr + 1) * 128)
                        psy, psi = {}, {}
                        for g in gs:
                            sl = g % GI
                            hs = slice(g * 8, (g + 1) * 8)
                            ps = self.next_ps()
                            pvb = ps[:, 0:64].bitcast(BF16)
                            cx.op("pe", lambda e: e.transpose(pvb, BT[:, g, tsl], self.ident_b[:]),
                                  [BT.k, self.ident_b.k], [ps.k])
                            cx.op("act", lambda e: e.copy(out=btok[sl][:], in_=pvb), [ps.k], [btok[sl].k])
                            ps = self.next_ps()
                            cx.op("pe", lambda e: e.matmul(ps[:, 0:128], BT[:, g, tsl], CT[:, g, tsl], start=True, stop=True),
                                  [BT.k, CT.k], [ps.k])
                            cx.op("dve", lambda e: e.tensor_tensor(out=cbm[sl][:], in0=ps[:, 0:128], in1=mask2, op=ALU.mult),
                                  [ps.k, self.consts_b.k], [cbm[sl].k])
                            for dgx, bx in ((dg_hi, b_hi), (dg_lo, b_lo)):
                                cx.op("dve", lambda e: e.tensor_tensor(
                                    out=dgx[:], in0=identf[:].unsqueeze(1).to_broadcast([128, 8, 128]),
                                    in1=bc(bx[:, pr, hs], 128), op=ALU.mult), [identf.k, bx.k], [dgx.k])
                            pb0, pb1 = self.next_ps(hold=True), self.next_ps(hold=True)
                            for pbx, lo4 in ((pb0, 0), (pb1, 4)):
                                for hl, dgx in enumerate((dg_hi, dg_lo)):
                                    cx.op("pe", lambda e: e.matmul(
                                        pbx[:], onesb[:], dgx[:, lo4:lo4 + 4, :].rearrange("p a b -> p (a b)"),
                                        start=(hl == 0), stop=(hl == 1)), [onesb.k, dgx.k], [pbx.k])
                            cx.op("dve", lambda e: e.tensor_tensor(
                                out=rel[sl][:, 0:4, :], in0=pb0[:].rearrange("p (a b) -> p a b", b=128),
                                in1=bc(b_tok[:, pr, g * 8:g * 8 + 4], 128), op=ALU.subtract), [pb0.k, b_tok.k], [rel[sl].k])
                            cx.op("dve", lambda e: e.tensor_tensor(
                                out=rel[sl][:, 4:8, :], in0=pb1[:].rearrange("p (a b) -> p a b", b=128),
                                in1=bc(b_tok[:, pr, g * 8 + 4:g * 8 + 8], 128), op=ALU.subtract), [pb1.k, b_tok.k], [rel[sl].k])
                            self.release(pb0)
                            self.release(pb1)
                            cx.op("pool", lambda e: e.tensor_scalar(out=rel[sl][:], in0=rel[sl][:], scalar1=0.0, scalar2=None,
                                                                    op0=ALU.min), [rel[sl].k], [rel[sl].k])
                            cx.op("act", lambda e: e.activation(out=rel[sl][:], in_=rel[sl][:], func=AF.Exp), [rel[sl].k], [rel[sl].k])
                            cx.op("pool", lambda e: e.tensor_tensor(
                                out=Mb[sl][:], in0=rel[sl][:], in1=cbm[sl][:].unsqueeze(1).to_broadcast([128, 8, 128]),
                                op=ALU.mult), [rel[sl].k, cbm[sl].k], [Mb[sl].k])
                            cx.op("pool", lambda e: e.tensor_tensor(
                                out=xdt[sl][:].rearrange("p (a b) -> p a b", b=64),
                                in0=x_tok[sl][:, pr, :].rearrange("p (a b) -> p a b", b=64),
                                in1=bc(dt_tok[:, pr, hs], 64), op=ALU.mult), [x_tok[sl].k, dt_tok.k], [xdt[sl].k])
                            cx.op("pool", lambda e: e.tensor_tensor(
                                out=xw[sl][:].rearrange("p (a b) -> p a b", b=64),
                                in0=xdt[sl][:].rearrange("p (a b) -> p a b", b=64),
                                in1=bc(we_tok[:, pr, hs], 64), op=ALU.mult), [xdt[sl].k, we_tok.k], [xw[sl].k])
                            psy[g] = self.next_ps(hold=True)
                            for h8 in range(8):
                                cx.op("pe", lambda e: e.matmul(
                                    psy[g][:, h8 * 64:(h8 + 1) * 64], Mb[sl][:, h8, :], xdt[sl][:, h8 * 64:(h8 + 1) * 64],
                                    start=(h8 == 0), stop=True, skip_group_check=True), [Mb[sl].k, xdt[sl].k], [psy[g].k])
                            psi[g] = self.next_ps(hold=True)
                        for c2 in range(2):
                            rows = slice(c2 * 64, (c2 + 1) * 64)
                            tcs = slice(pr * 128 + c2 * 64, pr * 128 + (c2 + 1) * 64)
                            for g in gs:
                                kw = {"tile_position": (0, 64)} if c2 == 1 else {}
                                cx.op("pe", lambda e: e.matmul(psi[g][rows, :], CT[:, g, tcs], Sbf[g][:], start=True, stop=True,
                                                               skip_group_check=True, **kw), [CT.k, Sbf[g].k], [psi[g].k])
                            for g in gs:
                                sl = g % GI
                                hs = slice(g * 8, (g + 1) * 8)
                                ps = self.next_ps()
                                cx.op("pe", lambda e: e.matmul(ps[:], btok[sl][rows, :], xw[sl][rows, :], start=True, stop=True),
                                      [btok[sl].k, xw[sl].k], [ps.k])
                                cx.op("dve", lambda e: e.tensor_tensor(
                                    out=S32[g][:].rearrange("p (a b) -> p a b", b=64),
                                    in0=S32[g][:].rearrange("p (a b) -> p a b", b=64),
                                    in1=bc(Dc[:, pr, c2, hs], 64), op=ALU.mult), [S32[g].k, Dc.k], [S32[g].k])
                                cx.op("dve", lambda e: e.tensor_tensor(out=S32[g][:], in0=ps[:], in1=S32[g][:], op=ALU.add),
                                      [ps.k, S32[g].k], [S32[g].k])
                                cx.op("act", lambda e: e.copy(out=Sbf[g][:], in_=S32[g][:]), [S32[g].k], [Sbf[g].k])
                        for g in gs:
                            sl = g % GI
                            hs = slice(g * 8, (g + 1) * 8)
                            y = ytmp[sl]
                            tt = View(rel[sl].t[:, 0:4, :].rearrange("p a b -> p (a b)"), rel[sl].k)
                            cx.op("dve", lambda e: e.tensor_tensor(
                                out=y[:].rearrange("p (a b) -> p a b", b=64),
                                in0=psi[g][:].rearrange("p (a b) -> p a b", b=64),
                                in1=bc(eb_tok[:, pr, hs], 64), op=ALU.mult), [psi[g].k, eb_tok.k], [y.k])
                            cx.op("dve", lambda e: e.tensor_tensor(out=y[:], in0=psy[g][:], in1=y[:], op=ALU.add),
                                  [psy[g].k, y.k], [y.k])
                            self.release(psy[g])
                            self.release(psi[g])
                            cx.op("pool", lambda e: e.tensor_tensor(
                                out=tt[:].rearrange("p (a b) -> p a b", b=64),
                                in0=x_tok[sl][:, pr, :].rearrange("p (a b) -> p a b", b=64),
                                in1=bc(ddr[:, hs], 64), op=ALU.mult), [x_tok[sl].k, ddr.k], [tt.k])
                            cx.op("pool", lambda e: e.tensor_tensor(out=y[:], in0=y[:], in1=tt[:], op=ALU.add), [y.k, tt.k], [y.k])
                            cx.op("pool", lambda e: e.tensor_tensor(out=y[:], in0=y[:], in1=zg[sl][:, pr, :], op=ALU.mult),
                                  [y.k, zg[sl].k], [y.k])
                            cx.op("pool", lambda e: e.tensor_tensor(out=tt[:], in0=y[:], in1=y[:], op=ALU.mult), [y.k], [tt.k])
                            cx.op("dve", lambda e: e.reduce_sum(out=ss[sl][:], in_=tt[:], axis=mybir.AxisListType.X),
                                  [tt.k], [ss[sl].k])
                            cx.op("act", lambda e: e.activation(out=ss[sl][:], in_=ss[sl][:], func=AF.Sqrt, bias=EPS,
                                                                scale=1.0 / 512.0), [ss[sl].k], [ss[sl].k])
                            cx.op("dve", lambda e: e.reciprocal(out=ss[sl][:], in_=ss[sl][:]), [ss[sl].k], [ss[sl].k])
                            cx.op("dve", lambda e: e.tensor_scalar(out=yn[sl][:], in0=y[:], scalar1=ss[sl][:, 0:1], scalar2=None,
                                                                   op0=ALU.mult), [y.k, ss[sl].k], [yn[sl].k])
                            ps = self.next_ps()
                            pv = ps[:, 0:256].bitcast(BF16).rearrange("p (a b) -> p a b", b=128)
                            for a in range(4):
                                cx.op("pe", lambda e: e.transpose(pv[:, a, :], yn[sl][:, a * 128:(a + 1) * 128],
                                                                  self.ident_b[:]), [yn[sl].k, self.ident_b.k], [ps.k])
                            for a in range(4):
                                cx.op("act", lambda e: e.activation(
                                    out=yT[:, g * 4 + a, tsl], in_=pv[:, a, :], func=AF.Copy,
                                    scale=self.col("ssm_norm_%d" % j, g * 4 + a)), [ps.k, self.cols_b.k], [yT.k])
                cx.barrier(end=False)
                if SSM_STAGE < 5:
                    continue
                self.load_h(h2, it)

                def epi_out(blk, bw, ps):
                    cx.op("dve", lambda e: e.tensor_tensor(out=h2[:, blk, :], in0=ps[:], in1=h2[:, blk, :], op=ALU.add),
                          [ps.k, h2.k], [h2.k])
                self.linear_fm(yT, 32, w_out, 0, D, epi_out)
                self.store_tile(h2, self.hT, 0, KC, it, self.hT_k[it])
                cx.barrier(end=False)
            self.h_stored()
            cx.barrier()


def _weight(inputs, nm):
    base, idx = nm.rsplit("_", 1)
    table = {
        "w_up": inputs["w_up"], "w_down": inputs["w_down"],
        "w_ple_proj": inputs["w_ple_proj"], "w_ple_gate": inputs["w_ple_gate"],
        "gla_w_in": inputs["gla_w_in"], "gla_w_out": inputs["gla_w_out"],
        "hgrn_w_in": inputs["hgrn_w_in"], "hgrn_w_out": inputs["hgrn_w_out"],
        "ssm_w_in": inputs["ssm_w_in"], "ssm_w_out": inputs["ssm_w_out"],
    }
    return table[base][int(idx)]


def make_consts():
    c = np.zeros((128, 4, 128), np.float32)
    s = np.arange(128)[:, None]
    t = np.arange(128)[None, :]
    c[:, 0, :] = ((s // 64 == t // 64) & (s <= t)).astype(np.float32)
    c[:, 1, :] = (((s == 63) & (t < 64)) | ((s == 127) & (t >= 64))).astype(np.float32)
    c[:, 2, :] = (s == 63).astype(np.float32) * np.ones_like(t)
    c[:, 3, :] = (s == 127).astype(np.float32) * np.ones_like(t)
    return c


def run(inputs, depth, T, enable_mix=True, trace=False):
    x = np.asarray(inputs["x"])
    p = np.asarray(inputs["p"])
    B, L, _ = x.shape
    segs = NCORES // B
    assert L == segs * T
    prog = Prog(T, depth, enable_mix)
    nc = prog.build()
    lay, ncol, _ = col_layout(depth)
    cols = np.zeros((128, ncol), np.float32)

    def put(nm, v):
        off, n = lay[nm]
        cols[:, off:off + n] = to_cols(v)
    for i in range(depth):
        put("norm_mix_%d" % i, inputs["norm_mix"][i])
        put("norm_mlp_%d" % i, inputs["norm_mlp"][i])
        put("norm_ple_%d" % i, inputs["norm_ple"][i])
        kind, j = kind_of(i)
        if kind == 0:
            put("gla_b_gk_%d" % j, inputs["gla_b_gk"][j])
            put("gla_gn_%d" % j, inputs["gla_gn"][j])
        elif kind == 1:
            put("hgrn_gn_%d" % j, inputs["hgrn_gn"][j])
            for l in range(depth):
                put("hgrn_lb_%d" % l, inputs["hgrn_lb_logits"][l])
        else:
            for t in range(4):
                put("ssm_conv_w_%d_%d" % (j, t), inputs["ssm_conv_w"][j][t])
            put("ssm_conv_b_%d" % j, inputs["ssm_conv_b"][j])
            put("ssm_norm_%d" % j, inputs["ssm_norm"][j])
    put("norm_final", inputs["norm_final"])
    consts = make_consts()
    shared = {"cols": cols, "consts": consts}
    for i in range(depth):
        kind, j = kind_of(i)
        if kind == 0:
            shared["gla_w_gk2_%d" % j] = np.ascontiguousarray(inputs["gla_w_gk2"][j], np.float32)
        if kind == 2:
            for nm in ("ssm_dt_bias", "ssm_a_log", "ssm_d"):
                shared["%s_%d" % (nm, j)] = np.ascontiguousarray(
                    np.asarray(inputs[nm][j], np.float32).reshape(1, 64))
    in_maps = []
    for c in range(NCORES):
        b, s = c // segs, c % segs
        m = dict(shared)
        m["xT"] = np.ascontiguousarray(x[b, s * T:(s + 1) * T, :].T)
        m["pT"] = np.ascontiguousarray(np.transpose(p[:depth, b, s * T:(s + 1) * T, :], (0, 2, 1)))
        cm = np.zeros((128, 16), np.float32)
        for r in range(NCORES):
            rb, rs = r // segs, r % segs
            if rb == b and rs < s:
                cm[:, r] = 1.0
            if rb == b and rs == s - 1:
                cm[:, 8 + r] = 1.0
        m["cmask"] = cm
        for nm, K, N in big_weights(depth):
            W = _weight(inputs, nm)
            if USE_CC:
                r = K // NCORES
                m[nm] = np.ascontiguousarray(W[c * r:(c + 1) * r, :], np.float32)
            else:
                m[nm] = np.ascontiguousarray(W, np.float32)
        in_maps.append(m)
    res = run_bass_kernel_spmd(nc, in_maps, core_ids=list(range(NCORES)), trace=trace)
    out = np.empty((B, L, D), np.float32)
    for c in range(NCORES):
        b, s = c // segs, c % segs
        out[b, s * T:(s + 1) * T, :] = res.results[c]["outT"].T
    if DEBUG:
        return out, res
    if trace:
        return out, res
    return out


def kernel(**inputs):
    depth = int(np.asarray(inputs["p"]).shape[0])
    B, L, _ = np.asarray(inputs["x"]).shape
    return run(inputs, depth, L * B // NCORES)
```

```python
import numpy as np
from contextlib import ExitStack
import concourse.bass as bass
import concourse.mybir as mybir
from concourse.bass_utils import run_bass_kernel_spmd

F32 = mybir.dt.float32
BF16 = mybir.dt.bfloat16
ALU = mybir.AluOpType
AF = mybir.ActivationFunctionType

NCORES = 2
USE_CC = False
D = 2048
KC = D // 128
TT = 512
EPS = 1e-6
PLE = 256
DFF = 8192
N_MIX = 3
DEBUG = False
SSM_STAGE = 99
KINDS = None


def kind_of(i):
    if KINDS is None:
        return i % N_MIX, i // N_MIX
    k = KINDS[i]
    return k, sum(1 for x in KINDS[:i] if x == k)


class Trk:
    __slots__ = ("name", "w", "r", "dsem", "dcnt", "excl")

    def __init__(self, name="", excl=False):
        self.name = name
        self.excl = excl
        self.w = None
        self.r = {}
        self.dsem = None
        self.dcnt = 0


class Ctx:
    def __init__(self, nc, es):
        self.nc = nc
        self.es = es
        self.sems = {}
        self.engs = {}
        for nm, h in (("pe", nc.tensor), ("act", nc.scalar), ("dve", nc.vector),
                      ("pool", nc.gpsimd), ("sp", nc.sync)):
            self.sems[nm] = es.enter_context(nc.semaphore("sem_" + nm))
            self.engs[nm] = {"h": h, "cnt": 0, "known": {}}
        self.ndsem = 0
        self.phase_trks = []
        self.phase_evs = {}
        self.free_dsems = []
        self.uid = 0

    def name(self, p):
        self.uid += 1
        return "%s_%d" % (p, self.uid)

    def _dsem(self, t):
        if t.dsem is None:
            if self.free_dsems:
                t.dsem, t.dcnt = self.free_dsems.pop()
            else:
                t.dsem = "d%d" % self.ndsem
                self.ndsem += 1
                self.sems[t.dsem] = self.es.enter_context(self.nc.semaphore("dsem_%s" % t.dsem))
        return t.dsem

    def _wait(self, eng, k, v):
        e = self.engs[eng]
        if k == eng and v > e["cnt"]:
            return
        if v > 0 and e["known"].get(k, 0) < v:
            e["h"].wait_ge(self.sems[k], v)
            e["known"][k] = v

    def _waits(self, eng, reads, writes):
        need = {}
        for t in reads:
            if t.w is not None:
                k, v = t.w
                if need.get(k, 0) < v:
                    need[k] = v
            if t.excl:
                for k, v in t.r.items():
                    if k != eng and need.get(k, 0) < v:
                        need[k] = v
        for t in writes:
            if t.w is not None:
                k, v = t.w
                if need.get(k, 0) < v:
                    need[k] = v
            for k, v in t.r.items():
                if need.get(k, 0) < v:
                    need[k] = v
        for k, v in need.items():
            self._wait(eng, k, v)

    def op(self, eng, fn, reads=(), writes=(), inc=True):
        self._waits(eng, reads, writes)
        e = self.engs[eng]
        ins = fn(e["h"])
        if inc:
            e["cnt"] += 1
            ins.then_inc(self.sems[eng], 1)
            c = e["cnt"]
        else:
            c = e["cnt"] + 1
        for t in reads:
            if t.r.get(eng, 0) < c:
                t.r[eng] = c
        for t in writes:
            t.w = (eng, c)
            t.r = {}
        return ins

    def dma(self, q, out_ap, in_ap, out_t, in_t, **kw):
        reads = [in_t] if in_t is not None else []
        k = self._dsem(out_t)
        saved = out_t.w
        if saved is not None and saved[0] == k:
            out_t.w = None
        self._waits(q, reads, [out_t])
        out_t.w = saved
        e = self.engs[q]
        ins = e["h"].dma_start(out=out_ap, in_=in_ap, **kw)
        out_t.dcnt += 16
        ins.then_inc(self.sems[k], 16)
        if q != "sp" or in_t is not None:
            self.phase_evs[k] = out_t.dcnt
        if in_t is not None and in_t.r.get(k, 0) < out_t.dcnt:
            in_t.r[k] = out_t.dcnt
        out_t.w = (k, out_t.dcnt)
        out_t.r = {}
        return ins

    def allgather(self, out_ap, in_ap, out_t, in_t, groups):
        q = "pool"
        self._waits(q, [in_t], [out_t])
        e = self.engs[q]
        k = self._dsem(out_t)
        ins = e["h"].collective_compute("AllGather", ALU.bypass, replica_groups=groups,
                                        ins=[in_ap], outs=[out_ap])
        out_t.dcnt += 1
        ins.then_inc(self.sems[k], 1)
        if in_t.r.get(k, 0) < out_t.dcnt:
            in_t.r[k] = out_t.dcnt
        out_t.w = (k, out_t.dcnt)
        out_t.r = {}
        return ins

    def finish(self, eng, trks):
        self._waits(eng, trks, [])

    def barrier(self, extra=(), end=True):
        evs = {}
        for nm in ("pe", "act", "dve", "pool"):
            evs[nm] = self.engs[nm]["cnt"]
        for t in list(self.phase_trks) + list(extra):
            if t.dsem is not None and t.dcnt > 0:
                evs[t.dsem] = t.dcnt
        evs.update(self.phase_evs)
        self.phase_evs = {}
        for nm in ("pe", "act", "dve", "pool"):
            for k, v in evs.items():
                self._wait(nm, k, v)
        if end:
            for t in self.phase_trks:
                if t.dsem is not None:
                    self.free_dsems.append((t.dsem, t.dcnt))
                    t.dsem = None
            self.phase_trks = []


class Buf:
    def __init__(self, cx, es, name, shape, dt, psum=False, phase=True):
        nm = cx.name(name)
        if psum:
            self.t = es.enter_context(cx.nc.psum_tensor(nm, list(shape), dt))
        else:
            self.t = es.enter_context(cx.nc.sbuf_tensor(nm, list(shape), dt))
        self.k = Trk(nm, excl=psum)
        if phase:
            cx.phase_trks.append(self.k)

    def __getitem__(self, key):
        return self.t[key]


def dram(nc, cx, name, shape, dt, kind=None):
    if kind is None:
        t = nc.dram_tensor(name, list(shape), dt)
    else:
        t = nc.dram_tensor(name, list(shape), dt, kind=kind)
    return t


GLA_IN = 6160
HGRN_IN = 8192
SSM_IN = 10304


def big_weights(depth):
    out = []
    for i in range(depth):
        kind, j = kind_of(i)
        if kind == 0:
            out.append(("gla_w_in_%d" % j, D, GLA_IN))
            out.append(("gla_w_out_%d" % j, D, D))
        elif kind == 1:
            out.append(("hgrn_w_in_%d" % j, D, HGRN_IN))
            out.append(("hgrn_w_out_%d" % j, D, D))
        else:
            out.append(("ssm_w_in_%d" % j, D, SSM_IN))
            out.append(("ssm_w_out_%d" % j, 2 * D, D))
        out.append(("w_up_%d" % i, D, DFF))
        out.append(("w_down_%d" % i, DFF, D))
        out.append(("w_ple_gate_%d" % i, D, D))
        out.append(("w_ple_proj_%d" % i, PLE, D))
    return out


def col_layout(depth):
    items = []
    for i in range(depth):
        items += [("norm_mix_%d" % i, KC), ("norm_mlp_%d" % i, KC), ("norm_ple_%d" % i, KC)]
    items.append(("norm_final", KC))
    n_norm = sum(n for _, n in items)
    for i in range(depth):
        kind, j = kind_of(i)
        if kind == 0:
            items += [("gla_b_gk_%d" % j, 8), ("gla_gn_%d" % j, 4)]
        elif kind == 1:
            items += [("hgrn_gn_%d" % j, 1)]
            for l in range(depth):
                items.append(("hgrn_lb_%d" % l, KC))
        else:
            for t in range(4):
                items.append(("ssm_conv_w_%d_%d" % (j, t), 48))
            items += [("ssm_conv_b_%d" % j, 48), ("ssm_norm_%d" % j, 32)]
    lay = {}
    off = 0
    for nm, n in items:
        if nm not in lay:
            lay[nm] = (off, n)
            off += n
    return lay, off, n_norm


def to_cols(v):
    v = np.asarray(v, np.float32).reshape(-1, 128)
    return np.ascontiguousarray(v.T)


class Prog:
    def __init__(self, T, depth, enable_mix=True):
        self.T = T
        self.depth = depth
        self.NT = T // TT
        self.enable_mix = enable_mix
        self.nc = bass.Bass("TRN2", target_bir_lowering=False)
        self.lay, self.ncol, self.n_norm = col_layout(depth)
        self.dbg = {}
        self.debug = DEBUG

    def declare(self):
        nc, T, depth = self.nc, self.T, self.depth
        self.xT = nc.dram_tensor("xT", [D, T], F32, kind="ExternalInput")
        self.pT = nc.dram_tensor("pT", [depth, PLE, T], F32, kind="ExternalInput")
        self.cols_d = nc.dram_tensor("cols", [128, self.ncol], F32, kind="ExternalInput")
        self.consts_d = nc.dram_tensor("consts", [128, 4, 128], F32, kind="ExternalInput")
        self.cmask_d = nc.dram_tensor("cmask", [128, 16], F32, kind="ExternalInput")
        self.outT = nc.dram_tensor("outT", [D, T], F32, kind="ExternalOutput")
        self.wsh, self.wshb, self.wg, self.wg_k = {}, {}, {}, {}
        for nm, K, N in big_weights(depth):
            rows = K // NCORES if USE_CC else K
            self.wsh[nm] = nc.dram_tensor(nm, [rows, N], F32, kind="ExternalInput")
            if USE_CC:
                self.wshb[nm] = nc.dram_tensor(nm + "_sb", [rows, N], BF16)
            self.wg[nm] = nc.dram_tensor(nm + "_g", [K, N], BF16)
            self.wg_k[nm] = Trk(nm + "_g")
        self.wg_need = {}
        self.hT = nc.dram_tensor("hT_scr", [D, T], F32)
        self.hT_k = [Trk("hT%d" % i) for i in range(self.NT)]
        self.out_k = [Trk("out%d" % i) for i in range(self.NT)]
        self.oloc = nc.dram_tensor("oloc_scr", [2 * D, T], F32)
        self.oloc_k = [[Trk("oloc%d_%d" % (i, b)) for b in range(2)] for i in range(self.NT)]
        self.qp = nc.dram_tensor("qp_scr", [D, T], BF16)
        self.qp_k = [Trk("qp%d" % i) for i in range(self.NT)]
        self.rows_d = {}
        self.small_d = {}
        for i in range(depth):
            kind, j = kind_of(i)
            if kind == 0:
                self.small_d["gla_w_gk2_%d" % j] = nc.dram_tensor(
                    "gla_w_gk2_%d" % j, [16, 1024], F32, kind="ExternalInput")
            if kind == 2:
                for nm in ("ssm_dt_bias", "ssm_a_log", "ssm_d"):
                    self.rows_d["%s_%d" % (nm, j)] = nc.dram_tensor(
                        "%s_%d" % (nm, j), [1, 64], F32, kind="ExternalInput")

    def dump_sb(self, name, buf, shape, dt=F32):
        if not getattr(self, "debug", False) or name in self.dbg:
            return
        d = self.nc.dram_tensor("dbg_" + name, list(shape), dt, kind="ExternalOutput")
        k = Trk(name)
        self.dbg[name] = k
        self.cx.dma("pool", d.ap(), buf[:], k, buf.k)

    def dump_dram(self, name, dt_, shape, trks, dt=F32):
        if not getattr(self, "debug", False) or name in self.dbg:
            return
        d = self.nc.dram_tensor("dbg_" + name, list(shape), dt, kind="ExternalOutput")
        k = Trk(name)
        self.dbg[name] = k
        self.cx._waits("pool", trks, [])
        self.cx.dma("pool", d.ap(), dt_.ap(), k, None)

    def col(self, name, i=0, n=1):
        off, cnt = self.lay[name]
        return self.cols[:, off + i:off + i + n]

    def build(self):
        nc = self.nc
        self.declare()
        with ExitStack() as es:
            cx = self.cx = Ctx(nc, es)
            self.cols_b = Buf(cx, es, "cols", [128, self.ncol], F32, phase=False)
            self.cols = self.cols_b.t
            self.consts_b = Buf(cx, es, "consts", [128, 4, 128], F32, phase=False)
            self.cmask_b = Buf(cx, es, "cmask", [128, 16], F32, phase=False)
            self.ones_b = Buf(cx, es, "ones", [128, 128], BF16, phase=False)
            self.ident_b = Buf(cx, es, "ident", [128, 128], BF16, phase=False)
            self.cmsk_b = Buf(cx, es, "chunkmask", [128, TT], F32, phase=False)
            self.NSLAB = 2
            self.slabs = [Buf(cx, es, "slab%d" % i, [128, 16, 512], BF16, phase=False)
                          for i in range(self.NSLAB)]
            self.slab_i = 0
            self.psum = [Buf(cx, es, "ps%d" % i, [128, 512], F32, psum=True, phase=False)
                         for i in range(8)]
            self.ps_i = 0
            self.held = set()
            self.rr = 0
            cx.dma("pool", self.cols[:], self.cols_d.ap(), self.cols_b.k, None)
            cx.dma("pool", self.consts_b[:], self.consts_d.ap(), self.consts_b.k, None)
            cx.dma("pool", self.cmask_b[:], self.cmask_d.ap(), self.cmask_b.k, None)
            cx.op("pool", lambda e: e.memset(self.ones_b[:], 1.0), [], [self.ones_b.k])
            cx.op("pool", lambda e: e.memset(self.ident_b[:], 0.0), [], [self.ident_b.k])
            cx.op("pool", lambda e: e.affine_select(
                out=self.ident_b[:], in_=self.ident_b[:], pattern=[[-1, 128]],
                compare_op=ALU.not_equal, fill=1.0, base=0, channel_multiplier=1),
                [self.ident_b.k], [self.ident_b.k])
            cx.op("pool", lambda e: e.memset(self.cmsk_b[:], 1.0), [], [self.cmsk_b.k])
            cx.op("pool", lambda e: e.memset(self.cmsk_b[:, 0:TT:64], 0.0), [], [self.cmsk_b.k])

            self.phase_weights()
            self.hsrc, self.hsrc_k = self.xT, [None] * self.NT
            for i in range(self.depth):
                kind, j = kind_of(i)
                if self.enable_mix:
                    if kind == 0:
                        self.phase_lin_mixer(i, j, "gla")
                    elif kind == 1:
                        self.phase_lin_mixer(i, j, "hgrn")
                    else:
                        self.phase_ssm(i, j)
                self.phase_mlp(i)
                self.phase_ple(i)
            self.phase_out()
            for q in ("pool", "sp", "act", "dve", "pe"):
                cx.finish(q, self.out_k + list(self.dbg.values()))
        return nc

    def next_ps(self, hold=False):
        for _ in range(16):
            b = self.psum[self.ps_i % 8]
            self.ps_i += 1
            if id(b) not in self.held:
                if hold:
                    self.held.add(id(b))
                return b
        raise RuntimeError("all PSUM banks held")

    def release(self, b):
        self.held.discard(id(b))

    def ew(self):
        self.rr += 1
        return "dve" if self.rr % 2 else "pool"

    def phase_weights(self):
        cx = self.cx
        PIECE = 4096
        with ExitStack() as es:
            NB = 3
            wdst = [Trk("wdst%d" % i) for i in range(NB)]
            fin = [Buf(cx, es, "wc_in%d" % i, [128, PIECE], F32) for i in range(NB)]
            fout = [Buf(cx, es, "wc_out%d" % i, [128, PIECE], BF16) for i in range(NB)]
            n = 0
            for nm, K, N in big_weights(self.depth):
                rows = K // NCORES if USE_CC else K
                tot = rows * N
                per = tot // 128
                assert per * 128 == tot
                src = self.wsh[nm].ap().rearrange("r n -> (r n)").rearrange("(p f) -> p f", p=128)
                dstt = self.wshb[nm] if USE_CC else self.wg[nm]
                dst = dstt.ap().rearrange("r n -> (r n)").rearrange("(p f) -> p f", p=128)
                shk = Trk(nm + "_sb") if USE_CC else None
                off = 0
                while off < per:
                    w = min(PIECE, per - off)
                    a, b = fin[n % NB], fout[n % NB]
                    cx.dma("sp", a[:, 0:w], src[:, off:off + w], a.k, None)
                    eng = ("act", "dve", "act", "pool", "act", "dve")[n % 6]
                    if eng == "act":
                        cx.op("act", lambda e: e.copy(out=b[:, 0:w], in_=a[:, 0:w]), [a.k], [b.k])
                    else:
                        cx.op(eng, lambda e: e.tensor_copy(out=b[:, 0:w], in_=a[:, 0:w]), [a.k], [b.k])
                    cx.dma("pool", dst[:, off:off + w], b[:, 0:w], shk if USE_CC else wdst[n % NB], b.k)
                    off += w
                    n += 1
                if USE_CC:
                    cx.allgather(self.wg[nm].ap(), self.wshb[nm].ap(), self.wg_k[nm], shk,
                                 [list(range(NCORES))])
                else:
                    self.wg_need[nm] = [(t.dsem, t.dcnt) for t in wdst if t.dsem is not None]
            cx.barrier()

    def phase_copy_in(self):
        cx = self.cx
        for it in range(self.NT):
            cx.dma("pool", self.hT.ap()[:, it * TT:(it + 1) * TT],
                   self.xT.ap()[:, it * TT:(it + 1) * TT], self.hT_k[it], None)

    def load_h(self, buf, it):
        self.load_tile(buf, self.hsrc, 0, KC, it, self.hsrc_k[it])

    def h_stored(self):
        self.hsrc, self.hsrc_k = self.hT, self.hT_k

    def load_tile(self, buf, dram_t, row0, nblk, it, trk, blk0=0):
        cx = self.cx
        src = dram_t.ap()[row0:row0 + nblk * 128, it * TT:(it + 1) * TT].rearrange(
            "(kc p) t -> p kc t", p=128)
        step = 4
        for q in range(0, nblk, step):
            n = min(step, nblk - q)
            cx.dma("pool", buf[:, blk0 + q:blk0 + q + n, :], src[:, q:q + n, :], buf.k, trk)

    def store_tile(self, buf, dram_t, row0, nblk, it, trk, blk0=0):
        cx = self.cx
        dst = dram_t.ap()[row0:row0 + nblk * 128, it * TT:(it + 1) * TT].rearrange(
            "(kc p) t -> p kc t", p=128)
        step = 4
        for q in range(0, nblk, step):
            n = min(step, nblk - q)
            cx.dma("pool", dst[:, q:q + n, :], buf[:, blk0 + q:blk0 + q + n, :], trk, buf.k)

    def rstd(self, src, blk0, nblk, n_feat, sqs, rs):
        cx = self.cx
        ps = self.next_ps()
        for b in range(nblk):
            sq = sqs[b % len(sqs)]
            cx.op("act", lambda e: e.activation(out=sq[:], in_=src[:, blk0 + b, :], func=AF.Square),
                  [src.k], [sq.k])
            cx.op("pe", lambda e: e.matmul(ps[:], self.ones_b[:], sq[:], start=(b == 0),
                                           stop=(b == nblk - 1)), [self.ones_b.k, sq.k], [ps.k])
        cx.op("act", lambda e: e.activation(out=rs[:], in_=ps[:], func=AF.Sqrt, bias=EPS,
                                            scale=1.0 / n_feat), [ps.k], [rs.k])
        cx.op("dve", lambda e: e.reciprocal(out=rs[:], in_=rs[:]), [rs.k], [rs.k])

    def norm_u(self, h, gain, u, sqs, rs):
        cx = self.cx
        self.rstd(h, 0, KC, D, sqs, rs)
        for kc in range(KC):
            cx.op("dve", lambda e: e.scalar_tensor_tensor(
                out=u[:, kc, :], in0=h[:, kc, :], scalar=self.col(gain, kc), in1=rs[:],
                op0=ALU.mult, op1=ALU.mult), [h.k, rs.k, self.cols_b.k], [u.k])

    def load_slab(self, wname, k0, nk, c0, w):
        cx = self.cx
        slab = self.slabs[self.slab_i % self.NSLAB]
        self.slab_i += 1
        src = self.wg[wname].ap()[k0 * 128:(k0 + nk) * 128, c0:c0 + w].rearrange(
            "(kc p) n -> p kc n", p=128)
        step = 4
        if not USE_CC:
            for k, v in self.wg_need[wname]:
                cx._wait("sp", k, v)
        for q in range(0, nk, step):
            n = min(step, nk - q)
            cx.dma("sp", slab[:, q:q + n, 0:w], src[:, q:q + n, :], slab.k, self.wg_k[wname] if USE_CC else None)
        return slab

    def linear_fm(self, src, nkc, wname, col0, ncols, epi, src_blk0=0):
        cx = self.cx
        ng = (ncols + 511) // 512
        nks = (nkc + 15) // 16
        for g in range(ng):
            gw = min(512, ncols - g * 512)
            nnb = (gw + 127) // 128
            pss = [self.next_ps(hold=True) for _ in range(nnb)]
            for ks in range(nks):
                nk = min(16, nkc - ks * 16)
                slab = self.load_slab(wname, ks * 16, nk, col0 + g * 512, gw)
                for nb in range(nnb):
                    bw = min(128, gw - nb * 128)
                    for kc in range(nk):
                        kk = ks * 16 + kc
                        cx.op("pe", lambda e: e.matmul(
                            pss[nb][0:bw, :], slab[:, kc, nb * 128:nb * 128 + bw],
                            src[:, src_blk0 + kk, :], start=(kk == 0), stop=(kk == nkc - 1)),
                            [slab.k, src.k], [pss[nb].k], inc=(kc == nk - 1))
            for nb in range(nnb):
                bw = min(128, gw - nb * 128)
                epi(g * 4 + nb, bw, pss[nb])
                self.release(pss[nb])

    def linear_tm(self, src, wname, col0, gw, epi):
        cx = self.cx
        slab = self.load_slab(wname, 0, KC, col0, gw)
        for ts in range(TT // 128):
            ps = self.next_ps()
            for kc in range(KC):
                cx.op("pe", lambda e: e.matmul(
                    ps[:, 0:gw], src[:, kc, ts * 128:(ts + 1) * 128], slab[:, kc, 0:gw],
                    start=(kc == 0), stop=(kc == KC - 1)), [slab.k, src.k], [ps.k], inc=(kc == KC - 1))
            epi(ts, ps)

    def phase_mlp(self, i):
        cx = self.cx
        with ExitStack() as es:
            hs = [Buf(cx, es, "h", [128, KC, TT], F32) for _ in range(2)]
            u = Buf(cx, es, "u", [128, KC, TT], BF16)
            hid = Buf(cx, es, "hid", [128, DFF // 128, TT], BF16)
            sqs = [Buf(cx, es, "sq", [128, TT], BF16) for _ in range(2)]
            rs = Buf(cx, es, "rs", [128, TT], F32)
            tmps = [Buf(cx, es, "tmp", [128, TT], F32) for _ in range(3)]
            cnt = [0]
            self.load_h(hs[0], 0)
            self.norm_u(hs[0], "norm_mlp_%d" % i, u, sqs, rs)
            for it in range(self.NT):
                h = hs[it % 2]

                def epi_up(blk, bw, ps):
                    t = tmps[cnt[0] % 3]
                    cnt[0] += 1
                    cx.op("act", lambda e: e.activation(out=t[:], in_=ps[:], func=AF.Relu), [ps.k], [t.k])
                    cx.op(self.ew(), lambda e: e.tensor_tensor(out=hid[:, blk, :], in0=t[:], in1=t[:],
                                                               op=ALU.mult), [t.k], [hid.k])
                self.linear_fm(u, KC, "w_up_%d" % i, 0, DFF, epi_up)
                if it + 1 < self.NT:
                    self.load_h(hs[(it + 1) % 2], it + 1)
                    self.norm_u(hs[(it + 1) % 2], "norm_mlp_%d" % i, u, sqs, rs)

                def epi_down(blk, bw, ps):
                    cx.op("dve", lambda e: e.tensor_tensor(out=h[:, blk, :], in0=ps[:], in1=h[:, blk, :],
                                                           op=ALU.add), [ps.k, h.k], [h.k])
                self.linear_fm(hid, DFF // 128, "w_down_%d" % i, 0, D, epi_down)
                self.store_tile(h, self.hT, 0, KC, it, self.hT_k[it])
            self.h_stored()
            cx.barrier()

    def phase_ple(self, i):
        cx = self.cx
        with ExitStack() as es:
            h = Buf(cx, es, "h", [128, KC, TT], F32)
            u = Buf(cx, es, "u", [128, KC, TT], BF16)
            pp = Buf(cx, es, "pp", [128, KC, TT], F32)
            pf = Buf(cx, es, "pf", [128, 2, TT], F32)
            pb = Buf(cx, es, "pb", [128, 2, TT], BF16)
            sqs = [Buf(cx, es, "sq", [128, TT], BF16) for _ in range(2)]
            rs = Buf(cx, es, "rs", [128, TT], F32)
            tmps = [Buf(cx, es, "tmp", [128, TT], F32) for _ in range(3)]
            cnt = [0]
            for it in range(self.NT):
                self.load_h(h, it)
                src = self.pT.ap()[i, :, it * TT:(it + 1) * TT].rearrange("(kc p) t -> p kc t", p=128)
                cx.dma("pool", pf[:], src, pf.k, None)
                cx.op("dve", lambda e: e.tensor_copy(out=pb[:], in_=pf[:]), [pf.k], [pb.k])

                def epi_pp(blk, bw, ps):
                    cx.op("act", lambda e: e.copy(out=pp[:, blk, :], in_=ps[:]), [ps.k], [pp.k])
                self.linear_fm(pb, 2, "w_ple_proj_%d" % i, 0, D, epi_pp)
                self.norm_u(h, "norm_ple_%d" % i, u, sqs, rs)

                def epi_gate(blk, bw, ps):
                    t = tmps[cnt[0] % 3]
                    cnt[0] += 1
                    cx.op("act", lambda e: e.activation(out=t[:], in_=ps[:], func=AF.Sigmoid), [ps.k], [t.k])
                    cx.op("pool", lambda e: e.tensor_tensor(out=t[:], in0=t[:], in1=pp[:, blk, :],
                                                            op=ALU.mult), [t.k, pp.k], [t.k])
                    cx.op("dve", lambda e: e.tensor_tensor(out=h[:, blk, :], in0=t[:], in1=h[:, blk, :],
                                                           op=ALU.add), [t.k, h.k], [h.k])
                self.linear_fm(u, KC, "w_ple_gate_%d" % i, 0, D, epi_gate)
                self.store_tile(h, self.hT, 0, KC, it, self.hT_k[it])
            self.h_stored()
            cx.barrier()

    def phase_out(self):
        cx = self.cx
        with ExitStack() as es:
            h = Buf(cx, es, "h", [128, KC, TT], F32)
            o = Buf(cx, es, "o", [128, KC, TT], F32)
            sqs = [Buf(cx, es, "sq", [128, TT], BF16) for _ in range(2)]
            rs = Buf(cx, es, "rs", [128, TT], F32)
            for it in range(self.NT):
                self.load_h(h, it)
                self.norm_u(h, "norm_final", o, sqs, rs)
                self.store_tile(o, self.outT, 0, KC, it, self.out_k[it])
            cx.barrier(extra=self.out_k)

    def phase_lin_mixer(self, i, j, kind):
        cx, nc = self.cx, self.nc
        if kind == "gla":
            w_in, w_out = "gla_w_in_%d" % j, "gla_w_out_%d" % j
            NG, Hg, dkb, dvb, dv = 2, 2, 2, 4, 512
            qcol = lambda g: g * 512
            kcol = lambda g: 1024 + g * 512
            vcol = lambda g: 2048 + g * 1024
            ogcol = 4096
            qscale = 256 ** -0.5
            gn = "gla_gn_%d" % j
        else:
            w_in, w_out = "hgrn_w_in_%d" % j, "hgrn_w_out_%d" % j
            NG, Hg, dkb, dvb, dv = 4, 4, 1, 1, 128
            qcol = lambda g: g * 512
            kcol = lambda g: 2048 + g * 512
            vcol = lambda g: 4096 + g * 512
            ogcol = 6144
            qscale = 128 ** -0.5
            gn = "hgrn_gn_%d" % j
        QG = 4
        VW = Hg * dv
        NQ = NG * QG
        NS0 = NQ * dv
        NS = NS0 + NQ
        sloc = nc.dram_tensor("sloc_%d" % i, [128, NS], F32)
        sg = nc.dram_tensor("sg_%d" % i, [NCORES * 128, NS], F32)
        sloc_k, sg_k = Trk("sloc"), Trk("sg")
        mask2 = self.consts_b[:, 0, :]
        NP = TT // 128
        seqpar = USE_CC

        with ExitStack() as es:
            S32 = [[Buf(cx, es, "S32", [128, dkb, dv], F32) for _ in range(Hg)] for _ in range(NG)]
            Sbf = [[Buf(cx, es, "Sbf", [128, dkb, dv], BF16) for _ in range(Hg)] for _ in range(NG)]
            eo = [Buf(cx, es, "eo", [128, QG, 9], F32) for _ in range(NG)]
            for g in range(NG):
                for hh in range(Hg):
                    cx.op("pool", lambda e: e.memset(S32[g][hh][:], 0.0), [], [S32[g][hh].k])
                    cx.op("pool", lambda e: e.memset(Sbf[g][hh][:], 0.0), [], [Sbf[g][hh].k])
                cx.op("pool", lambda e: e.memset(eo[g][:], 1.0), [], [eo[g].k])
            if kind == "gla":
                wgk2 = Buf(cx, es, "wgk2", [16, 1024], F32)
                wgk2b = Buf(cx, es, "wgk2b", [16, 1024], BF16)
                negb = Buf(cx, es, "negb", [128, 8], F32)
                glr = Buf(cx, es, "glr", [16, TT], BF16)
                cx.dma("pool", wgk2[:], self.small_d["gla_w_gk2_%d" % j].ap(), wgk2.k, None)
                cx.op("dve", lambda e: e.tensor_copy(out=wgk2b[:], in_=wgk2[:]), [wgk2.k], [wgk2b.k])
                cx.op("dve", lambda e: e.tensor_scalar(out=negb[:], in0=self.col("gla_b_gk_%d" % j, 0, 8),
                                                       scalar1=-1.0, scalar2=None, op0=ALU.mult),
                      [self.cols_b.k], [negb.k])
            else:
                lb = Buf(cx, es, "lb", [128, KC], F32)
                oml = Buf(cx, es, "oml", [128, KC], F32)
                ex = Buf(cx, es, "ex", [128, self.depth, KC], F32)
                mx = Buf(cx, es, "mx", [128, KC], F32)
                sm = Buf(cx, es, "sm", [128, KC], F32)
                lg = lambda l: self.col("hgrn_lb_%d" % l, 0, KC)
                cx.op("dve", lambda e: e.tensor_copy(out=mx[:], in_=lg(0)), [self.cols_b.k], [mx.k])
                for l in range(1, self.depth):
                    cx.op("dve", lambda e: e.tensor_tensor(out=mx[:], in0=mx[:], in1=lg(l), op=ALU.max),
                          [mx.k, self.cols_b.k], [mx.k])
                for l in range(self.depth):
                    cx.op("dve", lambda e: e.tensor_tensor(out=ex[:, l, :], in0=lg(l), in1=mx[:], op=ALU.subtract),
                          [mx.k, self.cols_b.k], [ex.k])
                cx.op("act", lambda e: e.activation(out=ex[:], in_=ex[:], func=AF.Exp), [ex.k], [ex.k])
                cx.op("dve", lambda e: e.tensor_copy(out=sm[:], in_=ex[:, 0, :]), [ex.k], [sm.k])
                cx.op("dve", lambda e: e.memset(lb[:], 0.0), [], [lb.k])
                for l in range(1, self.depth):
                    cx.op("dve", lambda e: e.tensor_tensor(out=sm[:], in0=sm[:], in1=ex[:, l, :], op=ALU.add),
                          [sm.k, ex.k], [sm.k])
                    if l <= i:
                        cx.op("dve", lambda e: e.tensor_tensor(out=lb[:], in0=lb[:], in1=ex[:, l, :], op=ALU.add),
                              [lb.k, ex.k], [lb.k])
                cx.op("dve", lambda e: e.reciprocal(out=sm[:], in_=sm[:]), [sm.k], [sm.k])
                cx.op("dve", lambda e: e.tensor_tensor(out=lb[:], in0=lb[:], in1=sm[:], op=ALU.mult),
                      [lb.k, sm.k], [lb.k])
                cx.op("dve", lambda e: e.tensor_scalar(out=oml[:], in0=lb[:], scalar1=-1.0, scalar2=1.0,
                                                       op0=ALU.mult, op1=ALU.add), [lb.k], [oml.k])
            h = Buf(cx, es, "h", [128, KC, TT], F32)
            u = Buf(cx, es, "u", [128, KC, TT], BF16)
            sqs = [Buf(cx, es, "sq", [128, TT], BF16) for _ in range(2)]
            rs = Buf(cx, es, "rs", [128, TT], F32)
            tmps = [Buf(cx, es, "tmp", [128, TT], F32) for _ in range(6)]
            bb = Buf(cx, es, "bb", [128, QG, TT], F32)
            dl = Buf(cx, es, "dl", [128, QG, 8], F32)
            kt = Buf(cx, es, "kt", [128, QG, TT], BF16)
            kdT = Buf(cx, es, "kdT", [128, QG, TT], BF16)
            qt = Buf(cx, es, "qt", [128, QG, TT], BF16)
            qpb = Buf(cx, es, "qpb", [128, QG, TT], BF16)
            kd_tok = Buf(cx, es, "kd_tok", [128, NP, QG * 128], BF16)
            v_tok = Buf(cx, es, "v_tok", [128, NP, VW], BF16)
            atts = [Buf(cx, es, "att", [128, 128], BF16) for _ in range(4)]
            osts = [Buf(cx, es, "ost", [128, 4, 128], F32) for _ in range(2)]
            tc = [0]
            ac = [0]
            oc = [0]

            def tmp():
                tc[0] += 1
                return tmps[tc[0] % len(tmps)]

            def k_finish(g, qb, src_ap, src_k):
                te = tmp()
                cx.op("act", lambda e: e.activation(out=te[:], in_=bb[:, qb, :], func=AF.Exp, scale=-1.0),
                      [bb.k], [te.k])
                cx.op("act", lambda e: e.activation(out=dl[:, qb, :], in_=bb[:, qb, 63:TT:64], func=AF.Exp),
                      [bb.k], [dl.k])
                cx.op("dve", lambda e: e.tensor_tensor(out=te[:], in0=src_ap, in1=te[:], op=ALU.mult),
                      [src_k, te.k], [te.k])
                cx.op("pool", lambda e: e.tensor_copy(out=kt[:, qb, :], in_=te[:]), [te.k], [kt.k])
                cx.op("pool", lambda e: e.tensor_tensor(
                    out=kdT[:, qb, :].rearrange("p (c t) -> p c t", t=64),
                    in0=te[:].rearrange("p (c t) -> p c t", t=64),
                    in1=dl[:, qb, :].unsqueeze(2).to_broadcast([128, 8, 64]), op=ALU.mult),
                    [te.k, dl.k], [kdT.k])

            for it in range(self.NT):
                self.load_h(h, it)
                self.norm_u(h, "norm_mix_%d" % i, u, sqs, rs)
                if kind == "gla":
                    def epi_glr(blk, bw, ps):
                        cx.op("act", lambda e: e.copy(out=glr[:], in_=ps[0:16, :]), [ps.k], [glr.k])
                    self.linear_fm(u, KC, w_in, 6144, 16, epi_glr)
                for g in range(NG):
                    if kind == "gla":
                        for qb in range(QG):
                            gq = g * QG + qb
                            ps = self.next_ps()
                            cx.op("pe", lambda e: e.matmul(ps[:], wgk2b[0:16, gq * 128:(gq + 1) * 128], glr[:],
                                                           start=True, stop=True), [wgk2b.k, glr.k], [ps.k])
                            t1 = tmp()
                            cx.op("act", lambda e: e.activation(out=t1[:], in_=ps[:], func=AF.Exp, scale=-1.0,
                                                                bias=negb[:, gq:gq + 1]), [ps.k, negb.k], [t1.k])
                            cx.op("act", lambda e: e.activation(out=t1[:], in_=t1[:], func=AF.Ln, bias=1.0),
                                  [t1.k], [t1.k])
                            cx.op("pool", lambda e: e.tensor_scalar(out=t1[:], in0=t1[:], scalar1=-1.0 / 16.0,
                                                                    scalar2=None, op0=ALU.mult), [t1.k], [t1.k])
                            cx.op("dve", lambda e: e.tensor_tensor_scan(
                                out=bb[:, qb, :], data0=self.cmsk_b[:], data1=t1[:], initial=0.0,
                                op0=ALU.mult, op1=ALU.add), [t1.k, self.cmsk_b.k], [bb.k])

                        def epi_k(blk, bw, ps):
                            k_finish(g, blk, ps[:], ps.k)
                        self.linear_fm(u, KC, w_in, kcol(g), 512, epi_k)
                    else:
                        def epi_f(blk, bw, ps):
                            gq = g * QG + blk
                            t1, t2 = tmp(), tmp()
                            cx.op("act", lambda e: e.activation(out=t1[:], in_=ps[:], func=AF.Sigmoid), [ps.k], [t1.k])
                            cx.op("dve", lambda e: e.tensor_scalar(
                                out=t1[:], in0=t1[:], scalar1=oml[:, gq:gq + 1], scalar2=lb[:, gq:gq + 1],
                                op0=ALU.mult, op1=ALU.add), [t1.k, oml.k, lb.k], [t1.k])
                            cx.op("act", lambda e: e.activation(out=t2[:], in_=t1[:], func=AF.Ln), [t1.k], [t2.k])
                            cx.op("dve", lambda e: e.tensor_tensor_scan(
                                out=bb[:, blk, :], data0=self.cmsk_b[:], data1=t2[:], initial=0.0,
                                op0=ALU.mult, op1=ALU.add), [t2.k, self.cmsk_b.k], [bb.k])
                            cx.op("dve", lambda e: e.tensor_scalar(out=t1[:], in0=t1[:], scalar1=-1.0, scalar2=1.0,
                                                                   op0=ALU.mult, op1=ALU.add), [t1.k], [t1.k])
                            k_finish(g, blk, t1[:], t1.k)
                        self.linear_fm(u, KC, w_in, kcol(g), 512, epi_f)

                    def epi_q(blk, bw, ps):
                        te = tmp()
                        cx.op("act", lambda e: e.activation(out=te[:], in_=bb[:, blk, :], func=AF.Exp), [bb.k], [te.k])
                        if kind == "gla":
                            cx.op("dve", lambda e: e.scalar_tensor_tensor(
                                out=qt[:, blk, :], in0=ps[:], scalar=qscale, in1=te[:], op0=ALU.mult, op1=ALU.mult),
                                [ps.k, te.k], [qt.k])
                        else:
                            t2 = tmp()
                            cx.op("act", lambda e: e.activation(out=t2[:], in_=ps[:], func=AF.Silu), [ps.k], [t2.k])
                            cx.op("dve", lambda e: e.scalar_tensor_tensor(
                                out=qt[:, blk, :], in0=t2[:], scalar=qscale, in1=te[:], op0=ALU.mult, op1=ALU.mult),
                                [t2.k, te.k], [qt.k])
                    self.linear_fm(u, KC, w_in, qcol(g), 512, epi_q)

                    for c in (range(8) if seqpar else []):
                        cx.op("dve", lambda e: e.tensor_tensor(out=eo[g][:, :, c + 1], in0=eo[g][:, :, c],
                                                               in1=dl[:, :, c], op=ALU.mult), [eo[g].k, dl.k], [eo[g].k])
                    for qb in (range(QG) if seqpar else []):
                        cx.op("pool", lambda e: e.tensor_tensor(
                            out=qpb[:, qb, :].rearrange("p (c t) -> p c t", t=64),
                            in0=qt[:, qb, :].rearrange("p (c t) -> p c t", t=64),
                            in1=eo[g][:, qb, 0:8].unsqueeze(2).to_broadcast([128, 8, 64]), op=ALU.mult),
                            [qt.k, eo[g].k], [qpb.k])
                    if seqpar:
                        self.store_tile(qpb, self.qp, g * QG * 128, QG, it, self.qp_k[it])
                        cx.op("dve", lambda e: e.tensor_copy(out=eo[g][:, :, 0], in_=eo[g][:, :, 8]), [eo[g].k], [eo[g].k])

                    if it == 0 and g == 0:
                        self.dump_sb("u", u, [128, KC, TT], BF16)
                        self.dump_sb("bb", bb, [128, QG, TT])
                        self.dump_sb("kt", kt, [128, QG, TT], BF16)
                        self.dump_sb("qt", qt, [128, QG, TT], BF16)
                        self.dump_sb("kdT", kdT, [128, QG, TT], BF16)
                    for s in range(VW // 512):
                        def epi_v(ts, ps):
                            cx.op("act", lambda e: e.copy(out=v_tok[:, ts, s * 512:(s + 1) * 512], in_=ps[:]),
                                  [ps.k], [v_tok.k])
                        self.linear_tm(u, w_in, vcol(g) + s * 512, 512, epi_v)

                    for ts in range(NP):
                        ps = self.next_ps()
                        pv = ps[:, 0:256].bitcast(BF16).rearrange("p (a b) -> p a b", b=128)
                        for qb in range(QG):
                            cx.op("pe", lambda e: e.transpose(pv[:, qb, :], kdT[:, qb, ts * 128:(ts + 1) * 128],
                                                              self.ident_b[:]), [kdT.k, self.ident_b.k], [ps.k])
                        cx.op("act", lambda e: e.copy(out=kd_tok[:, ts, :].rearrange("p (a b) -> p a b", b=128),
                                                      in_=pv), [ps.k], [kd_tok.k])

                    if it == 0 and g == 0:
                        self.dump_sb("v_tok", v_tok, [128, NP, VW], BF16)
                        self.dump_sb("kd_tok", kd_tok, [128, NP, QG * 128], BF16)
                    if kind == "gla":
                        obanks = [[(hh, vb) for vb in range(4)] for hh in range(Hg)]
                    else:
                        obanks = [[(hh, 0) for hh in range(Hg)]]
                    for pr in range(NP):
                        tsl = slice(pr * 128, (pr + 1) * 128)
                        attm = {}
                        for hh in range(Hg):
                            ps = self.next_ps()
                            for jj in range(dkb):
                                qb = hh * dkb + jj
                                cx.op("pe", lambda e: e.matmul(ps[:, 0:128], kt[:, qb, tsl], qt[:, qb, tsl],
                                                               start=(jj == 0), stop=(jj == dkb - 1)),
                                      [kt.k, qt.k], [ps.k])
                            a = atts[ac[0] % 4]
                            ac[0] += 1
                            cx.op("dve", lambda e: e.tensor_tensor(out=a[:], in0=ps[:, 0:128], in1=mask2, op=ALU.mult),
                                  [ps.k, self.consts_b.k], [a.k])
                            attm[hh] = a
                        ops = []
                        for bank in obanks:
                            ps = self.next_ps(hold=True)
                            ops.append(ps)
                            for slot, (hh, vb) in enumerate(bank):
                                osl = slice(slot * 128, (slot + 1) * 128)
                                cx.op("pe", lambda e: e.matmul(
                                    ps[:, osl], v_tok[:, pr, hh * dv + vb * 128:hh * dv + (vb + 1) * 128],
                                    attm[hh][:], start=(slot == 0), stop=False, skip_group_check=True),
                                    [v_tok.k, attm[hh].k], [ps.k])
                        for c2 in range(2):
                            c = pr * 2 + c2
                            csl = slice(pr * 128 + c2 * 64, pr * 128 + (c2 + 1) * 64)
                            rows = slice(c2 * 64, (c2 + 1) * 64)
                            for bi, bank in enumerate(obanks):
                                ps = ops[bi]
                                for slot, (hh, vb) in enumerate(bank):
                                    for jj in range(dkb):
                                        qb = hh * dkb + jj
                                        cx.op("pe", lambda e: e.matmul(
                                            ps[:, slot * 128 + c2 * 64:slot * 128 + (c2 + 1) * 64],
                                            Sbf[g][hh][:, jj, vb * 128:(vb + 1) * 128], qt[:, qb, csl],
                                            start=False, stop=(jj == dkb - 1), skip_group_check=True),
                                            [Sbf[g][hh].k, qt.k], [ps.k])
                            if kind == "gla":
                                for hh in range(Hg):
                                    for jj in range(dkb):
                                        qb = hh * dkb + jj
                                        ps = self.next_ps()
                                        cx.op("pe", lambda e: e.matmul(
                                            ps[:, 0:dv], kd_tok[rows, pr, qb * 128:(qb + 1) * 128],
                                            v_tok[rows, pr, hh * dv:(hh + 1) * dv], start=True, stop=True),
                                            [kd_tok.k, v_tok.k], [ps.k])
                                        cx.op("dve", lambda e: e.scalar_tensor_tensor(
                                            out=S32[g][hh][:, jj, :], in0=S32[g][hh][:, jj, :], scalar=dl[:, qb, c:c + 1],
                                            in1=ps[:, 0:dv], op0=ALU.mult, op1=ALU.add),
                                            [S32[g][hh].k, dl.k, ps.k], [S32[g][hh].k])
                                    cx.op("act", lambda e: e.copy(out=Sbf[g][hh][:], in_=S32[g][hh][:]),
                                          [S32[g][hh].k], [Sbf[g][hh].k])
                            else:
                                ps = self.next_ps()
                                for hh in range(Hg):
                                    cx.op("pe", lambda e: e.matmul(
                                        ps[:, hh * 128:(hh + 1) * 128], kd_tok[rows, pr, hh * 128:(hh + 1) * 128],
                                        v_tok[rows, pr, hh * dv:(hh + 1) * dv], start=True, stop=True),
                                        [kd_tok.k, v_tok.k], [ps.k])
                                for hh in range(Hg):
                                    cx.op("dve", lambda e: e.scalar_tensor_tensor(
                                        out=S32[g][hh][:, 0, :], in0=S32[g][hh][:, 0, :], scalar=dl[:, hh, c:c + 1],
                                        in1=ps[:, hh * 128:(hh + 1) * 128], op0=ALU.mult, op1=ALU.add),
                                        [S32[g][hh].k, dl.k, ps.k], [S32[g][hh].k])
                                    cx.op("act", lambda e: e.copy(out=Sbf[g][hh][:], in_=S32[g][hh][:]),
                                          [S32[g][hh].k], [Sbf[g][hh].k])
                        for bi, bank in enumerate(obanks):
                            ost = osts[oc[0] % 2]
                            oc[0] += 1
                            cx.op("act", lambda e: e.copy(out=ost[:], in_=ops[bi][:].rearrange("p (a b) -> p a b", b=128)),
                                  [ops[bi].k], [ost.k])
                            hh0, vb0 = bank[0]
                            vblk0 = (g * Hg + hh0) * dvb + vb0
                            dst = self.oloc.ap()[vblk0 * 128:(vblk0 + 4) * 128,
                                                 it * TT + pr * 128:it * TT + (pr + 1) * 128].rearrange(
                                "(a p) t -> p a t", p=128)
                            cx.dma("pool", dst, ost[:], self.oloc_k[it][(oc[0] - 1) % 2], ost.k)
                            self.release(ops[bi])
            for g in (range(NG) if seqpar else []):
                for hh in range(Hg):
                    c0 = ((g * Hg + hh) * dkb) * dv
                    cx.dma("pool", sloc.ap()[:, c0:c0 + dkb * dv].rearrange("p (a b) -> p a b", b=dv),
                           S32[g][hh][:], sloc_k, S32[g][hh].k)
                eoc = Buf(cx, es, "eoc", [128, QG], F32)
                cx.op("dve", lambda e: e.tensor_copy(out=eoc[:], in_=eo[g][:, :, 0]), [eo[g].k], [eoc.k])
                cx.dma("pool", sloc.ap()[:, NS0 + g * QG:NS0 + (g + 1) * QG], eoc[:], sloc_k, eoc.k)
            if seqpar:
                cx.allgather(sg.ap(), sloc.ap(), sg_k, sloc_k, [list(range(NCORES))])

            cx.barrier(extra=[t for pair in self.oloc_k for t in pair])

        with ExitStack() as es:
            Sin = Buf(cx, es, "Sin", [128, NQ, dv], F32)
            Sinb = Buf(cx, es, "Sinb", [128, NQ, dv], BF16)
            with ExitStack() as es2:
                if not seqpar:
                    NR = 0
                else:
                    NR = NCORES
                stg = [Buf(cx, es2, "stg", [128, NS], F32) for _ in range(2)]
                dd = [Buf(cx, es2, "dd", [128, NQ], F32) for _ in range(2)]
                cx.op("pool", lambda e: e.memset(Sin[:], 0.0), [], [Sin.k])
                for r in range(NR):
                    st, d1 = stg[r % 2], dd[r % 2]
                    m = self.cmask_b[:, r:r + 1]
                    cx.dma("pool", st[:], sg.ap()[r * 128:(r + 1) * 128, :], st.k, sg_k)
                    cx.op("dve", lambda e: e.tensor_scalar(out=d1[:], in0=st[:, NS0:NS], scalar1=-1.0, scalar2=m,
                                                           op0=ALU.add, op1=ALU.mult), [st.k, self.cmask_b.k], [d1.k])
                    cx.op("dve", lambda e: e.tensor_scalar(out=d1[:], in0=d1[:], scalar1=1.0, scalar2=None,
                                                           op0=ALU.add), [d1.k], [d1.k])
                    for q in range(NQ):
                        cx.op("dve", lambda e: e.tensor_scalar(out=Sin[:, q, :], in0=Sin[:, q, :], scalar1=d1[:, q:q + 1],
                                                               scalar2=None, op0=ALU.mult), [Sin.k, d1.k], [Sin.k])
                        cx.op("dve", lambda e: e.scalar_tensor_tensor(
                            out=Sin[:, q, :], in0=st[:, q * dv:(q + 1) * dv], scalar=m, in1=Sin[:, q, :],
                            op0=ALU.mult, op1=ALU.add), [st.k, Sin.k, self.cmask_b.k], [Sin.k])
                cx.op("act", lambda e: e.copy(out=Sinb[:], in_=Sin[:]), [Sin.k], [Sinb.k])
                cx.barrier(end=False)
            h = Buf(cx, es, "h", [128, KC, TT], F32)
            u = Buf(cx, es, "u", [128, KC, TT], BF16)
            o = Buf(cx, es, "o", [128, KC, TT], F32)
            qpl = Buf(cx, es, "qpl", [128, NQ, TT], BF16)
            ogb = Buf(cx, es, "ogb", [128, KC, TT], BF16)
            sqs = [Buf(cx, es, "sq", [128, TT], BF16) for _ in range(2)]
            rss = [Buf(cx, es, "rs", [128, TT], F32) for _ in range(2)]
            tmps = [Buf(cx, es, "tmp", [128, TT], F32) for _ in range(4)]
            tc = [0]
            hpb = dv // 128
            for it in range(self.NT):
                self.load_h(h, it)
                cx._waits("pool", self.oloc_k[it], [])
                self.load_tile(o, self.oloc, 0, KC, it, None)
                if seqpar:
                    self.load_tile(qpl, self.qp, 0, NQ, it, self.qp_k[it])
                self.norm_u(h, "norm_mix_%d" % i, u, sqs, rss[0])
                for hd in (range(NG * Hg) if seqpar else []):
                    for vb in range(hpb):
                        ps = self.next_ps()
                        for jj in range(dkb):
                            q = hd * dkb + jj
                            cx.op("pe", lambda e: e.matmul(ps[:], Sinb[:, q, vb * 128:(vb + 1) * 128], qpl[:, q, :],
                                                           start=(jj == 0), stop=(jj == dkb - 1)), [Sinb.k, qpl.k], [ps.k])
                        blk = hd * hpb + vb
                        cx.op("dve", lambda e: e.tensor_tensor(out=o[:, blk, :], in0=ps[:], in1=o[:, blk, :], op=ALU.add),
                              [ps.k, o.k], [o.k])
                cur = {"hd": -1, "rs": None}

                def epi_og(blk, bw, ps):
                    hd = blk // hpb
                    if hd != cur["hd"]:
                        cur["hd"] = hd
                        cur["rs"] = rss[hd % 2]
                        self.rstd(o, hd * hpb, hpb, dv, sqs, cur["rs"])
                    r = cur["rs"]
                    t1, t2 = tmps[tc[0] % 4], tmps[(tc[0] + 1) % 4]
                    tc[0] += 2
                    cx.op("act", lambda e: e.activation(out=t1[:], in_=ps[:],
                                                        func=(AF.Silu if kind == "gla" else AF.Sigmoid)), [ps.k], [t1.k])
                    cx.op("dve", lambda e: e.scalar_tensor_tensor(
                        out=t2[:], in0=o[:, blk, :], scalar=self.col(gn, blk % hpb), in1=r[:],
                        op0=ALU.mult, op1=ALU.mult), [o.k, r.k, self.cols_b.k], [t2.k])
                    cx.op("pool", lambda e: e.tensor_tensor(out=ogb[:, blk, :], in0=t2[:], in1=t1[:], op=ALU.mult),
                          [t1.k, t2.k], [ogb.k])
                self.linear_fm(u, KC, w_in, ogcol, D, epi_og)

                def epi_out(blk, bw, ps):
                    cx.op("dve", lambda e: e.tensor_tensor(out=h[:, blk, :], in0=ps[:], in1=h[:, blk, :], op=ALU.add),
                          [ps.k, h.k], [h.k])
                self.linear_fm(ogb, KC, w_out, 0, D, epi_out)
                self.store_tile(h, self.hT, 0, KC, it, self.hT_k[it])
            self.h_stored()
            cx.barrier()

    def phase_ssm(self, i, j):
        cx, nc = self.cx, self.nc
        w_in, w_out = "ssm_w_in_%d" % j, "ssm_w_out_%d" % j
        mask2 = self.consts_b[:, 0, :]
        selpair = self.consts_b[:, 1, :]
        sel63 = self.consts_b[:, 2, :]
        sel127 = self.consts_b[:, 3, :]
        NP = TT // 128
        G, HG, P, NST = 8, 8, 64, 128
        GI = 1

        class View:
            def __init__(self, ap, k):
                self.ap, self.k = ap, k

            def __getitem__(self, key):
                return self.ap[key]

        def bc(ap2, n):
            return ap2.unsqueeze(2).to_broadcast([128, ap2.shape[1], n])

        with ExitStack() as es:
            big1 = Buf(cx, es, "big1", [128, KC * TT], F32)
            big2 = Buf(cx, es, "big2", [128, KC * TT], F32)
            h1 = View(big1.t[:].rearrange("p (a b) -> p a b", b=TT), big1.k)
            yT = View(big1.t[:].bitcast(BF16).rearrange("p (a b) -> p a b", b=TT), big1.k)
            u = View(big2.t[:, 0:4096].bitcast(BF16).rearrange("p (a b) -> p a b", b=TT), Trk("u"))
            BT = View(big2.t[:, 4096:6144].bitcast(BF16).rearrange("p (a b) -> p a b", b=TT), Trk("BT"))
            CT = View(big2.t[:, 6144:8192].bitcast(BF16).rearrange("p (a b) -> p a b", b=TT), Trk("CT"))
            h2 = View(big2.t[:].rearrange("p (a b) -> p a b", b=TT), big2.k)
            S32 = [Buf(cx, es, "S32", [128, HG * P], F32) for _ in range(G)]
            Sbf = [Buf(cx, es, "Sbf", [128, HG * P], BF16) for _ in range(G)]
            halo = Buf(cx, es, "halo", [128, 48, 3], F32)
            identf = Buf(cx, es, "identf", [128, 128], F32)
            onesf = Buf(cx, es, "onesf", [128, 128], F32)
            dtb = Buf(cx, es, "dtb", [128, 64], F32)
            arow = Buf(cx, es, "arow", [128, 64], F32)
            ddr = Buf(cx, es, "ddr", [128, 64], F32)
            dt_tok = Buf(cx, es, "dt_tok", [128, NP, 64], F32)
            la_tok = Buf(cx, es, "la_tok", [128, NP, 64], F32)
            b_tok = Buf(cx, es, "b_tok", [128, NP, 64], F32)
            eb_tok = Buf(cx, es, "eb_tok", [128, NP, 64], F32)
            we_tok = Buf(cx, es, "we_tok", [128, NP, 64], F32)
            Dc = Buf(cx, es, "Dc", [128, NP, 2, 64], F32)
            sqs = [Buf(cx, es, "sq", [128, TT], BF16) for _ in range(1)]
            rs = Buf(cx, es, "rs", [128, TT], F32)
            tmpc = [Buf(cx, es, "tmpc", [128, TT + 3], F32) for _ in range(2)]
            acc = Buf(cx, es, "acc", [128, TT], F32)
            xTg = Buf(cx, es, "xTg", [128, 4, TT], BF16)
            x_tok = [Buf(cx, es, "x_tok", [128, NP, 512], BF16) for _ in range(GI)]
            zg = [Buf(cx, es, "zg", [128, NP, 512], F32) for _ in range(GI)]
            rel = [Buf(cx, es, "rel", [128, 8, 128], F32) for _ in range(GI)]
            Mb = [Buf(cx, es, "Mb", [128, 8, 128], BF16) for _ in range(GI)]
            xdt = [Buf(cx, es, "xdt", [128, 512], BF16) for _ in range(GI)]
            xw = [Buf(cx, es, "xw", [128, 512], BF16) for _ in range(GI)]
            ytmp = [Buf(cx, es, "ytmp", [128, 512], F32) for _ in range(GI)]
            yn = [Buf(cx, es, "yn", [128, 512], BF16) for _ in range(GI)]
            cbm = [Buf(cx, es, "cbm", [128, 128], F32) for _ in range(GI)]
            btok = [Buf(cx, es, "btok", [128, 128], BF16) for _ in range(GI)]
            ss = [Buf(cx, es, "ss", [128, 1], F32) for _ in range(GI)]
            for g in range(G):
                cx.op("pool", lambda e: e.memset(S32[g][:], 0.0), [], [S32[g].k])
                cx.op("pool", lambda e: e.memset(Sbf[g][:], 0.0), [], [Sbf[g].k])
            cx.op("pool", lambda e: e.memset(halo[:], 0.0), [], [halo.k])
            cx.op("pool", lambda e: e.memset(onesf[:], 1.0), [], [onesf.k])
            cx.op("pool", lambda e: e.memset(identf[:], 0.0), [], [identf.k])
            cx.op("pool", lambda e: e.affine_select(
                out=identf[:], in_=identf[:], pattern=[[-1, 128]], compare_op=ALU.not_equal, fill=1.0,
                base=0, channel_multiplier=1), [identf.k], [identf.k])
            cx.dma("pool", dtb[:], self.rows_d["ssm_dt_bias_%d" % j].ap().partition_broadcast(128), dtb.k, None)
            cx.dma("pool", arow[:], self.rows_d["ssm_a_log_%d" % j].ap().partition_broadcast(128), arow.k, None)
            cx.dma("pool", ddr[:], self.rows_d["ssm_d_%d" % j].ap().partition_broadcast(128), ddr.k, None)
            cx.op("act", lambda e: e.activation(out=arow[:], in_=arow[:], func=AF.Exp), [arow.k], [arow.k])
            cx.op("dve", lambda e: e.tensor_scalar(out=arow[:], in0=arow[:], scalar1=-1.0, scalar2=None, op0=ALU.mult),
                  [arow.k], [arow.k])
            cb16 = Buf(cx, es, "cb16", [128, 4, 128], BF16)
            onesb = self.ones_b
            cx.op("dve", lambda e: e.tensor_copy(out=cb16[:], in_=self.consts_b[:]), [self.consts_b.k], [cb16.k])
            la_hi = Buf(cx, es, "la_hi", [128, NP, 64], BF16)
            la_lo = Buf(cx, es, "la_lo", [128, NP, 64], BF16)
            b_hi = Buf(cx, es, "b_hi", [128, NP, 64], BF16)
            b_lo = Buf(cx, es, "b_lo", [128, NP, 64], BF16)
            hl_f = Buf(cx, es, "hl_f", [128, NP, 64], F32)
            dg_lo = Buf(cx, es, "dg_lo", [128, 8, 128], BF16)
            dg_hi = Buf(cx, es, "dg_hi", [128, 8, 128], BF16)

            def split(src, hi, lo):
                cx.op("dve", lambda e: e.tensor_copy(out=hi[:], in_=src[:]), [src.k], [hi.k])
                cx.op("dve", lambda e: e.tensor_copy(out=hl_f[:], in_=hi[:]), [hi.k], [hl_f.k])
                cx.op("dve", lambda e: e.tensor_tensor(out=lo[:], in0=src[:], in1=hl_f[:], op=ALU.subtract),
                      [src.k, hl_f.k], [lo.k])
            cc = [0]

            def conv_epi(cb, ps, dst_ap, dst_k):
                tcv = tmpc[cc[0] % 2]
                cc[0] += 1
                cx.op("pool", lambda e: e.tensor_copy(out=tcv[:, 0:3], in_=halo[:, cb, :]), [halo.k], [tcv.k])
                cx.op("act", lambda e: e.copy(out=tcv[:, 3:TT + 3], in_=ps[:]), [ps.k], [tcv.k])
                cx.op("pool", lambda e: e.tensor_copy(out=halo[:, cb, :], in_=tcv[:, TT:TT + 3]), [tcv.k], [halo.k])
                wc = lambda t: self.col("ssm_conv_w_%d_%d" % (j, t), cb)
                cx.op("dve", lambda e: e.tensor_scalar(out=acc[:], in0=tcv[:, 0:TT], scalar1=wc(0),
                                                       scalar2=self.col("ssm_conv_b_%d" % j, cb),
                                                       op0=ALU.mult, op1=ALU.add), [tcv.k, self.cols_b.k], [acc.k])
                for t in range(1, 4):
                    cx.op("dve", lambda e: e.scalar_tensor_tensor(out=acc[:], in0=tcv[:, t:t + TT], scalar=wc(t),
                                                                  in1=acc[:], op0=ALU.mult, op1=ALU.add),
                          [tcv.k, acc.k, self.cols_b.k], [acc.k])
                cx.op("act", lambda e: e.activation(out=dst_ap, in_=acc[:], func=AF.Silu), [acc.k], [dst_k])

            for it in range(self.NT):
                self.load_h(h1, it)
                self.norm_u(h1, "norm_mix_%d" % i, u, sqs, rs)

                if SSM_STAGE < 1:
                    cx.barrier(end=False)
                    continue
                def epi_bc(blk, bw, ps):
                    if blk < 8:
                        conv_epi(32 + blk, ps, BT[:, blk, :], BT.k)
                    else:
                        conv_epi(32 + blk, ps, CT[:, blk - 8, :], CT.k)
                self.linear_fm(u, KC, w_in, 8192, 2048, epi_bc)

                if SSM_STAGE < 2:
                    cx.barrier(end=False)
                    continue
                def epi_dt(ts, ps):
                    cx.op("dve", lambda e: e.tensor_tensor(out=dt_tok[:, ts, :], in0=ps[:, 0:64], in1=dtb[:], op=ALU.add),
                          [ps.k, dtb.k], [dt_tok.k])
                self.linear_tm(u, w_in, 10240, 64, epi_dt)
                cx.op("act", lambda e: e.activation(out=dt_tok[:], in_=dt_tok[:], func=AF.Exp), [dt_tok.k], [dt_tok.k])
                cx.op("act", lambda e: e.activation(out=dt_tok[:], in_=dt_tok[:], func=AF.Ln, bias=1.0), [dt_tok.k], [dt_tok.k])
                for ts in range(NP):
                    cx.op("dve", lambda e: e.tensor_tensor(out=la_tok[:, ts, :], in0=dt_tok[:, ts, :], in1=arow[:], op=ALU.mult),
                          [dt_tok.k, arow.k], [la_tok.k])
                split(la_tok, la_hi, la_lo)
                for ts in range(NP):
                    ps = self.next_ps()
                    cx.op("pe", lambda e: e.matmul(ps[:, 0:64], cb16[:, 0, :], la_hi[:, ts, :], start=True, stop=False),
                          [cb16.k, la_hi.k], [ps.k])
                    cx.op("pe", lambda e: e.matmul(ps[:, 0:64], cb16[:, 0, :], la_lo[:, ts, :], start=False, stop=True),
                          [cb16.k, la_lo.k], [ps.k])
                    cx.op("act", lambda e: e.copy(out=b_tok[:, ts, :], in_=ps[:, 0:64]), [ps.k], [b_tok.k])
                split(b_tok, b_hi, b_lo)
                cx.op("act", lambda e: e.activation(out=eb_tok[:], in_=b_tok[:], func=AF.Exp), [b_tok.k], [eb_tok.k])
                for ts in range(NP):
                    ps = self.next_ps()
                    for si in range(3):
                        for hl, src in enumerate((b_hi, b_lo)):
                            cx.op("pe", lambda e: e.matmul(ps[:, si * 64:(si + 1) * 64], cb16[:, 1 + si, :], src[:, ts, :],
                                                           start=(si == 0 and hl == 0), stop=(hl == 1),
                                                           skip_group_check=True), [cb16.k, src.k], [ps.k])
                    cx.op("dve", lambda e: e.tensor_tensor(out=we_tok[:, ts, :], in0=ps[:, 0:64], in1=b_tok[:, ts, :],
                                                           op=ALU.subtract), [ps.k, b_tok.k], [we_tok.k])
                    cx.op("act", lambda e: e.activation(out=Dc[:, ts, :, :], in_=ps[:, 64:192].rearrange("p (a b) -> p a b", b=64),
                                                        func=AF.Exp), [ps.k], [Dc.k])
                cx.op("act", lambda e: e.activation(out=we_tok[:], in_=we_tok[:], func=AF.Exp), [we_tok.k], [we_tok.k])

                if SSM_STAGE < 3:
                    cx.barrier(end=False)
                    continue
                for gp in range(G // GI):
                    gs = tuple(range(gp * GI, (gp + 1) * GI))
                    for g in gs:
                        sl = g % GI
                        def epi_x(blk, bw, ps):
                            conv_epi(g * 4 + blk, ps, xTg[:, blk, :], xTg.k)
                        self.linear_fm(u, KC, w_in, 4096 + g * 512, 512, epi_x)
                        for ts in range(NP):
                            ps = self.next_ps()
                            pv = ps[:, 0:256].bitcast(BF16).rearrange("p (a b) -> p a b", b=128)
                            for a in range(4):
                                cx.op("pe", lambda e: e.transpose(pv[:, a, :], xTg[:, a, ts * 128:(ts + 1) * 128],
                                                                  self.ident_b[:]), [xTg.k, self.ident_b.k], [ps.k])
                            cx.op("act", lambda e: e.copy(out=x_tok[sl][:, ts, :].rearrange("p (a b) -> p a b", b=128),
                                                          in_=pv), [ps.k], [x_tok[sl].k])
                        def epi_z(ts, ps):
                            cx.op("act", lambda e: e.activation(out=zg[sl][:, ts, :], in_=ps[:], func=AF.Silu),
                                  [ps.k], [zg[sl].k])
                        self.linear_tm(u, w_in, g * 512, 512, epi_z)

                    for pr in (range(NP) if SSM_STAGE >= 4 else []):
                        tsl = slice(pr * 128, (pr + 1) * 128)
                        psy, psi = {}, {}
                        for g in gs:
                            sl = g % GI
                            hs = slice(g * 8, (g + 1) * 8)
                            ps = self.next_ps()
                            pvb = ps[:, 0:64].bitcast(BF16)
                            cx.op("pe", lambda e: e.transpose(pvb, BT[:, g, tsl], self.ident_b[:]),
                                  [BT.k, self.ident_b.k], [ps.k])
                            cx.op("act", lambda e: e.copy(out=btok[sl][:], in_=pvb), [ps.k], [btok[sl].k])
                            ps = self.next_ps()
                            cx.op("pe", lambda e: e.matmul(ps[:, 0:128], BT[:, g, tsl], CT[:, g, tsl], start=True, stop=True),
                                  [BT.k, CT.k], [ps.k])
                            cx.op("dve", lambda e: e.tensor_tensor(out=cbm[sl][:], in0=ps[:, 0:128], in1=mask2, op=ALU.mult),
                                  [ps.k, self.consts_b.k], [cbm[sl].k])
                            for dgx, bx in ((dg_hi, b_hi), (dg_lo, b_lo)):
                                cx.op("dve", lambda e: e.tensor_tensor(
                                    out=dgx[:], in0=identf[:].unsqueeze(1).to_broadcast([128, 8, 128]),
                                    in1=bc(bx[:, pr, hs], 128), op=ALU.mult), [identf.k, bx.k], [dgx.k])
                            pb0, pb1 = self.next_ps(hold=True), self.next_ps(hold=True)
                            for pbx, lo4 in ((pb0, 0), (pb1, 4)):
                                for hl, dgx in enumerate((dg_hi, dg_lo)):
                                    cx.op("pe", lambda e: e.matmul(
                                        pbx[:], onesb[:], dgx[:, lo4:lo4 + 4, :].rearrange("p a b -> p (a b)"),
                                        start=(hl == 0), stop=(hl == 1)), [onesb.k, dgx.k], [pbx.k])
                            cx.op("dve", lambda e: e.tensor_tensor(
                                out=rel[sl][:, 0:4, :], in0=pb0[:].rearrange("p (a b) -> p a b", b=128),
                                in1=bc(b_tok[:, pr, g * 8:g * 8 + 4], 128), op=ALU.subtract), [pb0.k, b_tok.k], [rel[sl].k])
                            cx.op("dve", lambda e: e.tensor_tensor(
                                out=rel[sl][:, 4:8, :], in0=pb1[:].rearrange("p (a b) -> p a b", b=128),
                                in1=bc(b_tok[:, pr, g * 8 + 4:g * 8 + 8], 128), op=ALU.subtract), [pb1.k, b_tok.k], [rel[sl].k])
                            self.release(pb0)
                            self.release(pb1)
                            cx.op("act", lambda e: e.activation(out=rel[sl][:], in_=rel[sl][:], func=AF.Relu, scale=-1.0),
                                  [rel[sl].k], [rel[sl].k])
                            cx.op("act", lambda e: e.activation(out=rel[sl][:], in_=rel[sl][:], func=AF.Exp, scale=-1.0),
                                  [rel[sl].k], [rel[sl].k])
                            cx.op("pool", lambda e: e.tensor_tensor(
                                out=Mb[sl][:], in0=rel[sl][:], in1=cbm[sl][:].unsqueeze(1).to_broadcast([128, 8, 128]),
                                op=ALU.mult), [rel[sl].k, cbm[sl].k], [Mb[sl].k])
                            cx.op("pool", lambda e: e.tensor_tensor(
                                out=xdt[sl][:].rearrange("p (a b) -> p a b", b=64),
                                in0=x_tok[sl][:, pr, :].rearrange("p (a b) -> p a b", b=64),
                                in1=bc(dt_tok[:, pr, hs], 64), op=ALU.mult), [x_tok[sl].k, dt_tok.k], [xdt[sl].k])
                            cx.op("pool", lambda e: e.tensor_tensor(
                                out=xw[sl][:].rearrange("p (a b) -> p a b", b=64),
                                in0=xdt[sl][:].rearrange("p (a b) -> p a b", b=64),
                                in1=bc(we_tok[:, pr, hs], 64), op=ALU.mult), [xdt[sl].k, we_tok.k], [xw[sl].k])
                            psy[g] = self.next_ps(hold=True)
                            for h8 in range(8):
                                cx.op("pe", lambda e: e.matmul(
                                    psy[g][:, h8 * 64:(h8 + 1) * 64], Mb[sl][:, h8, :], xdt[sl][:, h8 * 64:(h8 + 1) * 64],
                                    start=(h8 == 0), stop=True, skip_group_check=True), [Mb[sl].k, xdt[sl].k], [psy[g].k])
                            psi[g] = self.next_ps(hold=True)
                        for c2 in range(2):
                            rows = slice(c2 * 64, (c2 + 1) * 64)
                            tcs = slice(pr * 128 + c2 * 64, pr * 128 + (c2 + 1) * 64)
                            for g in gs:
                                kw = {"tile_position": (0, 64)} if c2 == 1 else {}
                                cx.op("pe", lambda e: e.matmul(psi[g][rows, :], CT[:, g, tcs], Sbf[g][:], start=True, stop=True,
                                                               skip_group_check=True, **kw), [CT.k, Sbf[g].k], [psi[g].k])
                            for g in gs:
                                sl = g % GI
                                hs = slice(g * 8, (g + 1) * 8)
                                ps = self.next_ps()
                                cx.op("pe", lambda e: e.matmul(ps[:], btok[sl][rows, :], xw[sl][rows, :], start=True, stop=True),
                                      [btok[sl].k, xw[sl].k], [ps.k])
                                cx.op("dve", lambda e: e.tensor_tensor(
                                    out=S32[g][:].rearrange("p (a b) -> p a b", b=64),
                                    in0=S32[g][:].rearrange("p (a b) -> p a b", b=64),
                                    in1=bc(Dc[:, pr, c2, hs], 64), op=ALU.mult), [S32[g].k, Dc.k], [S32[g].k])
                                cx.op("dve", lambda e: e.tensor_tensor(out=S32[g][:], in0=ps[:], in1=S32[g][:], op=ALU.add),
                                      [ps.k, S32[g].k], [S32[g].k])
                                cx.op("act", lambda e: e.copy(out=Sbf[g][:], in_=S32[g][:]), [S32[g].k], [Sbf[g].k])
                        for g in gs:
                            sl = g % GI
                            hs = slice(g * 8, (g + 1) * 8)
                            y = ytmp[sl]
                            tt = View(rel[sl].t[:, 0:4, :].rearrange("p a b -> p (a b)"), rel[sl].k)
                            cx.op("dve", lambda e: e.tensor_tensor(
                                out=y[:].rearrange("p (a b) -> p a b", b=64),
                                in0=psi[g][:].rearrange("p (a b) -> p a b", b=64),
                                in1=bc(eb_tok[:, pr, hs], 64), op=ALU.mult), [psi[g].k, eb_tok.k], [y.k])
                            cx.op("dve", lambda e: e.tensor_tensor(out=y[:], in0=psy[g][:], in1=y[:], op=ALU.add),
                                  [psy[g].k, y.k], [y.k])
                            self.release(psy[g])
                            self.release(psi[g])
                            cx.op("pool", lambda e: e.tensor_tensor(
                                out=tt[:].rearrange("p (a b) -> p a b", b=64),
                                in0=x_tok[sl][:, pr, :].rearrange("p (a b) -> p a b", b=64),
                                in1=bc(ddr[:, hs], 64), op=ALU.mult), [x_tok[sl].k, ddr.k], [tt.k])
                            cx.op("pool", lambda e: e.tensor_tensor(out=y[:], in0=y[:], in1=tt[:], op=ALU.add), [y.k, tt.k], [y.k])
                            cx.op("pool", lambda e: e.tensor_tensor(out=y[:], in0=y[:], in1=zg[sl][:, pr, :], op=ALU.mult),
                                  [y.k, zg[sl].k], [y.k])
                            cx.op("pool", lambda e: e.tensor_tensor(out=tt[:], in0=y[:], in1=y[:], op=ALU.mult), [y.k], [tt.k])
                            cx.op("dve", lambda e: e.reduce_sum(out=ss[sl][:], in_=tt[:], axis=mybir.AxisListType.X),
                                  [tt.k], [ss[sl].k])
                            cx.op("act", lambda e: e.activation(out=ss[sl][:], in_=ss[sl][:], func=AF.Sqrt, bias=EPS,
                                                                scale=1.0 / 512.0), [ss[sl].k], [ss[sl].k])
                            cx.op("dve", lambda e: e.reciprocal(out=ss[sl][:], in_=ss[sl][:]), [ss[sl].k], [ss[sl].k])
                            cx.op("dve", lambda e: e.tensor_scalar(out=yn[sl][:], in0=y[:], scalar1=ss[sl][:, 0:1], scalar2=None,
                                                                   op0=ALU.mult), [y.k, ss[sl].k], [yn[sl].k])
                            ps = self.next_ps()
                            pv = ps[:, 0:256].bitcast(BF16).rearrange("p (a b) -> p a b", b=128)
                            for a in range(4):
                                cx.op("pe", lambda e: e.transpose(pv[:, a, :], yn[sl][:, a * 128:(a + 1) * 128],
                                                                  self.ident_b[:]), [yn[sl].k, self.ident_b.k], [ps.k])
                            for a in range(4):
                                cx.op("act", lambda e: e.activation(
                                    out=yT[:, g * 4 + a, tsl], in_=pv[:, a, :], func=AF.Copy,
                                    scale=self.col("ssm_norm_%d" % j, g * 4 + a)), [ps.k, self.cols_b.k], [yT.k])
                cx.barrier(end=False)
                if SSM_STAGE < 5:
                    continue
                self.load_h(h2, it)

                def epi_out(blk, bw, ps):
                    cx.op("dve", lambda e: e.tensor_tensor(out=h2[:, blk, :], in0=ps[:], in1=h2[:, blk, :], op=ALU.add),
                          [ps.k, h2.k], [h2.k])
                self.linear_fm(yT, 32, w_out, 0, D, epi_out)
                self.store_tile(h2, self.hT, 0, KC, it, self.hT_k[it])
                cx.barrier(end=False)
            self.h_stored()
            cx.barrier()


def _weight(inputs, nm):
    base, idx = nm.rsplit("_", 1)
    table = {
        "w_up": inputs["w_up"], "w_down": inputs["w_down"],
        "w_ple_proj": inputs["w_ple_proj"], "w_ple_gate": inputs["w_ple_gate"],
        "gla_w_in": inputs["gla_w_in"], "gla_w_out": inputs["gla_w_out"],
        "hgrn_w_in": inputs["hgrn_w_in"], "hgrn_w_out": inputs["hgrn_w_out"],
        "ssm_w_in": inputs["ssm_w_in"], "ssm_w_out": inputs["ssm_w_out"],
    }
    return table[base][int(idx)]


def make_consts():
    c = np.zeros((128, 4, 128), np.float32)
    s = np.arange(128)[:, None]
    t = np.arange(128)[None, :]
    c[:, 0, :] = ((s // 64 == t // 64) & (s <= t)).astype(np.float32)
    c[:, 1, :] = (((s == 63) & (t < 64)) | ((s == 127) & (t >= 64))).astype(np.float32)
    c[:, 2, :] = (s == 63).astype(np.float32) * np.ones_like(t)
    c[:, 3, :] = (s == 127).astype(np.float32) * np.ones_like(t)
    return c


def run(inputs, depth, T, enable_mix=True, trace=False):
    x = np.asarray(inputs["x"])
    p = np.asarray(inputs["p"])
    B, L, _ = x.shape
    segs = NCORES // B
    assert L == segs * T
    prog = Prog(T, depth, enable_mix)
    nc = prog.build()
    lay, ncol, _ = col_layout(depth)
    cols = np.zeros((128, ncol), np.float32)

    def put(nm, v):
        off, n = lay[nm]
        cols[:, off:off + n] = to_cols(v)
    for i in range(depth):
        put("norm_mix_%d" % i, inputs["norm_mix"][i])
        put("norm_mlp_%d" % i, inputs["norm_mlp"][i])
        put("norm_ple_%d" % i, inputs["norm_ple"][i])
        kind, j = kind_of(i)
        if kind == 0:
            put("gla_b_gk_%d" % j, inputs["gla_b_gk"][j])
            put("gla_gn_%d" % j, inputs["gla_gn"][j])
        elif kind == 1:
            put("hgrn_gn_%d" % j, inputs["hgrn_gn"][j])
            for l in range(depth):
                put("hgrn_lb_%d" % l, inputs["hgrn_lb_logits"][l])
        else:
            for t in range(4):
                put("ssm_conv_w_%d_%d" % (j, t), inputs["ssm_conv_w"][j][t])
            put("ssm_conv_b_%d" % j, inputs["ssm_conv_b"][j])
            put("ssm_norm_%d" % j, inputs["ssm_norm"][j])
    put("norm_final", inputs["norm_final"])
    consts = make_consts()
    shared = {"cols": cols, "consts": consts}
    for i in range(depth):
        kind, j = kind_of(i)
        if kind == 0:
            shared["gla_w_gk2_%d" % j] = np.ascontiguousarray(inputs["gla_w_gk2"][j], np.float32)
        if kind == 2:
            for nm in ("ssm_dt_bias", "ssm_a_log", "ssm_d"):
                shared["%s_%d" % (nm, j)] = np.ascontiguousarray(
                    np.asarray(inputs[nm][j], np.float32).reshape(1, 64))
    in_maps = []
    for c in range(NCORES):
        b, s = c // segs, c % segs
        m = dict(shared)
        m["xT"] = np.ascontiguousarray(x[b, s * T:(s + 1) * T, :].T)
        m["pT"] = np.ascontiguousarray(np.transpose(p[:depth, b, s * T:(s + 1) * T, :], (0, 2, 1)))
        cm = np.zeros((128, 16), np.float32)
        for r in range(NCORES):
            rb, rs = r // segs, r % segs
            if rb == b and rs < s:
                cm[:, r] = 1.0
            if rb == b and rs == s - 1:
                cm[:, 8 + r] = 1.0
        m["cmask"] = cm
        for nm, K, N in big_weights(depth):
            W = _weight(inputs, nm)
            if USE_CC:
                r = K // NCORES
                m[nm] = np.ascontiguousarray(W[c * r:(c + 1) * r, :], np.float32)
            else:
                m[nm] = np.ascontiguousarray(W, np.float32)
        in_maps.append(m)
    res = run_bass_kernel_spmd(nc, in_maps, core_ids=list(range(NCORES)), trace=trace)
    out = np.empty((B, L, D), np.float32)
    for c in range(NCORES):
        b, s = c // segs, c % segs
        out[b, s * T:(s + 1) * T, :] = res.results[c]["outT"].T
    if DEBUG:
        return out, res
    if trace:
        return out, res
    return out


def kernel(**inputs):
    depth = int(np.asarray(inputs["p"]).shape[0])
    B, L, _ = np.asarray(inputs["x"]).shape
    return run(inputs, depth, L * B // NCORES)
```

```python
import numpy as np
from contextlib import ExitStack
import concourse.bass as bass
import concourse.mybir as mybir
from concourse.bass_utils import run_bass_kernel_spmd

F32 = mybir.dt.float32
BF16 = mybir.dt.bfloat16
ALU = mybir.AluOpType
AF = mybir.ActivationFunctionType

NCORES = 2
USE_CC = False
D = 2048
KC = D // 128
TT = 512
EPS = 1e-6
PLE = 256
DFF = 8192
N_MIX = 3
DEBUG = False
SSM_STAGE = 99
KINDS = None


def kind_of(i):
    if KINDS is None:
        return i % N_MIX, i // N_MIX
    k = KINDS[i]
    return k, sum(1 for x in KINDS[:i] if x == k)


class Trk:
    __slots__ = ("name", "w", "r", "dsem", "dcnt", "excl")

    def __init__(self, name="", excl=False):
        self.name = name
        self.excl = excl
        self.w = None
        self.r = {}
        self.dsem = None
        self.dcnt = 0


class Ctx:
    def __init__(self, nc, es):
        self.nc = nc
        self.es = es
        self.sems = {}
        self.engs = {}
        for nm, h in (("pe", nc.tensor), ("act", nc.scalar), ("dve", nc.vector),
                      ("pool", nc.gpsimd), ("sp", nc.sync)):
            self.sems[nm] = es.enter_context(nc.semaphore("sem_" + nm))
            self.engs[nm] = {"h": h, "cnt": 0, "known": {}}
        self.ndsem = 0
        self.phase_trks = []
        self.phase_evs = {}
        self.free_dsems = []
        self.uid = 0

    def name(self, p):
        self.uid += 1
        return "%s_%d" % (p, self.uid)

    def _dsem(self, t):
        if t.dsem is None:
            if self.free_dsems:
                t.dsem, t.dcnt = self.free_dsems.pop()
            else:
                t.dsem = "d%d" % self.ndsem
                self.ndsem += 1
                self.sems[t.dsem] = self.es.enter_context(self.nc.semaphore("dsem_%s" % t.dsem))
        return t.dsem

    def _wait(self, eng, k, v):
        e = self.engs[eng]
        if k == eng and v > e["cnt"]:
            return
        if v > 0 and e["known"].get(k, 0) < v:
            e["h"].wait_ge(self.sems[k], v)
            e["known"][k] = v

    def _waits(self, eng, reads, writes):
        need = {}
        for t in reads:
            if t.w is not None:
                k, v = t.w
                if need.get(k, 0) < v:
                    need[k] = v
            if t.excl:
                for k, v in t.r.items():
                    if k != eng and need.get(k, 0) < v:
                        need[k] = v
        for t in writes:
            if t.w is not None:
                k, v = t.w
                if need.get(k, 0) < v:
                    need[k] = v
            for k, v in t.r.items():
                if need.get(k, 0) < v:
                    need[k] = v
        for k, v in need.items():
            self._wait(eng, k, v)

    def op(self, eng, fn, reads=(), writes=(), inc=True):
        self._waits(eng, reads, writes)
        e = self.engs[eng]
        ins = fn(e["h"])
        if inc:
            e["cnt"] += 1
            ins.then_inc(self.sems[eng], 1)
            c = e["cnt"]
        else:
            c = e["cnt"] + 1
        for t in reads:
            if t.r.get(eng, 0) < c:
                t.r[eng] = c
        for t in writes:
            t.w = (eng, c)
            t.r = {}
        return ins

    def dma(self, q, out_ap, in_ap, out_t, in_t, **kw):
        reads = [in_t] if in_t is not None else []
        k = self._dsem(out_t)
        saved = out_t.w
        if saved is not None and saved[0] == k:
            out_t.w = None
        self._waits(q, reads, [out_t])
        out_t.w = saved
        e = self.engs[q]
        ins = e["h"].dma_start(out=out_ap, in_=in_ap, **kw)
        out_t.dcnt += 16
        ins.then_inc(self.sems[k], 16)
        if q != "sp" or in_t is not None:
            self.phase_evs[k] = out_t.dcnt
        if in_t is not None and in_t.r.get(k, 0) < out_t.dcnt:
            in_t.r[k] = out_t.dcnt
        out_t.w = (k, out_t.dcnt)
        out_t.r = {}
        return ins

    def allgather(self, out_ap, in_ap, out_t, in_t, groups):
        q = "pool"
        self._waits(q, [in_t], [out_t])
        e = self.engs[q]
        k = self._dsem(out_t)
        ins = e["h"].collective_compute("AllGather", ALU.bypass, replica_groups=groups,
                                        ins=[in_ap], outs=[out_ap])
        out_t.dcnt += 1
        ins.then_inc(self.sems[k], 1)
        if in_t.r.get(k, 0) < out_t.dcnt:
            in_t.r[k] = out_t.dcnt
        out_t.w = (k, out_t.dcnt)
        out_t.r = {}
        return ins

    def finish(self, eng, trks):
        self._waits(eng, trks, [])

    def barrier(self, extra=(), end=True):
        evs = {}
        for nm in ("pe", "act", "dve", "pool"):
            evs[nm] = self.engs[nm]["cnt"]
        for t in list(self.phase_trks) + list(extra):
            if t.dsem is not None and t.dcnt > 0:
                evs[t.dsem] = t.dcnt
        evs.update(self.phase_evs)
        self.phase_evs = {}
        for nm in ("pe", "act", "dve", "pool"):
            for k, v in evs.items():
                self._wait(nm, k, v)
        if end:
            for t in self.phase_trks:
                if t.dsem is not None:
                    self.free_dsems.append((t.dsem, t.dcnt))
                    t.dsem = None
            self.phase_trks = []


class Buf:
    def __init__(self, cx, es, name, shape, dt, psum=False, phase=True):
        nm = cx.name(name)
        if psum:
            self.t = es.enter_context(cx.nc.psum_tensor(nm, list(shape), dt))
        else:
            self.t = es.enter_context(cx.nc.sbuf_tensor(nm, list(shape), dt))
        self.k = Trk(nm, excl=psum)
        if phase:
            cx.phase_trks.append(self.k)

    def __getitem__(self, key):
        return self.t[key]


def dram(nc, cx, name, shape, dt, kind=None):
    if kind is None:
        t = nc.dram_tensor(name, list(shape), dt)
    else:
        t = nc.dram_tensor(name, list(shape), dt, kind=kind)
    return t


GLA_IN = 6160
HGRN_IN = 8192
SSM_IN = 10304


def big_weights(depth):
    out = []
    for i in range(depth):
        kind, j = kind_of(i)
        if kind == 0:
            out.append(("gla_w_in_%d" % j, D, GLA_IN))
            out.append(("gla_w_out_%d" % j, D, D))
        elif kind == 1:
            out.append(("hgrn_w_in_%d" % j, D, HGRN_IN))
            out.append(("hgrn_w_out_%d" % j, D, D))
        else:
            out.append(("ssm_w_in_%d" % j, D, SSM_IN))
            out.append(("ssm_w_out_%d" % j, 2 * D, D))
        out.append(("w_up_%d" % i, D, DFF))
        out.append(("w_down_%d" % i, DFF, D))
        out.append(("w_ple_gate_%d" % i, D, D))
        out.append(("w_ple_proj_%d" % i, PLE, D))
    return out


def col_layout(depth):
    items = []
    for i in range(depth):
        items += [("norm_mix_%d" % i, KC), ("norm_mlp_%d" % i, KC), ("norm_ple_%d" % i, KC)]
    items.append(("norm_final", KC))
    n_norm = sum(n for _, n in items)
    for i in range(depth):
        kind, j = kind_of(i)
        if kind == 0:
            items += [("gla_b_gk_%d" % j, 8), ("gla_gn_%d" % j, 4)]
        elif kind == 1:
            items += [("hgrn_gn_%d" % j, 1)]
            for l in range(depth):
                items.append(("hgrn_lb_%d" % l, KC))
        else:
            for t in range(4):
                items.append(("ssm_conv_w_%d_%d" % (j, t), 48))
            items += [("ssm_conv_b_%d" % j, 48), ("ssm_norm_%d" % j, 32)]
    lay = {}
    off = 0
    for nm, n in items:
        if nm not in lay:
            lay[nm] = (off, n)
            off += n
    return lay, off, n_norm


def to_cols(v):
    v = np.asarray(v, np.float32).reshape(-1, 128)
    return np.ascontiguousarray(v.T)


class Prog:
    def __init__(self, T, depth, enable_mix=True):
        self.T = T
        self.depth = depth
        self.NT = T // TT
        self.enable_mix = enable_mix
        self.nc = bass.Bass("TRN2", target_bir_lowering=False)
        self.lay, self.ncol, self.n_norm = col_layout(depth)
        self.dbg = {}
        self.debug = DEBUG

    def declare(self):
        nc, T, depth = self.nc, self.T, self.depth
        self.xT = nc.dram_tensor("xT", [D, T], F32, kind="ExternalInput")
        self.pT = nc.dram_tensor("pT", [depth, PLE, T], F32, kind="ExternalInput")
        self.cols_d = nc.dram_tensor("cols", [128, self.ncol], F32, kind="ExternalInput")
        self.consts_d = nc.dram_tensor("consts", [128, 4, 128], F32, kind="ExternalInput")
        self.cmask_d = nc.dram_tensor("cmask", [128, 16], F32, kind="ExternalInput")
        self.outT = nc.dram_tensor("outT", [D, T], F32, kind="ExternalOutput")
        self.wsh, self.wshb, self.wg, self.wg_k = {}, {}, {}, {}
        for nm, K, N in big_weights(depth):
            rows = K // NCORES if USE_CC else K
            self.wsh[nm] = nc.dram_tensor(nm, [rows, N], F32, kind="ExternalInput")
            if USE_CC:
                self.wshb[nm] = nc.dram_tensor(nm + "_sb", [rows, N], BF16)
            self.wg[nm] = nc.dram_tensor(nm + "_g", [K, N], BF16)
            self.wg_k[nm] = Trk(nm + "_g")
        self.wg_need = {}
        self.hT = nc.dram_tensor("hT_scr", [D, T], F32)
        self.hT_k = [Trk("hT%d" % i) for i in range(self.NT)]
        self.out_k = [Trk("out%d" % i) for i in range(self.NT)]
        self.oloc = nc.dram_tensor("oloc_scr", [2 * D, T], F32)
        self.oloc_k = [[Trk("oloc%d_%d" % (i, b)) for b in range(2)] for i in range(self.NT)]
        self.qp = nc.dram_tensor("qp_scr", [D, T], BF16)
        self.qp_k = [Trk("qp%d" % i) for i in range(self.NT)]
        self.rows_d = {}
        self.small_d = {}
        for i in range(depth):
            kind, j = kind_of(i)
            if kind == 0:
                self.small_d["gla_w_gk2_%d" % j] = nc.dram_tensor(
                    "gla_w_gk2_%d" % j, [16, 1024], F32, kind="ExternalInput")
            if kind == 2:
                for nm in ("ssm_dt_bias", "ssm_a_log", "ssm_d"):
                    self.rows_d["%s_%d" % (nm, j)] = nc.dram_tensor(
                        "%s_%d" % (nm, j), [1, 64], F32, kind="ExternalInput")

    def dump_sb(self, name, buf, shape, dt=F32):
        if not getattr(self, "debug", False) or name in self.dbg:
            return
        d = self.nc.dram_tensor("dbg_" + name, list(shape), dt, kind="ExternalOutput")
        k = Trk(name)
        self.dbg[name] = k
        self.cx.dma("pool", d.ap(), buf[:], k, buf.k)

    def dump_dram(self, name, dt_, shape, trks, dt=F32):
        if not getattr(self, "debug", False) or name in self.dbg:
            return
        d = self.nc.dram_tensor("dbg_" + name, list(shape), dt, kind="ExternalOutput")
        k = Trk(name)
        self.dbg[name] = k
        self.cx._waits("pool", trks, [])
        self.cx.dma("pool", d.ap(), dt_.ap(), k, None)

    def col(self, name, i=0, n=1):
        off, cnt = self.lay[name]
        return self.cols[:, off + i:off + i + n]

    def build(self):
        nc = self.nc
        self.declare()
        with ExitStack() as es:
            cx = self.cx = Ctx(nc, es)
            self.cols_b = Buf(cx, es, "cols", [128, self.ncol], F32, phase=False)
            self.cols = self.cols_b.t
            self.consts_b = Buf(cx, es, "consts", [128, 4, 128], F32, phase=False)
            self.cmask_b = Buf(cx, es, "cmask", [128, 16], F32, phase=False)
            self.ones_b = Buf(cx, es, "ones", [128, 128], BF16, phase=False)
            self.ident_b = Buf(cx, es, "ident", [128, 128], BF16, phase=False)
            self.cmsk_b = Buf(cx, es, "chunkmask", [128, TT], F32, phase=False)
            self.NSLAB = 2
            self.slabs = [Buf(cx, es, "slab%d" % i, [128, 16, 512], BF16, phase=False)
                          for i in range(self.NSLAB)]
            self.slab_i = 0
            self.psum = [Buf(cx, es, "ps%d" % i, [128, 512], F32, psum=True, phase=False)
                         for i in range(8)]
            self.ps_i = 0
            self.held = set()
            self.rr = 0
            cx.dma("pool", self.cols[:], self.cols_d.ap(), self.cols_b.k, None)
            cx.dma("pool", self.consts_b[:], self.consts_d.ap(), self.consts_b.k, None)
            cx.dma("pool", self.cmask_b[:], self.cmask_d.ap(), self.cmask_b.k, None)
            cx.op("pool", lambda e: e.memset(self.ones_b[:], 1.0), [], [self.ones_b.k])
            cx.op("pool", lambda e: e.memset(self.ident_b[:], 0.0), [], [self.ident_b.k])
            cx.op("pool", lambda e: e.affine_select(
                out=self.ident_b[:], in_=self.ident_b[:], pattern=[[-1, 128]],
                compare_op=ALU.not_equal, fill=1.0, base=0, channel_multiplier=1),
                [self.ident_b.k], [self.ident_b.k])
            cx.op("pool", lambda e: e.memset(self.cmsk_b[:], 1.0), [], [self.cmsk_b.k])
            cx.op("pool", lambda e: e.memset(self.cmsk_b[:, 0:TT:64], 0.0), [], [self.cmsk_b.k])

            self.phase_weights()
            self.hsrc, self.hsrc_k = self.xT, [None] * self.NT
            for i in range(self.depth):
                kind, j = kind_of(i)
                if self.enable_mix:
                    if kind == 0:
                        self.phase_lin_mixer(i, j, "gla")
                    elif kind == 1:
                        self.phase_lin_mixer(i, j, "hgrn")
                    else:
                        self.phase_ssm(i, j)
                self.phase_mlp(i)
                self.phase_ple(i)
            self.phase_out()
            for q in ("pool", "sp", "act", "dve", "pe"):
                cx.finish(q, self.out_k + list(self.dbg.values()))
        return nc

    def next_ps(self, hold=False):
        for _ in range(16):
            b = self.psum[self.ps_i % 8]
            self.ps_i += 1
            if id(b) not in self.held:
                if hold:
                    self.held.add(id(b))
                return b
        raise RuntimeError("all PSUM banks held")

    def release(self, b):
        self.held.discard(id(b))

    def ew(self):
        self.rr += 1
        return "dve" if self.rr % 2 else "pool"

    def phase_weights(self):
        cx = self.cx
        PIECE = 4096
        with ExitStack() as es:
            NB = 3
            wdst = [Trk("wdst%d" % i) for i in range(NB)]
            fin = [Buf(cx, es, "wc_in%d" % i, [128, PIECE], F32) for i in range(NB)]
            fout = [Buf(cx, es, "wc_out%d" % i, [128, PIECE], BF16) for i in range(NB)]
            n = 0
            for nm, K, N in big_weights(self.depth):
                rows = K // NCORES if USE_CC else K
                tot = rows * N
                per = tot // 128
                assert per * 128 == tot
                src = self.wsh[nm].ap().rearrange("r n -> (r n)").rearrange("(p f) -> p f", p=128)
                dstt = self.wshb[nm] if USE_CC else self.wg[nm]
                dst = dstt.ap().rearrange("r n -> (r n)").rearrange("(p f) -> p f", p=128)
                shk = Trk(nm + "_sb") if USE_CC else None
                off = 0
                while off < per:
                    w = min(PIECE, per - off)
                    a, b = fin[n % NB], fout[n % NB]
                    cx.dma("sp", a[:, 0:w], src[:, off:off + w], a.k, None)
                    eng = ("act", "dve", "act", "pool", "act", "dve")[n % 6]
                    if eng == "act":
                        cx.op("act", lambda e: e.copy(out=b[:, 0:w], in_=a[:, 0:w]), [a.k], [b.k])
                    else:
                        cx.op(eng, lambda e: e.tensor_copy(out=b[:, 0:w], in_=a[:, 0:w]), [a.k], [b.k])
                    cx.dma("pool", dst[:, off:off + w], b[:, 0:w], shk if USE_CC else wdst[n % NB], b.k)
                    off += w
                    n += 1
                if USE_CC:
                    cx.allgather(self.wg[nm].ap(), self.wshb[nm].ap(), self.wg_k[nm], shk,
                                 [list(range(NCORES))])
                else:
                    self.wg_need[nm] = [(t.dsem, t.dcnt) for t in wdst if t.dsem is not None]
            cx.barrier()

    def phase_copy_in(self):
        cx = self.cx
        for it in range(self.NT):
            cx.dma("pool", self.hT.ap()[:, it * TT:(it + 1) * TT],
                   self.xT.ap()[:, it * TT:(it + 1) * TT], self.hT_k[it], None)

    def load_h(self, buf, it):
        self.load_tile(buf, self.hsrc, 0, KC, it, self.hsrc_k[it])

    def h_stored(self):
        self.hsrc, self.hsrc_k = self.hT, self.hT_k

    def load_tile(self, buf, dram_t, row0, nblk, it, trk, blk0=0):
        cx = self.cx
        src = dram_t.ap()[row0:row0 + nblk * 128, it * TT:(it + 1) * TT].rearrange(
            "(kc p) t -> p kc t", p=128)
        step = 4
        for q in range(0, nblk, step):
            n = min(step, nblk - q)
            cx.dma("pool", buf[:, blk0 + q:blk0 + q + n, :], src[:, q:q + n, :], buf.k, trk)

    def store_tile(self, buf, dram_t, row0, nblk, it, trk, blk0=0):
        cx = self.cx
        dst = dram_t.ap()[row0:row0 + nblk * 128, it * TT:(it + 1) * TT].rearrange(
            "(kc p) t -> p kc t", p=128)
        step = 4
        for q in range(0, nblk, step):
            n = min(step, nblk - q)
            cx.dma("pool", dst[:, q:q + n, :], buf[:, blk0 + q:blk0 + q + n, :], trk, buf.k)

    def rstd(self, src, blk0, nblk, n_feat, sqs, rs):
        cx = self.cx
        ps = self.next_ps()
        for b in range(nblk):
            sq = sqs[b % len(sqs)]
            cx.op("act", lambda e: e.activation(out=sq[:], in_=src[:, blk0 + b, :], func=AF.Square),
                  [src.k], [sq.k])
            cx.op("pe", lambda e: e.matmul(ps[:], self.ones_b[:], sq[:], start=(b == 0),
                                           stop=(b == nblk - 1)), [self.ones_b.k, sq.k], [ps.k])
        cx.op("act", lambda e: e.activation(out=rs[:], in_=ps[:], func=AF.Sqrt, bias=EPS,
                                            scale=1.0 / n_feat), [ps.k], [rs.k])
        cx.op("dve", lambda e: e.reciprocal(out=rs[:], in_=rs[:]), [rs.k], [rs.k])

    def norm_u(self, h, gain, u, sqs, rs):
        cx = self.cx
        self.rstd(h, 0, KC, D, sqs, rs)
        for kc in range(KC):
            cx.op("dve", lambda e: e.scalar_tensor_tensor(
                out=u[:, kc, :], in0=h[:, kc, :], scalar=self.col(gain, kc), in1=rs[:],
                op0=ALU.mult, op1=ALU.mult), [h.k, rs.k, self.cols_b.k], [u.k])

    def load_slab(self, wname, k0, nk, c0, w):
        cx = self.cx
        slab = self.slabs[self.slab_i % self.NSLAB]
        self.slab_i += 1
        src = self.wg[wname].ap()[k0 * 128:(k0 + nk) * 128, c0:c0 + w].rearrange(
            "(kc p) n -> p kc n", p=128)
        step = 4
        if not USE_CC:
            for k, v in self.wg_need[wname]:
                cx._wait("sp", k, v)
        for q in range(0, nk, step):
            n = min(step, nk - q)
            cx.dma("sp", slab[:, q:q + n, 0:w], src[:, q:q + n, :], slab.k, self.wg_k[wname] if USE_CC else None)
        return slab

    def linear_fm(self, src, nkc, wname, col0, ncols, epi, src_blk0=0):
        cx = self.cx
        ng = (ncols + 511) // 512
        nks = (nkc + 15) // 16
        for g in range(ng):
            gw = min(512, ncols - g * 512)
            nnb = (gw + 127) // 128
            pss = [self.next_ps(hold=True) for _ in range(nnb)]
            for ks in range(nks):
                nk = min(16, nkc - ks * 16)
                slab = self.load_slab(wname, ks * 16, nk, col0 + g * 512, gw)
                for nb in range(nnb):
                    bw = min(128, gw - nb * 128)
                    for kc in range(nk):
                        kk = ks * 16 + kc
                        cx.op("pe", lambda e: e.matmul(
                            pss[nb][0:bw, :], slab[:, kc, nb * 128:nb * 128 + bw],
                            src[:, src_blk0 + kk, :], start=(kk == 0), stop=(kk == nkc - 1)),
                            [slab.k, src.k], [pss[nb].k], inc=(kc == nk - 1))
            for nb in range(nnb):
                bw = min(128, gw - nb * 128)
                epi(g * 4 + nb, bw, pss[nb])
                self.release(pss[nb])

    def linear_tm(self, src, wname, col0, gw, epi):
        cx = self.cx
        slab = self.load_slab(wname, 0, KC, col0, gw)
        for ts in range(TT // 128):
            ps = self.next_ps()
            for kc in range(KC):
                cx.op("pe", lambda e: e.matmul(
                    ps[:, 0:gw], src[:, kc, ts * 128:(ts + 1) * 128], slab[:, kc, 0:gw],
                    start=(kc == 0), stop=(kc == KC - 1)), [slab.k, src.k], [ps.k], inc=(kc == KC - 1))
            epi(ts, ps)

    def phase_mlp(self, i):
        cx = self.cx
        with ExitStack() as es:
            hs = [Buf(cx, es, "h", [128, KC, TT], F32) for _ in range(2)]
            u = Buf(cx, es, "u", [128, KC, TT], BF16)
            hid = Buf(cx, es, "hid", [128, DFF // 128, TT], BF16)
            sqs = [Buf(cx, es, "sq", [128, TT], BF16) for _ in range(2)]
            rs = Buf(cx, es, "rs", [128, TT], F32)
            tmps = [Buf(cx, es, "tmp", [128, TT], F32) for _ in range(3)]
            cnt = [0]
            self.load_h(hs[0], 0)
            self.norm_u(hs[0], "norm_mlp_%d" % i, u, sqs, rs)
            for it in range(self.NT):
                h = hs[it % 2]

                def epi_up(blk, bw, ps):
                    t = tmps[cnt[0] % 3]
                    cnt[0] += 1
                    cx.op("act", lambda e: e.activation(out=t[:], in_=ps[:], func=AF.Relu), [ps.k], [t.k])
                    cx.op(self.ew(), lambda e: e.tensor_tensor(out=hid[:, blk, :], in0=t[:], in1=t[:],
                                                               op=ALU.mult), [t.k], [hid.k])
                self.linear_fm(u, KC, "w_up_%d" % i, 0, DFF, epi_up)
                if it + 1 < self.NT:
                    self.load_h(hs[(it + 1) % 2], it + 1)
                    self.norm_u(hs[(it + 1) % 2], "norm_mlp_%d" % i, u, sqs, rs)

                def epi_down(blk, bw, ps):
                    cx.op("dve", lambda e: e.tensor_tensor(out=h[:, blk, :], in0=ps[:], in1=h[:, blk, :],
                                                           op=ALU.add), [ps.k, h.k], [h.k])
                self.linear_fm(hid, DFF // 128, "w_down_%d" % i, 0, D, epi_down)
                self.store_tile(h, self.hT, 0, KC, it, self.hT_k[it])
            self.h_stored()
            cx.barrier()

    def phase_ple(self, i):
        cx = self.cx
        with ExitStack() as es:
            hs = [Buf(cx, es, "h", [128, KC, TT], F32) for _ in range(2)]
            us = [Buf(cx, es, "u", [128, KC, TT], BF16) for _ in range(2)]
            pp = Buf(cx, es, "pp", [128, KC, TT], F32)
            pf = Buf(cx, es, "pf", [128, 2, TT], F32)
            pbs = [Buf(cx, es, "pb", [128, 2, TT], BF16) for _ in range(2)]
            sqs = [Buf(cx, es, "sq", [128, TT], BF16) for _ in range(2)]
            rs = Buf(cx, es, "rs", [128, TT], F32)
            tmps = [Buf(cx, es, "tmp", [128, TT], F32) for _ in range(3)]
            cnt = [0]

            def prefetch(it):
                self.load_h(hs[it % 2], it)
                src = self.pT.ap()[i, :, it * TT:(it + 1) * TT].rearrange("(kc p) t -> p kc t", p=128)
                cx.dma("pool", pf[:], src, pf.k, None)
                cx.op("dve", lambda e: e.tensor_copy(out=pbs[it % 2][:], in_=pf[:]), [pf.k], [pbs[it % 2].k])
                self.norm_u(hs[it % 2], "norm_ple_%d" % i, us[it % 2], sqs, rs)
            prefetch(0)
            for it in range(self.NT):
                h, u, pb = hs[it % 2], us[it % 2], pbs[it % 2]

                def epi_pp(blk, bw, ps):
                    cx.op("act", lambda e: e.copy(out=pp[:, blk, :], in_=ps[:]), [ps.k], [pp.k])
                self.linear_fm(pb, 2, "w_ple_proj_%d" % i, 0, D, epi_pp)
                if it + 1 < self.NT:
                    prefetch(it + 1)

                def epi_gate(blk, bw, ps):
                    t = tmps[cnt[0] % 3]
                    cnt[0] += 1
                    cx.op("act", lambda e: e.activation(out=t[:], in_=ps[:], func=AF.Sigmoid), [ps.k], [t.k])
                    cx.op("pool", lambda e: e.tensor_tensor(out=t[:], in0=t[:], in1=pp[:, blk, :],
                                                            op=ALU.mult), [t.k, pp.k], [t.k])
                    cx.op("dve", lambda e: e.tensor_tensor(out=h[:, blk, :], in0=t[:], in1=h[:, blk, :],
                                                           op=ALU.add), [t.k, h.k], [h.k])
                self.linear_fm(u, KC, "w_ple_gate_%d" % i, 0, D, epi_gate)
                self.store_tile(h, self.hT, 0, KC, it, self.hT_k[it])
            self.h_stored()
            cx.barrier()

    def phase_out(self):
        cx = self.cx
        with ExitStack() as es:
            h = Buf(cx, es, "h", [128, KC, TT], F32)
            o = Buf(cx, es, "o", [128, KC, TT], F32)
            sqs = [Buf(cx, es, "sq", [128, TT], BF16) for _ in range(2)]
            rs = Buf(cx, es, "rs", [128, TT], F32)
            for it in range(self.NT):
                self.load_h(h, it)
                self.norm_u(h, "norm_final", o, sqs, rs)
                self.store_tile(o, self.outT, 0, KC, it, self.out_k[it])
            cx.barrier(extra=self.out_k)

    def phase_lin_mixer(self, i, j, kind):
        cx, nc = self.cx, self.nc
        if kind == "gla":
            w_in, w_out = "gla_w_in_%d" % j, "gla_w_out_%d" % j
            NG, Hg, dkb, dvb, dv = 2, 2, 2, 4, 512
            qcol = lambda g: g * 512
            kcol = lambda g: 1024 + g * 512
            vcol = lambda g: 2048 + g * 1024
            ogcol = 4096
            qscale = 256 ** -0.5
            gn = "gla_gn_%d" % j
        else:
            w_in, w_out = "hgrn_w_in_%d" % j, "hgrn_w_out_%d" % j
            NG, Hg, dkb, dvb, dv = 4, 4, 1, 1, 128
            qcol = lambda g: g * 512
            kcol = lambda g: 2048 + g * 512
            vcol = lambda g: 4096 + g * 512
            ogcol = 6144
            qscale = 128 ** -0.5
            gn = "hgrn_gn_%d" % j
        QG = 4
        VW = Hg * dv
        NQ = NG * QG
        NS0 = NQ * dv
        NS = NS0 + NQ
        sloc = nc.dram_tensor("sloc_%d" % i, [128, NS], F32)
        sg = nc.dram_tensor("sg_%d" % i, [NCORES * 128, NS], F32)
        sloc_k, sg_k = Trk("sloc"), Trk("sg")
        mask2 = self.consts_b[:, 0, :]
        NP = TT // 128
        seqpar = USE_CC

        with ExitStack() as es:
            S32 = [[Buf(cx, es, "S32", [128, dkb, dv], F32) for _ in range(Hg)] for _ in range(NG)]
            Sbf = [[Buf(cx, es, "Sbf", [128, dkb, dv], BF16) for _ in range(Hg)] for _ in range(NG)]
            eo = [Buf(cx, es, "eo", [128, QG, 9], F32) for _ in range(NG)]
            for g in range(NG):
                for hh in range(Hg):
                    cx.op("pool", lambda e: e.memset(S32[g][hh][:], 0.0), [], [S32[g][hh].k])
                    cx.op("pool", lambda e: e.memset(Sbf[g][hh][:], 0.0), [], [Sbf[g][hh].k])
                cx.op("pool", lambda e: e.memset(eo[g][:], 1.0), [], [eo[g].k])
            if kind == "gla":
                wgk2 = Buf(cx, es, "wgk2", [16, 1024], F32)
                wgk2b = Buf(cx, es, "wgk2b", [16, 1024], BF16)
                negb = Buf(cx, es, "negb", [128, 8], F32)
                glr = Buf(cx, es, "glr", [16, TT], BF16)
                cx.dma("pool", wgk2[:], self.small_d["gla_w_gk2_%d" % j].ap(), wgk2.k, None)
                cx.op("dve", lambda e: e.tensor_copy(out=wgk2b[:], in_=wgk2[:]), [wgk2.k], [wgk2b.k])
                cx.op("dve", lambda e: e.tensor_scalar(out=negb[:], in0=self.col("gla_b_gk_%d" % j, 0, 8),
                                                       scalar1=-1.0, scalar2=None, op0=ALU.mult),
                      [self.cols_b.k], [negb.k])
            else:
                lb = Buf(cx, es, "lb", [128, KC], F32)
                oml = Buf(cx, es, "oml", [128, KC], F32)
                ex = Buf(cx, es, "ex", [128, self.depth, KC], F32)
                mx = Buf(cx, es, "mx", [128, KC], F32)
                sm = Buf(cx, es, "sm", [128, KC], F32)
                lg = lambda l: self.col("hgrn_lb_%d" % l, 0, KC)
                cx.op("dve", lambda e: e.tensor_copy(out=mx[:], in_=lg(0)), [self.cols_b.k], [mx.k])
                for l in range(1, self.depth):
                    cx.op("dve", lambda e: e.tensor_tensor(out=mx[:], in0=mx[:], in1=lg(l), op=ALU.max),
                          [mx.k, self.cols_b.k], [mx.k])
                for l in range(self.depth):
                    cx.op("dve", lambda e: e.tensor_tensor(out=ex[:, l, :], in0=lg(l), in1=mx[:], op=ALU.subtract),
                          [mx.k, self.cols_b.k], [ex.k])
                cx.op("act", lambda e: e.activation(out=ex[:], in_=ex[:], func=AF.Exp), [ex.k], [ex.k])
                cx.op("dve", lambda e: e.tensor_copy(out=sm[:], in_=ex[:, 0, :]), [ex.k], [sm.k])
                cx.op("dve", lambda e: e.memset(lb[:], 0.0), [], [lb.k])
                for l in range(1, self.depth):
                    cx.op("dve", lambda e: e.tensor_tensor(out=sm[:], in0=sm[:], in1=ex[:, l, :], op=ALU.add),
                          [sm.k, ex.k], [sm.k])
                    if l <= i:
                        cx.op("dve", lambda e: e.tensor_tensor(out=lb[:], in0=lb[:], in1=ex[:, l, :], op=ALU.add),
                              [lb.k, ex.k], [lb.k])
                cx.op("dve", lambda e: e.reciprocal(out=sm[:], in_=sm[:]), [sm.k], [sm.k])
                cx.op("dve", lambda e: e.tensor_tensor(out=lb[:], in0=lb[:], in1=sm[:], op=ALU.mult),
                      [lb.k, sm.k], [lb.k])
                cx.op("dve", lambda e: e.tensor_scalar(out=oml[:], in0=lb[:], scalar1=-1.0, scalar2=1.0,
                                                       op0=ALU.mult, op1=ALU.add), [lb.k], [oml.k])
            h = Buf(cx, es, "h", [128, KC, TT], F32)
            u = Buf(cx, es, "u", [128, KC, TT], BF16)
            sqs = [Buf(cx, es, "sq", [128, TT], BF16) for _ in range(2)]
            rs = Buf(cx, es, "rs", [128, TT], F32)
            tmps = [Buf(cx, es, "tmp", [128, TT], F32) for _ in range(6)]
            bb = Buf(cx, es, "bb", [128, QG, TT], F32)
            dl = Buf(cx, es, "dl", [128, QG, 8], F32)
            kt = Buf(cx, es, "kt", [128, QG, TT], BF16)
            kdT = Buf(cx, es, "kdT", [128, QG, TT], BF16)
            qt = Buf(cx, es, "qt", [128, QG, TT], BF16)
            qpb = Buf(cx, es, "qpb", [128, QG, TT], BF16)
            kd_tok = Buf(cx, es, "kd_tok", [128, NP, QG * 128], BF16)
            v_tok = Buf(cx, es, "v_tok", [128, NP, VW], BF16)
            atts = [Buf(cx, es, "att", [128, 128], BF16) for _ in range(4)]
            osts = [Buf(cx, es, "ost", [128, 4, 128], F32) for _ in range(2)]
            tc = [0]
            ac = [0]
            oc = [0]

            def tmp():
                tc[0] += 1
                return tmps[tc[0] % len(tmps)]

            def k_finish(g, qb, src_ap, src_k):
                te = tmp()
                cx.op("act", lambda e: e.activation(out=te[:], in_=bb[:, qb, :], func=AF.Exp, scale=-1.0),
                      [bb.k], [te.k])
                cx.op("act", lambda e: e.activation(out=dl[:, qb, :], in_=bb[:, qb, 63:TT:64], func=AF.Exp),
                      [bb.k], [dl.k])
                cx.op("dve", lambda e: e.tensor_tensor(out=te[:], in0=src_ap, in1=te[:], op=ALU.mult),
                      [src_k, te.k], [te.k])
                cx.op("pool", lambda e: e.tensor_copy(out=kt[:, qb, :], in_=te[:]), [te.k], [kt.k])
                cx.op("pool", lambda e: e.tensor_tensor(
                    out=kdT[:, qb, :].rearrange("p (c t) -> p c t", t=64),
                    in0=te[:].rearrange("p (c t) -> p c t", t=64),
                    in1=dl[:, qb, :].unsqueeze(2).to_broadcast([128, 8, 64]), op=ALU.mult),
                    [te.k, dl.k], [kdT.k])

            for it in range(self.NT):
                self.load_h(h, it)
                self.norm_u(h, "norm_mix_%d" % i, u, sqs, rs)
                if kind == "gla":
                    def epi_glr(blk, bw, ps):
                        cx.op("act", lambda e: e.copy(out=glr[:], in_=ps[0:16, :]), [ps.k], [glr.k])
                    self.linear_fm(u, KC, w_in, 6144, 16, epi_glr)
                for g in range(NG):
                    if kind == "gla":
                        for qb in range(QG):
                            gq = g * QG + qb
                            ps = self.next_ps()
                            cx.op("pe", lambda e: e.matmul(ps[:], wgk2b[0:16, gq * 128:(gq + 1) * 128], glr[:],
                                                           start=True, stop=True), [wgk2b.k, glr.k], [ps.k])
                            t1 = tmp()
                            cx.op("act", lambda e: e.activation(out=t1[:], in_=ps[:], func=AF.Exp, scale=-1.0,
                                                                bias=negb[:, gq:gq + 1]), [ps.k, negb.k], [t1.k])
                            cx.op("act", lambda e: e.activation(out=t1[:], in_=t1[:], func=AF.Ln, bias=1.0),
                                  [t1.k], [t1.k])
                            cx.op("pool", lambda e: e.tensor_scalar(out=t1[:], in0=t1[:], scalar1=-1.0 / 16.0,
                                                                    scalar2=None, op0=ALU.mult), [t1.k], [t1.k])
                            cx.op("dve", lambda e: e.tensor_tensor_scan(
                                out=bb[:, qb, :], data0=self.cmsk_b[:], data1=t1[:], initial=0.0,
                                op0=ALU.mult, op1=ALU.add), [t1.k, self.cmsk_b.k], [bb.k])

                        def epi_k(blk, bw, ps):
                            k_finish(g, blk, ps[:], ps.k)
                        self.linear_fm(u, KC, w_in, kcol(g), 512, epi_k)
                    else:
                        def epi_f(blk, bw, ps):
                            gq = g * QG + blk
                            t1, t2 = tmp(), tmp()
                            cx.op("act", lambda e: e.activation(out=t1[:], in_=ps[:], func=AF.Sigmoid), [ps.k], [t1.k])
                            cx.op("dve", lambda e: e.tensor_scalar(
                                out=t1[:], in0=t1[:], scalar1=oml[:, gq:gq + 1], scalar2=lb[:, gq:gq + 1],
                                op0=ALU.mult, op1=ALU.add), [t1.k, oml.k, lb.k], [t1.k])
                            cx.op("act", lambda e: e.activation(out=t2[:], in_=t1[:], func=AF.Ln), [t1.k], [t2.k])
                            cx.op("dve", lambda e: e.tensor_tensor_scan(
                                out=bb[:, blk, :], data0=self.cmsk_b[:], data1=t2[:], initial=0.0,
                                op0=ALU.mult, op1=ALU.add), [t2.k, self.cmsk_b.k], [bb.k])
                            cx.op("dve", lambda e: e.tensor_scalar(out=t1[:], in0=t1[:], scalar1=-1.0, scalar2=1.0,
                                                                   op0=ALU.mult, op1=ALU.add), [t1.k], [t1.k])
                            k_finish(g, blk, t1[:], t1.k)
                        self.linear_fm(u, KC, w_in, kcol(g), 512, epi_f)

                    def epi_q(blk, bw, ps):
                        te = tmp()
                        cx.op("act", lambda e: e.activation(out=te[:], in_=bb[:, blk, :], func=AF.Exp), [bb.k], [te.k])
                        if kind == "gla":
                            cx.op("dve", lambda e: e.scalar_tensor_tensor(
                                out=qt[:, blk, :], in0=ps[:], scalar=qscale, in1=te[:], op0=ALU.mult, op1=ALU.mult),
                                [ps.k, te.k], [qt.k])
                        else:
                            t2 = tmp()
                            cx.op("act", lambda e: e.activation(out=t2[:], in_=ps[:], func=AF.Silu), [ps.k], [t2.k])
                            cx.op("dve", lambda e: e.scalar_tensor_tensor(
                                out=qt[:, blk, :], in0=t2[:], scalar=qscale, in1=te[:], op0=ALU.mult, op1=ALU.mult),
                                [t2.k, te.k], [qt.k])
                    self.linear_fm(u, KC, w_in, qcol(g), 512, epi_q)

                    for c in (range(8) if seqpar else []):
                        cx.op("dve", lambda e: e.tensor_tensor(out=eo[g][:, :, c + 1], in0=eo[g][:, :, c],
                                                               in1=dl[:, :, c], op=ALU.mult), [eo[g].k, dl.k], [eo[g].k])
                    for qb in (range(QG) if seqpar else []):
                        cx.op("pool", lambda e: e.tensor_tensor(
                            out=qpb[:, qb, :].rearrange("p (c t) -> p c t", t=64),
                            in0=qt[:, qb, :].rearrange("p (c t) -> p c t", t=64),
                            in1=eo[g][:, qb, 0:8].unsqueeze(2).to_broadcast([128, 8, 64]), op=ALU.mult),
                            [qt.k, eo[g].k], [qpb.k])
                    if seqpar:
                        self.store_tile(qpb, self.qp, g * QG * 128, QG, it, self.qp_k[it])
                        cx.op("dve", lambda e: e.tensor_copy(out=eo[g][:, :, 0], in_=eo[g][:, :, 8]), [eo[g].k], [eo[g].k])

                    if it == 0 and g == 0:
                        self.dump_sb("u", u, [128, KC, TT], BF16)
                        self.dump_sb("bb", bb, [128, QG, TT])
                        self.dump_sb("kt", kt, [128, QG, TT], BF16)
                        self.dump_sb("qt", qt, [128, QG, TT], BF16)
                        self.dump_sb("kdT", kdT, [128, QG, TT], BF16)
                    for s in range(VW // 512):
                        def epi_v(ts, ps):
                            cx.op("act", lambda e: e.copy(out=v_tok[:, ts, s * 512:(s + 1) * 512], in_=ps[:]),
                                  [ps.k], [v_tok.k])
                        self.linear_tm(u, w_in, vcol(g) + s * 512, 512, epi_v)

                    for ts in range(NP):
                        ps = self.next_ps()
                        pv = ps[:, 0:256].bitcast(BF16).rearrange("p (a b) -> p a b", b=128)
                        for qb in range(QG):
                            cx.op("pe", lambda e: e.transpose(pv[:, qb, :], kdT[:, qb, ts * 128:(ts + 1) * 128],
                                                              self.ident_b[:]), [kdT.k, self.ident_b.k], [ps.k])
                        cx.op("act", lambda e: e.copy(out=kd_tok[:, ts, :].rearrange("p (a b) -> p a b", b=128),
                                                      in_=pv), [ps.k], [kd_tok.k])

                    if it == 0 and g == 0:
                        self.dump_sb("v_tok", v_tok, [128, NP, VW], BF16)
                        self.dump_sb("kd_tok", kd_tok, [128, NP, QG * 128], BF16)
                    if kind == "gla":
                        obanks = [[(hh, vb) for vb in range(4)] for hh in range(Hg)]
                    else:
                        obanks = [[(hh, 0) for hh in range(Hg)]]
                    for pr in range(NP):
                        tsl = slice(pr * 128, (pr + 1) * 128)
                        attm = {}
                        for hh in range(Hg):
                            ps = self.next_ps()
                            for jj in range(dkb):
                                qb = hh * dkb + jj
                                cx.op("pe", lambda e: e.matmul(ps[:, 0:128], kt[:, qb, tsl], qt[:, qb, tsl],
                                                               start=(jj == 0), stop=(jj == dkb - 1)),
                                      [kt.k, qt.k], [ps.k])
                            a = atts[ac[0] % 4]
                            ac[0] += 1
                            cx.op("dve", lambda e: e.tensor_tensor(out=a[:], in0=ps[:, 0:128], in1=mask2, op=ALU.mult),
                                  [ps.k, self.consts_b.k], [a.k])
                            attm[hh] = a
                        ops = []
                        for bank in obanks:
                            ps = self.next_ps(hold=True)
                            ops.append(ps)
                            for slot, (hh, vb) in enumerate(bank):
                                osl = slice(slot * 128, (slot + 1) * 128)
                                cx.op("pe", lambda e: e.matmul(
                                    ps[:, osl], v_tok[:, pr, hh * dv + vb * 128:hh * dv + (vb + 1) * 128],
                                    attm[hh][:], start=(slot == 0), stop=False, skip_group_check=True),
                                    [v_tok.k, attm[hh].k], [ps.k])
                        for c2 in range(2):
                            c = pr * 2 + c2
                            csl = slice(pr * 128 + c2 * 64, pr * 128 + (c2 + 1) * 64)
                            rows = slice(c2 * 64, (c2 + 1) * 64)
                            for bi, bank in enumerate(obanks):
                                ps = ops[bi]
                                for slot, (hh, vb) in enumerate(bank):
                                    for jj in range(dkb):
                                        qb = hh * dkb + jj
                                        cx.op("pe", lambda e: e.matmul(
                                            ps[:, slot * 128 + c2 * 64:slot * 128 + (c2 + 1) * 64],
                                            Sbf[g][hh][:, jj, vb * 128:(vb + 1) * 128], qt[:, qb, csl],
                                            start=False, stop=(jj == dkb - 1), skip_group_check=True),
                                            [Sbf[g][hh].k, qt.k], [ps.k])
                            if kind == "gla":
                                for hh in range(Hg):
                                    for jj in range(dkb):
                                        qb = hh * dkb + jj
                                        ps = self.next_ps()
                                        cx.op("pe", lambda e: e.matmul(
                                            ps[:, 0:dv], kd_tok[rows, pr, qb * 128:(qb + 1) * 128],
                                            v_tok[rows, pr, hh * dv:(hh + 1) * dv], start=True, stop=True),
                                            [kd_tok.k, v_tok.k], [ps.k])
                                        cx.op("dve", lambda e: e.scalar_tensor_tensor(
                                            out=S32[g][hh][:, jj, :], in0=S32[g][hh][:, jj, :], scalar=dl[:, qb, c:c + 1],
                                            in1=ps[:, 0:dv], op0=ALU.mult, op1=ALU.add),
                                            [S32[g][hh].k, dl.k, ps.k], [S32[g][hh].k])
                                    cx.op("act", lambda e: e.copy(out=Sbf[g][hh][:], in_=S32[g][hh][:]),
                                          [S32[g][hh].k], [Sbf[g][hh].k])
                            else:
                                ps = self.next_ps()
                                for hh in range(Hg):
                                    cx.op("pe", lambda e: e.matmul(
                                        ps[:, hh * 128:(hh + 1) * 128], kd_tok[rows, pr, hh * 128:(hh + 1) * 128],
                                        v_tok[rows, pr, hh * dv:(hh + 1) * dv], start=True, stop=True),
                                        [kd_tok.k, v_tok.k], [ps.k])
                                for hh in range(Hg):
                                    cx.op("dve", lambda e: e.scalar_tensor_tensor(
                                        out=S32[g][hh][:, 0, :], in0=S32[g][hh][:, 0, :], scalar=dl[:, hh, c:c + 1],
                                        in1=ps[:, hh * 128:(hh + 1) * 128], op0=ALU.mult, op1=ALU.add),
                                        [S32[g][hh].k, dl.k, ps.k], [S32[g][hh].k])
                                    cx.op("act", lambda e: e.copy(out=Sbf[g][hh][:], in_=S32[g][hh][:]),
                                          [S32[g][hh].k], [Sbf[g][hh].k])
                        for bi, bank in enumerate(obanks):
                            ost = osts[oc[0] % 2]
                            oc[0] += 1
                            cx.op("act", lambda e: e.copy(out=ost[:], in_=ops[bi][:].rearrange("p (a b) -> p a b", b=128)),
                                  [ops[bi].k], [ost.k])
                            hh0, vb0 = bank[0]
                            vblk0 = (g * Hg + hh0) * dvb + vb0
                            dst = self.oloc.ap()[vblk0 * 128:(vblk0 + 4) * 128,
                                                 it * TT + pr * 128:it * TT + (pr + 1) * 128].rearrange(
                                "(a p) t -> p a t", p=128)
                            cx.dma("pool", dst, ost[:], self.oloc_k[it][(oc[0] - 1) % 2], ost.k)
                            self.release(ops[bi])
            for g in (range(NG) if seqpar else []):
                for hh in range(Hg):
                    c0 = ((g * Hg + hh) * dkb) * dv
                    cx.dma("pool", sloc.ap()[:, c0:c0 + dkb * dv].rearrange("p (a b) -> p a b", b=dv),
                           S32[g][hh][:], sloc_k, S32[g][hh].k)
                eoc = Buf(cx, es, "eoc", [128, QG], F32)
                cx.op("dve", lambda e: e.tensor_copy(out=eoc[:], in_=eo[g][:, :, 0]), [eo[g].k], [eoc.k])
                cx.dma("pool", sloc.ap()[:, NS0 + g * QG:NS0 + (g + 1) * QG], eoc[:], sloc_k, eoc.k)
            if seqpar:
                cx.allgather(sg.ap(), sloc.ap(), sg_k, sloc_k, [list(range(NCORES))])

            cx.barrier(extra=[t for pair in self.oloc_k for t in pair])

        with ExitStack() as es:
            Sin = Buf(cx, es, "Sin", [128, NQ, dv], F32)
            Sinb = Buf(cx, es, "Sinb", [128, NQ, dv], BF16)
            with ExitStack() as es2:
                if not seqpar:
                    NR = 0
                else:
                    NR = NCORES
                stg = [Buf(cx, es2, "stg", [128, NS], F32) for _ in range(2)]
                dd = [Buf(cx, es2, "dd", [128, NQ], F32) for _ in range(2)]
                cx.op("pool", lambda e: e.memset(Sin[:], 0.0), [], [Sin.k])
                for r in range(NR):
                    st, d1 = stg[r % 2], dd[r % 2]
                    m = self.cmask_b[:, r:r + 1]
                    cx.dma("pool", st[:], sg.ap()[r * 128:(r + 1) * 128, :], st.k, sg_k)
                    cx.op("dve", lambda e: e.tensor_scalar(out=d1[:], in0=st[:, NS0:NS], scalar1=-1.0, scalar2=m,
                                                           op0=ALU.add, op1=ALU.mult), [st.k, self.cmask_b.k], [d1.k])
                    cx.op("dve", lambda e: e.tensor_scalar(out=d1[:], in0=d1[:], scalar1=1.0, scalar2=None,
                                                           op0=ALU.add), [d1.k], [d1.k])
                    for q in range(NQ):
                        cx.op("dve", lambda e: e.tensor_scalar(out=Sin[:, q, :], in0=Sin[:, q, :], scalar1=d1[:, q:q + 1],
                                                               scalar2=None, op0=ALU.mult), [Sin.k, d1.k], [Sin.k])
                        cx.op("dve", lambda e: e.scalar_tensor_tensor(
                            out=Sin[:, q, :], in0=st[:, q * dv:(q + 1) * dv], scalar=m, in1=Sin[:, q, :],
                            op0=ALU.mult, op1=ALU.add), [st.k, Sin.k, self.cmask_b.k], [Sin.k])
                cx.op("act", lambda e: e.copy(out=Sinb[:], in_=Sin[:]), [Sin.k], [Sinb.k])
                cx.barrier(end=False)
            h = Buf(cx, es, "h", [128, KC, TT], F32)
            u = Buf(cx, es, "u", [128, KC, TT], BF16)
            o = Buf(cx, es, "o", [128, KC, TT], F32)
            qpl = Buf(cx, es, "qpl", [128, NQ, TT], BF16)
            ogb = Buf(cx, es, "ogb", [128, KC, TT], BF16)
            sqs = [Buf(cx, es, "sq", [128, TT], BF16) for _ in range(2)]
            rss = [Buf(cx, es, "rs", [128, TT], F32) for _ in range(2)]
            tmps = [Buf(cx, es, "tmp", [128, TT], F32) for _ in range(4)]
            tc = [0]
            hpb = dv // 128
            for it in range(self.NT):
                self.load_h(h, it)
                cx._waits("pool", self.oloc_k[it], [])
                self.load_tile(o, self.oloc, 0, KC, it, None)
                if seqpar:
                    self.load_tile(qpl, self.qp, 0, NQ, it, self.qp_k[it])
                self.norm_u(h, "norm_mix_%d" % i, u, sqs, rss[0])
                for hd in (range(NG * Hg) if seqpar else []):
                    for vb in range(hpb):
                        ps = self.next_ps()
                        for jj in range(dkb):
                            q = hd * dkb + jj
                            cx.op("pe", lambda e: e.matmul(ps[:], Sinb[:, q, vb * 128:(vb + 1) * 128], qpl[:, q, :],
                                                           start=(jj == 0), stop=(jj == dkb - 1)), [Sinb.k, qpl.k], [ps.k])
                        blk = hd * hpb + vb
                        cx.op("dve", lambda e: e.tensor_tensor(out=o[:, blk, :], in0=ps[:], in1=o[:, blk, :], op=ALU.add),
                              [ps.k, o.k], [o.k])
                cur = {"hd": -1, "rs": None}

                def epi_og(blk, bw, ps):
                    hd = blk // hpb
                    if hd != cur["hd"]:
                        cur["hd"] = hd
                        cur["rs"] = rss[hd % 2]
                        self.rstd(o, hd * hpb, hpb, dv, sqs, cur["rs"])
                    r = cur["rs"]
                    t1, t2 = tmps[tc[0] % 4], tmps[(tc[0] + 1) % 4]
                    tc[0] += 2
                    cx.op("act", lambda e: e.activation(out=t1[:], in_=ps[:],
                                                        func=(AF.Silu if kind == "gla" else AF.Sigmoid)), [ps.k], [t1.k])
                    cx.op("dve", lambda e: e.scalar_tensor_tensor(
                        out=t2[:], in0=o[:, blk, :], scalar=self.col(gn, blk % hpb), in1=r[:],
                        op0=ALU.mult, op1=ALU.mult), [o.k, r.k, self.cols_b.k], [t2.k])
                    cx.op("pool", lambda e: e.tensor_tensor(out=ogb[:, blk, :], in0=t2[:], in1=t1[:], op=ALU.mult),
                          [t1.k, t2.k], [ogb.k])
                self.linear_fm(u, KC, w_in, ogcol, D, epi_og)

                def epi_out(blk, bw, ps):
                    cx.op("dve", lambda e: e.tensor_tensor(out=h[:, blk, :], in0=ps[:], in1=h[:, blk, :], op=ALU.add),
                          [ps.k, h.k], [h.k])
                self.linear_fm(ogb, KC, w_out, 0, D, epi_out)
                self.store_tile(h, self.hT, 0, KC, it, self.hT_k[it])
            self.h_stored()
            cx.barrier()

    def phase_ssm(self, i, j):
        cx, nc = self.cx, self.nc
        w_in, w_out = "ssm_w_in_%d" % j, "ssm_w_out_%d" % j
        mask2 = self.consts_b[:, 0, :]
        selpair = self.consts_b[:, 1, :]
        sel63 = self.consts_b[:, 2, :]
        sel127 = self.consts_b[:, 3, :]
        NP = TT // 128
        G, HG, P, NST = 8, 8, 64, 128
        GI = 1

        class View:
            def __init__(self, ap, k):
                self.ap, self.k = ap, k

            def __getitem__(self, key):
                return self.ap[key]

        def bc(ap2, n):
            return ap2.unsqueeze(2).to_broadcast([128, ap2.shape[1], n])

        with ExitStack() as es:
            big1 = Buf(cx, es, "big1", [128, KC * TT], F32)
            big2 = Buf(cx, es, "big2", [128, KC * TT], F32)
            h1 = View(big1.t[:].rearrange("p (a b) -> p a b", b=TT), big1.k)
            yT = View(big1.t[:].bitcast(BF16).rearrange("p (a b) -> p a b", b=TT), big1.k)
            u = View(big2.t[:, 0:4096].bitcast(BF16).rearrange("p (a b) -> p a b", b=TT), Trk("u"))
            BT = View(big2.t[:, 4096:6144].bitcast(BF16).rearrange("p (a b) -> p a b", b=TT), Trk("BT"))
            CT = View(big2.t[:, 6144:8192].bitcast(BF16).rearrange("p (a b) -> p a b", b=TT), Trk("CT"))
            h2 = View(big2.t[:].rearrange("p (a b) -> p a b", b=TT), big2.k)
            S32 = [Buf(cx, es, "S32", [128, HG * P], F32) for _ in range(G)]
            Sbf = [Buf(cx, es, "Sbf", [128, HG * P], BF16) for _ in range(G)]
            halo = Buf(cx, es, "halo", [128, 48, 3], F32)
            identf = Buf(cx, es, "identf", [128, 128], F32)
            onesf = Buf(cx, es, "onesf", [128, 128], F32)
            dtb = Buf(cx, es, "dtb", [128, 64], F32)
            arow = Buf(cx, es, "arow", [128, 64], F32)
            ddr = Buf(cx, es, "ddr", [128, 64], F32)
            dt_tok = Buf(cx, es, "dt_tok", [128, NP, 64], F32)
            la_tok = Buf(cx, es, "la_tok", [128, NP, 64], F32)
            b_tok = Buf(cx, es, "b_tok", [128, NP, 64], F32)
            eb_tok = Buf(cx, es, "eb_tok", [128, NP, 64], F32)
            we_tok = Buf(cx, es, "we_tok", [128, NP, 64], F32)
            Dc = Buf(cx, es, "Dc", [128, NP, 2, 64], F32)
            sqs = [Buf(cx, es, "sq", [128, TT], BF16) for _ in range(1)]
            rs = Buf(cx, es, "rs", [128, TT], F32)
            tmpc = [Buf(cx, es, "tmpc", [128, TT + 3], F32) for _ in range(2)]
            acc = Buf(cx, es, "acc", [128, TT], F32)
            xTg = Buf(cx, es, "xTg", [128, 4, TT], BF16)
            x_tok = [Buf(cx, es, "x_tok", [128, NP, 512], BF16) for _ in range(GI)]
            zg = [Buf(cx, es, "zg", [128, NP, 512], F32) for _ in range(GI)]
            rel = [Buf(cx, es, "rel", [128, 8, 128], F32) for _ in range(GI)]
            Mb = [Buf(cx, es, "Mb", [128, 8, 128], BF16) for _ in range(GI)]
            xdt = [Buf(cx, es, "xdt", [128, 512], BF16) for _ in range(GI)]
            xw = [Buf(cx, es, "xw", [128, 512], BF16) for _ in range(GI)]
            ytmp = [Buf(cx, es, "ytmp", [128, 512], F32) for _ in range(GI)]
            yn = [Buf(cx, es, "yn", [128, 512], BF16) for _ in range(GI)]
            cbm = [Buf(cx, es, "cbm", [128, 128], F32) for _ in range(GI)]
            btok = [Buf(cx, es, "btok", [128, 128], BF16) for _ in range(GI)]
            ss = [Buf(cx, es, "ss", [128, 1], F32) for _ in range(GI)]
            for g in range(G):
                cx.op("pool", lambda e: e.memset(S32[g][:], 0.0), [], [S32[g].k])
                cx.op("pool", lambda e: e.memset(Sbf[g][:], 0.0), [], [Sbf[g].k])
            cx.op("pool", lambda e: e.memset(halo[:], 0.0), [], [halo.k])
            cx.op("pool", lambda e: e.memset(onesf[:], 1.0), [], [onesf.k])
            cx.op("pool", lambda e: e.memset(identf[:], 0.0), [], [identf.k])
            cx.op("pool", lambda e: e.affine_select(
                out=identf[:], in_=identf[:], pattern=[[-1, 128]], compare_op=ALU.not_equal, fill=1.0,
                base=0, channel_multiplier=1), [identf.k], [identf.k])
            cx.dma("pool", dtb[:], self.rows_d["ssm_dt_bias_%d" % j].ap().partition_broadcast(128), dtb.k, None)
            cx.dma("pool", arow[:], self.rows_d["ssm_a_log_%d" % j].ap().partition_broadcast(128), arow.k, None)
            cx.dma("pool", ddr[:], self.rows_d["ssm_d_%d" % j].ap().partition_broadcast(128), ddr.k, None)
            cx.op("act", lambda e: e.activation(out=arow[:], in_=arow[:], func=AF.Exp), [arow.k], [arow.k])
            cx.op("dve", lambda e: e.tensor_scalar(out=arow[:], in0=arow[:], scalar1=-1.0, scalar2=None, op0=ALU.mult),
                  [arow.k], [arow.k])
            cb16 = Buf(cx, es, "cb16", [128, 4, 128], BF16)
            onesb = self.ones_b
            cx.op("dve", lambda e: e.tensor_copy(out=cb16[:], in_=self.consts_b[:]), [self.consts_b.k], [cb16.k])
            la_hi = Buf(cx, es, "la_hi", [128, NP, 64], BF16)
            la_lo = Buf(cx, es, "la_lo", [128, NP, 64], BF16)
            b_hi = Buf(cx, es, "b_hi", [128, NP, 64], BF16)
            b_lo = Buf(cx, es, "b_lo", [128, NP, 64], BF16)
            hl_f = Buf(cx, es, "hl_f", [128, NP, 64], F32)
            dg_lo = Buf(cx, es, "dg_lo", [128, 8, 128], BF16)
            dg_hi = Buf(cx, es, "dg_hi", [128, 8, 128], BF16)

            def split(src, hi, lo):
                cx.op("dve", lambda e: e.tensor_copy(out=hi[:], in_=src[:]), [src.k], [hi.k])
                cx.op("dve", lambda e: e.tensor_copy(out=hl_f[:], in_=hi[:]), [hi.k], [hl_f.k])
                cx.op("dve", lambda e: e.tensor_tensor(out=lo[:], in0=src[:], in1=hl_f[:], op=ALU.subtract),
                      [src.k, hl_f.k], [lo.k])
            cc = [0]

            def conv_epi(cb, ps, dst_ap, dst_k):
                tcv = tmpc[cc[0] % 2]
                cc[0] += 1
                cx.op("pool", lambda e: e.tensor_copy(out=tcv[:, 0:3], in_=halo[:, cb, :]), [halo.k], [tcv.k])
                cx.op("act", lambda e: e.copy(out=tcv[:, 3:TT + 3], in_=ps[:]), [ps.k], [tcv.k])
                cx.op("pool", lambda e: e.tensor_copy(out=halo[:, cb, :], in_=tcv[:, TT:TT + 3]), [tcv.k], [halo.k])
                wc = lambda t: self.col("ssm_conv_w_%d_%d" % (j, t), cb)
                cx.op("dve", lambda e: e.tensor_scalar(out=acc[:], in0=tcv[:, 0:TT], scalar1=wc(0),
                                                       scalar2=self.col("ssm_conv_b_%d" % j, cb),
                                                       op0=ALU.mult, op1=ALU.add), [tcv.k, self.cols_b.k], [acc.k])
                for t in range(1, 4):
                    cx.op("dve", lambda e: e.scalar_tensor_tensor(out=acc[:], in0=tcv[:, t:t + TT], scalar=wc(t),
                                                                  in1=acc[:], op0=ALU.mult, op1=ALU.add),
                          [tcv.k, acc.k, self.cols_b.k], [acc.k])
                cx.op("act", lambda e: e.activation(out=dst_ap, in_=acc[:], func=AF.Silu), [acc.k], [dst_k])

            for it in range(self.NT):
                self.load_h(h1, it)
                self.norm_u(h1, "norm_mix_%d" % i, u, sqs, rs)

                if SSM_STAGE < 1:
                    cx.barrier(end=False)
                    continue
                def epi_bc(blk, bw, ps):
                    if blk < 8:
                        conv_epi(32 + blk, ps, BT[:, blk, :], BT.k)
                    else:
                        conv_epi(32 + blk, ps, CT[:, blk - 8, :], CT.k)
                self.linear_fm(u, KC, w_in, 8192, 2048, epi_bc)

                if SSM_STAGE < 2:
                    cx.barrier(end=False)
                    continue
                def epi_dt(ts, ps):
                    cx.op("dve", lambda e: e.tensor_tensor(out=dt_tok[:, ts, :], in0=ps[:, 0:64], in1=dtb[:], op=ALU.add),
                          [ps.k, dtb.k], [dt_tok.k])
                self.linear_tm(u, w_in, 10240, 64, epi_dt)
                cx.op("act", lambda e: e.activation(out=dt_tok[:], in_=dt_tok[:], func=AF.Exp), [dt_tok.k], [dt_tok.k])
                cx.op("act", lambda e: e.activation(out=dt_tok[:], in_=dt_tok[:], func=AF.Ln, bias=1.0), [dt_tok.k], [dt_tok.k])
                for ts in range(NP):
                    cx.op("dve", lambda e: e.tensor_tensor(out=la_tok[:, ts, :], in0=dt_tok[:, ts, :], in1=arow[:], op=ALU.mult),
                          [dt_tok.k, arow.k], [la_tok.k])
                split(la_tok, la_hi, la_lo)
                for ts in range(NP):
                    ps = self.next_ps()
                    cx.op("pe", lambda e: e.matmul(ps[:, 0:64], cb16[:, 0, :], la_hi[:, ts, :], start=True, stop=False),
                          [cb16.k, la_hi.k], [ps.k])
                    cx.op("pe", lambda e: e.matmul(ps[:, 0:64], cb16[:, 0, :], la_lo[:, ts, :], start=False, stop=True),
                          [cb16.k, la_lo.k], [ps.k])
                    cx.op("act", lambda e: e.copy(out=b_tok[:, ts, :], in_=ps[:, 0:64]), [ps.k], [b_tok.k])
                split(b_tok, b_hi, b_lo)
                cx.op("act", lambda e: e.activation(out=eb_tok[:], in_=b_tok[:], func=AF.Exp), [b_tok.k], [eb_tok.k])
                for ts in range(NP):
                    ps = self.next_ps()
                    for si in range(3):
                        for hl, src in enumerate((b_hi, b_lo)):
                            cx.op("pe", lambda e: e.matmul(ps[:, si * 64:(si + 1) * 64], cb16[:, 1 + si, :], src[:, ts, :],
                                                           start=(si == 0 and hl == 0), stop=(hl == 1),
                                                           skip_group_check=True), [cb16.k, src.k], [ps.k])
                    cx.op("dve", lambda e: e.tensor_tensor(out=we_tok[:, ts, :], in0=ps[:, 0:64], in1=b_tok[:, ts, :],
                                                           op=ALU.subtract), [ps.k, b_tok.k], [we_tok.k])
                    cx.op("act", lambda e: e.activation(out=Dc[:, ts, :, :], in_=ps[:, 64:192].rearrange("p (a b) -> p a b", b=64),
                                                        func=AF.Exp), [ps.k], [Dc.k])
                cx.op("act", lambda e: e.activation(out=we_tok[:], in_=we_tok[:], func=AF.Exp), [we_tok.k], [we_tok.k])

                if SSM_STAGE < 3:
                    cx.barrier(end=False)
                    continue
                for gp in range(G // GI):
                    gs = tuple(range(gp * GI, (gp + 1) * GI))
                    for g in gs:
                        sl = g % GI
                        def epi_x(blk, bw, ps):
                            conv_epi(g * 4 + blk, ps, xTg[:, blk, :], xTg.k)
                        self.linear_fm(u, KC, w_in, 4096 + g * 512, 512, epi_x)
                        for ts in range(NP):
                            ps = self.next_ps()
                            pv = ps[:, 0:256].bitcast(BF16).rearrange("p (a b) -> p a b", b=128)
                            for a in range(4):
                                cx.op("pe", lambda e: e.transpose(pv[:, a, :], xTg[:, a, ts * 128:(ts + 1) * 128],
                                                                  self.ident_b[:]), [xTg.k, self.ident_b.k], [ps.k])
                            cx.op("act", lambda e: e.copy(out=x_tok[sl][:, ts, :].rearrange("p (a b) -> p a b", b=128),
                                                          in_=pv), [ps.k], [x_tok[sl].k])
                        def epi_z(ts, ps):
                            cx.op("act", lambda e: e.activation(out=zg[sl][:, ts, :], in_=ps[:], func=AF.Silu),
                                  [ps.k], [zg[sl].k])
                        self.linear_tm(u, w_in, g * 512, 512, epi_z)

                    for pr in (range(NP) if SSM_STAGE >= 4 else []):
                        tsl = slice(pr * 128, (pr + 1) * 128)
                        psy, psi = {}, {}
                        for g in gs:
                            sl = g % GI
                            hs = slice(g * 8, (g + 1) * 8)
                            ps = self.next_ps()
                            pvb = ps[:, 0:64].bitcast(BF16)
                            cx.op("pe", lambda e: e.transpose(pvb, BT[:, g, tsl], self.ident_b[:]),
                                  [BT.k, self.ident_b.k], [ps.k])
                            cx.op("act", lambda e: e.copy(out=btok[sl][:], in_=pvb), [ps.k], [btok[sl].k])
                            ps = self.next_ps()
                            cx.op("pe", lambda e: e.matmul(ps[:, 0:128], BT[:, g, tsl], CT[:, g, tsl], start=True, stop=True),
                                  [BT.k, CT.k], [ps.k])
                            cx.op("dve", lambda e: e.tensor_tensor(out=cbm[sl][:], in0=ps[:, 0:128], in1=mask2, op=ALU.mult),
                                  [ps.k, self.consts_b.k], [cbm[sl].k])
                            for dgx, bx in ((dg_hi, b_hi), (dg_lo, b_lo)):
                                cx.op("dve", lambda e: e.tensor_tensor(
                                    out=dgx[:], in0=identf[:].unsqueeze(1).to_broadcast([128, 8, 128]),
                                    in1=bc(bx[:, pr, hs], 128), op=ALU.mult), [identf.k, bx.k], [dgx.k])
                            pb0, pb1 = self.next_ps(hold=True), self.next_ps(hold=True)
                            for pbx, lo4 in ((pb0, 0), (pb1, 4)):
                                for hl, dgx in enumerate((dg_hi, dg_lo)):
                                    cx.op("pe", lambda e: e.matmul(
                                        pbx[:], onesb[:], dgx[:, lo4:lo4 + 4, :].rearrange("p a b -> p (a b)"),
                                        start=(hl == 0), stop=(hl == 1)), [onesb.k, dgx.k], [pbx.k])
                            cx.op("dve", lambda e: e.tensor_tensor(
                                out=rel[sl][:, 0:4, :], in0=pb0[:].rearrange("p (a b) -> p a b", b=128),
                                in1=bc(b_tok[:, pr, g * 8:g * 8 + 4], 128), op=ALU.subtract), [pb0.k, b_tok.k], [rel[sl].k])
                            cx.op("dve", lambda e: e.tensor_tensor(
                                out=rel[sl][:, 4:8, :], in0=pb1[:].rearrange("p (a b) -> p a b", b=128),
                                in1=bc(b_tok[:, pr, g * 8 + 4:g * 8 + 8], 128), op=ALU.subtract), [pb1.k, b_tok.k], [rel[sl].k])
                            self.release(pb0)
                            self.release(pb1)
                            cx.op("act", lambda e: e.activation(out=rel[sl][:], in_=rel[sl][:], func=AF.Relu, scale=-1.0),
                                  [rel[sl].k], [rel[sl].k])
                            cx.op("act", lambda e: e.activation(out=rel[sl][:], in_=rel[sl][:], func=AF.Exp, scale=-1.0),
                                  [rel[sl].k], [rel[sl].k])
                            cx.op("pool", lambda e: e.tensor_tensor(
                                out=Mb[sl][:], in0=rel[sl][:], in1=cbm[sl][:].unsqueeze(1).to_broadcast([128, 8, 128]),
                                op=ALU.mult), [rel[sl].k, cbm[sl].k], [Mb[sl].k])
                            cx.op("pool", lambda e: e.tensor_tensor(
                                out=xdt[sl][:].rearrange("p (a b) -> p a b", b=64),
                                in0=x_tok[sl][:, pr, :].rearrange("p (a b) -> p a b", b=64),
                                in1=bc(dt_tok[:, pr, hs], 64), op=ALU.mult), [x_tok[sl].k, dt_tok.k], [xdt[sl].k])
                            cx.op("pool", lambda e: e.tensor_tensor(
                                out=xw[sl][:].rearrange("p (a b) -> p a b", b=64),
                                in0=xdt[sl][:].rearrange("p (a b) -> p a b", b=64),
                                in1=bc(we_tok[:, pr, hs], 64), op=ALU.mult), [xdt[sl].k, we_tok.k], [xw[sl].k])
                            psy[g] = self.next_ps(hold=True)
                            for h8 in range(8):
                                cx.op("pe", lambda e: e.matmul(
                                    psy[g][:, h8 * 64:(h8 + 1) * 64], Mb[sl][:, h8, :], xdt[sl][:, h8 * 64:(h8 + 1) * 64],
                                    start=(h8 == 0), stop=True, skip_group_check=True), [Mb[sl].k, xdt[sl].k], [psy[g].k])
                            psi[g] = self.next_ps(hold=True)
                        for c2 in range(2):
                            rows = slice(c2 * 64, (c2 + 1) * 64)
                            tcs = slice(pr * 128 + c2 * 64, pr * 128 + (c2 + 1) * 64)
                            for g in gs:
                                kw = {"tile_position": (0, 64)} if c2 == 1 else {}
                                cx.op("pe", lambda e: e.matmul(psi[g][rows, :], CT[:, g, tcs], Sbf[g][:], start=True, stop=True,
                                                               skip_group_check=True, **kw), [CT.k, Sbf[g].k], [psi[g].k])
                            for g in gs:
                                sl = g % GI
                                hs = slice(g * 8, (g + 1) * 8)
                                ps = self.next_ps()
                                cx.op("pe", lambda e: e.matmul(ps[:], btok[sl][rows, :], xw[sl][rows, :], start=True, stop=True),
                                      [btok[sl].k, xw[sl].k], [ps.k])
                                cx.op("dve", lambda e: e.tensor_tensor(
                                    out=S32[g][:].rearrange("p (a b) -> p a b", b=64),
                                    in0=S32[g][:].rearrange("p (a b) -> p a b", b=64),
                                    in1=bc(Dc[:, pr, c2, hs], 64), op=ALU.mult), [S32[g].k, Dc.k], [S32[g].k])
                                cx.op("dve", lambda e: e.tensor_tensor(out=S32[g][:], in0=ps[:], in1=S32[g][:], op=ALU.add),
                                      [ps.k, S32[g].k], [S32[g].k])
                                cx.op("act", lambda e: e.copy(out=Sbf[g][:], in_=S32[g][:]), [S32[g].k], [Sbf[g].k])
                        for g in gs:
                            sl = g % GI
                            hs = slice(g * 8, (g + 1) * 8)
                            y = ytmp[sl]
                            tt = View(rel[sl].t[:, 0:4, :].rearrange("p a b -> p (a b)"), rel[sl].k)
                            cx.op("dve", lambda e: e.tensor_tensor(
                                out=y[:].rearrange("p (a b) -> p a b", b=64),
                                in0=psi[g][:].rearrange("p (a b) -> p a b", b=64),
                                in1=bc(eb_tok[:, pr, hs], 64), op=ALU.mult), [psi[g].k, eb_tok.k], [y.k])
                            cx.op("dve", lambda e: e.tensor_tensor(out=y[:], in0=psy[g][:], in1=y[:], op=ALU.add),
                                  [psy[g].k, y.k], [y.k])
                            self.release(psy[g])
                            self.release(psi[g])
                            cx.op("pool", lambda e: e.tensor_tensor(
                                out=tt[:].rearrange("p (a b) -> p a b", b=64),
                                in0=x_tok[sl][:, pr, :].rearrange("p (a b) -> p a b", b=64),
                                in1=bc(ddr[:, hs], 64), op=ALU.mult), [x_tok[sl].k, ddr.k], [tt.k])
                            cx.op("pool", lambda e: e.tensor_tensor(out=y[:], in0=y[:], in1=tt[:], op=ALU.add), [y.k, tt.k], [y.k])
                            cx.op("pool", lambda e: e.tensor_tensor(out=y[:], in0=y[:], in1=zg[sl][:, pr, :], op=ALU.mult),
                                  [y.k, zg[sl].k], [y.k])
                            cx.op("pool", lambda e: e.tensor_tensor(out=tt[:], in0=y[:], in1=y[:], op=ALU.mult), [y.k], [tt.k])
                            cx.op("dve", lambda e: e.reduce_sum(out=ss[sl][:], in_=tt[:], axis=mybir.AxisListType.X),
                                  [tt.k], [ss[sl].k])
                            cx.op("act", lambda e: e.activation(out=ss[sl][:], in_=ss[sl][:], func=AF.Sqrt, bias=EPS,
                                                                scale=1.0 / 512.0), [ss[sl].k], [ss[sl].k])
                            cx.op("dve", lambda e: e.reciprocal(out=ss[sl][:], in_=ss[sl][:]), [ss[sl].k], [ss[sl].k])
                            cx.op("dve", lambda e: e.tensor_scalar(out=yn[sl][:], in0=y[:], scalar1=ss[sl][:, 0:1], scalar2=None,
                                                                   op0=ALU.mult), [y.k, ss[sl].k], [yn[sl].k])
                            ps = self.next_ps()
                            pv = ps[:, 0:256].bitcast(BF16).rearrange("p (a b) -> p a b", b=128)
                            for a in range(4):
                                cx.op("pe", lambda e: e.transpose(pv[:, a, :], yn[sl][:, a * 128:(a + 1) * 128],
                                                                  self.ident_b[:]), [yn[sl].k, self.ident_b.k], [ps.k])
                            for a in range(4):
                                cx.op("act", lambda e: e.activation(
                                    out=yT[:, g * 4 + a, tsl], in_=pv[:, a, :], func=AF.Copy,
                                    scale=self.col("ssm_norm_%d" % j, g * 4 + a)), [ps.k, self.cols_b.k], [yT.k])
                cx.barrier(end=False)
                if SSM_STAGE < 5:
                    continue
                self.load_h(h2, it)

                def epi_out(blk, bw, ps):
                    cx.op("dve", lambda e: e.tensor_tensor(out=h2[:, blk, :], in0=ps[:], in1=h2[:, blk, :], op=ALU.add),
                          [ps.k, h2.k], [h2.k])
                self.linear_fm(yT, 32, w_out, 0, D, epi_out)
                self.store_tile(h2, self.hT, 0, KC, it, self.hT_k[it])
                cx.barrier(end=False)
            self.h_stored()
            cx.barrier()


def _weight(inputs, nm):
    base, idx = nm.rsplit("_", 1)
    table = {
        "w_up": inputs["w_up"], "w_down": inputs["w_down"],
        "w_ple_proj": inputs["w_ple_proj"], "w_ple_gate": inputs["w_ple_gate"],
        "gla_w_in": inputs["gla_w_in"], "gla_w_out": inputs["gla_w_out"],
        "hgrn_w_in": inputs["hgrn_w_in"], "hgrn_w_out": inputs["hgrn_w_out"],
        "ssm_w_in": inputs["ssm_w_in"], "ssm_w_out": inputs["ssm_w_out"],
    }
    return table[base][int(idx)]


def make_consts():
    c = np.zeros((128, 4, 128), np.float32)
    s = np.arange(128)[:, None]
    t = np.arange(128)[None, :]
    c[:, 0, :] = ((s // 64 == t // 64) & (s <= t)).astype(np.float32)
    c[:, 1, :] = (((s == 63) & (t < 64)) | ((s == 127) & (t >= 64))).astype(np.float32)
    c[:, 2, :] = (s == 63).astype(np.float32) * np.ones_like(t)
    c[:, 3, :] = (s == 127).astype(np.float32) * np.ones_like(t)
    return c


def run(inputs, depth, T, enable_mix=True, trace=False):
    x = np.asarray(inputs["x"])
    p = np.asarray(inputs["p"])
    B, L, _ = x.shape
    segs = NCORES // B
    assert L == segs * T
    prog = Prog(T, depth, enable_mix)
    nc = prog.build()
    lay, ncol, _ = col_layout(depth)
    cols = np.zeros((128, ncol), np.float32)

    def put(nm, v):
        off, n = lay[nm]
        cols[:, off:off + n] = to_cols(v)
    for i in range(depth):
        put("norm_mix_%d" % i, inputs["norm_mix"][i])
        put("norm_mlp_%d" % i, inputs["norm_mlp"][i])
        put("norm_ple_%d" % i, inputs["norm_ple"][i])
        kind, j = kind_of(i)
        if kind == 0:
            put("gla_b_gk_%d" % j, inputs["gla_b_gk"][j])
            put("gla_gn_%d" % j, inputs["gla_gn"][j])
        elif kind == 1:
            put("hgrn_gn_%d" % j, inputs["hgrn_gn"][j])
            for l in range(depth):
                put("hgrn_lb_%d" % l, inputs["hgrn_lb_logits"][l])
        else:
            for t in range(4):
                put("ssm_conv_w_%d_%d" % (j, t), inputs["ssm_conv_w"][j][t])
            put("ssm_conv_b_%d" % j, inputs["ssm_conv_b"][j])
            put("ssm_norm_%d" % j, inputs["ssm_norm"][j])
    put("norm_final", inputs["norm_final"])
    consts = make_consts()
    shared = {"cols": cols, "consts": consts}
    for i in range(depth):
        kind, j = kind_of(i)
        if kind == 0:
            shared["gla_w_gk2_%d" % j] = np.ascontiguousarray(inputs["gla_w_gk2"][j], np.float32)
        if kind == 2:
            for nm in ("ssm_dt_bias", "ssm_a_log", "ssm_d"):
                shared["%s_%d" % (nm, j)] = np.ascontiguousarray(
                    np.asarray(inputs[nm][j], np.float32).reshape(1, 64))
    in_maps = []
    for c in range(NCORES):
        b, s = c // segs, c % segs
        m = dict(shared)
        m["xT"] = np.ascontiguousarray(x[b, s * T:(s + 1) * T, :].T)
        m["pT"] = np.ascontiguousarray(np.transpose(p[:depth, b, s * T:(s + 1) * T, :], (0, 2, 1)))
        cm = np.zeros((128, 16), np.float32)
        for r in range(NCORES):
            rb, rs = r // segs, r % segs
            if rb == b and rs < s:
                cm[:, r] = 1.0
            if rb == b and rs == s - 1:
                cm[:, 8 + r] = 1.0
        m["cmask"] = cm
        for nm, K, N in big_weights(depth):
            W = _weight(inputs, nm)
            if USE_CC:
                r = K // NCORES
                m[nm] = np.ascontiguousarray(W[c * r:(c + 1) * r, :], np.float32)
            else:
                m[nm] = np.ascontiguousarray(W, np.float32)
        in_maps.append(m)
    res = run_bass_kernel_spmd(nc, in_maps, core_ids=list(range(NCORES)), trace=trace)
    out = np.empty((B, L, D), np.float32)
    for c in range(NCORES):
        b, s = c // segs, c % segs
        out[b, s * T:(s + 1) * T, :] = res.results[c]["outT"].T
    if DEBUG:
        return out, res
    if trace:
        return out, res
    return out


def kernel(**inputs):
    depth = int(np.asarray(inputs["p"]).shape[0])
    B, L, _ = np.asarray(inputs["x"]).shape
    return run(inputs, depth, L * B // NCORES)
```

```python
import numpy as np
from contextlib import ExitStack
import concourse.bass as bass
import concourse.mybir as mybir
from concourse.bass_utils import run_bass_kernel_spmd

F32 = mybir.dt.float32
BF16 = mybir.dt.bfloat16
ALU = mybir.AluOpType
AF = mybir.ActivationFunctionType

NCORES = 2
USE_CC = False
D = 2048
KC = D // 128
TT = 512
EPS = 1e-6
PLE = 256
DFF = 8192
N_MIX = 3
DEBUG = False
SSM_STAGE = 99
KINDS = None


def kind_of(i):
    if KINDS is None:
        return i % N_MIX, i // N_MIX
    k = KINDS[i]
    return k, sum(1 for x in KINDS[:i] if x == k)


class Trk:
    __slots__ = ("name", "w", "r", "dsem", "dcnt", "excl")

    def __init__(self, name="", excl=False):
        self.name = name
        self.excl = excl
        self.w = None
        self.r = {}
        self.dsem = None
        self.dcnt = 0


class Ctx:
    def __init__(self, nc, es):
        self.nc = nc
        self.es = es
        self.sems = {}
        self.engs = {}
        for nm, h in (("pe", nc.tensor), ("act", nc.scalar), ("dve", nc.vector),
                      ("pool", nc.gpsimd), ("sp", nc.sync)):
            self.sems[nm] = es.enter_context(nc.semaphore("sem_" + nm))
            self.engs[nm] = {"h": h, "cnt": 0, "known": {}}
        self.ndsem = 0
        self.phase_trks = []
        self.phase_evs = {}
        self.free_dsems = []
        self.uid = 0

    def name(self, p):
        self.uid += 1
        return "%s_%d" % (p, self.uid)

    def _dsem(self, t):
        if t.dsem is None:
            if self.free_dsems:
                t.dsem, t.dcnt = self.free_dsems.pop()
            else:
                t.dsem = "d%d" % self.ndsem
                self.ndsem += 1
                self.sems[t.dsem] = self.es.enter_context(self.nc.semaphore("dsem_%s" % t.dsem))
        return t.dsem

    def _wait(self, eng, k, v):
        e = self.engs[eng]
        if k == eng and v > e["cnt"]:
            return
        if v > 0 and e["known"].get(k, 0) < v:
            e["h"].wait_ge(self.sems[k], v)
            e["known"][k] = v

    def _waits(self, eng, reads, writes):
        need = {}
        for t in reads:
            if t.w is not None:
                k, v = t.w
                if need.get(k, 0) < v:
                    need[k] = v
            if t.excl:
                for k, v in t.r.items():
                    if k != eng and need.get(k, 0) < v:
                        need[k] = v
        for t in writes:
            if t.w is not None:
                k, v = t.w
                if need.get(k, 0) < v:
                    need[k] = v
            for k, v in t.r.items():
                if need.get(k, 0) < v:
                    need[k] = v
        for k, v in need.items():
            self._wait(eng, k, v)

    def op(self, eng, fn, reads=(), writes=(), inc=True):
        self._waits(eng, reads, writes)
        e = self.engs[eng]
        ins = fn(e["h"])
        if inc:
            e["cnt"] += 1
            ins.then_inc(self.sems[eng], 1)
            c = e["cnt"]
        else:
            c = e["cnt"] + 1
        for t in reads:
            if t.r.get(eng, 0) < c:
                t.r[eng] = c
        for t in writes:
            t.w = (eng, c)
            t.r = {}
        return ins

    def dma(self, q, out_ap, in_ap, out_t, in_t, **kw):
        reads = [in_t] if in_t is not None else []
        k = self._dsem(out_t)
        saved = out_t.w
        if saved is not None and saved[0] == k:
            out_t.w = None
        self._waits(q, reads, [out_t])
        out_t.w = saved
        e = self.engs[q]
        ins = e["h"].dma_start(out=out_ap, in_=in_ap, **kw)
        out_t.dcnt += 16
        ins.then_inc(self.sems[k], 16)
        if q != "sp" or in_t is not None:
            self.phase_evs[k] = out_t.dcnt
        if in_t is not None and in_t.r.get(k, 0) < out_t.dcnt:
            in_t.r[k] = out_t.dcnt
        out_t.w = (k, out_t.dcnt)
        out_t.r = {}
        return ins

    def allgather(self, out_ap, in_ap, out_t, in_t, groups):
        q = "pool"
        self._waits(q, [in_t], [out_t])
        e = self.engs[q]
        k = self._dsem(out_t)
        ins = e["h"].collective_compute("AllGather", ALU.bypass, replica_groups=groups,
                                        ins=[in_ap], outs=[out_ap])
        out_t.dcnt += 1
        ins.then_inc(self.sems[k], 1)
        if in_t.r.get(k, 0) < out_t.dcnt:
            in_t.r[k] = out_t.dcnt
        out_t.w = (k, out_t.dcnt)
        out_t.r = {}
        return ins

    def finish(self, eng, trks):
        self._waits(eng, trks, [])

    def barrier(self, extra=(), end=True):
        evs = {}
        for nm in ("pe", "act", "dve", "pool"):
            evs[nm] = self.engs[nm]["cnt"]
        for t in list(self.phase_trks) + list(extra):
            if t.dsem is not None and t.dcnt > 0:
                evs[t.dsem] = t.dcnt
        evs.update(self.phase_evs)
        self.phase_evs = {}
        for nm in ("pe", "act", "dve", "pool"):
            for k, v in evs.items():
                self._wait(nm, k, v)
        if end:
            for t in self.phase_trks:
                if t.dsem is not None:
                    self.free_dsems.append((t.dsem, t.dcnt))
                    t.dsem = None
            self.phase_trks = []


class Buf:
    def __init__(self, cx, es, name, shape, dt, psum=False, phase=True):
        nm = cx.name(name)
        if psum:
            self.t = es.enter_context(cx.nc.psum_tensor(nm, list(shape), dt))
        else:
            self.t = es.enter_context(cx.nc.sbuf_tensor(nm, list(shape), dt))
        self.k = Trk(nm, excl=psum)
        if phase:
            cx.phase_trks.append(self.k)

    def __getitem__(self, key):
        return self.t[key]


def dram(nc, cx, name, shape, dt, kind=None):
    if kind is None:
        t = nc.dram_tensor(name, list(shape), dt)
    else:
        t = nc.dram_tensor(name, list(shape), dt, kind=kind)
    return t


GLA_IN = 6160
HGRN_IN = 8192
SSM_IN = 10304


def big_weights(depth):
    out = []
    for i in range(depth):
        kind, j = kind_of(i)
        if kind == 0:
            out.append(("gla_w_in_%d" % j, D, GLA_IN))
            out.append(("gla_w_out_%d" % j, D, D))
        elif kind == 1:
            out.append(("hgrn_w_in_%d" % j, D, HGRN_IN))
            out.append(("hgrn_w_out_%d" % j, D, D))
        else:
            out.append(("ssm_w_in_%d" % j, D, SSM_IN))
            out.append(("ssm_w_out_%d" % j, 2 * D, D))
        out.append(("w_up_%d" % i, D, DFF))
        out.append(("w_down_%d" % i, DFF, D))
        out.append(("w_ple_gate_%d" % i, D, D))
        out.append(("w_ple_proj_%d" % i, PLE, D))
    return out


def col_layout(depth):
    items = []
    for i in range(depth):
        items += [("norm_mix_%d" % i, KC), ("norm_mlp_%d" % i, KC), ("norm_ple_%d" % i, KC)]
    items.append(("norm_final", KC))
    n_norm = sum(n for _, n in items)
    for i in range(depth):
        kind, j = kind_of(i)
        if kind == 0:
            items += [("gla_b_gk_%d" % j, 8), ("gla_gn_%d" % j, 4)]
        elif kind == 1:
            items += [("hgrn_gn_%d" % j, 1)]
            for l in range(depth):
                items.append(("hgrn_lb_%d" % l, KC))
        else:
            for t in range(4):
                items.append(("ssm_conv_w_%d_%d" % (j, t), 48))
            items += [("ssm_conv_b_%d" % j, 48), ("ssm_norm_%d" % j, 32)]
    lay = {}
    off = 0
    for nm, n in items:
        if nm not in lay:
            lay[nm] = (off, n)
            off += n
    return lay, off, n_norm


def to_cols(v):
    v = np.asarray(v, np.float32).reshape(-1, 128)
    return np.ascontiguousarray(v.T)


class Prog:
    def __init__(self, T, depth, enable_mix=True):
        self.T = T
        self.depth = depth
        self.NT = T // TT
        self.enable_mix = enable_mix
        self.nc = bass.Bass("TRN2", target_bir_lowering=False)
        self.lay, self.ncol, self.n_norm = col_layout(depth)
        self.dbg = {}
        self.debug = DEBUG

    def declare(self):
        nc, T, depth = self.nc, self.T, self.depth
        self.xT = nc.dram_tensor("xT", [D, T], F32, kind="ExternalInput")
        self.pT = nc.dram_tensor("pT", [depth, PLE, T], F32, kind="ExternalInput")
        self.cols_d = nc.dram_tensor("cols", [128, self.ncol], F32, kind="ExternalInput")
        self.consts_d = nc.dram_tensor("consts", [128, 4, 128], F32, kind="ExternalInput")
        self.cmask_d = nc.dram_tensor("cmask", [128, 16], F32, kind="ExternalInput")
        self.outT = nc.dram_tensor("outT", [D, T], F32, kind="ExternalOutput")
        self.wsh, self.wshb, self.wg, self.wg_k = {}, {}, {}, {}
        for nm, K, N in big_weights(depth):
            rows = K // NCORES if USE_CC else K
            self.wsh[nm] = nc.dram_tensor(nm, [rows, N], F32, kind="ExternalInput")
            if USE_CC:
                self.wshb[nm] = nc.dram_tensor(nm + "_sb", [rows, N], BF16)
            self.wg[nm] = nc.dram_tensor(nm + "_g", [K, N], BF16)
            self.wg_k[nm] = Trk(nm + "_g")
        self.wg_need = {}
        self.hT = nc.dram_tensor("hT_scr", [D, T], F32)
        self.hT_k = [Trk("hT%d" % i) for i in range(self.NT)]
        self.out_k = [Trk("out%d" % i) for i in range(self.NT)]
        self.oloc = nc.dram_tensor("oloc_scr", [2 * D, T], F32)
        self.oloc_k = [[Trk("oloc%d_%d" % (i, b)) for b in range(2)] for i in range(self.NT)]
        self.uscr = nc.dram_tensor("u_scr", [D, T], BF16)
        self.u_k = [Trk("u%d" % i) for i in range(self.NT)]
        self.qp = nc.dram_tensor("qp_scr", [D, T], BF16)
        self.qp_k = [Trk("qp%d" % i) for i in range(self.NT)]
        self.rows_d = {}
        self.small_d = {}
        for i in range(depth):
            kind, j = kind_of(i)
            if kind == 0:
                self.small_d["gla_w_gk2_%d" % j] = nc.dram_tensor(
                    "gla_w_gk2_%d" % j, [16, 1024], F32, kind="ExternalInput")
            if kind == 2:
                for nm in ("ssm_dt_bias", "ssm_a_log", "ssm_d"):
                    self.rows_d["%s_%d" % (nm, j)] = nc.dram_tensor(
                        "%s_%d" % (nm, j), [1, 64], F32, kind="ExternalInput")

    def dump_sb(self, name, buf, shape, dt=F32):
        if not getattr(self, "debug", False) or name in self.dbg:
            return
        d = self.nc.dram_tensor("dbg_" + name, list(shape), dt, kind="ExternalOutput")
        k = Trk(name)
        self.dbg[name] = k
        self.cx.dma("pool", d.ap(), buf[:], k, buf.k)

    def dump_dram(self, name, dt_, shape, trks, dt=F32):
        if not getattr(self, "debug", False) or name in self.dbg:
            return
        d = self.nc.dram_tensor("dbg_" + name, list(shape), dt, kind="ExternalOutput")
        k = Trk(name)
        self.dbg[name] = k
        self.cx._waits("pool", trks, [])
        self.cx.dma("pool", d.ap(), dt_.ap(), k, None)

    def col(self, name, i=0, n=1):
        off, cnt = self.lay[name]
        return self.cols[:, off + i:off + i + n]

    def build(self):
        nc = self.nc
        self.declare()
        with ExitStack() as es:
            cx = self.cx = Ctx(nc, es)
            self.cols_b = Buf(cx, es, "cols", [128, self.ncol], F32, phase=False)
            self.cols = self.cols_b.t
            self.consts_b = Buf(cx, es, "consts", [128, 4, 128], F32, phase=False)
            self.cmask_b = Buf(cx, es, "cmask", [128, 16], F32, phase=False)
            self.ones_b = Buf(cx, es, "ones", [128, 128], BF16, phase=False)
            self.ident_b = Buf(cx, es, "ident", [128, 128], BF16, phase=False)
            self.cmsk_b = Buf(cx, es, "chunkmask", [128, TT], F32, phase=False)
            self.NSLAB = 2
            self.slabs = [Buf(cx, es, "slab%d" % i, [128, 16, 512], BF16, phase=False)
                          for i in range(self.NSLAB)]
            self.slab_i = 0
            self.psum = [Buf(cx, es, "ps%d" % i, [128, 512], F32, psum=True, phase=False)
                         for i in range(8)]
            self.ps_i = 0
            self.held = set()
            self.rr = 0
            cx.dma("pool", self.cols[:], self.cols_d.ap(), self.cols_b.k, None)
            cx.dma("pool", self.consts_b[:], self.consts_d.ap(), self.consts_b.k, None)
            cx.dma("pool", self.cmask_b[:], self.cmask_d.ap(), self.cmask_b.k, None)
            cx.op("pool", lambda e: e.memset(self.ones_b[:], 1.0), [], [self.ones_b.k])
            cx.op("pool", lambda e: e.memset(self.ident_b[:], 0.0), [], [self.ident_b.k])
            cx.op("pool", lambda e: e.affine_select(
                out=self.ident_b[:], in_=self.ident_b[:], pattern=[[-1, 128]],
                compare_op=ALU.not_equal, fill=1.0, base=0, channel_multiplier=1),
                [self.ident_b.k], [self.ident_b.k])
            cx.op("pool", lambda e: e.memset(self.cmsk_b[:], 1.0), [], [self.cmsk_b.k])
            cx.op("pool", lambda e: e.memset(self.cmsk_b[:, 0:TT:64], 0.0), [], [self.cmsk_b.k])

            self.phase_weights()
            self.hsrc, self.hsrc_k = self.xT, [None] * self.NT
            for i in range(self.depth):
                kind, j = kind_of(i)
                if self.enable_mix:
                    if kind == 0:
                        self.phase_lin_mixer(i, j, "gla")
                    elif kind == 1:
                        self.phase_lin_mixer(i, j, "hgrn")
                    else:
                        self.phase_ssm(i, j)
                self.phase_mlp(i)
                self.phase_ple(i)
            self.phase_out()
            for q in ("pool", "sp", "act", "dve", "pe"):
                cx.finish(q, self.out_k + list(self.dbg.values()))
        return nc

    def next_ps(self, hold=False):
        for _ in range(16):
            b = self.psum[self.ps_i % 8]
            self.ps_i += 1
            if id(b) not in self.held:
                if hold:
                    self.held.add(id(b))
                return b
        raise RuntimeError("all PSUM banks held")

    def release(self, b):
        self.held.discard(id(b))

    def ew(self):
        self.rr += 1
        return "dve" if self.rr % 2 else "pool"

    def phase_weights(self):
        cx = self.cx
        PIECE = 4096
        with ExitStack() as es:
            NB = 3
            wdst = [Trk("wdst%d" % i) for i in range(NB)]
            fin = [Buf(cx, es, "wc_in%d" % i, [128, PIECE], F32) for i in range(NB)]
            fout = [Buf(cx, es, "wc_out%d" % i, [128, PIECE], BF16) for i in range(NB)]
            n = 0
            for nm, K, N in big_weights(self.depth):
                rows = K // NCORES if USE_CC else K
                tot = rows * N
                per = tot // 128
                assert per * 128 == tot
                src = self.wsh[nm].ap().rearrange("r n -> (r n)").rearrange("(p f) -> p f", p=128)
                dstt = self.wshb[nm] if USE_CC else self.wg[nm]
                dst = dstt.ap().rearrange("r n -> (r n)").rearrange("(p f) -> p f", p=128)
                shk = Trk(nm + "_sb") if USE_CC else None
                off = 0
                while off < per:
                    w = min(PIECE, per - off)
                    a, b = fin[n % NB], fout[n % NB]
                    cx.dma("sp", a[:, 0:w], src[:, off:off + w], a.k, None)
                    eng = ("act", "dve", "act", "pool", "act", "dve")[n % 6]
                    if eng == "act":
                        cx.op("act", lambda e: e.copy(out=b[:, 0:w], in_=a[:, 0:w]), [a.k], [b.k])
                    else:
                        cx.op(eng, lambda e: e.tensor_copy(out=b[:, 0:w], in_=a[:, 0:w]), [a.k], [b.k])
                    cx.dma("pool", dst[:, off:off + w], b[:, 0:w], shk if USE_CC else wdst[n % NB], b.k)
                    off += w
                    n += 1
                if USE_CC:
                    cx.allgather(self.wg[nm].ap(), self.wshb[nm].ap(), self.wg_k[nm], shk,
                                 [list(range(NCORES))])
                else:
                    self.wg_need[nm] = [(t.dsem, t.dcnt) for t in wdst if t.dsem is not None]
            cx.barrier()

    def phase_copy_in(self):
        cx = self.cx
        for it in range(self.NT):
            cx.dma("pool", self.hT.ap()[:, it * TT:(it + 1) * TT],
                   self.xT.ap()[:, it * TT:(it + 1) * TT], self.hT_k[it], None)

    def load_h(self, buf, it):
        self.load_tile(buf, self.hsrc, 0, KC, it, self.hsrc_k[it])

    def h_stored(self):
        self.hsrc, self.hsrc_k = self.hT, self.hT_k

    def load_tile(self, buf, dram_t, row0, nblk, it, trk, blk0=0):
        cx = self.cx
        src = dram_t.ap()[row0:row0 + nblk * 128, it * TT:(it + 1) * TT].rearrange(
            "(kc p) t -> p kc t", p=128)
        step = 4
        for q in range(0, nblk, step):
            n = min(step, nblk - q)
            cx.dma("pool", buf[:, blk0 + q:blk0 + q + n, :], src[:, q:q + n, :], buf.k, trk)

    def store_tile(self, buf, dram_t, row0, nblk, it, trk, blk0=0):
        cx = self.cx
        dst = dram_t.ap()[row0:row0 + nblk * 128, it * TT:(it + 1) * TT].rearrange(
            "(kc p) t -> p kc t", p=128)
        step = 4
        for q in range(0, nblk, step):
            n = min(step, nblk - q)
            cx.dma("pool", dst[:, q:q + n, :], buf[:, blk0 + q:blk0 + q + n, :], trk, buf.k)

    def rstd(self, src, blk0, nblk, n_feat, sqs, rs):
        cx = self.cx
        ps = self.next_ps()
        for b in range(nblk):
            sq = sqs[b % len(sqs)]
            cx.op("act", lambda e: e.activation(out=sq[:], in_=src[:, blk0 + b, :], func=AF.Square),
                  [src.k], [sq.k])
            cx.op("pe", lambda e: e.matmul(ps[:], self.ones_b[:], sq[:], start=(b == 0),
                                           stop=(b == nblk - 1)), [self.ones_b.k, sq.k], [ps.k])
        cx.op("act", lambda e: e.activation(out=rs[:], in_=ps[:], func=AF.Sqrt, bias=EPS,
                                            scale=1.0 / n_feat), [ps.k], [rs.k])
        cx.op("dve", lambda e: e.reciprocal(out=rs[:], in_=rs[:]), [rs.k], [rs.k])

    def norm_u(self, h, gain, u, sqs, rs):
        cx = self.cx
        self.rstd(h, 0, KC, D, sqs, rs)
        for kc in range(KC):
            cx.op("dve", lambda e: e.scalar_tensor_tensor(
                out=u[:, kc, :], in0=h[:, kc, :], scalar=self.col(gain, kc), in1=rs[:],
                op0=ALU.mult, op1=ALU.mult), [h.k, rs.k, self.cols_b.k], [u.k])

    def load_slab(self, wname, k0, nk, c0, w):
        cx = self.cx
        slab = self.slabs[self.slab_i % self.NSLAB]
        self.slab_i += 1
        src = self.wg[wname].ap()[k0 * 128:(k0 + nk) * 128, c0:c0 + w].rearrange(
            "(kc p) n -> p kc n", p=128)
        step = 4
        if not USE_CC:
            for k, v in self.wg_need[wname]:
                cx._wait("sp", k, v)
        for q in range(0, nk, step):
            n = min(step, nk - q)
            cx.dma("sp", slab[:, q:q + n, 0:w], src[:, q:q + n, :], slab.k, self.wg_k[wname] if USE_CC else None)
        return slab

    def linear_fm(self, src, nkc, wname, col0, ncols, epi, src_blk0=0):
        cx = self.cx
        ng = (ncols + 511) // 512
        nks = (nkc + 15) // 16
        for g in range(ng):
            gw = min(512, ncols - g * 512)
            nnb = (gw + 127) // 128
            pss = [self.next_ps(hold=True) for _ in range(nnb)]
            for ks in range(nks):
                nk = min(16, nkc - ks * 16)
                slab = self.load_slab(wname, ks * 16, nk, col0 + g * 512, gw)
                for nb in range(nnb):
                    bw = min(128, gw - nb * 128)
                    for kc in range(nk):
                        kk = ks * 16 + kc
                        cx.op("pe", lambda e: e.matmul(
                            pss[nb][0:bw, :], slab[:, kc, nb * 128:nb * 128 + bw],
                            src[:, src_blk0 + kk, :], start=(kk == 0), stop=(kk == nkc - 1)),
                            [slab.k, src.k], [pss[nb].k], inc=(kc == nk - 1))
            for nb in range(nnb):
                bw = min(128, gw - nb * 128)
                epi(g * 4 + nb, bw, pss[nb])
                self.release(pss[nb])

    def linear_tm(self, src, wname, col0, gw, epi):
        cx = self.cx
        slab = self.load_slab(wname, 0, KC, col0, gw)
        for ts in range(TT // 128):
            ps = self.next_ps()
            for kc in range(KC):
                cx.op("pe", lambda e: e.matmul(
                    ps[:, 0:gw], src[:, kc, ts * 128:(ts + 1) * 128], slab[:, kc, 0:gw],
                    start=(kc == 0), stop=(kc == KC - 1)), [slab.k, src.k], [ps.k], inc=(kc == KC - 1))
            epi(ts, ps)

    def phase_mlp(self, i):
        cx = self.cx
        with ExitStack() as es:
            hs = [Buf(cx, es, "h", [128, KC, TT], F32) for _ in range(2)]
            u = Buf(cx, es, "u", [128, KC, TT], BF16)
            hid = Buf(cx, es, "hid", [128, DFF // 128, TT], BF16)
            sqs = [Buf(cx, es, "sq", [128, TT], BF16) for _ in range(2)]
            rs = Buf(cx, es, "rs", [128, TT], F32)
            tmps = [Buf(cx, es, "tmp", [128, TT], F32) for _ in range(3)]
            cnt = [0]
            self.load_h(hs[0], 0)
            self.norm_u(hs[0], "norm_mlp_%d" % i, u, sqs, rs)
            for it in range(self.NT):
                h = hs[it % 2]

                def epi_up(blk, bw, ps):
                    t = tmps[cnt[0] % 3]
                    cnt[0] += 1
                    cx.op("act", lambda e: e.activation(out=t[:], in_=ps[:], func=AF.Relu), [ps.k], [t.k])
                    cx.op(self.ew(), lambda e: e.tensor_tensor(out=hid[:, blk, :], in0=t[:], in1=t[:],
                                                               op=ALU.mult), [t.k], [hid.k])
                self.linear_fm(u, KC, "w_up_%d" % i, 0, DFF, epi_up)
                if it + 1 < self.NT:
                    self.load_h(hs[(it + 1) % 2], it + 1)
                    self.norm_u(hs[(it + 1) % 2], "norm_mlp_%d" % i, u, sqs, rs)

                def epi_down(blk, bw, ps):
                    cx.op("dve", lambda e: e.tensor_tensor(out=h[:, blk, :], in0=ps[:], in1=h[:, blk, :],
                                                           op=ALU.add), [ps.k, h.k], [h.k])
                self.linear_fm(hid, DFF // 128, "w_down_%d" % i, 0, D, epi_down)
                self.store_tile(h, self.hT, 0, KC, it, self.hT_k[it])
            self.h_stored()
            cx.barrier()

    def phase_ple(self, i):
        cx = self.cx
        with ExitStack() as es:
            hs = [Buf(cx, es, "h", [128, KC, TT], F32) for _ in range(2)]
            us = [Buf(cx, es, "u", [128, KC, TT], BF16) for _ in range(2)]
            pp = Buf(cx, es, "pp", [128, KC, TT], F32)
            pf = Buf(cx, es, "pf", [128, 2, TT], F32)
            pbs = [Buf(cx, es, "pb", [128, 2, TT], BF16) for _ in range(2)]
            sqs = [Buf(cx, es, "sq", [128, TT], BF16) for _ in range(2)]
            rs = Buf(cx, es, "rs", [128, TT], F32)
            tmps = [Buf(cx, es, "tmp", [128, TT], F32) for _ in range(3)]
            cnt = [0]

            def prefetch(it):
                self.load_h(hs[it % 2], it)
                src = self.pT.ap()[i, :, it * TT:(it + 1) * TT].rearrange("(kc p) t -> p kc t", p=128)
                cx.dma("pool", pf[:], src, pf.k, None)
                cx.op("dve", lambda e: e.tensor_copy(out=pbs[it % 2][:], in_=pf[:]), [pf.k], [pbs[it % 2].k])
                self.norm_u(hs[it % 2], "norm_ple_%d" % i, us[it % 2], sqs, rs)
            prefetch(0)
            for it in range(self.NT):
                h, u, pb = hs[it % 2], us[it % 2], pbs[it % 2]

                def epi_pp(blk, bw, ps):
                    cx.op("act", lambda e: e.copy(out=pp[:, blk, :], in_=ps[:]), [ps.k], [pp.k])
                self.linear_fm(pb, 2, "w_ple_proj_%d" % i, 0, D, epi_pp)
                if it + 1 < self.NT:
                    prefetch(it + 1)

                def epi_gate(blk, bw, ps):
                    t = tmps[cnt[0] % 3]
                    cnt[0] += 1
                    cx.op("act", lambda e: e.activation(out=t[:], in_=ps[:], func=AF.Sigmoid), [ps.k], [t.k])
                    cx.op("pool", lambda e: e.tensor_tensor(out=t[:], in0=t[:], in1=pp[:, blk, :],
                                                            op=ALU.mult), [t.k, pp.k], [t.k])
                    cx.op("dve", lambda e: e.tensor_tensor(out=h[:, blk, :], in0=t[:], in1=h[:, blk, :],
                                                           op=ALU.add), [t.k, h.k], [h.k])
                self.linear_fm(u, KC, "w_ple_gate_%d" % i, 0, D, epi_gate)
                self.store_tile(h, self.hT, 0, KC, it, self.hT_k[it])
            self.h_stored()
            cx.barrier()

    def phase_out(self):
        cx = self.cx
        with ExitStack() as es:
            h = Buf(cx, es, "h", [128, KC, TT], F32)
            o = Buf(cx, es, "o", [128, KC, TT], F32)
            sqs = [Buf(cx, es, "sq", [128, TT], BF16) for _ in range(2)]
            rs = Buf(cx, es, "rs", [128, TT], F32)
            for it in range(self.NT):
                self.load_h(h, it)
                self.norm_u(h, "norm_final", o, sqs, rs)
                self.store_tile(o, self.outT, 0, KC, it, self.out_k[it])
            cx.barrier(extra=self.out_k)

    def phase_lin_mixer(self, i, j, kind):
        cx, nc = self.cx, self.nc
        if kind == "gla":
            w_in, w_out = "gla_w_in_%d" % j, "gla_w_out_%d" % j
            NG, Hg, dkb, dvb, dv = 2, 2, 2, 4, 512
            qcol = lambda g: g * 512
            kcol = lambda g: 1024 + g * 512
            vcol = lambda g: 2048 + g * 1024
            ogcol = 4096
            qscale = 256 ** -0.5
            gn = "gla_gn_%d" % j
        else:
            w_in, w_out = "hgrn_w_in_%d" % j, "hgrn_w_out_%d" % j
            NG, Hg, dkb, dvb, dv = 4, 4, 1, 1, 128
            qcol = lambda g: g * 512
            kcol = lambda g: 2048 + g * 512
            vcol = lambda g: 4096 + g * 512
            ogcol = 6144
            qscale = 128 ** -0.5
            gn = "hgrn_gn_%d" % j
        QG = 4
        VW = Hg * dv
        NQ = NG * QG
        NS0 = NQ * dv
        NS = NS0 + NQ
        sloc = nc.dram_tensor("sloc_%d" % i, [128, NS], F32)
        sg = nc.dram_tensor("sg_%d" % i, [NCORES * 128, NS], F32)
        sloc_k, sg_k = Trk("sloc"), Trk("sg")
        mask2 = self.consts_b[:, 0, :]
        NP = TT // 128
        seqpar = USE_CC

        with ExitStack() as es:
            S32 = [[Buf(cx, es, "S32", [128, dkb, dv], F32) for _ in range(Hg)] for _ in range(NG)]
            Sbf = [[Buf(cx, es, "Sbf", [128, dkb, dv], BF16) for _ in range(Hg)] for _ in range(NG)]
            eo = [Buf(cx, es, "eo", [128, QG, 9], F32) for _ in range(NG)]
            for g in range(NG):
                for hh in range(Hg):
                    cx.op("pool", lambda e: e.memset(S32[g][hh][:], 0.0), [], [S32[g][hh].k])
                    cx.op("pool", lambda e: e.memset(Sbf[g][hh][:], 0.0), [], [Sbf[g][hh].k])
                cx.op("pool", lambda e: e.memset(eo[g][:], 1.0), [], [eo[g].k])
            if kind == "gla":
                wgk2 = Buf(cx, es, "wgk2", [16, 1024], F32)
                wgk2b = Buf(cx, es, "wgk2b", [16, 1024], BF16)
                negb = Buf(cx, es, "negb", [128, 8], F32)
                glr = Buf(cx, es, "glr", [16, TT], BF16)
                cx.dma("pool", wgk2[:], self.small_d["gla_w_gk2_%d" % j].ap(), wgk2.k, None)
                cx.op("dve", lambda e: e.tensor_copy(out=wgk2b[:], in_=wgk2[:]), [wgk2.k], [wgk2b.k])
                cx.op("dve", lambda e: e.tensor_scalar(out=negb[:], in0=self.col("gla_b_gk_%d" % j, 0, 8),
                                                       scalar1=-1.0, scalar2=None, op0=ALU.mult),
                      [self.cols_b.k], [negb.k])
            else:
                lb = Buf(cx, es, "lb", [128, KC], F32)
                oml = Buf(cx, es, "oml", [128, KC], F32)
                ex = Buf(cx, es, "ex", [128, self.depth, KC], F32)
                mx = Buf(cx, es, "mx", [128, KC], F32)
                sm = Buf(cx, es, "sm", [128, KC], F32)
                lg = lambda l: self.col("hgrn_lb_%d" % l, 0, KC)
                cx.op("dve", lambda e: e.tensor_copy(out=mx[:], in_=lg(0)), [self.cols_b.k], [mx.k])
                for l in range(1, self.depth):
                    cx.op("dve", lambda e: e.tensor_tensor(out=mx[:], in0=mx[:], in1=lg(l), op=ALU.max),
                          [mx.k, self.cols_b.k], [mx.k])
                for l in range(self.depth):
                    cx.op("dve", lambda e: e.tensor_tensor(out=ex[:, l, :], in0=lg(l), in1=mx[:], op=ALU.subtract),
                          [mx.k, self.cols_b.k], [ex.k])
                cx.op("act", lambda e: e.activation(out=ex[:], in_=ex[:], func=AF.Exp), [ex.k], [ex.k])
                cx.op("dve", lambda e: e.tensor_copy(out=sm[:], in_=ex[:, 0, :]), [ex.k], [sm.k])
                cx.op("dve", lambda e: e.memset(lb[:], 0.0), [], [lb.k])
                for l in range(1, self.depth):
                    cx.op("dve", lambda e: e.tensor_tensor(out=sm[:], in0=sm[:], in1=ex[:, l, :], op=ALU.add),
                          [sm.k, ex.k], [sm.k])
                    if l <= i:
                        cx.op("dve", lambda e: e.tensor_tensor(out=lb[:], in0=lb[:], in1=ex[:, l, :], op=ALU.add),
                              [lb.k, ex.k], [lb.k])
                cx.op("dve", lambda e: e.reciprocal(out=sm[:], in_=sm[:]), [sm.k], [sm.k])
                cx.op("dve", lambda e: e.tensor_tensor(out=lb[:], in0=lb[:], in1=sm[:], op=ALU.mult),
                      [lb.k, sm.k], [lb.k])
                cx.op("dve", lambda e: e.tensor_scalar(out=oml[:], in0=lb[:], scalar1=-1.0, scalar2=1.0,
                                                       op0=ALU.mult, op1=ALU.add), [lb.k], [oml.k])
            h = Buf(cx, es, "h", [128, KC, TT], F32)
            u = Buf(cx, es, "u", [128, KC, TT], BF16)
            sqs = [Buf(cx, es, "sq", [128, TT], BF16) for _ in range(2)]
            rs = Buf(cx, es, "rs", [128, TT], F32)
            tmps = [Buf(cx, es, "tmp", [128, TT], F32) for _ in range(6)]
            bb = Buf(cx, es, "bb", [128, QG, TT], F32)
            dl = Buf(cx, es, "dl", [128, QG, 8], F32)
            kt = Buf(cx, es, "kt", [128, QG, TT], BF16)
            kdT = Buf(cx, es, "kdT", [128, QG, TT], BF16)
            qt = Buf(cx, es, "qt", [128, QG, TT], BF16)
            qpb = Buf(cx, es, "qpb", [128, QG, TT], BF16)
            kd_tok = Buf(cx, es, "kd_tok", [128, NP, QG * 128], BF16)
            v_tok = Buf(cx, es, "v_tok", [128, NP, VW], BF16)
            atts = [Buf(cx, es, "att", [128, 128], BF16) for _ in range(4)]
            osts = [Buf(cx, es, "ost", [128, 4, 128], F32) for _ in range(2)]
            tc = [0]
            ac = [0]
            oc = [0]

            def tmp():
                tc[0] += 1
                return tmps[tc[0] % len(tmps)]

            def k_finish(g, qb, src_ap, src_k):
                te = tmp()
                cx.op("act", lambda e: e.activation(out=te[:], in_=bb[:, qb, :], func=AF.Exp, scale=-1.0),
                      [bb.k], [te.k])
                cx.op("act", lambda e: e.activation(out=dl[:, qb, :], in_=bb[:, qb, 63:TT:64], func=AF.Exp),
                      [bb.k], [dl.k])
                cx.op("dve", lambda e: e.tensor_tensor(out=te[:], in0=src_ap, in1=te[:], op=ALU.mult),
                      [src_k, te.k], [te.k])
                cx.op("pool", lambda e: e.tensor_copy(out=kt[:, qb, :], in_=te[:]), [te.k], [kt.k])
                cx.op("pool", lambda e: e.tensor_tensor(
                    out=kdT[:, qb, :].rearrange("p (c t) -> p c t", t=64),
                    in0=te[:].rearrange("p (c t) -> p c t", t=64),
                    in1=dl[:, qb, :].unsqueeze(2).to_broadcast([128, 8, 64]), op=ALU.mult),
                    [te.k, dl.k], [kdT.k])

            for it in range(self.NT):
                self.load_h(h, it)
                self.norm_u(h, "norm_mix_%d" % i, u, sqs, rs)
                self.store_tile(u, self.uscr, 0, KC, it, self.u_k[it])
                if kind == "gla":
                    def epi_glr(blk, bw, ps):
                        cx.op("act", lambda e: e.copy(out=glr[:], in_=ps[0:16, :]), [ps.k], [glr.k])
                    self.linear_fm(u, KC, w_in, 6144, 16, epi_glr)
                for g in range(NG):
                    if kind == "gla":
                        for qb in range(QG):
                            gq = g * QG + qb
                            ps = self.next_ps()
                            cx.op("pe", lambda e: e.matmul(ps[:], wgk2b[0:16, gq * 128:(gq + 1) * 128], glr[:],
                                                           start=True, stop=True), [wgk2b.k, glr.k], [ps.k])
                            t1 = tmp()
                            cx.op("act", lambda e: e.activation(out=t1[:], in_=ps[:], func=AF.Exp, scale=-1.0,
                                                                bias=negb[:, gq:gq + 1]), [ps.k, negb.k], [t1.k])
                            cx.op("act", lambda e: e.activation(out=t1[:], in_=t1[:], func=AF.Ln, bias=1.0),
                                  [t1.k], [t1.k])
                            cx.op("pool", lambda e: e.tensor_scalar(out=t1[:], in0=t1[:], scalar1=-1.0 / 16.0,
                                                                    scalar2=None, op0=ALU.mult), [t1.k], [t1.k])
                            cx.op("dve", lambda e: e.tensor_tensor_scan(
                                out=bb[:, qb, :], data0=self.cmsk_b[:], data1=t1[:], initial=0.0,
                                op0=ALU.mult, op1=ALU.add), [t1.k, self.cmsk_b.k], [bb.k])

                        def epi_k(blk, bw, ps):
                            k_finish(g, blk, ps[:], ps.k)
                        self.linear_fm(u, KC, w_in, kcol(g), 512, epi_k)
                    else:
                        def epi_f(blk, bw, ps):
                            gq = g * QG + blk
                            t1, t2 = tmp(), tmp()
                            cx.op("act", lambda e: e.activation(out=t1[:], in_=ps[:], func=AF.Sigmoid), [ps.k], [t1.k])
                            cx.op("dve", lambda e: e.tensor_scalar(
                                out=t1[:], in0=t1[:], scalar1=oml[:, gq:gq + 1], scalar2=lb[:, gq:gq + 1],
                                op0=ALU.mult, op1=ALU.add), [t1.k, oml.k, lb.k], [t1.k])
                            cx.op("act", lambda e: e.activation(out=t2[:], in_=t1[:], func=AF.Ln), [t1.k], [t2.k])
                            cx.op("dve", lambda e: e.tensor_tensor_scan(
                                out=bb[:, blk, :], data0=self.cmsk_b[:], data1=t2[:], initial=0.0,
                                op0=ALU.mult, op1=ALU.add), [t2.k, self.cmsk_b.k], [bb.k])
                            cx.op("dve", lambda e: e.tensor_scalar(out=t1[:], in0=t1[:], scalar1=-1.0, scalar2=1.0,
                                                                   op0=ALU.mult, op1=ALU.add), [t1.k], [t1.k])
                            k_finish(g, blk, t1[:], t1.k)
                        self.linear_fm(u, KC, w_in, kcol(g), 512, epi_f)

                    def epi_q(blk, bw, ps):
                        te = tmp()
                        cx.op("act", lambda e: e.activation(out=te[:], in_=bb[:, blk, :], func=AF.Exp), [bb.k], [te.k])
                        if kind == "gla":
                            cx.op("dve", lambda e: e.scalar_tensor_tensor(
                                out=qt[:, blk, :], in0=ps[:], scalar=qscale, in1=te[:], op0=ALU.mult, op1=ALU.mult),
                                [ps.k, te.k], [qt.k])
                        else:
                            t2 = tmp()
                            cx.op("act", lambda e: e.activation(out=t2[:], in_=ps[:], func=AF.Silu), [ps.k], [t2.k])
                            cx.op("dve", lambda e: e.scalar_tensor_tensor(
                                out=qt[:, blk, :], in0=t2[:], scalar=qscale, in1=te[:], op0=ALU.mult, op1=ALU.mult),
                                [t2.k, te.k], [qt.k])
                    self.linear_fm(u, KC, w_in, qcol(g), 512, epi_q)

                    for c in (range(8) if seqpar else []):
                        cx.op("dve", lambda e: e.tensor_tensor(out=eo[g][:, :, c + 1], in0=eo[g][:, :, c],
                                                               in1=dl[:, :, c], op=ALU.mult), [eo[g].k, dl.k], [eo[g].k])
                    for qb in (range(QG) if seqpar else []):
                        cx.op("pool", lambda e: e.tensor_tensor(
                            out=qpb[:, qb, :].rearrange("p (c t) -> p c t", t=64),
                            in0=qt[:, qb, :].rearrange("p (c t) -> p c t", t=64),
                            in1=eo[g][:, qb, 0:8].unsqueeze(2).to_broadcast([128, 8, 64]), op=ALU.mult),
                            [qt.k, eo[g].k], [qpb.k])
                    if seqpar:
                        self.store_tile(qpb, self.qp, g * QG * 128, QG, it, self.qp_k[it])
                        cx.op("dve", lambda e: e.tensor_copy(out=eo[g][:, :, 0], in_=eo[g][:, :, 8]), [eo[g].k], [eo[g].k])

                    if it == 0 and g == 0:
                        self.dump_sb("u", u, [128, KC, TT], BF16)
                        self.dump_sb("bb", bb, [128, QG, TT])
                        self.dump_sb("kt", kt, [128, QG, TT], BF16)
                        self.dump_sb("qt", qt, [128, QG, TT], BF16)
                        self.dump_sb("kdT", kdT, [128, QG, TT], BF16)
                    for s in range(VW // 512):
                        def epi_v(ts, ps):
                            cx.op("act", lambda e: e.copy(out=v_tok[:, ts, s * 512:(s + 1) * 512], in_=ps[:]),
                                  [ps.k], [v_tok.k])
                        self.linear_tm(u, w_in, vcol(g) + s * 512, 512, epi_v)

                    for ts in range(NP):
                        ps = self.next_ps()
                        pv = ps[:, 0:256].bitcast(BF16).rearrange("p (a b) -> p a b", b=128)
                        for qb in range(QG):
                            cx.op("pe", lambda e: e.transpose(pv[:, qb, :], kdT[:, qb, ts * 128:(ts + 1) * 128],
                                                              self.ident_b[:]), [kdT.k, self.ident_b.k], [ps.k])
                        cx.op("act", lambda e: e.copy(out=kd_tok[:, ts, :].rearrange("p (a b) -> p a b", b=128),
                                                      in_=pv), [ps.k], [kd_tok.k])

                    if it == 0 and g == 0:
                        self.dump_sb("v_tok", v_tok, [128, NP, VW], BF16)
                        self.dump_sb("kd_tok", kd_tok, [128, NP, QG * 128], BF16)
                    if kind == "gla":
                        obanks = [[(hh, vb) for vb in range(4)] for hh in range(Hg)]
                    else:
                        obanks = [[(hh, 0) for hh in range(Hg)]]
                    for pr in range(NP):
                        tsl = slice(pr * 128, (pr + 1) * 128)
                        attm = {}
                        for hh in range(Hg):
                            ps = self.next_ps()
                            for jj in range(dkb):
                                qb = hh * dkb + jj
                                cx.op("pe", lambda e: e.matmul(ps[:, 0:128], kt[:, qb, tsl], qt[:, qb, tsl],
                                                               start=(jj == 0), stop=(jj == dkb - 1)),
                                      [kt.k, qt.k], [ps.k])
                            a = atts[ac[0] % 4]
                            ac[0] += 1
                            cx.op("dve", lambda e: e.tensor_tensor(out=a[:], in0=ps[:, 0:128], in1=mask2, op=ALU.mult),
                                  [ps.k, self.consts_b.k], [a.k])
                            attm[hh] = a
                        ops = []
                        for bank in obanks:
                            ps = self.next_ps(hold=True)
                            ops.append(ps)
                            for slot, (hh, vb) in enumerate(bank):
                                osl = slice(slot * 128, (slot + 1) * 128)
                                cx.op("pe", lambda e: e.matmul(
                                    ps[:, osl], v_tok[:, pr, hh * dv + vb * 128:hh * dv + (vb + 1) * 128],
                                    attm[hh][:], start=(slot == 0), stop=False, skip_group_check=True),
                                    [v_tok.k, attm[hh].k], [ps.k])
                        for c2 in range(2):
                            c = pr * 2 + c2
                            csl = slice(pr * 128 + c2 * 64, pr * 128 + (c2 + 1) * 64)
                            rows = slice(c2 * 64, (c2 + 1) * 64)
                            for bi, bank in enumerate(obanks):
                                ps = ops[bi]
                                for slot, (hh, vb) in enumerate(bank):
                                    for jj in range(dkb):
                                        qb = hh * dkb + jj
                                        cx.op("pe", lambda e: e.matmul(
                                            ps[:, slot * 128 + c2 * 64:slot * 128 + (c2 + 1) * 64],
                                            Sbf[g][hh][:, jj, vb * 128:(vb + 1) * 128], qt[:, qb, csl],
                                            start=False, stop=(jj == dkb - 1), skip_group_check=True),
                                            [Sbf[g][hh].k, qt.k], [ps.k])
                            if kind == "gla":
                                for hh in range(Hg):
                                    for jj in range(dkb):
                                        qb = hh * dkb + jj
                                        ps = self.next_ps()
                                        cx.op("pe", lambda e: e.matmul(
                                            ps[:, 0:dv], kd_tok[rows, pr, qb * 128:(qb + 1) * 128],
                                            v_tok[rows, pr, hh * dv:(hh + 1) * dv], start=True, stop=True),
                                            [kd_tok.k, v_tok.k], [ps.k])
                                        cx.op("dve", lambda e: e.scalar_tensor_tensor(
                                            out=S32[g][hh][:, jj, :], in0=S32[g][hh][:, jj, :], scalar=dl[:, qb, c:c + 1],
                                            in1=ps[:, 0:dv], op0=ALU.mult, op1=ALU.add),
                                            [S32[g][hh].k, dl.k, ps.k], [S32[g][hh].k])
                                    cx.op("act", lambda e: e.copy(out=Sbf[g][hh][:], in_=S32[g][hh][:]),
                                          [S32[g][hh].k], [Sbf[g][hh].k])
                            else:
                                ps = self.next_ps()
                                for hh in range(Hg):
                                    cx.op("pe", lambda e: e.matmul(
                                        ps[:, hh * 128:(hh + 1) * 128], kd_tok[rows, pr, hh * 128:(hh + 1) * 128],
                                        v_tok[rows, pr, hh * dv:(hh + 1) * dv], start=True, stop=True),
                                        [kd_tok.k, v_tok.k], [ps.k])
                                for hh in range(Hg):
                                    cx.op("dve", lambda e: e.scalar_tensor_tensor(
                                        out=S32[g][hh][:, 0, :], in0=S32[g][hh][:, 0, :], scalar=dl[:, hh, c:c + 1],
                                        in1=ps[:, hh * 128:(hh + 1) * 128], op0=ALU.mult, op1=ALU.add),
                                        [S32[g][hh].k, dl.k, ps.k], [S32[g][hh].k])
                                    cx.op("act", lambda e: e.copy(out=Sbf[g][hh][:], in_=S32[g][hh][:]),
                                          [S32[g][hh].k], [Sbf[g][hh].k])
                        for bi, bank in enumerate(obanks):
                            ost = osts[oc[0] % 2]
                            oc[0] += 1
                            cx.op("act", lambda e: e.copy(out=ost[:], in_=ops[bi][:].rearrange("p (a b) -> p a b", b=128)),
                                  [ops[bi].k], [ost.k])
                            hh0, vb0 = bank[0]
                            vblk0 = (g * Hg + hh0) * dvb + vb0
                            dst = self.oloc.ap()[vblk0 * 128:(vblk0 + 4) * 128,
                                                 it * TT + pr * 128:it * TT + (pr + 1) * 128].rearrange(
                                "(a p) t -> p a t", p=128)
                            cx.dma("pool", dst, ost[:], self.oloc_k[it][(oc[0] - 1) % 2], ost.k)
                            self.release(ops[bi])
            for g in (range(NG) if seqpar else []):
                for hh in range(Hg):
                    c0 = ((g * Hg + hh) * dkb) * dv
                    cx.dma("pool", sloc.ap()[:, c0:c0 + dkb * dv].rearrange("p (a b) -> p a b", b=dv),
                           S32[g][hh][:], sloc_k, S32[g][hh].k)
                eoc = Buf(cx, es, "eoc", [128, QG], F32)
                cx.op("dve", lambda e: e.tensor_copy(out=eoc[:], in_=eo[g][:, :, 0]), [eo[g].k], [eoc.k])
                cx.dma("pool", sloc.ap()[:, NS0 + g * QG:NS0 + (g + 1) * QG], eoc[:], sloc_k, eoc.k)
            if seqpar:
                cx.allgather(sg.ap(), sloc.ap(), sg_k, sloc_k, [list(range(NCORES))])

            cx.barrier(extra=[t for pair in self.oloc_k for t in pair])

        with ExitStack() as es:
            Sin = Buf(cx, es, "Sin", [128, NQ, dv], F32)
            Sinb = Buf(cx, es, "Sinb", [128, NQ, dv], BF16)
            with ExitStack() as es2:
                if not seqpar:
                    NR = 0
                else:
                    NR = NCORES
                stg = [Buf(cx, es2, "stg", [128, NS], F32) for _ in range(2)]
                dd = [Buf(cx, es2, "dd", [128, NQ], F32) for _ in range(2)]
                cx.op("pool", lambda e: e.memset(Sin[:], 0.0), [], [Sin.k])
                for r in range(NR):
                    st, d1 = stg[r % 2], dd[r % 2]
                    m = self.cmask_b[:, r:r + 1]
                    cx.dma("pool", st[:], sg.ap()[r * 128:(r + 1) * 128, :], st.k, sg_k)
                    cx.op("dve", lambda e: e.tensor_scalar(out=d1[:], in0=st[:, NS0:NS], scalar1=-1.0, scalar2=m,
                                                           op0=ALU.add, op1=ALU.mult), [st.k, self.cmask_b.k], [d1.k])
                    cx.op("dve", lambda e: e.tensor_scalar(out=d1[:], in0=d1[:], scalar1=1.0, scalar2=None,
                                                           op0=ALU.add), [d1.k], [d1.k])
                    for q in range(NQ):
                        cx.op("dve", lambda e: e.tensor_scalar(out=Sin[:, q, :], in0=Sin[:, q, :], scalar1=d1[:, q:q + 1],
                                                               scalar2=None, op0=ALU.mult), [Sin.k, d1.k], [Sin.k])
                        cx.op("dve", lambda e: e.scalar_tensor_tensor(
                            out=Sin[:, q, :], in0=st[:, q * dv:(q + 1) * dv], scalar=m, in1=Sin[:, q, :],
                            op0=ALU.mult, op1=ALU.add), [st.k, Sin.k, self.cmask_b.k], [Sin.k])
                cx.op("act", lambda e: e.copy(out=Sinb[:], in_=Sin[:]), [Sin.k], [Sinb.k])
                cx.barrier(end=False)
            h = Buf(cx, es, "h", [128, KC, TT], F32)
            u = Buf(cx, es, "u", [128, KC, TT], BF16)
            o = Buf(cx, es, "o", [128, KC, TT], F32)
            qpl = Buf(cx, es, "qpl", [128, NQ, TT], BF16)
            ogb = Buf(cx, es, "ogb", [128, KC, TT], BF16)
            sqs = [Buf(cx, es, "sq", [128, TT], BF16) for _ in range(2)]
            rss = [Buf(cx, es, "rs", [128, TT], F32) for _ in range(2)]
            tmps = [Buf(cx, es, "tmp", [128, TT], F32) for _ in range(4)]
            tc = [0]
            hpb = dv // 128
            for it in range(self.NT):
                self.load_tile(u, self.uscr, 0, KC, it, self.u_k[it])
                cx._waits("pool", self.oloc_k[it], [])
                self.load_tile(o, self.oloc, 0, KC, it, None)
                self.load_h(h, it)
                if seqpar:
                    self.load_tile(qpl, self.qp, 0, NQ, it, self.qp_k[it])
                for hd in (range(NG * Hg) if seqpar else []):
                    for vb in range(hpb):
                        ps = self.next_ps()
                        for jj in range(dkb):
                            q = hd * dkb + jj
                            cx.op("pe", lambda e: e.matmul(ps[:], Sinb[:, q, vb * 128:(vb + 1) * 128], qpl[:, q, :],
                                                           start=(jj == 0), stop=(jj == dkb - 1)), [Sinb.k, qpl.k], [ps.k])
                        blk = hd * hpb + vb
                        cx.op("dve", lambda e: e.tensor_tensor(out=o[:, blk, :], in0=ps[:], in1=o[:, blk, :], op=ALU.add),
                              [ps.k, o.k], [o.k])
                cur = {"hd": -1, "rs": None}

                def epi_og(blk, bw, ps):
                    hd = blk // hpb
                    if hd != cur["hd"]:
                        cur["hd"] = hd
                        cur["rs"] = rss[hd % 2]
                        self.rstd(o, hd * hpb, hpb, dv, sqs, cur["rs"])
                    r = cur["rs"]
                    t1, t2 = tmps[tc[0] % 4], tmps[(tc[0] + 1) % 4]
                    tc[0] += 2
                    cx.op("act", lambda e: e.activation(out=t1[:], in_=ps[:],
                                                        func=(AF.Silu if kind == "gla" else AF.Sigmoid)), [ps.k], [t1.k])
                    cx.op("dve", lambda e: e.scalar_tensor_tensor(
                        out=t2[:], in0=o[:, blk, :], scalar=self.col(gn, blk % hpb), in1=r[:],
                        op0=ALU.mult, op1=ALU.mult), [o.k, r.k, self.cols_b.k], [t2.k])
                    cx.op("pool", lambda e: e.tensor_tensor(out=ogb[:, blk, :], in0=t2[:], in1=t1[:], op=ALU.mult),
                          [t1.k, t2.k], [ogb.k])
                self.linear_fm(u, KC, w_in, ogcol, D, epi_og)

                def epi_out(blk, bw, ps):
                    cx.op("dve", lambda e: e.tensor_tensor(out=h[:, blk, :], in0=ps[:], in1=h[:, blk, :], op=ALU.add),
                          [ps.k, h.k], [h.k])
                self.linear_fm(ogb, KC, w_out, 0, D, epi_out)
                self.store_tile(h, self.hT, 0, KC, it, self.hT_k[it])
            self.h_stored()
            cx.barrier()

    def phase_ssm(self, i, j):
        cx, nc = self.cx, self.nc
        w_in, w_out = "ssm_w_in_%d" % j, "ssm_w_out_%d" % j
        mask2 = self.consts_b[:, 0, :]
        selpair = self.consts_b[:, 1, :]
        sel63 = self.consts_b[:, 2, :]
        sel127 = self.consts_b[:, 3, :]
        NP = TT // 128
        G, HG, P, NST = 8, 8, 64, 128
        GI = 1

        class View:
            def __init__(self, ap, k):
                self.ap, self.k = ap, k

            def __getitem__(self, key):
                return self.ap[key]

        def bc(ap2, n):
            return ap2.unsqueeze(2).to_broadcast([128, ap2.shape[1], n])

        with ExitStack() as es:
            big1 = Buf(cx, es, "big1", [128, KC * TT], F32)
            big2 = Buf(cx, es, "big2", [128, KC * TT], F32)
            h1 = View(big1.t[:].rearrange("p (a b) -> p a b", b=TT), big1.k)
            yT = View(big1.t[:].bitcast(BF16).rearrange("p (a b) -> p a b", b=TT), big1.k)
            u = View(big2.t[:, 0:4096].bitcast(BF16).rearrange("p (a b) -> p a b", b=TT), Trk("u"))
            BT = View(big2.t[:, 4096:6144].bitcast(BF16).rearrange("p (a b) -> p a b", b=TT), Trk("BT"))
            CT = View(big2.t[:, 6144:8192].bitcast(BF16).rearrange("p (a b) -> p a b", b=TT), Trk("CT"))
            h2 = View(big2.t[:].rearrange("p (a b) -> p a b", b=TT), big2.k)
            S32 = [Buf(cx, es, "S32", [128, HG * P], F32) for _ in range(G)]
            Sbf = [Buf(cx, es, "Sbf", [128, HG * P], BF16) for _ in range(G)]
            halo = Buf(cx, es, "halo", [128, 48, 3], F32)
            identf = Buf(cx, es, "identf", [128, 128], F32)
            onesf = Buf(cx, es, "onesf", [128, 128], F32)
            dtb = Buf(cx, es, "dtb", [128, 64], F32)
            arow = Buf(cx, es, "arow", [128, 64], F32)
            ddr = Buf(cx, es, "ddr", [128, 64], F32)
            dt_tok = Buf(cx, es, "dt_tok", [128, NP, 64], F32)
            la_tok = Buf(cx, es, "la_tok", [128, NP, 64], F32)
            b_tok = Buf(cx, es, "b_tok", [128, NP, 64], F32)
            eb_tok = Buf(cx, es, "eb_tok", [128, NP, 64], F32)
            we_tok = Buf(cx, es, "we_tok", [128, NP, 64], F32)
            Dc = Buf(cx, es, "Dc", [128, NP, 2, 64], F32)
            sqs = [Buf(cx, es, "sq", [128, TT], BF16) for _ in range(1)]
            rs = Buf(cx, es, "rs", [128, TT], F32)
            tmpc = [Buf(cx, es, "tmpc", [128, TT + 3], F32) for _ in range(2)]
            acc = Buf(cx, es, "acc", [128, TT], F32)
            xTg = Buf(cx, es, "xTg", [128, 4, TT], BF16)
            x_tok = [Buf(cx, es, "x_tok", [128, NP, 512], BF16) for _ in range(GI)]
            zg = [Buf(cx, es, "zg", [128, NP, 512], F32) for _ in range(GI)]
            rel = [Buf(cx, es, "rel", [128, 8, 128], F32) for _ in range(GI)]
            Mb = [Buf(cx, es, "Mb", [128, 8, 128], BF16) for _ in range(GI)]
            xdt = [Buf(cx, es, "xdt", [128, 512], BF16) for _ in range(GI)]
            xw = [Buf(cx, es, "xw", [128, 512], BF16) for _ in range(GI)]
            ytmp = [Buf(cx, es, "ytmp", [128, 512], F32) for _ in range(GI)]
            yn = [Buf(cx, es, "yn", [128, 512], BF16) for _ in range(GI)]
            cbm = [Buf(cx, es, "cbm", [128, 128], F32) for _ in range(GI)]
            btok = [Buf(cx, es, "btok", [128, 128], BF16) for _ in range(GI)]
            ss = [Buf(cx, es, "ss", [128, 1], F32) for _ in range(GI)]
            for g in range(G):
                cx.op("pool", lambda e: e.memset(S32[g][:], 0.0), [], [S32[g].k])
                cx.op("pool", lambda e: e.memset(Sbf[g][:], 0.0), [], [Sbf[g].k])
            cx.op("pool", lambda e: e.memset(halo[:], 0.0), [], [halo.k])
            cx.op("pool", lambda e: e.memset(onesf[:], 1.0), [], [onesf.k])
            cx.op("pool", lambda e: e.memset(identf[:], 0.0), [], [identf.k])
            cx.op("pool", lambda e: e.affine_select(
                out=identf[:], in_=identf[:], pattern=[[-1, 128]], compare_op=ALU.not_equal, fill=1.0,
                base=0, channel_multiplier=1), [identf.k], [identf.k])
            cx.dma("pool", dtb[:], self.rows_d["ssm_dt_bias_%d" % j].ap().partition_broadcast(128), dtb.k, None)
            cx.dma("pool", arow[:], self.rows_d["ssm_a_log_%d" % j].ap().partition_broadcast(128), arow.k, None)
            cx.dma("pool", ddr[:], self.rows_d["ssm_d_%d" % j].ap().partition_broadcast(128), ddr.k, None)
            cx.op("act", lambda e: e.activation(out=arow[:], in_=arow[:], func=AF.Exp), [arow.k], [arow.k])
            cx.op("dve", lambda e: e.tensor_scalar(out=arow[:], in0=arow[:], scalar1=-1.0, scalar2=None, op0=ALU.mult),
                  [arow.k], [arow.k])
            cb16 = Buf(cx, es, "cb16", [128, 4, 128], BF16)
            onesb = self.ones_b
            cx.op("dve", lambda e: e.tensor_copy(out=cb16[:], in_=self.consts_b[:]), [self.consts_b.k], [cb16.k])
            la_hi = Buf(cx, es, "la_hi", [128, NP, 64], BF16)
            la_lo = Buf(cx, es, "la_lo", [128, NP, 64], BF16)
            b_hi = Buf(cx, es, "b_hi", [128, NP, 64], BF16)
            b_lo = Buf(cx, es, "b_lo", [128, NP, 64], BF16)
            hl_f = Buf(cx, es, "hl_f", [128, NP, 64], F32)
            dg_lo = Buf(cx, es, "dg_lo", [128, 8, 128], BF16)
            dg_hi = Buf(cx, es, "dg_hi", [128, 8, 128], BF16)

            def split(src, hi, lo):
                cx.op("dve", lambda e: e.tensor_copy(out=hi[:], in_=src[:]), [src.k], [hi.k])
                cx.op("dve", lambda e: e.tensor_copy(out=hl_f[:], in_=hi[:]), [hi.k], [hl_f.k])
                cx.op("dve", lambda e: e.tensor_tensor(out=lo[:], in0=src[:], in1=hl_f[:], op=ALU.subtract),
                      [src.k, hl_f.k], [lo.k])
            cc = [0]

            def conv_epi(cb, ps, dst_ap, dst_k):
                tcv = tmpc[cc[0] % 2]
                cc[0] += 1
                cx.op("pool", lambda e: e.tensor_copy(out=tcv[:, 0:3], in_=halo[:, cb, :]), [halo.k], [tcv.k])
                cx.op("act", lambda e: e.copy(out=tcv[:, 3:TT + 3], in_=ps[:]), [ps.k], [tcv.k])
                cx.op("pool", lambda e: e.tensor_copy(out=halo[:, cb, :], in_=tcv[:, TT:TT + 3]), [tcv.k], [halo.k])
                wc = lambda t: self.col("ssm_conv_w_%d_%d" % (j, t), cb)
                cx.op("dve", lambda e: e.tensor_scalar(out=acc[:], in0=tcv[:, 0:TT], scalar1=wc(0),
                                                       scalar2=self.col("ssm_conv_b_%d" % j, cb),
                                                       op0=ALU.mult, op1=ALU.add), [tcv.k, self.cols_b.k], [acc.k])
                for t in range(1, 4):
                    cx.op("dve", lambda e: e.scalar_tensor_tensor(out=acc[:], in0=tcv[:, t:t + TT], scalar=wc(t),
                                                                  in1=acc[:], op0=ALU.mult, op1=ALU.add),
                          [tcv.k, acc.k, self.cols_b.k], [acc.k])
                cx.op("act", lambda e: e.activation(out=dst_ap, in_=acc[:], func=AF.Silu), [acc.k], [dst_k])

            for it in range(self.NT):
                self.load_h(h1, it)
                self.norm_u(h1, "norm_mix_%d" % i, u, sqs, rs)

                if SSM_STAGE < 1:
                    cx.barrier(end=False)
                    continue
                def epi_bc(blk, bw, ps):
                    if blk < 8:
                        conv_epi(32 + blk, ps, BT[:, blk, :], BT.k)
                    else:
                        conv_epi(32 + blk, ps, CT[:, blk - 8, :], CT.k)
                self.linear_fm(u, KC, w_in, 8192, 2048, epi_bc)

                if SSM_STAGE < 2:
                    cx.barrier(end=False)
                    continue
                def epi_dt(ts, ps):
                    cx.op("dve", lambda e: e.tensor_tensor(out=dt_tok[:, ts, :], in0=ps[:, 0:64], in1=dtb[:], op=ALU.add),
                          [ps.k, dtb.k], [dt_tok.k])
                self.linear_tm(u, w_in, 10240, 64, epi_dt)
                cx.op("act", lambda e: e.activation(out=dt_tok[:], in_=dt_tok[:], func=AF.Exp), [dt_tok.k], [dt_tok.k])
                cx.op("act", lambda e: e.activation(out=dt_tok[:], in_=dt_tok[:], func=AF.Ln, bias=1.0), [dt_tok.k], [dt_tok.k])
                for ts in range(NP):
                    cx.op("dve", lambda e: e.tensor_tensor(out=la_tok[:, ts, :], in0=dt_tok[:, ts, :], in1=arow[:], op=ALU.mult),
                          [dt_tok.k, arow.k], [la_tok.k])
                split(la_tok, la_hi, la_lo)
                for ts in range(NP):
                    ps = self.next_ps()
                    cx.op("pe", lambda e: e.matmul(ps[:, 0:64], cb16[:, 0, :], la_hi[:, ts, :], start=True, stop=False),
                          [cb16.k, la_hi.k], [ps.k])
                    cx.op("pe", lambda e: e.matmul(ps[:, 0:64], cb16[:, 0, :], la_lo[:, ts, :], start=False, stop=True),
                          [cb16.k, la_lo.k], [ps.k])
                    cx.op("act", lambda e: e.copy(out=b_tok[:, ts, :], in_=ps[:, 0:64]), [ps.k], [b_tok.k])
                split(b_tok, b_hi, b_lo)
                cx.op("act", lambda e: e.activation(out=eb_tok[:], in_=b_tok[:], func=AF.Exp), [b_tok.k], [eb_tok.k])
                for ts in range(NP):
                    ps = self.next_ps()
                    for si in range(3):
                        for hl, src in enumerate((b_hi, b_lo)):
                            cx.op("pe", lambda e: e.matmul(ps[:, si * 64:(si + 1) * 64], cb16[:, 1 + si, :], src[:, ts, :],
                                                           start=(si == 0 and hl == 0), stop=(hl == 1),
                                                           skip_group_check=True), [cb16.k, src.k], [ps.k])
                    cx.op("dve", lambda e: e.tensor_tensor(out=we_tok[:, ts, :], in0=ps[:, 0:64], in1=b_tok[:, ts, :],
                                                           op=ALU.subtract), [ps.k, b_tok.k], [we_tok.k])
                    cx.op("act", lambda e: e.activation(out=Dc[:, ts, :, :], in_=ps[:, 64:192].rearrange("p (a b) -> p a b", b=64),
                                                        func=AF.Exp), [ps.k], [Dc.k])
                cx.op("act", lambda e: e.activation(out=we_tok[:], in_=we_tok[:], func=AF.Exp), [we_tok.k], [we_tok.k])

                if SSM_STAGE < 3:
                    cx.barrier(end=False)
                    continue
                for gp in range(G // GI):
                    gs = tuple(range(gp * GI, (gp + 1) * GI))
                    for g in gs:
                        sl = g % GI
                        def epi_x(blk, bw, ps):
                            conv_epi(g * 4 + blk, ps, xTg[:, blk, :], xTg.k)
                        self.linear_fm(u, KC, w_in, 4096 + g * 512, 512, epi_x)
                        for ts in range(NP):
                            ps = self.next_ps()
                            pv = ps[:, 0:256].bitcast(BF16).rearrange("p (a b) -> p a b", b=128)
                            for a in range(4):
                                cx.op("pe", lambda e: e.transpose(pv[:, a, :], xTg[:, a, ts * 128:(ts + 1) * 128],
                                                                  self.ident_b[:]), [xTg.k, self.ident_b.k], [ps.k])
                            cx.op("act", lambda e: e.copy(out=x_tok[sl][:, ts, :].rearrange("p (a b) -> p a b", b=128),
                                                          in_=pv), [ps.k], [x_tok[sl].k])
                        def epi_z(ts, ps):
                            cx.op("act", lambda e: e.activation(out=zg[sl][:, ts, :], in_=ps[:], func=AF.Silu),
                                  [ps.k], [zg[sl].k])
                        self.linear_tm(u, w_in, g * 512, 512, epi_z)

                    for pr in (range(NP) if SSM_STAGE >= 4 else []):
                        tsl = slice(pr * 128, (pr + 1) * 128)
                        psy, psi = {}, {}
                        for g in gs:
                            sl = g % GI
                            hs = slice(g * 8, (g + 1) * 8)
                            ps = self.next_ps()
                            pvb = ps[:, 0:64].bitcast(BF16)
                            cx.op("pe", lambda e: e.transpose(pvb, BT[:, g, tsl], self.ident_b[:]),
                                  [BT.k, self.ident_b.k], [ps.k])
                            cx.op("act", lambda e: e.copy(out=btok[sl][:], in_=pvb), [ps.k], [btok[sl].k])
                            ps = self.next_ps()
                            cx.op("pe", lambda e: e.matmul(ps[:, 0:128], BT[:, g, tsl], CT[:, g, tsl], start=True, stop=True),
                                  [BT.k, CT.k], [ps.k])
                            cx.op("dve", lambda e: e.tensor_tensor(out=cbm[sl][:], in0=ps[:, 0:128], in1=mask2, op=ALU.mult),
                                  [ps.k, self.consts_b.k], [cbm[sl].k])
                            for dgx, bx in ((dg_hi, b_hi), (dg_lo, b_lo)):
                                cx.op("dve", lambda e: e.tensor_tensor(
                                    out=dgx[:], in0=identf[:].unsqueeze(1).to_broadcast([128, 8, 128]),
                                    in1=bc(bx[:, pr, hs], 128), op=ALU.mult), [identf.k, bx.k], [dgx.k])
                            pb0, pb1 = self.next_ps(hold=True), self.next_ps(hold=True)
                            for pbx, lo4 in ((pb0, 0), (pb1, 4)):
                                for hl, dgx in enumerate((dg_hi, dg_lo)):
                                    cx.op("pe", lambda e: e.matmul(
                                        pbx[:], onesb[:], dgx[:, lo4:lo4 + 4, :].rearrange("p a b -> p (a b)"),
                                        start=(hl == 0), stop=(hl == 1)), [onesb.k, dgx.k], [pbx.k])
                            cx.op("dve", lambda e: e.tensor_tensor(
                                out=rel[sl][:, 0:4, :], in0=pb0[:].rearrange("p (a b) -> p a b", b=128),
                                in1=bc(b_tok[:, pr, g * 8:g * 8 + 4], 128), op=ALU.subtract), [pb0.k, b_tok.k], [rel[sl].k])
                            cx.op("dve", lambda e: e.tensor_tensor(
                                out=rel[sl][:, 4:8, :], in0=pb1[:].rearrange("p (a b) -> p a b", b=128),
                                in1=bc(b_tok[:, pr, g * 8 + 4:g * 8 + 8], 128), op=ALU.subtract), [pb1.k, b_tok.k], [rel[sl].k])
                            self.release(pb0)
                            self.release(pb1)
                            cx.op("act", lambda e: e.activation(out=rel[sl][:], in_=rel[sl][:], func=AF.Relu, scale=-1.0),
                                  [rel[sl].k], [rel[sl].k])
                            cx.op("act", lambda e: e.activation(out=rel[sl][:], in_=rel[sl][:], func=AF.Exp, scale=-1.0),
                                  [rel[sl].k], [rel[sl].k])
                            cx.op("pool", lambda e: e.tensor_tensor(
                                out=Mb[sl][:], in0=rel[sl][:], in1=cbm[sl][:].unsqueeze(1).to_broadcast([128, 8, 128]),
                                op=ALU.mult), [rel[sl].k, cbm[sl].k], [Mb[sl].k])
                            cx.op("pool", lambda e: e.tensor_tensor(
                                out=xdt[sl][:].rearrange("p (a b) -> p a b", b=64),
                                in0=x_tok[sl][:, pr, :].rearrange("p (a b) -> p a b", b=64),
                                in1=bc(dt_tok[:, pr, hs], 64), op=ALU.mult), [x_tok[sl].k, dt_tok.k], [xdt[sl].k])
                            cx.op("pool", lambda e: e.tensor_tensor(
                                out=xw[sl][:].rearrange("p (a b) -> p a b", b=64),
                                in0=xdt[sl][:].rearrange("p (a b) -> p a b", b=64),
                                in1=bc(we_tok[:, pr, hs], 64), op=ALU.mult), [xdt[sl].k, we_tok.k], [xw[sl].k])
                            psy[g] = self.next_ps(hold=True)
                            for h8 in range(8):
                                cx.op("pe", lambda e: e.matmul(
                                    psy[g][:, h8 * 64:(h8 + 1) * 64], Mb[sl][:, h8, :], xdt[sl][:, h8 * 64:(h8 + 1) * 64],
                                    start=(h8 == 0), stop=True, skip_group_check=True), [Mb[sl].k, xdt[sl].k], [psy[g].k])
                            psi[g] = self.next_ps(hold=True)
                        for c2 in range(2):
                            rows = slice(c2 * 64, (c2 + 1) * 64)
                            tcs = slice(pr * 128 + c2 * 64, pr * 128 + (c2 + 1) * 64)
                            for g in gs:
                                kw = {"tile_position": (0, 64)} if c2 == 1 else {}
                                cx.op("pe", lambda e: e.matmul(psi[g][rows, :], CT[:, g, tcs], Sbf[g][:], start=True, stop=True,
                                                               skip_group_check=True, **kw), [CT.k, Sbf[g].k], [psi[g].k])
                            for g in gs:
                                sl = g % GI
                                hs = slice(g * 8, (g + 1) * 8)
                                ps = self.next_ps()
                                cx.op("pe", lambda e: e.matmul(ps[:], btok[sl][rows, :], xw[sl][rows, :], start=True, stop=True),
                                      [btok[sl].k, xw[sl].k], [ps.k])
                                cx.op("dve", lambda e: e.tensor_tensor(
                                    out=S32[g][:].rearrange("p (a b) -> p a b", b=64),
                                    in0=S32[g][:].rearrange("p (a b) -> p a b", b=64),
                                    in1=bc(Dc[:, pr, c2, hs], 64), op=ALU.mult), [S32[g].k, Dc.k], [S32[g].k])
                                cx.op("dve", lambda e: e.tensor_tensor(out=S32[g][:], in0=ps[:], in1=S32[g][:], op=ALU.add),
                                      [ps.k, S32[g].k], [S32[g].k])
                                cx.op("act", lambda e: e.copy(out=Sbf[g][:], in_=S32[g][:]), [S32[g].k], [Sbf[g].k])
                        for g in gs:
                            sl = g % GI
                            hs = slice(g * 8, (g + 1) * 8)
                            y = ytmp[sl]
                            tt = View(rel[sl].t[:, 0:4, :].rearrange("p a b -> p (a b)"), rel[sl].k)
                            cx.op("dve", lambda e: e.tensor_tensor(
                                out=y[:].rearrange("p (a b) -> p a b", b=64),
                                in0=psi[g][:].rearrange("p (a b) -> p a b", b=64),
                                in1=bc(eb_tok[:, pr, hs], 64), op=ALU.mult), [psi[g].k, eb_tok.k], [y.k])
                            cx.op("dve", lambda e: e.tensor_tensor(out=y[:], in0=psy[g][:], in1=y[:], op=ALU.add),
                                  [psy[g].k, y.k], [y.k])
                            self.release(psy[g])
                            self.release(psi[g])
                            cx.op("pool", lambda e: e.tensor_tensor(
                                out=tt[:].rearrange("p (a b) -> p a b", b=64),
                                in0=x_tok[sl][:, pr, :].rearrange("p (a b) -> p a b", b=64),
                                in1=bc(ddr[:, hs], 64), op=ALU.mult), [x_tok[sl].k, ddr.k], [tt.k])
                            cx.op("pool", lambda e: e.tensor_tensor(out=y[:], in0=y[:], in1=tt[:], op=ALU.add), [y.k, tt.k], [y.k])
                            cx.op("pool", lambda e: e.tensor_tensor(out=y[:], in0=y[:], in1=zg[sl][:, pr, :], op=ALU.mult),
                                  [y.k, zg[sl].k], [y.k])
                            cx.op("pool", lambda e: e.tensor_tensor(out=tt[:], in0=y[:], in1=y[:], op=ALU.mult), [y.k], [tt.k])
                            cx.op("dve", lambda e: e.reduce_sum(out=ss[sl][:], in_=tt[:], axis=mybir.AxisListType.X),
                                  [tt.k], [ss[sl].k])
                            cx.op("act", lambda e: e.activation(out=ss[sl][:], in_=ss[sl][:], func=AF.Sqrt, bias=EPS,
                                                                scale=1.0 / 512.0), [ss[sl].k], [ss[sl].k])
                            cx.op("dve", lambda e: e.reciprocal(out=ss[sl][:], in_=ss[sl][:]), [ss[sl].k], [ss[sl].k])
                            cx.op("dve", lambda e: e.tensor_scalar(out=yn[sl][:], in0=y[:], scalar1=ss[sl][:, 0:1], scalar2=None,
                                                                   op0=ALU.mult), [y.k, ss[sl].k], [yn[sl].k])
                            ps = self.next_ps()
                            pv = ps[:, 0:256].bitcast(BF16).rearrange("p (a b) -> p a b", b=128)
                            for a in range(4):
                                cx.op("pe", lambda e: e.transpose(pv[:, a, :], yn[sl][:, a * 128:(a + 1) * 128],
                                                                  self.ident_b[:]), [yn[sl].k, self.ident_b.k], [ps.k])
                            for a in range(4):
                                cx.op("act", lambda e: e.activation(
                                    out=yT[:, g * 4 + a, tsl], in_=pv[:, a, :], func=AF.Copy,
                                    scale=self.col("ssm_norm_%d" % j, g * 4 + a)), [ps.k, self.cols_b.k], [yT.k])
                cx.barrier(end=False)
                if SSM_STAGE < 5:
                    continue
                self.load_h(h2, it)

                def epi_out(blk, bw, ps):
                    cx.op("dve", lambda e: e.tensor_tensor(out=h2[:, blk, :], in0=ps[:], in1=h2[:, blk, :], op=ALU.add),
                          [ps.k, h2.k], [h2.k])
                self.linear_fm(yT, 32, w_out, 0, D, epi_out)
                self.store_tile(h2, self.hT, 0, KC, it, self.hT_k[it])
                cx.barrier(end=False)
            self.h_stored()
            cx.barrier()


def _weight(inputs, nm):
    base, idx = nm.rsplit("_", 1)
    table = {
        "w_up": inputs["w_up"], "w_down": inputs["w_down"],
        "w_ple_proj": inputs["w_ple_proj"], "w_ple_gate": inputs["w_ple_gate"],
        "gla_w_in": inputs["gla_w_in"], "gla_w_out": inputs["gla_w_out"],
        "hgrn_w_in": inputs["hgrn_w_in"], "hgrn_w_out": inputs["hgrn_w_out"],
        "ssm_w_in": inputs["ssm_w_in"], "ssm_w_out": inputs["ssm_w_out"],
    }
    return table[base][int(idx)]


def make_consts():
    c = np.zeros((128, 4, 128), np.float32)
    s = np.arange(128)[:, None]
    t = np.arange(128)[None, :]
    c[:, 0, :] = ((s // 64 == t // 64) & (s <= t)).astype(np.float32)
    c[:, 1, :] = (((s == 63) & (t < 64)) | ((s == 127) & (t >= 64))).astype(np.float32)
    c[:, 2, :] = (s == 63).astype(np.float32) * np.ones_like(t)
    c[:, 3, :] = (s == 127).astype(np.float32) * np.ones_like(t)
    return c


def run(inputs, depth, T, enable_mix=True, trace=False):
    x = np.asarray(inputs["x"])
    p = np.asarray(inputs["p"])
    B, L, _ = x.shape
    segs = NCORES // B
    assert L == segs * T
    prog = Prog(T, depth, enable_mix)
    nc = prog.build()
    lay, ncol, _ = col_layout(depth)
    cols = np.zeros((128, ncol), np.float32)

    def put(nm, v):
        off, n = lay[nm]
        cols[:, off:off + n] = to_cols(v)
    for i in range(depth):
        put("norm_mix_%d" % i, inputs["norm_mix"][i])
        put("norm_mlp_%d" % i, inputs["norm_mlp"][i])
        put("norm_ple_%d" % i, inputs["norm_ple"][i])
        kind, j = kind_of(i)
        if kind == 0:
            put("gla_b_gk_%d" % j, inputs["gla_b_gk"][j])
            put("gla_gn_%d" % j, inputs["gla_gn"][j])
        elif kind == 1:
            put("hgrn_gn_%d" % j, inputs["hgrn_gn"][j])
            for l in range(depth):
                put("hgrn_lb_%d" % l, inputs["hgrn_lb_logits"][l])
        else:
            for t in range(4):
                put("ssm_conv_w_%d_%d" % (j, t), inputs["ssm_conv_w"][j][t])
            put("ssm_conv_b_%d" % j, inputs["ssm_conv_b"][j])
            put("ssm_norm_%d" % j, inputs["ssm_norm"][j])
    put("norm_final", inputs["norm_final"])
    consts = make_consts()
    shared = {"cols": cols, "consts": consts}
    for i in range(depth):
        kind, j = kind_of(i)
        if kind == 0:
            shared["gla_w_gk2_%d" % j] = np.ascontiguousarray(inputs["gla_w_gk2"][j], np.float32)
        if kind == 2:
            for nm in ("ssm_dt_bias", "ssm_a_log", "ssm_d"):
                shared["%s_%d" % (nm, j)] = np.ascontiguousarray(
                    np.asarray(inputs[nm][j], np.float32).reshape(1, 64))
    in_maps = []
    for c in range(NCORES):
        b, s = c // segs, c % segs
        m = dict(shared)
        m["xT"] = np.ascontiguousarray(x[b, s * T:(s + 1) * T, :].T)
        m["pT"] = np.ascontiguousarray(np.transpose(p[:depth, b, s * T:(s + 1) * T, :], (0, 2, 1)))
        cm = np.zeros((128, 16), np.float32)
        for r in range(NCORES):
            rb, rs = r // segs, r % segs
            if rb == b and rs < s:
                cm[:, r] = 1.0
            if rb == b and rs == s - 1:
                cm[:, 8 + r] = 1.0
        m["cmask"] = cm
        for nm, K, N in big_weights(depth):
            W = _weight(inputs, nm)
            if USE_CC:
                r = K // NCORES
                m[nm] = np.ascontiguousarray(W[c * r:(c + 1) * r, :], np.float32)
            else:
                m[nm] = np.ascontiguousarray(W, np.float32)
        in_maps.append(m)
    res = run_bass_kernel_spmd(nc, in_maps, core_ids=list(range(NCORES)), trace=trace)
    out = np.empty((B, L, D), np.float32)
    for c in range(NCORES):
        b, s = c // segs, c % segs
        out[b, s * T:(s + 1) * T, :] = res.results[c]["outT"].T
    if DEBUG:
        return out, res
    if trace:
        return out, res
    return out


def kernel(**inputs):
    depth = int(np.asarray(inputs["p"]).shape[0])
    B, L, _ = np.asarray(inputs["x"]).shape
    return run(inputs, depth, L * B // NCORES)
```

```python
import numpy as np
from contextlib import ExitStack
import concourse.bass as bass
import concourse.mybir as mybir
from concourse.bass_utils import run_bass_kernel_spmd

F32 = mybir.dt.float32
BF16 = mybir.dt.bfloat16
ALU = mybir.AluOpType
AF = mybir.ActivationFunctionType

NCORES = 2
USE_CC = False
D = 2048
KC = D // 128
TT = 512
EPS = 1e-6
PLE = 256
DFF = 8192
N_MIX = 3
DEBUG = False
SSM_STAGE = 99
KINDS = None


def kind_of(i):
    if KINDS is None:
        return i % N_MIX, i // N_MIX
    k = KINDS[i]
    return k, sum(1 for x in KINDS[:i] if x == k)


class Trk:
    __slots__ = ("name", "w", "r", "dsem", "dcnt", "excl")

    def __init__(self, name="", excl=False):
        self.name = name
        self.excl = excl
        self.w = None
        self.r = {}
        self.dsem = None
        self.dcnt = 0


class Ctx:
    def __init__(self, nc, es):
        self.nc = nc
        self.es = es
        self.sems = {}
        self.engs = {}
        for nm, h in (("pe", nc.tensor), ("act", nc.scalar), ("dve", nc.vector),
                      ("pool", nc.gpsimd), ("sp", nc.sync)):
            self.sems[nm] = es.enter_context(nc.semaphore("sem_" + nm))
            self.engs[nm] = {"h": h, "cnt": 0, "known": {}}
        self.ndsem = 0
        self.phase_trks = []
        self.phase_evs = {}
        self.free_dsems = []
        self.uid = 0

    def name(self, p):
        self.uid += 1
        return "%s_%d" % (p, self.uid)

    def _dsem(self, t):
        if t.dsem is None:
            if self.free_dsems:
                t.dsem, t.dcnt = self.free_dsems.pop()
            else:
                t.dsem = "d%d" % self.ndsem
                self.ndsem += 1
                self.sems[t.dsem] = self.es.enter_context(self.nc.semaphore("dsem_%s" % t.dsem))
        return t.dsem

    def _wait(self, eng, k, v):
        e = self.engs[eng]
        if k == eng and v > e["cnt"]:
            return
        if v > 0 and e["known"].get(k, 0) < v:
            e["h"].wait_ge(self.sems[k], v)
            e["known"][k] = v

    def _waits(self, eng, reads, writes):
        need = {}
        for t in reads:
            if t.w is not None:
                k, v = t.w
                if need.get(k, 0) < v:
                    need[k] = v
            if t.excl:
                for k, v in t.r.items():
                    if k != eng and need.get(k, 0) < v:
                        need[k] = v
        for t in writes:
            if t.w is not None:
                k, v = t.w
                if need.get(k, 0) < v:
                    need[k] = v
            for k, v in t.r.items():
                if need.get(k, 0) < v:
                    need[k] = v
        for k, v in need.items():
            self._wait(eng, k, v)

    def op(self, eng, fn, reads=(), writes=(), inc=True):
        self._waits(eng, reads, writes)
        e = self.engs[eng]
        ins = fn(e["h"])
        if inc:
            e["cnt"] += 1
            ins.then_inc(self.sems[eng], 1)
            c = e["cnt"]
        else:
            c = e["cnt"] + 1
        for t in reads:
            if t.r.get(eng, 0) < c:
                t.r[eng] = c
        for t in writes:
            t.w = (eng, c)
            t.r = {}
        return ins

    def dma(self, q, out_ap, in_ap, out_t, in_t, **kw):
        reads = [in_t] if in_t is not None else []
        k = self._dsem(out_t)
        saved = out_t.w
        if saved is not None and saved[0] == k:
            out_t.w = None
        self._waits(q, reads, [out_t])
        out_t.w = saved
        e = self.engs[q]
        ins = e["h"].dma_start(out=out_ap, in_=in_ap, **kw)
        out_t.dcnt += 16
        ins.then_inc(self.sems[k], 16)
        if q != "sp" or in_t is not None:
            self.phase_evs[k] = out_t.dcnt
        if in_t is not None and in_t.r.get(k, 0) < out_t.dcnt:
            in_t.r[k] = out_t.dcnt
        out_t.w = (k, out_t.dcnt)
        out_t.r = {}
        return ins

    def allgather(self, out_ap, in_ap, out_t, in_t, groups):
        q = "pool"
        self._waits(q, [in_t], [out_t])
        e = self.engs[q]
        k = self._dsem(out_t)
        ins = e["h"].collective_compute("AllGather", ALU.bypass, replica_groups=groups,
                                        ins=[in_ap], outs=[out_ap])
        out_t.dcnt += 1
        ins.then_inc(self.sems[k], 1)
        if in_t.r.get(k, 0) < out_t.dcnt:
            in_t.r[k] = out_t.dcnt
        out_t.w = (k, out_t.dcnt)
        out_t.r = {}
        return ins

    def finish(self, eng, trks):
        self._waits(eng, trks, [])

    def barrier(self, extra=(), end=True):
        evs = {}
        for nm in ("pe", "act", "dve", "pool"):
            evs[nm] = self.engs[nm]["cnt"]
        for t in list(self.phase_trks) + list(extra):
            if t.dsem is not None and t.dcnt > 0:
                evs[t.dsem] = t.dcnt
        evs.update(self.phase_evs)
        self.phase_evs = {}
        for nm in ("pe", "act", "dve", "pool"):
            for k, v in evs.items():
                self._wait(nm, k, v)
        if end:
            for t in self.phase_trks:
                if t.dsem is not None:
                    self.free_dsems.append((t.dsem, t.dcnt))
                    t.dsem = None
            self.phase_trks = []


class Buf:
    def __init__(self, cx, es, name, shape, dt, psum=False, phase=True):
        nm = cx.name(name)
        if psum:
            self.t = es.enter_context(cx.nc.psum_tensor(nm, list(shape), dt))
        else:
            self.t = es.enter_context(cx.nc.sbuf_tensor(nm, list(shape), dt))
        self.k = Trk(nm, excl=psum)
        if phase:
            cx.phase_trks.append(self.k)

    def __getitem__(self, key):
        return self.t[key]


def dram(nc, cx, name, shape, dt, kind=None):
    if kind is None:
        t = nc.dram_tensor(name, list(shape), dt)
    else:
        t = nc.dram_tensor(name, list(shape), dt, kind=kind)
    return t


GLA_IN = 6160
HGRN_IN = 8192
SSM_IN = 10304


def big_weights(depth):
    out = []
    for i in range(depth):
        kind, j = kind_of(i)
        if kind == 0:
            out.append(("gla_w_in_%d" % j, D, GLA_IN))
            out.append(("gla_w_out_%d" % j, D, D))
        elif kind == 1:
            out.append(("hgrn_w_in_%d" % j, D, HGRN_IN))
            out.append(("hgrn_w_out_%d" % j, D, D))
        else:
            out.append(("ssm_w_in_%d" % j, D, SSM_IN))
            out.append(("ssm_w_out_%d" % j, 2 * D, D))
        out.append(("w_up_%d" % i, D, DFF))
        out.append(("w_down_%d" % i, DFF, D))
        out.append(("w_ple_gate_%d" % i, D, D))
        out.append(("w_ple_proj_%d" % i, PLE, D))
    return out


def col_layout(depth):
    items = []
    for i in range(depth):
        items += [("norm_mix_%d" % i, KC), ("norm_mlp_%d" % i, KC), ("norm_ple_%d" % i, KC)]
    items.append(("norm_final", KC))
    n_norm = sum(n for _, n in items)
    for i in range(depth):
        kind, j = kind_of(i)
        if kind == 0:
            items += [("gla_b_gk_%d" % j, 8), ("gla_gn_%d" % j, 4)]
        elif kind == 1:
            items += [("hgrn_gn_%d" % j, 1)]
            for l in range(depth):
                items.append(("hgrn_lb_%d" % l, KC))
        else:
            for t in range(4):
                items.append(("ssm_conv_w_%d_%d" % (j, t), 48))
            items += [("ssm_conv_b_%d" % j, 48), ("ssm_norm_%d" % j, 32)]
    lay = {}
    off = 0
    for nm, n in items:
        if nm not in lay:
            lay[nm] = (off, n)
            off += n
    return lay, off, n_norm


def to_cols(v):
    v = np.asarray(v, np.float32).reshape(-1, 128)
    return np.ascontiguousarray(v.T)


class Prog:
    def __init__(self, T, depth, enable_mix=True):
        self.T = T
        self.depth = depth
        self.NT = T // TT
        self.enable_mix = enable_mix
        self.nc = bass.Bass("TRN2", target_bir_lowering=False)
        self.lay, self.ncol, self.n_norm = col_layout(depth)
        self.dbg = {}
        self.debug = DEBUG

    def declare(self):
        nc, T, depth = self.nc, self.T, self.depth
        self.xT = nc.dram_tensor("xT", [D, T], F32, kind="ExternalInput")
        self.pT = nc.dram_tensor("pT", [depth, PLE, T], F32, kind="ExternalInput")
        self.cols_d = nc.dram_tensor("cols", [128, self.ncol], F32, kind="ExternalInput")
        self.consts_d = nc.dram_tensor("consts", [128, 4, 128], F32, kind="ExternalInput")
        self.cmask_d = nc.dram_tensor("cmask", [128, 16], F32, kind="ExternalInput")
        self.outT = nc.dram_tensor("outT", [D, T], F32, kind="ExternalOutput")
        self.wsh, self.wshb, self.wg, self.wg_k = {}, {}, {}, {}
        for nm, K, N in big_weights(depth):
            rows = K // NCORES if USE_CC else K
            self.wsh[nm] = nc.dram_tensor(nm, [rows, N], F32, kind="ExternalInput")
            if USE_CC:
                self.wshb[nm] = nc.dram_tensor(nm + "_sb", [rows, N], BF16)
            self.wg[nm] = nc.dram_tensor(nm + "_g", [K, N], BF16)
            self.wg_k[nm] = Trk(nm + "_g")
        self.wg_need = {}
        self.hT = nc.dram_tensor("hT_scr", [D, T], F32)
        self.hT_k = [Trk("hT%d" % i) for i in range(self.NT)]
        self.out_k = [Trk("out%d" % i) for i in range(self.NT)]
        self.oloc = nc.dram_tensor("oloc_scr", [2 * D, T], F32)
        self.oloc_k = [[Trk("oloc%d_%d" % (i, b)) for b in range(2)] for i in range(self.NT)]
        self.uscr = nc.dram_tensor("u_scr", [D, T], BF16)
        self.u_k = [Trk("u%d" % i) for i in range(self.NT)]
        self.qp = nc.dram_tensor("qp_scr", [D, T], BF16)
        self.qp_k = [Trk("qp%d" % i) for i in range(self.NT)]
        self.rows_d = {}
        self.small_d = {}
        for i in range(depth):
            kind, j = kind_of(i)
            if kind == 0:
                self.small_d["gla_w_gk2_%d" % j] = nc.dram_tensor(
                    "gla_w_gk2_%d" % j, [16, 1024], F32, kind="ExternalInput")
            if kind == 2:
                for nm in ("ssm_dt_bias", "ssm_a_log", "ssm_d"):
                    self.rows_d["%s_%d" % (nm, j)] = nc.dram_tensor(
                        "%s_%d" % (nm, j), [1, 64], F32, kind="ExternalInput")

    def dump_sb(self, name, buf, shape, dt=F32):
        if not getattr(self, "debug", False) or name in self.dbg:
            return
        d = self.nc.dram_tensor("dbg_" + name, list(shape), dt, kind="ExternalOutput")
        k = Trk(name)
        self.dbg[name] = k
        self.cx.dma("pool", d.ap(), buf[:], k, buf.k)

    def dump_dram(self, name, dt_, shape, trks, dt=F32):
        if not getattr(self, "debug", False) or name in self.dbg:
            return
        d = self.nc.dram_tensor("dbg_" + name, list(shape), dt, kind="ExternalOutput")
        k = Trk(name)
        self.dbg[name] = k
        self.cx._waits("pool", trks, [])
        self.cx.dma("pool", d.ap(), dt_.ap(), k, None)

    def col(self, name, i=0, n=1):
        off, cnt = self.lay[name]
        return self.cols[:, off + i:off + i + n]

    def build(self):
        nc = self.nc
        self.declare()
        with ExitStack() as es:
            cx = self.cx = Ctx(nc, es)
            self.cols_b = Buf(cx, es, "cols", [128, self.ncol], F32, phase=False)
            self.cols = self.cols_b.t
            self.consts_b = Buf(cx, es, "consts", [128, 4, 128], F32, phase=False)
            self.cmask_b = Buf(cx, es, "cmask", [128, 16], F32, phase=False)
            self.ones_b = Buf(cx, es, "ones", [128, 128], BF16, phase=False)
            self.ident_b = Buf(cx, es, "ident", [128, 128], BF16, phase=False)
            self.cmsk_b = Buf(cx, es, "chunkmask", [128, TT], F32, phase=False)
            self.NSLAB = 2
            self.slabs = [Buf(cx, es, "slab%d" % i, [128, 16, 512], BF16, phase=False)
                          for i in range(self.NSLAB)]
            self.slab_i = 0
            self.psum = [Buf(cx, es, "ps%d" % i, [128, 512], F32, psum=True, phase=False)
                         for i in range(8)]
            self.ps_i = 0
            self.held = set()
            self.rr = 0
            cx.dma("pool", self.cols[:], self.cols_d.ap(), self.cols_b.k, None)
            cx.dma("pool", self.consts_b[:], self.consts_d.ap(), self.consts_b.k, None)
            cx.dma("pool", self.cmask_b[:], self.cmask_d.ap(), self.cmask_b.k, None)
            cx.op("pool", lambda e: e.memset(self.ones_b[:], 1.0), [], [self.ones_b.k])
            cx.op("pool", lambda e: e.memset(self.ident_b[:], 0.0), [], [self.ident_b.k])
            cx.op("pool", lambda e: e.affine_select(
                out=self.ident_b[:], in_=self.ident_b[:], pattern=[[-1, 128]],
                compare_op=ALU.not_equal, fill=1.0, base=0, channel_multiplier=1),
                [self.ident_b.k], [self.ident_b.k])
            cx.op("pool", lambda e: e.memset(self.cmsk_b[:], 1.0), [], [self.cmsk_b.k])
            cx.op("pool", lambda e: e.memset(self.cmsk_b[:, 0:TT:64], 0.0), [], [self.cmsk_b.k])

            self.phase_weights()
            self.hsrc, self.hsrc_k = self.xT, [None] * self.NT
            for i in range(self.depth):
                kind, j = kind_of(i)
                if self.enable_mix:
                    if kind == 0:
                        self.phase_lin_mixer(i, j, "gla")
                    elif kind == 1:
                        self.phase_lin_mixer(i, j, "hgrn")
                    else:
                        self.phase_ssm(i, j)
                self.phase_mlp(i)
                self.phase_ple(i)
            self.phase_out()
            for q in ("pool", "sp", "act", "dve", "pe"):
                cx.finish(q, self.out_k + list(self.dbg.values()))
        return nc

    def next_ps(self, hold=False):
        for _ in range(16):
            b = self.psum[self.ps_i % 8]
            self.ps_i += 1
            if id(b) not in self.held:
                if hold:
                    self.held.add(id(b))
                return b
        raise RuntimeError("all PSUM banks held")

    def release(self, b):
        self.held.discard(id(b))

    def ew(self):
        self.rr += 1
        return "dve" if self.rr % 2 else "pool"

    def phase_weights(self):
        cx = self.cx
        PIECE = 4096
        with ExitStack() as es:
            NB = 3
            wdst = [Trk("wdst%d" % i) for i in range(NB)]
            fin = [Buf(cx, es, "wc_in%d" % i, [128, PIECE], F32) for i in range(NB)]
            fout = [Buf(cx, es, "wc_out%d" % i, [128, PIECE], BF16) for i in range(NB)]
            n = 0
            for nm, K, N in big_weights(self.depth):
                rows = K // NCORES if USE_CC else K
                tot = rows * N
                per = tot // 128
                assert per * 128 == tot
                src = self.wsh[nm].ap().rearrange("r n -> (r n)").rearrange("(p f) -> p f", p=128)
                dstt = self.wshb[nm] if USE_CC else self.wg[nm]
                dst = dstt.ap().rearrange("r n -> (r n)").rearrange("(p f) -> p f", p=128)
                shk = Trk(nm + "_sb") if USE_CC else None
                off = 0
                while off < per:
                    w = min(PIECE, per - off)
                    a, b = fin[n % NB], fout[n % NB]
                    cx.dma("sp", a[:, 0:w], src[:, off:off + w], a.k, None)
                    eng = ("act", "dve", "act", "pool", "act", "dve")[n % 6]
                    if eng == "act":
                        cx.op("act", lambda e: e.copy(out=b[:, 0:w], in_=a[:, 0:w]), [a.k], [b.k])
                    else:
                        cx.op(eng, lambda e: e.tensor_copy(out=b[:, 0:w], in_=a[:, 0:w]), [a.k], [b.k])
                    cx.dma("pool", dst[:, off:off + w], b[:, 0:w], shk if USE_CC else wdst[n % NB], b.k)
                    off += w
                    n += 1
                if USE_CC:
                    cx.allgather(self.wg[nm].ap(), self.wshb[nm].ap(), self.wg_k[nm], shk,
                                 [list(range(NCORES))])
                else:
                    self.wg_need[nm] = [(t.dsem, t.dcnt) for t in wdst if t.dsem is not None]
            cx.barrier()

    def phase_copy_in(self):
        cx = self.cx
        for it in range(self.NT):
            cx.dma("pool", self.hT.ap()[:, it * TT:(it + 1) * TT],
                   self.xT.ap()[:, it * TT:(it + 1) * TT], self.hT_k[it], None)

    def load_h(self, buf, it):
        self.load_tile(buf, self.hsrc, 0, KC, it, self.hsrc_k[it])

    def h_stored(self):
        self.hsrc, self.hsrc_k = self.hT, self.hT_k

    def load_tile(self, buf, dram_t, row0, nblk, it, trk, blk0=0):
        cx = self.cx
        src = dram_t.ap()[row0:row0 + nblk * 128, it * TT:(it + 1) * TT].rearrange(
            "(kc p) t -> p kc t", p=128)
        step = 4
        for q in range(0, nblk, step):
            n = min(step, nblk - q)
            cx.dma("pool", buf[:, blk0 + q:blk0 + q + n, :], src[:, q:q + n, :], buf.k, trk)

    def store_tile(self, buf, dram_t, row0, nblk, it, trk, blk0=0):
        cx = self.cx
        dst = dram_t.ap()[row0:row0 + nblk * 128, it * TT:(it + 1) * TT].rearrange(
            "(kc p) t -> p kc t", p=128)
        step = 4
        for q in range(0, nblk, step):
            n = min(step, nblk - q)
            cx.dma("pool", dst[:, q:q + n, :], buf[:, blk0 + q:blk0 + q + n, :], trk, buf.k)

    def rstd(self, src, blk0, nblk, n_feat, sqs, rs):
        cx = self.cx
        ps = self.next_ps()
        for b in range(nblk):
            sq = sqs[b % len(sqs)]
            cx.op("act", lambda e: e.activation(out=sq[:], in_=src[:, blk0 + b, :], func=AF.Square),
                  [src.k], [sq.k])
            cx.op("pe", lambda e: e.matmul(ps[:], self.ones_b[:], sq[:], start=(b == 0),
                                           stop=(b == nblk - 1)), [self.ones_b.k, sq.k], [ps.k])
        cx.op("act", lambda e: e.activation(out=rs[:], in_=ps[:], func=AF.Sqrt, bias=EPS,
                                            scale=1.0 / n_feat), [ps.k], [rs.k])
        cx.op("dve", lambda e: e.reciprocal(out=rs[:], in_=rs[:]), [rs.k], [rs.k])

    def norm_u(self, h, gain, u, sqs, rs):
        cx = self.cx
        self.rstd(h, 0, KC, D, sqs, rs)
        for kc in range(KC):
            cx.op("dve", lambda e: e.scalar_tensor_tensor(
                out=u[:, kc, :], in0=h[:, kc, :], scalar=self.col(gain, kc), in1=rs[:],
                op0=ALU.mult, op1=ALU.mult), [h.k, rs.k, self.cols_b.k], [u.k])

    def load_slab(self, wname, k0, nk, c0, w):
        cx = self.cx
        slab = self.slabs[self.slab_i % self.NSLAB]
        self.slab_i += 1
        src = self.wg[wname].ap()[k0 * 128:(k0 + nk) * 128, c0:c0 + w].rearrange(
            "(kc p) n -> p kc n", p=128)
        step = 4
        if not USE_CC:
            for k, v in self.wg_need[wname]:
                cx._wait("sp", k, v)
        for q in range(0, nk, step):
            n = min(step, nk - q)
            cx.dma("sp", slab[:, q:q + n, 0:w], src[:, q:q + n, :], slab.k, self.wg_k[wname] if USE_CC else None)
        return slab

    def linear_fm(self, src, nkc, wname, col0, ncols, epi, src_blk0=0):
        cx = self.cx
        ng = (ncols + 511) // 512
        nks = (nkc + 15) // 16
        for g in range(ng):
            gw = min(512, ncols - g * 512)
            nnb = (gw + 127) // 128
            pss = [self.next_ps(hold=True) for _ in range(nnb)]
            for ks in range(nks):
                nk = min(16, nkc - ks * 16)
                slab = self.load_slab(wname, ks * 16, nk, col0 + g * 512, gw)
                for nb in range(nnb):
                    bw = min(128, gw - nb * 128)
                    for kc in range(nk):
                        kk = ks * 16 + kc
                        cx.op("pe", lambda e: e.matmul(
                            pss[nb][0:bw, :], slab[:, kc, nb * 128:nb * 128 + bw],
                            src[:, src_blk0 + kk, :], start=(kk == 0), stop=(kk == nkc - 1)),
                            [slab.k, src.k], [pss[nb].k], inc=(kc == nk - 1))
            for nb in range(nnb):
                bw = min(128, gw - nb * 128)
                epi(g * 4 + nb, bw, pss[nb])
                self.release(pss[nb])

    def linear_tm(self, src, wname, col0, gw, epi):
        cx = self.cx
        slab = self.load_slab(wname, 0, KC, col0, gw)
        for ts in range(TT // 128):
            ps = self.next_ps()
            for kc in range(KC):
                cx.op("pe", lambda e: e.matmul(
                    ps[:, 0:gw], src[:, kc, ts * 128:(ts + 1) * 128], slab[:, kc, 0:gw],
                    start=(kc == 0), stop=(kc == KC - 1)), [slab.k, src.k], [ps.k], inc=(kc == KC - 1))
            epi(ts, ps)

    def phase_mlp(self, i):
        cx = self.cx
        with ExitStack() as es:
            hs = [Buf(cx, es, "h", [128, KC, TT], F32) for _ in range(2)]
            u = Buf(cx, es, "u", [128, KC, TT], BF16)
            hid = Buf(cx, es, "hid", [128, DFF // 128, TT], BF16)
            sqs = [Buf(cx, es, "sq", [128, TT], BF16) for _ in range(2)]
            rs = Buf(cx, es, "rs", [128, TT], F32)
            tmps = [Buf(cx, es, "tmp", [128, TT], F32) for _ in range(3)]
            cnt = [0]
            self.load_h(hs[0], 0)
            self.norm_u(hs[0], "norm_mlp_%d" % i, u, sqs, rs)
            for it in range(self.NT):
                h = hs[it % 2]

                def epi_up(blk, bw, ps):
                    t = tmps[cnt[0] % 3]
                    cnt[0] += 1
                    cx.op("act", lambda e: e.activation(out=t[:], in_=ps[:], func=AF.Relu), [ps.k], [t.k])
                    cx.op(self.ew(), lambda e: e.tensor_tensor(out=hid[:, blk, :], in0=t[:], in1=t[:],
                                                               op=ALU.mult), [t.k], [hid.k])
                self.linear_fm(u, KC, "w_up_%d" % i, 0, DFF, epi_up)
                if it + 1 < self.NT:
                    self.load_h(hs[(it + 1) % 2], it + 1)
                    self.norm_u(hs[(it + 1) % 2], "norm_mlp_%d" % i, u, sqs, rs)

                def epi_down(blk, bw, ps):
                    cx.op("dve", lambda e: e.tensor_tensor(out=h[:, blk, :], in0=ps[:], in1=h[:, blk, :],
                                                           op=ALU.add), [ps.k, h.k], [h.k])
                self.linear_fm(hid, DFF // 128, "w_down_%d" % i, 0, D, epi_down)
                self.store_tile(h, self.hT, 0, KC, it, self.hT_k[it])
            self.h_stored()
            cx.barrier()

    def phase_ple(self, i):
        cx = self.cx
        with ExitStack() as es:
            hs = [Buf(cx, es, "h", [128, KC, TT], F32) for _ in range(2)]
            us = [Buf(cx, es, "u", [128, KC, TT], BF16) for _ in range(2)]
            pp = Buf(cx, es, "pp", [128, KC, TT], F32)
            pf = Buf(cx, es, "pf", [128, 2, TT], F32)
            pbs = [Buf(cx, es, "pb", [128, 2, TT], BF16) for _ in range(2)]
            sqs = [Buf(cx, es, "sq", [128, TT], BF16) for _ in range(2)]
            rs = Buf(cx, es, "rs", [128, TT], F32)
            tmps = [Buf(cx, es, "tmp", [128, TT], F32) for _ in range(3)]
            cnt = [0]

            def prefetch(it):
                self.load_h(hs[it % 2], it)
                src = self.pT.ap()[i, :, it * TT:(it + 1) * TT].rearrange("(kc p) t -> p kc t", p=128)
                cx.dma("pool", pf[:], src, pf.k, None)
                cx.op("dve", lambda e: e.tensor_copy(out=pbs[it % 2][:], in_=pf[:]), [pf.k], [pbs[it % 2].k])
                self.norm_u(hs[it % 2], "norm_ple_%d" % i, us[it % 2], sqs, rs)
            prefetch(0)
            for it in range(self.NT):
                h, u, pb = hs[it % 2], us[it % 2], pbs[it % 2]

                def epi_pp(blk, bw, ps):
                    cx.op("act", lambda e: e.copy(out=pp[:, blk, :], in_=ps[:]), [ps.k], [pp.k])
                self.linear_fm(pb, 2, "w_ple_proj_%d" % i, 0, D, epi_pp)
                if it + 1 < self.NT:
                    prefetch(it + 1)

                def epi_gate(blk, bw, ps):
                    t = tmps[cnt[0] % 3]
                    cnt[0] += 1
                    cx.op("act", lambda e: e.activation(out=t[:], in_=ps[:], func=AF.Sigmoid), [ps.k], [t.k])
                    cx.op("pool", lambda e: e.tensor_tensor(out=t[:], in0=t[:], in1=pp[:, blk, :],
                                                            op=ALU.mult), [t.k, pp.k], [t.k])
                    cx.op("dve", lambda e: e.tensor_tensor(out=h[:, blk, :], in0=t[:], in1=h[:, blk, :],
                                                           op=ALU.add), [t.k, h.k], [h.k])
                self.linear_fm(u, KC, "w_ple_gate_%d" % i, 0, D, epi_gate)
                self.store_tile(h, self.hT, 0, KC, it, self.hT_k[it])
            self.h_stored()
            cx.barrier()

    def phase_out(self):
        cx = self.cx
        with ExitStack() as es:
            h = Buf(cx, es, "h", [128, KC, TT], F32)
            o = Buf(cx, es, "o", [128, KC, TT], F32)
            sqs = [Buf(cx, es, "sq", [128, TT], BF16) for _ in range(2)]
            rs = Buf(cx, es, "rs", [128, TT], F32)
            for it in range(self.NT):
                self.load_h(h, it)
                self.norm_u(h, "norm_final", o, sqs, rs)
                self.store_tile(o, self.outT, 0, KC, it, self.out_k[it])
            cx.barrier(extra=self.out_k)

    def phase_lin_mixer(self, i, j, kind):
        cx, nc = self.cx, self.nc
        if kind == "gla":
            w_in, w_out = "gla_w_in_%d" % j, "gla_w_out_%d" % j
            NG, Hg, dkb, dvb, dv = 2, 2, 2, 4, 512
            qcol = lambda g: g * 512
            kcol = lambda g: 1024 + g * 512
            vcol = lambda g: 2048 + g * 1024
            ogcol = 4096
            qscale = 256 ** -0.5
            gn = "gla_gn_%d" % j
        else:
            w_in, w_out = "hgrn_w_in_%d" % j, "hgrn_w_out_%d" % j
            NG, Hg, dkb, dvb, dv = 4, 4, 1, 1, 128
            qcol = lambda g: g * 512
            kcol = lambda g: 2048 + g * 512
            vcol = lambda g: 4096 + g * 512
            ogcol = 6144
            qscale = 128 ** -0.5
            gn = "hgrn_gn_%d" % j
        QG = 4
        VW = Hg * dv
        NQ = NG * QG
        NS0 = NQ * dv
        NS = NS0 + NQ
        sloc = nc.dram_tensor("sloc_%d" % i, [128, NS], F32)
        sg = nc.dram_tensor("sg_%d" % i, [NCORES * 128, NS], F32)
        sloc_k, sg_k = Trk("sloc"), Trk("sg")
        mask2 = self.consts_b[:, 0, :]
        NP = TT // 128
        seqpar = USE_CC

        with ExitStack() as es:
            S32 = [[Buf(cx, es, "S32", [128, dkb, dv], F32) for _ in range(Hg)] for _ in range(NG)]
            Sbf = [[Buf(cx, es, "Sbf", [128, dkb, dv], BF16) for _ in range(Hg)] for _ in range(NG)]
            eo = [Buf(cx, es, "eo", [128, QG, 9], F32) for _ in range(NG)]
            for g in range(NG):
                for hh in range(Hg):
                    cx.op("pool", lambda e: e.memset(S32[g][hh][:], 0.0), [], [S32[g][hh].k])
                    cx.op("pool", lambda e: e.memset(Sbf[g][hh][:], 0.0), [], [Sbf[g][hh].k])
                cx.op("pool", lambda e: e.memset(eo[g][:], 1.0), [], [eo[g].k])
            if kind == "gla":
                wgk2 = Buf(cx, es, "wgk2", [16, 1024], F32)
                wgk2b = Buf(cx, es, "wgk2b", [16, 1024], BF16)
                negb = Buf(cx, es, "negb", [128, 8], F32)
                glr = Buf(cx, es, "glr", [16, TT], BF16)
                cx.dma("pool", wgk2[:], self.small_d["gla_w_gk2_%d" % j].ap(), wgk2.k, None)
                cx.op("dve", lambda e: e.tensor_copy(out=wgk2b[:], in_=wgk2[:]), [wgk2.k], [wgk2b.k])
                cx.op("dve", lambda e: e.tensor_scalar(out=negb[:], in0=self.col("gla_b_gk_%d" % j, 0, 8),
                                                       scalar1=-1.0, scalar2=None, op0=ALU.mult),
                      [self.cols_b.k], [negb.k])
            else:
                lb = Buf(cx, es, "lb", [128, KC], F32)
                oml = Buf(cx, es, "oml", [128, KC], F32)
                ex = Buf(cx, es, "ex", [128, self.depth, KC], F32)
                mx = Buf(cx, es, "mx", [128, KC], F32)
                sm = Buf(cx, es, "sm", [128, KC], F32)
                lg = lambda l: self.col("hgrn_lb_%d" % l, 0, KC)
                cx.op("dve", lambda e: e.tensor_copy(out=mx[:], in_=lg(0)), [self.cols_b.k], [mx.k])
                for l in range(1, self.depth):
                    cx.op("dve", lambda e: e.tensor_tensor(out=mx[:], in0=mx[:], in1=lg(l), op=ALU.max),
                          [mx.k, self.cols_b.k], [mx.k])
                for l in range(self.depth):
                    cx.op("dve", lambda e: e.tensor_tensor(out=ex[:, l, :], in0=lg(l), in1=mx[:], op=ALU.subtract),
                          [mx.k, self.cols_b.k], [ex.k])
                cx.op("act", lambda e: e.activation(out=ex[:], in_=ex[:], func=AF.Exp), [ex.k], [ex.k])
                cx.op("dve", lambda e: e.tensor_copy(out=sm[:], in_=ex[:, 0, :]), [ex.k], [sm.k])
                cx.op("dve", lambda e: e.memset(lb[:], 0.0), [], [lb.k])
                for l in range(1, self.depth):
                    cx.op("dve", lambda e: e.tensor_tensor(out=sm[:], in0=sm[:], in1=ex[:, l, :], op=ALU.add),
                          [sm.k, ex.k], [sm.k])
                    if l <= i:
                        cx.op("dve", lambda e: e.tensor_tensor(out=lb[:], in0=lb[:], in1=ex[:, l, :], op=ALU.add),
                              [lb.k, ex.k], [lb.k])
                cx.op("dve", lambda e: e.reciprocal(out=sm[:], in_=sm[:]), [sm.k], [sm.k])
                cx.op("dve", lambda e: e.tensor_tensor(out=lb[:], in0=lb[:], in1=sm[:], op=ALU.mult),
                      [lb.k, sm.k], [lb.k])
                cx.op("dve", lambda e: e.tensor_scalar(out=oml[:], in0=lb[:], scalar1=-1.0, scalar2=1.0,
                                                       op0=ALU.mult, op1=ALU.add), [lb.k], [oml.k])
            h = Buf(cx, es, "h", [128, KC, TT], F32)
            u = Buf(cx, es, "u", [128, KC, TT], BF16)
            sqs = [Buf(cx, es, "sq", [128, TT], BF16) for _ in range(2)]
            rs = Buf(cx, es, "rs", [128, TT], F32)
            tmps = [Buf(cx, es, "tmp", [128, TT], F32) for _ in range(6)]
            bb = Buf(cx, es, "bb", [128, QG, TT], F32)
            dl = Buf(cx, es, "dl", [128, QG, 8], F32)
            kt = Buf(cx, es, "kt", [128, QG, TT], BF16)
            kdT = Buf(cx, es, "kdT", [128, QG, TT], BF16)
            qt = Buf(cx, es, "qt", [128, QG, TT], BF16)
            qpb = Buf(cx, es, "qpb", [128, QG, TT], BF16)
            kd_tok = Buf(cx, es, "kd_tok", [128, NP, QG * 128], BF16)
            v_tok = Buf(cx, es, "v_tok", [128, NP, VW], BF16)
            atts = [Buf(cx, es, "att", [128, 128], BF16) for _ in range(4)]
            osts = [Buf(cx, es, "ost", [128, 4, 128], F32) for _ in range(2)]
            tc = [0]
            ac = [0]
            oc = [0]

            def tmp():
                tc[0] += 1
                return tmps[tc[0] % len(tmps)]

            def k_finish(g, qb, src_ap, src_k):
                te = tmp()
                cx.op("act", lambda e: e.activation(out=te[:], in_=bb[:, qb, :], func=AF.Exp, scale=-1.0),
                      [bb.k], [te.k])
                cx.op("act", lambda e: e.activation(out=dl[:, qb, :], in_=bb[:, qb, 63:TT:64], func=AF.Exp),
                      [bb.k], [dl.k])
                cx.op("dve", lambda e: e.tensor_tensor(out=te[:], in0=src_ap, in1=te[:], op=ALU.mult),
                      [src_k, te.k], [te.k])
                cx.op("pool", lambda e: e.tensor_copy(out=kt[:, qb, :], in_=te[:]), [te.k], [kt.k])
                cx.op("pool", lambda e: e.tensor_tensor(
                    out=kdT[:, qb, :].rearrange("p (c t) -> p c t", t=64),
                    in0=te[:].rearrange("p (c t) -> p c t", t=64),
                    in1=dl[:, qb, :].unsqueeze(2).to_broadcast([128, 8, 64]), op=ALU.mult),
                    [te.k, dl.k], [kdT.k])

            for it in range(self.NT):
                self.load_h(h, it)
                self.norm_u(h, "norm_mix_%d" % i, u, sqs, rs)
                self.store_tile(u, self.uscr, 0, KC, it, self.u_k[it])
                if kind == "gla":
                    def epi_glr(blk, bw, ps):
                        cx.op("act", lambda e: e.copy(out=glr[:], in_=ps[0:16, :]), [ps.k], [glr.k])
                    self.linear_fm(u, KC, w_in, 6144, 16, epi_glr)
                for g in range(NG):
                    if kind == "gla":
                        for qb in range(QG):
                            gq = g * QG + qb
                            ps = self.next_ps()
                            cx.op("pe", lambda e: e.matmul(ps[:], wgk2b[0:16, gq * 128:(gq + 1) * 128], glr[:],
                                                           start=True, stop=True), [wgk2b.k, glr.k], [ps.k])
                            t1 = tmp()
                            cx.op("act", lambda e: e.activation(out=t1[:], in_=ps[:], func=AF.Exp, scale=-1.0,
                                                                bias=negb[:, gq:gq + 1]), [ps.k, negb.k], [t1.k])
                            cx.op("act", lambda e: e.activation(out=t1[:], in_=t1[:], func=AF.Ln, bias=1.0),
                                  [t1.k], [t1.k])
                            cx.op("pool", lambda e: e.tensor_scalar(out=t1[:], in0=t1[:], scalar1=-1.0 / 16.0,
                                                                    scalar2=None, op0=ALU.mult), [t1.k], [t1.k])
                            cx.op("dve", lambda e: e.tensor_tensor_scan(
                                out=bb[:, qb, :], data0=self.cmsk_b[:], data1=t1[:], initial=0.0,
                                op0=ALU.mult, op1=ALU.add), [t1.k, self.cmsk_b.k], [bb.k])

                        def epi_k(blk, bw, ps):
                            k_finish(g, blk, ps[:], ps.k)
                        self.linear_fm(u, KC, w_in, kcol(g), 512, epi_k)
                    else:
                        def epi_f(blk, bw, ps):
                            gq = g * QG + blk
                            t1, t2 = tmp(), tmp()
                            cx.op("act", lambda e: e.activation(out=t1[:], in_=ps[:], func=AF.Sigmoid), [ps.k], [t1.k])
                            cx.op("dve", lambda e: e.tensor_scalar(
                                out=t1[:], in0=t1[:], scalar1=oml[:, gq:gq + 1], scalar2=lb[:, gq:gq + 1],
                                op0=ALU.mult, op1=ALU.add), [t1.k, oml.k, lb.k], [t1.k])
                            cx.op("act", lambda e: e.activation(out=t2[:], in_=t1[:], func=AF.Ln), [t1.k], [t2.k])
                            cx.op("dve", lambda e: e.tensor_tensor_scan(
                                out=bb[:, blk, :], data0=self.cmsk_b[:], data1=t2[:], initial=0.0,
                                op0=ALU.mult, op1=ALU.add), [t2.k, self.cmsk_b.k], [bb.k])
                            cx.op("dve", lambda e: e.tensor_scalar(out=t1[:], in0=t1[:], scalar1=-1.0, scalar2=1.0,
                                                                   op0=ALU.mult, op1=ALU.add), [t1.k], [t1.k])
                            k_finish(g, blk, t1[:], t1.k)
                        self.linear_fm(u, KC, w_in, kcol(g), 512, epi_f)

                    def epi_q(blk, bw, ps):
                        te = tmp()
                        cx.op("act", lambda e: e.activation(out=te[:], in_=bb[:, blk, :], func=AF.Exp), [bb.k], [te.k])
                        if kind == "gla":
                            cx.op("dve", lambda e: e.scalar_tensor_tensor(
                                out=qt[:, blk, :], in0=ps[:], scalar=qscale, in1=te[:], op0=ALU.mult, op1=ALU.mult),
                                [ps.k, te.k], [qt.k])
                        else:
                            t2 = tmp()
                            cx.op("act", lambda e: e.activation(out=t2[:], in_=ps[:], func=AF.Silu), [ps.k], [t2.k])
                            cx.op("dve", lambda e: e.scalar_tensor_tensor(
                                out=qt[:, blk, :], in0=t2[:], scalar=qscale, in1=te[:], op0=ALU.mult, op1=ALU.mult),
                                [t2.k, te.k], [qt.k])
                    self.linear_fm(u, KC, w_in, qcol(g), 512, epi_q)

                    for c in (range(8) if seqpar else []):
                        cx.op("dve", lambda e: e.tensor_tensor(out=eo[g][:, :, c + 1], in0=eo[g][:, :, c],
                                                               in1=dl[:, :, c], op=ALU.mult), [eo[g].k, dl.k], [eo[g].k])
                    for qb in (range(QG) if seqpar else []):
                        cx.op("pool", lambda e: e.tensor_tensor(
                            out=qpb[:, qb, :].rearrange("p (c t) -> p c t", t=64),
                            in0=qt[:, qb, :].rearrange("p (c t) -> p c t", t=64),
                            in1=eo[g][:, qb, 0:8].unsqueeze(2).to_broadcast([128, 8, 64]), op=ALU.mult),
                            [qt.k, eo[g].k], [qpb.k])
                    if seqpar:
                        self.store_tile(qpb, self.qp, g * QG * 128, QG, it, self.qp_k[it])
                        cx.op("dve", lambda e: e.tensor_copy(out=eo[g][:, :, 0], in_=eo[g][:, :, 8]), [eo[g].k], [eo[g].k])

                    if it == 0 and g == 0:
                        self.dump_sb("u", u, [128, KC, TT], BF16)
                        self.dump_sb("bb", bb, [128, QG, TT])
                        self.dump_sb("kt", kt, [128, QG, TT], BF16)
                        self.dump_sb("qt", qt, [128, QG, TT], BF16)
                        self.dump_sb("kdT", kdT, [128, QG, TT], BF16)
                    for s in range(VW // 512):
                        def epi_v(ts, ps):
                            cx.op("act", lambda e: e.copy(out=v_tok[:, ts, s * 512:(s + 1) * 512], in_=ps[:]),
                                  [ps.k], [v_tok.k])
                        self.linear_tm(u, w_in, vcol(g) + s * 512, 512, epi_v)

                    for ts in range(NP):
                        ps = self.next_ps()
                        pv = ps[:, 0:256].bitcast(BF16).rearrange("p (a b) -> p a b", b=128)
                        for qb in range(QG):
                            cx.op("pe", lambda e: e.transpose(pv[:, qb, :], kdT[:, qb, ts * 128:(ts + 1) * 128],
                                                              self.ident_b[:]), [kdT.k, self.ident_b.k], [ps.k])
                        cx.op("act", lambda e: e.copy(out=kd_tok[:, ts, :].rearrange("p (a b) -> p a b", b=128),
                                                      in_=pv), [ps.k], [kd_tok.k])

                    if it == 0 and g == 0:
                        self.dump_sb("v_tok", v_tok, [128, NP, VW], BF16)
                        self.dump_sb("kd_tok", kd_tok, [128, NP, QG * 128], BF16)
                    if kind == "gla":
                        obanks = [[(hh, vb) for vb in range(4)] for hh in range(Hg)]
                    else:
                        obanks = [[(hh, 0) for hh in range(Hg)]]
                    for pr in range(NP):
                        tsl = slice(pr * 128, (pr + 1) * 128)
                        attm = {}
                        for hh in range(Hg):
                            ps = self.next_ps()
                            for jj in range(dkb):
                                qb = hh * dkb + jj
                                cx.op("pe", lambda e: e.matmul(ps[:, 0:128], kt[:, qb, tsl], qt[:, qb, tsl],
                                                               start=(jj == 0), stop=(jj == dkb - 1)),
                                      [kt.k, qt.k], [ps.k])
                            a = atts[ac[0] % 4]
                            ac[0] += 1
                            cx.op("dve", lambda e: e.tensor_tensor(out=a[:], in0=ps[:, 0:128], in1=mask2, op=ALU.mult),
                                  [ps.k, self.consts_b.k], [a.k])
                            attm[hh] = a
                        ops = []
                        for bank in obanks:
                            ps = self.next_ps(hold=True)
                            ops.append(ps)
                            for slot, (hh, vb) in enumerate(bank):
                                osl = slice(slot * 128, (slot + 1) * 128)
                                cx.op("pe", lambda e: e.matmul(
                                    ps[:, osl], v_tok[:, pr, hh * dv + vb * 128:hh * dv + (vb + 1) * 128],
                                    attm[hh][:], start=(slot == 0), stop=False, skip_group_check=True),
                                    [v_tok.k, attm[hh].k], [ps.k])
                        for c2 in range(2):
                            c = pr * 2 + c2
                            csl = slice(pr * 128 + c2 * 64, pr * 128 + (c2 + 1) * 64)
                            rows = slice(c2 * 64, (c2 + 1) * 64)
                            for bi, bank in enumerate(obanks):
                                ps = ops[bi]
                                for slot, (hh, vb) in enumerate(bank):
                                    for jj in range(dkb):
                                        qb = hh * dkb + jj
                                        cx.op("pe", lambda e: e.matmul(
                                            ps[:, slot * 128 + c2 * 64:slot * 128 + (c2 + 1) * 64],
                                            Sbf[g][hh][:, jj, vb * 128:(vb + 1) * 128], qt[:, qb, csl],
                                            start=False, stop=(jj == dkb - 1), skip_group_check=True),
                                            [Sbf[g][hh].k, qt.k], [ps.k])
                            if kind == "gla":
                                for hh in range(Hg):
                                    for jj in range(dkb):
                                        qb = hh * dkb + jj
                                        ps = self.next_ps()
                                        cx.op("pe", lambda e: e.matmul(
                                            ps[:, 0:dv], kd_tok[rows, pr, qb * 128:(qb + 1) * 128],
                                            v_tok[rows, pr, hh * dv:(hh + 1) * dv], start=True, stop=True),
                                            [kd_tok.k, v_tok.k], [ps.k])
                                        cx.op("dve", lambda e: e.scalar_tensor_tensor(
                                            out=S32[g][hh][:, jj, :], in0=S32[g][hh][:, jj, :], scalar=dl[:, qb, c:c + 1],
                                            in1=ps[:, 0:dv], op0=ALU.mult, op1=ALU.add),
                                            [S32[g][hh].k, dl.k, ps.k], [S32[g][hh].k])
                                    cx.op("act", lambda e: e.copy(out=Sbf[g][hh][:], in_=S32[g][hh][:]),
                                          [S32[g][hh].k], [Sbf[g][hh].k])
                            else:
                                ps = self.next_ps()
                                for hh in range(Hg):
                                    cx.op("pe", lambda e: e.matmul(
                                        ps[:, hh * 128:(hh + 1) * 128], kd_tok[rows, pr, hh * 128:(hh + 1) * 128],
                                        v_tok[rows, pr, hh * dv:(hh + 1) * dv], start=True, stop=True),
                                        [kd_tok.k, v_tok.k], [ps.k])
                                for hh in range(Hg):
                                    cx.op("dve", lambda e: e.scalar_tensor_tensor(
                                        out=S32[g][hh][:, 0, :], in0=S32[g][hh][:, 0, :], scalar=dl[:, hh, c:c + 1],
                                        in1=ps[:, hh * 128:(hh + 1) * 128], op0=ALU.mult, op1=ALU.add),
                                        [S32[g][hh].k, dl.k, ps.k], [S32[g][hh].k])
                                    cx.op("act", lambda e: e.copy(out=Sbf[g][hh][:], in_=S32[g][hh][:]),
                                          [S32[g][hh].k], [Sbf[g][hh].k])
                        for bi, bank in enumerate(obanks):
                            ost = osts[oc[0] % 2]
                            oc[0] += 1
                            cx.op("act", lambda e: e.copy(out=ost[:], in_=ops[bi][:].rearrange("p (a b) -> p a b", b=128)),
                                  [ops[bi].k], [ost.k])
                            hh0, vb0 = bank[0]
                            vblk0 = (g * Hg + hh0) * dvb + vb0
                            dst = self.oloc.ap()[vblk0 * 128:(vblk0 + 4) * 128,
                                                 it * TT + pr * 128:it * TT + (pr + 1) * 128].rearrange(
                                "(a p) t -> p a t", p=128)
                            cx.dma("pool", dst, ost[:], self.oloc_k[it][(oc[0] - 1) % 2], ost.k)
                            self.release(ops[bi])
            for g in (range(NG) if seqpar else []):
                for hh in range(Hg):
                    c0 = ((g * Hg + hh) * dkb) * dv
                    cx.dma("pool", sloc.ap()[:, c0:c0 + dkb * dv].rearrange("p (a b) -> p a b", b=dv),
                           S32[g][hh][:], sloc_k, S32[g][hh].k)
                eoc = Buf(cx, es, "eoc", [128, QG], F32)
                cx.op("dve", lambda e: e.tensor_copy(out=eoc[:], in_=eo[g][:, :, 0]), [eo[g].k], [eoc.k])
                cx.dma("pool", sloc.ap()[:, NS0 + g * QG:NS0 + (g + 1) * QG], eoc[:], sloc_k, eoc.k)
            if seqpar:
                cx.allgather(sg.ap(), sloc.ap(), sg_k, sloc_k, [list(range(NCORES))])

            cx.barrier(extra=[t for pair in self.oloc_k for t in pair])

        with ExitStack() as es:
            Sin = Buf(cx, es, "Sin", [128, NQ, dv], F32)
            Sinb = Buf(cx, es, "Sinb", [128, NQ, dv], BF16)
            with ExitStack() as es2:
                if not seqpar:
                    NR = 0
                else:
                    NR = NCORES
                stg = [Buf(cx, es2, "stg", [128, NS], F32) for _ in range(2)]
                dd = [Buf(cx, es2, "dd", [128, NQ], F32) for _ in range(2)]
                cx.op("pool", lambda e: e.memset(Sin[:], 0.0), [], [Sin.k])
                for r in range(NR):
                    st, d1 = stg[r % 2], dd[r % 2]
                    m = self.cmask_b[:, r:r + 1]
                    cx.dma("pool", st[:], sg.ap()[r * 128:(r + 1) * 128, :], st.k, sg_k)
                    cx.op("dve", lambda e: e.tensor_scalar(out=d1[:], in0=st[:, NS0:NS], scalar1=-1.0, scalar2=m,
                                                           op0=ALU.add, op1=ALU.mult), [st.k, self.cmask_b.k], [d1.k])
                    cx.op("dve", lambda e: e.tensor_scalar(out=d1[:], in0=d1[:], scalar1=1.0, scalar2=None,
                                                           op0=ALU.add), [d1.k], [d1.k])
                    for q in range(NQ):
                        cx.op("dve", lambda e: e.tensor_scalar(out=Sin[:, q, :], in0=Sin[:, q, :], scalar1=d1[:, q:q + 1],
                                                               scalar2=None, op0=ALU.mult), [Sin.k, d1.k], [Sin.k])
                        cx.op("dve", lambda e: e.scalar_tensor_tensor(
                            out=Sin[:, q, :], in0=st[:, q * dv:(q + 1) * dv], scalar=m, in1=Sin[:, q, :],
                            op0=ALU.mult, op1=ALU.add), [st.k, Sin.k, self.cmask_b.k], [Sin.k])
                cx.op("act", lambda e: e.copy(out=Sinb[:], in_=Sin[:]), [Sin.k], [Sinb.k])
                cx.barrier(end=False)
            h = Buf(cx, es, "h", [128, KC, TT], F32)
            u = Buf(cx, es, "u", [128, KC, TT], BF16)
            o = Buf(cx, es, "o", [128, KC, TT], F32)
            qpl = Buf(cx, es, "qpl", [128, NQ, TT], BF16)
            ogb = Buf(cx, es, "ogb", [128, KC, TT], BF16)
            sqs = [Buf(cx, es, "sq", [128, TT], BF16) for _ in range(2)]
            rss = [Buf(cx, es, "rs", [128, TT], F32) for _ in range(2)]
            tmps = [Buf(cx, es, "tmp", [128, TT], F32) for _ in range(4)]
            tc = [0]
            hpb = dv // 128
            for it in range(self.NT):
                self.load_tile(u, self.uscr, 0, KC, it, self.u_k[it])
                cx._waits("pool", self.oloc_k[it], [])
                self.load_tile(o, self.oloc, 0, KC, it, None)
                self.load_h(h, it)
                if seqpar:
                    self.load_tile(qpl, self.qp, 0, NQ, it, self.qp_k[it])
                for hd in (range(NG * Hg) if seqpar else []):
                    for vb in range(hpb):
                        ps = self.next_ps()
                        for jj in range(dkb):
                            q = hd * dkb + jj
                            cx.op("pe", lambda e: e.matmul(ps[:], Sinb[:, q, vb * 128:(vb + 1) * 128], qpl[:, q, :],
                                                           start=(jj == 0), stop=(jj == dkb - 1)), [Sinb.k, qpl.k], [ps.k])
                        blk = hd * hpb + vb
                        cx.op("dve", lambda e: e.tensor_tensor(out=o[:, blk, :], in0=ps[:], in1=o[:, blk, :], op=ALU.add),
                              [ps.k, o.k], [o.k])
                cur = {"hd": -1, "rs": None}

                def epi_og(blk, bw, ps):
                    hd = blk // hpb
                    if hd != cur["hd"]:
                        cur["hd"] = hd
                        cur["rs"] = rss[hd % 2]
                        self.rstd(o, hd * hpb, hpb, dv, sqs, cur["rs"])
                    r = cur["rs"]
                    t1, t2 = tmps[tc[0] % 4], tmps[(tc[0] + 1) % 4]
                    tc[0] += 2
                    cx.op("act", lambda e: e.activation(out=t1[:], in_=ps[:],
                                                        func=(AF.Silu if kind == "gla" else AF.Sigmoid)), [ps.k], [t1.k])
                    cx.op("dve", lambda e: e.scalar_tensor_tensor(
                        out=t2[:], in0=o[:, blk, :], scalar=self.col(gn, blk % hpb), in1=r[:],
                        op0=ALU.mult, op1=ALU.mult), [o.k, r.k, self.cols_b.k], [t2.k])
                    cx.op("pool", lambda e: e.tensor_tensor(out=ogb[:, blk, :], in0=t2[:], in1=t1[:], op=ALU.mult),
                          [t1.k, t2.k], [ogb.k])
                self.linear_fm(u, KC, w_in, ogcol, D, epi_og)

                def epi_out(blk, bw, ps):
                    cx.op("dve", lambda e: e.tensor_tensor(out=h[:, blk, :], in0=ps[:], in1=h[:, blk, :], op=ALU.add),
                          [ps.k, h.k], [h.k])
                self.linear_fm(ogb, KC, w_out, 0, D, epi_out)
                self.store_tile(h, self.hT, 0, KC, it, self.hT_k[it])
            self.h_stored()
            cx.barrier()

    def phase_ssm(self, i, j):
        cx, nc = self.cx, self.nc
        w_in, w_out = "ssm_w_in_%d" % j, "ssm_w_out_%d" % j
        mask2 = self.consts_b[:, 0, :]
        selpair = self.consts_b[:, 1, :]
        sel63 = self.consts_b[:, 2, :]
        sel127 = self.consts_b[:, 3, :]
        NP = TT // 128
        G, HG, P, NST = 8, 8, 64, 128
        GI = 2

        class View:
            def __init__(self, ap, k):
                self.ap, self.k = ap, k

            def __getitem__(self, key):
                return self.ap[key]

        def bc(ap2, n):
            return ap2.unsqueeze(2).to_broadcast([128, ap2.shape[1], n])

        with ExitStack() as es:
            big1 = Buf(cx, es, "big1", [128, KC * TT], F32)
            big2 = Buf(cx, es, "big2", [128, KC * TT], F32)
            h1 = View(big1.t[:].rearrange("p (a b) -> p a b", b=TT), big1.k)
            yT = View(big1.t[:].bitcast(BF16).rearrange("p (a b) -> p a b", b=TT), big1.k)
            u = View(big2.t[:, 0:4096].bitcast(BF16).rearrange("p (a b) -> p a b", b=TT), Trk("u"))
            BT = View(big2.t[:, 4096:6144].bitcast(BF16).rearrange("p (a b) -> p a b", b=TT), Trk("BT"))
            CT = View(big2.t[:, 6144:8192].bitcast(BF16).rearrange("p (a b) -> p a b", b=TT), Trk("CT"))
            h2 = View(big2.t[:].rearrange("p (a b) -> p a b", b=TT), big2.k)
            S32 = [Buf(cx, es, "S32", [128, HG * P], F32) for _ in range(G)]
            Sbf = [Buf(cx, es, "Sbf", [128, HG * P], BF16) for _ in range(G)]
            halo = Buf(cx, es, "halo", [128, 48, 3], F32)
            identf = Buf(cx, es, "identf", [128, 128], F32)
            onesf = Buf(cx, es, "onesf", [128, 128], F32)
            dtb = Buf(cx, es, "dtb", [128, 64], F32)
            arow = Buf(cx, es, "arow", [128, 64], F32)
            ddr = Buf(cx, es, "ddr", [128, 64], F32)
            dt_tok = Buf(cx, es, "dt_tok", [128, NP, 64], F32)
            la_tok = Buf(cx, es, "la_tok", [128, NP, 64], F32)
            b_tok = Buf(cx, es, "b_tok", [128, NP, 64], F32)
            eb_tok = Buf(cx, es, "eb_tok", [128, NP, 64], F32)
            we_tok = Buf(cx, es, "we_tok", [128, NP, 64], F32)
            Dc = Buf(cx, es, "Dc", [128, NP, 2, 64], F32)
            sqs = [Buf(cx, es, "sq", [128, TT], BF16) for _ in range(1)]
            rs = Buf(cx, es, "rs", [128, TT], F32)
            tmpc = [Buf(cx, es, "tmpc", [128, TT + 3], F32) for _ in range(2)]
            acc = Buf(cx, es, "acc", [128, TT], F32)
            xTg = Buf(cx, es, "xTg", [128, 4, TT], BF16)
            x_tok = [Buf(cx, es, "x_tok", [128, NP, 512], BF16) for _ in range(GI)]
            zg = [Buf(cx, es, "zg", [128, NP, 512], F32) for _ in range(GI)]
            rel = [Buf(cx, es, "rel", [128, 8, 128], F32) for _ in range(GI)]
            Mb = [Buf(cx, es, "Mb", [128, 8, 128], BF16) for _ in range(GI)]
            xdt = [Buf(cx, es, "xdt", [128, 512], BF16) for _ in range(GI)]
            xw = [Buf(cx, es, "xw", [128, 512], BF16) for _ in range(GI)]
            ytmp = [Buf(cx, es, "ytmp", [128, 512], F32) for _ in range(GI)]
            yn = [Buf(cx, es, "yn", [128, 512], BF16) for _ in range(GI)]
            cbm = [Buf(cx, es, "cbm", [128, 128], F32) for _ in range(GI)]
            btok = [Buf(cx, es, "btok", [128, 128], BF16) for _ in range(GI)]
            ss = [Buf(cx, es, "ss", [128, 1], F32) for _ in range(GI)]
            for g in range(G):
                cx.op("pool", lambda e: e.memset(S32[g][:], 0.0), [], [S32[g].k])
                cx.op("pool", lambda e: e.memset(Sbf[g][:], 0.0), [], [Sbf[g].k])
            cx.op("pool", lambda e: e.memset(halo[:], 0.0), [], [halo.k])
            cx.op("pool", lambda e: e.memset(onesf[:], 1.0), [], [onesf.k])
            cx.op("pool", lambda e: e.memset(identf[:], 0.0), [], [identf.k])
            cx.op("pool", lambda e: e.affine_select(
                out=identf[:], in_=identf[:], pattern=[[-1, 128]], compare_op=ALU.not_equal, fill=1.0,
                base=0, channel_multiplier=1), [identf.k], [identf.k])
            cx.dma("pool", dtb[:], self.rows_d["ssm_dt_bias_%d" % j].ap().partition_broadcast(128), dtb.k, None)
            cx.dma("pool", arow[:], self.rows_d["ssm_a_log_%d" % j].ap().partition_broadcast(128), arow.k, None)
            cx.dma("pool", ddr[:], self.rows_d["ssm_d_%d" % j].ap().partition_broadcast(128), ddr.k, None)
            cx.op("act", lambda e: e.activation(out=arow[:], in_=arow[:], func=AF.Exp), [arow.k], [arow.k])
            cx.op("dve", lambda e: e.tensor_scalar(out=arow[:], in0=arow[:], scalar1=-1.0, scalar2=None, op0=ALU.mult),
                  [arow.k], [arow.k])
            cb16 = Buf(cx, es, "cb16", [128, 4, 128], BF16)
            onesb = self.ones_b
            cx.op("dve", lambda e: e.tensor_copy(out=cb16[:], in_=self.consts_b[:]), [self.consts_b.k], [cb16.k])
            la_hi = Buf(cx, es, "la_hi", [128, NP, 64], BF16)
            la_lo = Buf(cx, es, "la_lo", [128, NP, 64], BF16)
            b_hi = Buf(cx, es, "b_hi", [128, NP, 64], BF16)
            b_lo = Buf(cx, es, "b_lo", [128, NP, 64], BF16)
            hl_f = Buf(cx, es, "hl_f", [128, NP, 64], F32)
            dg_lo = Buf(cx, es, "dg_lo", [128, 8, 128], BF16)
            dg_hi = Buf(cx, es, "dg_hi", [128, 8, 128], BF16)

            def split(src, hi, lo):
                cx.op("dve", lambda e: e.tensor_copy(out=hi[:], in_=src[:]), [src.k], [hi.k])
                cx.op("dve", lambda e: e.tensor_copy(out=hl_f[:], in_=hi[:]), [hi.k], [hl_f.k])
                cx.op("dve", lambda e: e.tensor_tensor(out=lo[:], in0=src[:], in1=hl_f[:], op=ALU.subtract),
                      [src.k, hl_f.k], [lo.k])
            cc = [0]

            def conv_epi(cb, ps, dst_ap, dst_k):
                tcv = tmpc[cc[0] % 2]
                cc[0] += 1
                cx.op("pool", lambda e: e.tensor_copy(out=tcv[:, 0:3], in_=halo[:, cb, :]), [halo.k], [tcv.k])
                cx.op("act", lambda e: e.copy(out=tcv[:, 3:TT + 3], in_=ps[:]), [ps.k], [tcv.k])
                cx.op("pool", lambda e: e.tensor_copy(out=halo[:, cb, :], in_=tcv[:, TT:TT + 3]), [tcv.k], [halo.k])
                wc = lambda t: self.col("ssm_conv_w_%d_%d" % (j, t), cb)
                cx.op("dve", lambda e: e.tensor_scalar(out=acc[:], in0=tcv[:, 0:TT], scalar1=wc(0),
                                                       scalar2=self.col("ssm_conv_b_%d" % j, cb),
                                                       op0=ALU.mult, op1=ALU.add), [tcv.k, self.cols_b.k], [acc.k])
                for t in range(1, 4):
                    cx.op("dve", lambda e: e.scalar_tensor_tensor(out=acc[:], in0=tcv[:, t:t + TT], scalar=wc(t),
                                                                  in1=acc[:], op0=ALU.mult, op1=ALU.add),
                          [tcv.k, acc.k, self.cols_b.k], [acc.k])
                cx.op("act", lambda e: e.activation(out=dst_ap, in_=acc[:], func=AF.Silu), [acc.k], [dst_k])

            for it in range(self.NT):
                self.load_h(h1, it)
                self.norm_u(h1, "norm_mix_%d" % i, u, sqs, rs)

                if SSM_STAGE < 1:
                    cx.barrier(end=False)
                    continue
                def epi_bc(blk, bw, ps):
                    if blk < 8:
                        conv_epi(32 + blk, ps, BT[:, blk, :], BT.k)
                    else:
                        conv_epi(32 + blk, ps, CT[:, blk - 8, :], CT.k)
                self.linear_fm(u, KC, w_in, 8192, 2048, epi_bc)

                if SSM_STAGE < 2:
                    cx.barrier(end=False)
                    continue
                def epi_dt(ts, ps):
                    cx.op("dve", lambda e: e.tensor_tensor(out=dt_tok[:, ts, :], in0=ps[:, 0:64], in1=dtb[:], op=ALU.add),
                          [ps.k, dtb.k], [dt_tok.k])
                self.linear_tm(u, w_in, 10240, 64, epi_dt)
                cx.op("act", lambda e: e.activation(out=dt_tok[:], in_=dt_tok[:], func=AF.Exp), [dt_tok.k], [dt_tok.k])
                cx.op("act", lambda e: e.activation(out=dt_tok[:], in_=dt_tok[:], func=AF.Ln, bias=1.0), [dt_tok.k], [dt_tok.k])
                for ts in range(NP):
                    cx.op("dve", lambda e: e.tensor_tensor(out=la_tok[:, ts, :], in0=dt_tok[:, ts, :], in1=arow[:], op=ALU.mult),
                          [dt_tok.k, arow.k], [la_tok.k])
                split(la_tok, la_hi, la_lo)
                for ts in range(NP):
                    ps = self.next_ps()
                    cx.op("pe", lambda e: e.matmul(ps[:, 0:64], cb16[:, 0, :], la_hi[:, ts, :], start=True, stop=False),
                          [cb16.k, la_hi.k], [ps.k])
                    cx.op("pe", lambda e: e.matmul(ps[:, 0:64], cb16[:, 0, :], la_lo[:, ts, :], start=False, stop=True),
                          [cb16.k, la_lo.k], [ps.k])
                    cx.op("act", lambda e: e.copy(out=b_tok[:, ts, :], in_=ps[:, 0:64]), [ps.k], [b_tok.k])
                split(b_tok, b_hi, b_lo)
                cx.op("act", lambda e: e.activation(out=eb_tok[:], in_=b_tok[:], func=AF.Exp), [b_tok.k], [eb_tok.k])
                for ts in range(NP):
                    ps = self.next_ps()
                    for si in range(3):
                        for hl, src in enumerate((b_hi, b_lo)):
                            cx.op("pe", lambda e: e.matmul(ps[:, si * 64:(si + 1) * 64], cb16[:, 1 + si, :], src[:, ts, :],
                                                           start=(si == 0 and hl == 0), stop=(hl == 1),
                                                           skip_group_check=True), [cb16.k, src.k], [ps.k])
                    cx.op("dve", lambda e: e.tensor_tensor(out=we_tok[:, ts, :], in0=ps[:, 0:64], in1=b_tok[:, ts, :],
                                                           op=ALU.subtract), [ps.k, b_tok.k], [we_tok.k])
                    cx.op("act", lambda e: e.activation(out=Dc[:, ts, :, :], in_=ps[:, 64:192].rearrange("p (a b) -> p a b", b=64),
                                                        func=AF.Exp), [ps.k], [Dc.k])
                cx.op("act", lambda e: e.activation(out=we_tok[:], in_=we_tok[:], func=AF.Exp), [we_tok.k], [we_tok.k])

                if SSM_STAGE < 3:
                    cx.barrier(end=False)
                    continue
                for gp in range(G // GI):
                    gs = tuple(range(gp * GI, (gp + 1) * GI))
                    for g in gs:
                        sl = g % GI
                        def epi_x(blk, bw, ps):
                            conv_epi(g * 4 + blk, ps, xTg[:, blk, :], xTg.k)
                        self.linear_fm(u, KC, w_in, 4096 + g * 512, 512, epi_x)
                        for ts in range(NP):
                            ps = self.next_ps()
                            pv = ps[:, 0:256].bitcast(BF16).rearrange("p (a b) -> p a b", b=128)
                            for a in range(4):
                                cx.op("pe", lambda e: e.transpose(pv[:, a, :], xTg[:, a, ts * 128:(ts + 1) * 128],
                                                                  self.ident_b[:]), [xTg.k, self.ident_b.k], [ps.k])
                            cx.op("act", lambda e: e.copy(out=x_tok[sl][:, ts, :].rearrange("p (a b) -> p a b", b=128),
                                                          in_=pv), [ps.k], [x_tok[sl].k])
                        def epi_z(ts, ps):
                            cx.op("act", lambda e: e.activation(out=zg[sl][:, ts, :], in_=ps[:], func=AF.Silu),
                                  [ps.k], [zg[sl].k])
                        self.linear_tm(u, w_in, g * 512, 512, epi_z)

                    for pr in (range(NP) if SSM_STAGE >= 4 else []):
                        tsl = slice(pr * 128, (pr + 1) * 128)
                        psy, psi = {}, {}
                        for g in gs:
                            sl = g % GI
                            hs = slice(g * 8, (g + 1) * 8)
                            ps = self.next_ps()
                            pvb = ps[:, 0:64].bitcast(BF16)
                            cx.op("pe", lambda e: e.transpose(pvb, BT[:, g, tsl], self.ident_b[:]),
                                  [BT.k, self.ident_b.k], [ps.k])
                            cx.op("act", lambda e: e.copy(out=btok[sl][:], in_=pvb), [ps.k], [btok[sl].k])
                            ps = self.next_ps()
                            cx.op("pe", lambda e: e.matmul(ps[:, 0:128], BT[:, g, tsl], CT[:, g, tsl], start=True, stop=True),
                                  [BT.k, CT.k], [ps.k])
                            cx.op("dve", lambda e: e.tensor_tensor(out=cbm[sl][:], in0=ps[:, 0:128], in1=mask2, op=ALU.mult),
                                  [ps.k, self.consts_b.k], [cbm[sl].k])
                            for dgx, bx in ((dg_hi, b_hi), (dg_lo, b_lo)):
                                cx.op("dve", lambda e: e.tensor_tensor(
                                    out=dgx[:], in0=identf[:].unsqueeze(1).to_broadcast([128, 8, 128]),
                                    in1=bc(bx[:, pr, hs], 128), op=ALU.mult), [identf.k, bx.k], [dgx.k])
                            pb0, pb1 = self.next_ps(hold=True), self.next_ps(hold=True)
                            for pbx, lo4 in ((pb0, 0), (pb1, 4)):
                                for hl, dgx in enumerate((dg_hi, dg_lo)):
                                    cx.op("pe", lambda e: e.matmul(
                                        pbx[:], onesb[:], dgx[:, lo4:lo4 + 4, :].rearrange("p a b -> p (a b)"),
                                        start=(hl == 0), stop=(hl == 1)), [onesb.k, dgx.k], [pbx.k])
                            cx.op("dve", lambda e: e.tensor_tensor(
                                out=rel[sl][:, 0:4, :], in0=pb0[:].rearrange("p (a b) -> p a b", b=128),
                                in1=bc(b_tok[:, pr, g * 8:g * 8 + 4], 128), op=ALU.subtract), [pb0.k, b_tok.k], [rel[sl].k])
                            cx.op("dve", lambda e: e.tensor_tensor(
                                out=rel[sl][:, 4:8, :], in0=pb1[:].rearrange("p (a b) -> p a b", b=128),
                                in1=bc(b_tok[:, pr, g * 8 + 4:g * 8 + 8], 128), op=ALU.subtract), [pb1.k, b_tok.k], [rel[sl].k])
                            self.release(pb0)
                            self.release(pb1)
                            cx.op("act", lambda e: e.activation(out=rel[sl][:], in_=rel[sl][:], func=AF.Relu, scale=-1.0),
                                  [rel[sl].k], [rel[sl].k])
                            cx.op("act", lambda e: e.activation(out=rel[sl][:], in_=rel[sl][:], func=AF.Exp, scale=-1.0),
                                  [rel[sl].k], [rel[sl].k])
                            cx.op("pool", lambda e: e.tensor_tensor(
                                out=Mb[sl][:], in0=rel[sl][:], in1=cbm[sl][:].unsqueeze(1).to_broadcast([128, 8, 128]),
                                op=ALU.mult), [rel[sl].k, cbm[sl].k], [Mb[sl].k])
                            cx.op("pool", lambda e: e.tensor_tensor(
                                out=xdt[sl][:].rearrange("p (a b) -> p a b", b=64),
                                in0=x_tok[sl][:, pr, :].rearrange("p (a b) -> p a b", b=64),
                                in1=bc(dt_tok[:, pr, hs], 64), op=ALU.mult), [x_tok[sl].k, dt_tok.k], [xdt[sl].k])
                            cx.op("pool", lambda e: e.tensor_tensor(
                                out=xw[sl][:].rearrange("p (a b) -> p a b", b=64),
                                in0=xdt[sl][:].rearrange("p (a b) -> p a b", b=64),
                                in1=bc(we_tok[:, pr, hs], 64), op=ALU.mult), [xdt[sl].k, we_tok.k], [xw[sl].k])
                            psy[g] = self.next_ps(hold=True)
                            for h8 in range(8):
                                cx.op("pe", lambda e: e.matmul(
                                    psy[g][:, h8 * 64:(h8 + 1) * 64], Mb[sl][:, h8, :], xdt[sl][:, h8 * 64:(h8 + 1) * 64],
                                    start=(h8 == 0), stop=True, skip_group_check=True), [Mb[sl].k, xdt[sl].k], [psy[g].k])
                            psi[g] = self.next_ps(hold=True)
                        for c2 in range(2):
                            rows = slice(c2 * 64, (c2 + 1) * 64)
                            tcs = slice(pr * 128 + c2 * 64, pr * 128 + (c2 + 1) * 64)
                            for g in gs:
                                kw = {"tile_position": (0, 64)} if c2 == 1 else {}
                                cx.op("pe", lambda e: e.matmul(psi[g][rows, :], CT[:, g, tcs], Sbf[g][:], start=True, stop=True,
                                                               skip_group_check=True, **kw), [CT.k, Sbf[g].k], [psi[g].k])
                            for g in gs:
                                sl = g % GI
                                hs = slice(g * 8, (g + 1) * 8)
                                ps = self.next_ps()
                                cx.op("pe", lambda e: e.matmul(ps[:], btok[sl][rows, :], xw[sl][rows, :], start=True, stop=True),
                                      [btok[sl].k, xw[sl].k], [ps.k])
                                cx.op("dve", lambda e: e.tensor_tensor(
                                    out=S32[g][:].rearrange("p (a b) -> p a b", b=64),
                                    in0=S32[g][:].rearrange("p (a b) -> p a b", b=64),
                                    in1=bc(Dc[:, pr, c2, hs], 64), op=ALU.mult), [S32[g].k, Dc.k], [S32[g].k])
                                cx.op("dve", lambda e: e.tensor_tensor(out=S32[g][:], in0=ps[:], in1=S32[g][:], op=ALU.add),
                                      [ps.k, S32[g].k], [S32[g].k])
                                cx.op("act", lambda e: e.copy(out=Sbf[g][:], in_=S32[g][:]), [S32[g].k], [Sbf[g].k])
                        for g in gs:
                            sl = g % GI
                            hs = slice(g * 8, (g + 1) * 8)
                            y = ytmp[sl]
                            tt = View(rel[sl].t[:, 0:4, :].rearrange("p a b -> p (a b)"), rel[sl].k)
                            cx.op("dve", lambda e: e.tensor_tensor(
                                out=y[:].rearrange("p (a b) -> p a b", b=64),
                                in0=psi[g][:].rearrange("p (a b) -> p a b", b=64),
                                in1=bc(eb_tok[:, pr, hs], 64), op=ALU.mult), [psi[g].k, eb_tok.k], [y.k])
                            cx.op("dve", lambda e: e.tensor_tensor(out=y[:], in0=psy[g][:], in1=y[:], op=ALU.add),
                                  [psy[g].k, y.k], [y.k])
                            self.release(psy[g])
                            self.release(psi[g])
                            cx.op("pool", lambda e: e.tensor_tensor(
                                out=tt[:].rearrange("p (a b) -> p a b", b=64),
                                in0=x_tok[sl][:, pr, :].rearrange("p (a b) -> p a b", b=64),
                                in1=bc(ddr[:, hs], 64), op=ALU.mult), [x_tok[sl].k, ddr.k], [tt.k])
                            cx.op("pool", lambda e: e.tensor_tensor(out=y[:], in0=y[:], in1=tt[:], op=ALU.add), [y.k, tt.k], [y.k])
                            cx.op("pool", lambda e: e.tensor_tensor(out=y[:], in0=y[:], in1=zg[sl][:, pr, :], op=ALU.mult),
                                  [y.k, zg[sl].k], [y.k])
                            cx.op("pool", lambda e: e.tensor_tensor(out=tt[:], in0=y[:], in1=y[:], op=ALU.mult), [y.k], [tt.k])
                            cx.op("dve", lambda e: e.reduce_sum(out=ss[sl][:], in_=tt[:], axis=mybir.AxisListType.X),
                                  [tt.k], [ss[sl].k])
                            cx.op("act", lambda e: e.activation(out=ss[sl][:], in_=ss[sl][:], func=AF.Sqrt, bias=EPS,
                                                                scale=1.0 / 512.0), [ss[sl].k], [ss[sl].k])
                            cx.op("dve", lambda e: e.reciprocal(out=ss[sl][:], in_=ss[sl][:]), [ss[sl].k], [ss[sl].k])
                            cx.op("dve", lambda e: e.tensor_scalar(out=yn[sl][:], in0=y[:], scalar1=ss[sl][:, 0:1], scalar2=None,
                                                                   op0=ALU.mult), [y.k, ss[sl].k], [yn[sl].k])
                            ps = self.next_ps()
                            pv = ps[:, 0:256].bitcast(BF16).rearrange("p (a b) -> p a b", b=128)
                            for a in range(4):
                                cx.op("pe", lambda e: e.transpose(pv[:, a, :], yn[sl][:, a * 128:(a + 1) * 128],
                                                                  self.ident_b[:]), [yn[sl].k, self.ident_b.k], [ps.k])
                            for a in range(4):
                                cx.op("act", lambda e: e.activation(
                                    out=yT[:, g * 4 + a, tsl], in_=pv[:, a, :], func=AF.Copy,
                                    scale=self.col("ssm_norm_%d" % j, g * 4 + a)), [ps.k, self.cols_b.k], [yT.k])
                cx.barrier(end=False)
                if SSM_STAGE < 5:
                    continue
                self.load_h(h2, it)

                def epi_out(blk, bw, ps):
                    cx.op("dve", lambda e: e.tensor_tensor(out=h2[:, blk, :], in0=ps[:], in1=h2[:, blk, :], op=ALU.add),
                          [ps.k, h2.k], [h2.k])
                self.linear_fm(yT, 32, w_out, 0, D, epi_out)
                self.store_tile(h2, self.hT, 0, KC, it, self.hT_k[it])
                cx.barrier(end=False)
            self.h_stored()
            cx.barrier()


def _weight(inputs, nm):
    base, idx = nm.rsplit("_", 1)
    table = {
        "w_up": inputs["w_up"], "w_down": inputs["w_down"],
        "w_ple_proj": inputs["w_ple_proj"], "w_ple_gate": inputs["w_ple_gate"],
        "gla_w_in": inputs["gla_w_in"], "gla_w_out": inputs["gla_w_out"],
        "hgrn_w_in": inputs["hgrn_w_in"], "hgrn_w_out": inputs["hgrn_w_out"],
        "ssm_w_in": inputs["ssm_w_in"], "ssm_w_out": inputs["ssm_w_out"],
    }
    return table[base][int(idx)]


def make_consts():
    c = np.zeros((128, 4, 128), np.float32)
    s = np.arange(128)[:, None]
    t = np.arange(128)[None, :]
    c[:, 0, :] = ((s // 64 == t // 64) & (s <= t)).astype(np.float32)
    c[:, 1, :] = (((s == 63) & (t < 64)) | ((s == 127) & (t >= 64))).astype(np.float32)
    c[:, 2, :] = (s == 63).astype(np.float32) * np.ones_like(t)
    c[:, 3, :] = (s == 127).astype(np.float32) * np.ones_like(t)
    return c


def run(inputs, depth, T, enable_mix=True, trace=False):
    x = np.asarray(inputs["x"])
    p = np.asarray(inputs["p"])
    B, L, _ = x.shape
    segs = NCORES // B
    assert L == segs * T
    prog = Prog(T, depth, enable_mix)
    nc = prog.build()
    lay, ncol, _ = col_layout(depth)
    cols = np.zeros((128, ncol), np.float32)

    def put(nm, v):
        off, n = lay[nm]
        cols[:, off:off + n] = to_cols(v)
    for i in range(depth):
        put("norm_mix_%d" % i, inputs["norm_mix"][i])
        put("norm_mlp_%d" % i, inputs["norm_mlp"][i])
        put("norm_ple_%d" % i, inputs["norm_ple"][i])
        kind, j = kind_of(i)
        if kind == 0:
            put("gla_b_gk_%d" % j, inputs["gla_b_gk"][j])
            put("gla_gn_%d" % j, inputs["gla_gn"][j])
        elif kind == 1:
            put("hgrn_gn_%d" % j, inputs["hgrn_gn"][j])
            for l in range(depth):
                put("hgrn_lb_%d" % l, inputs["hgrn_lb_logits"][l])
        else:
            for t in range(4):
                put("ssm_conv_w_%d_%d" % (j, t), inputs["ssm_conv_w"][j][t])
            put("ssm_conv_b_%d" % j, inputs["ssm_conv_b"][j])
            put("ssm_norm_%d" % j, inputs["ssm_norm"][j])
    put("norm_final", inputs["norm_final"])
    consts = make_consts()
    shared = {"cols": cols, "consts": consts}
    for i in range(depth):
        kind, j = kind_of(i)
        if kind == 0:
            shared["gla_w_gk2_%d" % j] = np.ascontiguousarray(inputs["gla_w_gk2"][j], np.float32)
        if kind == 2:
            for nm in ("ssm_dt_bias", "ssm_a_log", "ssm_d"):
                shared["%s_%d" % (nm, j)] = np.ascontiguousarray(
                    np.asarray(inputs[nm][j], np.float32).reshape(1, 64))
    in_maps = []
    for c in range(NCORES):
        b, s = c // segs, c % segs
        m = dict(shared)
        m["xT"] = np.ascontiguousarray(x[b, s * T:(s + 1) * T, :].T)
        m["pT"] = np.ascontiguousarray(np.transpose(p[:depth, b, s * T:(s + 1) * T, :], (0, 2, 1)))
        cm = np.zeros((128, 16), np.float32)
        for r in range(NCORES):
            rb, rs = r // segs, r % segs
            if rb == b and rs < s:
                cm[:, r] = 1.0
            if rb == b and rs == s - 1:
                cm[:, 8 + r] = 1.0
        m["cmask"] = cm
        for nm, K, N in big_weights(depth):
            W = _weight(inputs, nm)
            if USE_CC:
                r = K // NCORES
                m[nm] = np.ascontiguousarray(W[c * r:(c + 1) * r, :], np.float32)
            else:
                m[nm] = np.ascontiguousarray(W, np.float32)
        in_maps.append(m)
    res = run_bass_kernel_spmd(nc, in_maps, core_ids=list(range(NCORES)), trace=trace)
    out = np.empty((B, L, D), np.float32)
    for c in range(NCORES):
        b, s = c // segs, c % segs
        out[b, s * T:(s + 1) * T, :] = res.results[c]["outT"].T
    if DEBUG:
        return out, res
    if trace:
        return out, res
    return out


def kernel(**inputs):
    depth = int(np.asarray(inputs["p"]).shape[0])
    B, L, _ = np.asarray(inputs["x"]).shape
    return run(inputs, depth, L * B // NCORES)
```

```python
import numpy as np
from contextlib import ExitStack
import concourse.bass as bass
import concourse.mybir as mybir
from concourse.bass_utils import run_bass_kernel_spmd

F32 = mybir.dt.float32
BF16 = mybir.dt.bfloat16
ALU = mybir.AluOpType
AF = mybir.ActivationFunctionType

NCORES = 2
USE_CC = False
D = 2048
KC = D // 128
TT = 512
EPS = 1e-6
PLE = 256
DFF = 8192
N_MIX = 3
DEBUG = False
SSM_STAGE = 99
KINDS = None


def kind_of(i):
    if KINDS is None:
        return i % N_MIX, i // N_MIX
    k = KINDS[i]
    return k, sum(1 for x in KINDS[:i] if x == k)


class Trk:
    __slots__ = ("name", "w", "r", "dsem", "dcnt", "excl")

    def __init__(self, name="", excl=False):
        self.name = name
        self.excl = excl
        self.w = None
        self.r = {}
        self.dsem = None
        self.dcnt = 0


class Ctx:
    def __init__(self, nc, es):
        self.nc = nc
        self.es = es
        self.sems = {}
        self.engs = {}
        for nm, h in (("pe", nc.tensor), ("act", nc.scalar), ("dve", nc.vector),
                      ("pool", nc.gpsimd), ("sp", nc.sync)):
            self.sems[nm] = es.enter_context(nc.semaphore("sem_" + nm))
            self.engs[nm] = {"h": h, "cnt": 0, "known": {}}
        self.ndsem = 0
        self.phase_trks = []
        self.phase_evs = {}
        self.free_dsems = []
        self.uid = 0

    def name(self, p):
        self.uid += 1
        return "%s_%d" % (p, self.uid)

    def _dsem(self, t):
        if t.dsem is None:
            if self.free_dsems:
                t.dsem, t.dcnt = self.free_dsems.pop()
            else:
                t.dsem = "d%d" % self.ndsem
                self.ndsem += 1
                self.sems[t.dsem] = self.es.enter_context(self.nc.semaphore("dsem_%s" % t.dsem))
        return t.dsem

    def _wait(self, eng, k, v):
        e = self.engs[eng]
        if k == eng and v > e["cnt"]:
            return
        if v > 0 and e["known"].get(k, 0) < v:
            e["h"].wait_ge(self.sems[k], v)
            e["known"][k] = v

    def _waits(self, eng, reads, writes):
        need = {}
        for t in reads:
            if t.w is not None:
                k, v = t.w
                if need.get(k, 0) < v:
                    need[k] = v
            if t.excl:
                for k, v in t.r.items():
                    if k != eng and need.get(k, 0) < v:
                        need[k] = v
        for t in writes:
            if t.w is not None:
                k, v = t.w
                if need.get(k, 0) < v:
                    need[k] = v
            for k, v in t.r.items():
                if need.get(k, 0) < v:
                    need[k] = v
        for k, v in need.items():
            self._wait(eng, k, v)

    def op(self, eng, fn, reads=(), writes=(), inc=True):
        self._waits(eng, reads, writes)
        e = self.engs[eng]
        ins = fn(e["h"])
        if inc:
            e["cnt"] += 1
            ins.then_inc(self.sems[eng], 1)
            c = e["cnt"]
        else:
            c = e["cnt"] + 1
        for t in reads:
            if t.r.get(eng, 0) < c:
                t.r[eng] = c
        for t in writes:
            t.w = (eng, c)
            t.r = {}
        return ins

    def dma(self, q, out_ap, in_ap, out_t, in_t, **kw):
        reads = [in_t] if in_t is not None else []
        k = self._dsem(out_t)
        saved = out_t.w
        if saved is not None and saved[0] == k:
            out_t.w = None
        self._waits(q, reads, [out_t])
        out_t.w = saved
        e = self.engs[q]
        ins = e["h"].dma_start(out=out_ap, in_=in_ap, **kw)
        out_t.dcnt += 16
        ins.then_inc(self.sems[k], 16)
        if q != "sp" or in_t is not None:
            self.phase_evs[k] = out_t.dcnt
        if in_t is not None and in_t.r.get(k, 0) < out_t.dcnt:
            in_t.r[k] = out_t.dcnt
        out_t.w = (k, out_t.dcnt)
        out_t.r = {}
        return ins

    def allgather(self, out_ap, in_ap, out_t, in_t, groups):
        q = "pool"
        self._waits(q, [in_t], [out_t])
        e = self.engs[q]
        k = self._dsem(out_t)
        ins = e["h"].collective_compute("AllGather", ALU.bypass, replica_groups=groups,
                                        ins=[in_ap], outs=[out_ap])
        out_t.dcnt += 1
        ins.then_inc(self.sems[k], 1)
        if in_t.r.get(k, 0) < out_t.dcnt:
            in_t.r[k] = out_t.dcnt
        out_t.w = (k, out_t.dcnt)
        out_t.r = {}
        return ins

    def finish(self, eng, trks):
        self._waits(eng, trks, [])

    def barrier(self, extra=(), end=True):
        evs = {}
        for nm in ("pe", "act", "dve", "pool"):
            evs[nm] = self.engs[nm]["cnt"]
        for t in list(self.phase_trks) + list(extra):
            if t.dsem is not None and t.dcnt > 0:
                evs[t.dsem] = t.dcnt
        evs.update(self.phase_evs)
        self.phase_evs = {}
        for nm in ("pe", "act", "dve", "pool"):
            for k, v in evs.items():
                self._wait(nm, k, v)
        if end:
            for t in self.phase_trks:
                if t.dsem is not None:
                    self.free_dsems.append((t.dsem, t.dcnt))
                    t.dsem = None
            self.phase_trks = []


class Buf:
    def __init__(self, cx, es, name, shape, dt, psum=False, phase=True):
        nm = cx.name(name)
        if psum:
            self.t = es.enter_context(cx.nc.psum_tensor(nm, list(shape), dt))
        else:
            self.t = es.enter_context(cx.nc.sbuf_tensor(nm, list(shape), dt))
        self.k = Trk(nm, excl=psum)
        if phase:
            cx.phase_trks.append(self.k)

    def __getitem__(self, key):
        return self.t[key]


def dram(nc, cx, name, shape, dt, kind=None):
    if kind is None:
        t = nc.dram_tensor(name, list(shape), dt)
    else:
        t = nc.dram_tensor(name, list(shape), dt, kind=kind)
    return t


GLA_IN = 6160
HGRN_IN = 8192
SSM_IN = 10304


def big_weights(depth):
    out = []
    for i in range(depth):
        kind, j = kind_of(i)
        if kind == 0:
            out.append(("gla_w_in_%d" % j, D, GLA_IN))
            out.append(("gla_w_out_%d" % j, D, D))
        elif kind == 1:
            out.append(("hgrn_w_in_%d" % j, D, HGRN_IN))
            out.append(("hgrn_w_out_%d" % j, D, D))
        else:
            out.append(("ssm_w_in_%d" % j, D, SSM_IN))
            out.append(("ssm_w_out_%d" % j, 2 * D, D))
        out.append(("w_up_%d" % i, D, DFF))
        out.append(("w_down_%d" % i, DFF, D))
        out.append(("w_ple_gate_%d" % i, D, D))
        out.append(("w_ple_proj_%d" % i, PLE, D))
    return out


def col_layout(depth):
    items = []
    for i in range(depth):
        items += [("norm_mix_%d" % i, KC), ("norm_mlp_%d" % i, KC), ("norm_ple_%d" % i, KC)]
    items.append(("norm_final", KC))
    n_norm = sum(n for _, n in items)
    for i in range(depth):
        kind, j = kind_of(i)
        if kind == 0:
            items += [("gla_b_gk_%d" % j, 8), ("gla_gn_%d" % j, 4)]
        elif kind == 1:
            items += [("hgrn_gn_%d" % j, 1)]
            for l in range(depth):
                items.append(("hgrn_lb_%d" % l, KC))
        else:
            for t in range(4):
                items.append(("ssm_conv_w_%d_%d" % (j, t), 48))
            items += [("ssm_conv_b_%d" % j, 48), ("ssm_norm_%d" % j, 32)]
    lay = {}
    off = 0
    for nm, n in items:
        if nm not in lay:
            lay[nm] = (off, n)
            off += n
    return lay, off, n_norm


def to_cols(v):
    v = np.asarray(v, np.float32).reshape(-1, 128)
    return np.ascontiguousarray(v.T)


class Prog:
    def __init__(self, T, depth, enable_mix=True):
        self.T = T
        self.depth = depth
        self.NT = T // TT
        self.enable_mix = enable_mix
        self.nc = bass.Bass("TRN2", target_bir_lowering=False)
        self.lay, self.ncol, self.n_norm = col_layout(depth)
        self.dbg = {}
        self.debug = DEBUG

    def declare(self):
        nc, T, depth = self.nc, self.T, self.depth
        self.xT = nc.dram_tensor("xT", [D, T], F32, kind="ExternalInput")
        self.pT = nc.dram_tensor("pT", [depth, PLE, T], F32, kind="ExternalInput")
        self.cols_d = nc.dram_tensor("cols", [128, self.ncol], F32, kind="ExternalInput")
        self.consts_d = nc.dram_tensor("consts", [128, 4, 128], F32, kind="ExternalInput")
        self.cmask_d = nc.dram_tensor("cmask", [128, 16], F32, kind="ExternalInput")
        self.outT = nc.dram_tensor("outT", [D, T], F32, kind="ExternalOutput")
        self.wsh, self.wshb, self.wg, self.wg_k = {}, {}, {}, {}
        for nm, K, N in big_weights(depth):
            rows = K // NCORES if USE_CC else K
            self.wsh[nm] = nc.dram_tensor(nm, [rows, N], F32, kind="ExternalInput")
            if USE_CC:
                self.wshb[nm] = nc.dram_tensor(nm + "_sb", [rows, N], BF16)
            self.wg[nm] = nc.dram_tensor(nm + "_g", [K, N], BF16)
            self.wg_k[nm] = Trk(nm + "_g")
        self.wg_need = {}
        self.hT = nc.dram_tensor("hT_scr", [D, T], F32)
        self.hT_k = [Trk("hT%d" % i) for i in range(self.NT)]
        self.out_k = [Trk("out%d" % i) for i in range(self.NT)]
        self.oloc = nc.dram_tensor("oloc_scr", [2 * D, T], F32)
        self.oloc_k = [[Trk("oloc%d_%d" % (i, b)) for b in range(2)] for i in range(self.NT)]
        self.uscr = nc.dram_tensor("u_scr", [D, T], BF16)
        self.u_k = [Trk("u%d" % i) for i in range(self.NT)]
        self.qp = nc.dram_tensor("qp_scr", [D, T], BF16)
        self.qp_k = [Trk("qp%d" % i) for i in range(self.NT)]
        self.rows_d = {}
        self.small_d = {}
        for i in range(depth):
            kind, j = kind_of(i)
            if kind == 0:
                self.small_d["gla_w_gk2_%d" % j] = nc.dram_tensor(
                    "gla_w_gk2_%d" % j, [16, 1024], F32, kind="ExternalInput")
            if kind == 2:
                for nm in ("ssm_dt_bias", "ssm_a_log", "ssm_d"):
                    self.rows_d["%s_%d" % (nm, j)] = nc.dram_tensor(
                        "%s_%d" % (nm, j), [1, 64], F32, kind="ExternalInput")

    def dump_sb(self, name, buf, shape, dt=F32):
        if not getattr(self, "debug", False) or name in self.dbg:
            return
        d = self.nc.dram_tensor("dbg_" + name, list(shape), dt, kind="ExternalOutput")
        k = Trk(name)
        self.dbg[name] = k
        self.cx.dma("pool", d.ap(), buf[:], k, buf.k)

    def dump_dram(self, name, dt_, shape, trks, dt=F32):
        if not getattr(self, "debug", False) or name in self.dbg:
            return
        d = self.nc.dram_tensor("dbg_" + name, list(shape), dt, kind="ExternalOutput")
        k = Trk(name)
        self.dbg[name] = k
        self.cx._waits("pool", trks, [])
        self.cx.dma("pool", d.ap(), dt_.ap(), k, None)

    def col(self, name, i=0, n=1):
        off, cnt = self.lay[name]
        return self.cols[:, off + i:off + i + n]

    def build(self):
        nc = self.nc
        self.declare()
        with ExitStack() as es:
            cx = self.cx = Ctx(nc, es)
            self.cols_b = Buf(cx, es, "cols", [128, self.ncol], F32, phase=False)
            self.cols = self.cols_b.t
            self.consts_b = Buf(cx, es, "consts", [128, 4, 128], F32, phase=False)
            self.cmask_b = Buf(cx, es, "cmask", [128, 16], F32, phase=False)
            self.ones_b = Buf(cx, es, "ones", [128, 128], BF16, phase=False)
            self.ident_b = Buf(cx, es, "ident", [128, 128], BF16, phase=False)
            self.cmsk_b = Buf(cx, es, "chunkmask", [128, TT], F32, phase=False)
            self.NSLAB = 2
            self.slabs = [Buf(cx, es, "slab%d" % i, [128, 16, 512], BF16, phase=False)
                          for i in range(self.NSLAB)]
            self.slab_i = 0
            self.psum = [Buf(cx, es, "ps%d" % i, [128, 512], F32, psum=True, phase=False)
                         for i in range(8)]
            self.ps_i = 0
            self.held = set()
            self.rr = 0
            cx.dma("pool", self.cols[:], self.cols_d.ap(), self.cols_b.k, None)
            cx.dma("pool", self.consts_b[:], self.consts_d.ap(), self.consts_b.k, None)
            cx.dma("pool", self.cmask_b[:], self.cmask_d.ap(), self.cmask_b.k, None)
            cx.op("pool", lambda e: e.memset(self.ones_b[:], 1.0), [], [self.ones_b.k])
            cx.op("pool", lambda e: e.memset(self.ident_b[:], 0.0), [], [self.ident_b.k])
            cx.op("pool", lambda e: e.affine_select(
                out=self.ident_b[:], in_=self.ident_b[:], pattern=[[-1, 128]],
                compare_op=ALU.not_equal, fill=1.0, base=0, channel_multiplier=1),
                [self.ident_b.k], [self.ident_b.k])
            cx.op("pool", lambda e: e.memset(self.cmsk_b[:], 1.0), [], [self.cmsk_b.k])
            cx.op("pool", lambda e: e.memset(self.cmsk_b[:, 0:TT:64], 0.0), [], [self.cmsk_b.k])

            self.phase_weights()
            self.hsrc, self.hsrc_k = self.xT, [None] * self.NT
            for i in range(self.depth):
                kind, j = kind_of(i)
                if self.enable_mix:
                    if kind == 0:
                        self.phase_lin_mixer(i, j, "gla")
                    elif kind == 1:
                        self.phase_lin_mixer(i, j, "hgrn")
                    else:
                        self.phase_ssm(i, j)
                self.phase_mlp(i)
                self.phase_ple(i)
            self.phase_out()
            for q in ("pool", "sp", "act", "dve", "pe"):
                cx.finish(q, self.out_k + list(self.dbg.values()))
        return nc

    def next_ps(self, hold=False):
        for _ in range(16):
            b = self.psum[self.ps_i % 8]
            self.ps_i += 1
            if id(b) not in self.held:
                if hold:
                    self.held.add(id(b))
                return b
        raise RuntimeError("all PSUM banks held")

    def release(self, b):
        self.held.discard(id(b))

    def ew(self):
        self.rr += 1
        return "dve" if self.rr % 2 else "pool"

    def phase_weights(self):
        cx = self.cx
        PIECE = 4096
        with ExitStack() as es:
            NB = 3
            wdst = [Trk("wdst%d" % i) for i in range(NB)]
            fin = [Buf(cx, es, "wc_in%d" % i, [128, PIECE], F32) for i in range(NB)]
            fout = [Buf(cx, es, "wc_out%d" % i, [128, PIECE], BF16) for i in range(NB)]
            n = 0
            for nm, K, N in big_weights(self.depth):
                rows = K // NCORES if USE_CC else K
                tot = rows * N
                per = tot // 128
                assert per * 128 == tot
                src = self.wsh[nm].ap().rearrange("r n -> (r n)").rearrange("(p f) -> p f", p=128)
                dstt = self.wshb[nm] if USE_CC else self.wg[nm]
                dst = dstt.ap().rearrange("r n -> (r n)").rearrange("(p f) -> p f", p=128)
                shk = Trk(nm + "_sb") if USE_CC else None
                off = 0
                while off < per:
                    w = min(PIECE, per - off)
                    a, b = fin[n % NB], fout[n % NB]
                    cx.dma("sp", a[:, 0:w], src[:, off:off + w], a.k, None)
                    eng = ("act", "dve", "act", "act", "dve", "act")[n % 6]
                    if eng == "act":
                        cx.op("act", lambda e: e.copy(out=b[:, 0:w], in_=a[:, 0:w]), [a.k], [b.k])
                    else:
                        cx.op(eng, lambda e: e.tensor_copy(out=b[:, 0:w], in_=a[:, 0:w]), [a.k], [b.k])
                    cx.dma("pool", dst[:, off:off + w], b[:, 0:w], shk if USE_CC else wdst[n % NB], b.k)
                    off += w
                    n += 1
                if USE_CC:
                    cx.allgather(self.wg[nm].ap(), self.wshb[nm].ap(), self.wg_k[nm], shk,
                                 [list(range(NCORES))])
                else:
                    self.wg_need[nm] = [(t.dsem, t.dcnt) for t in wdst if t.dsem is not None]
            cx.barrier()

    def phase_copy_in(self):
        cx = self.cx
        for it in range(self.NT):
            cx.dma("pool", self.hT.ap()[:, it * TT:(it + 1) * TT],
                   self.xT.ap()[:, it * TT:(it + 1) * TT], self.hT_k[it], None)

    def load_h(self, buf, it):
        self.load_tile(buf, self.hsrc, 0, KC, it, self.hsrc_k[it])

    def h_stored(self):
        self.hsrc, self.hsrc_k = self.hT, self.hT_k

    def load_tile(self, buf, dram_t, row0, nblk, it, trk, blk0=0):
        cx = self.cx
        src = dram_t.ap()[row0:row0 + nblk * 128, it * TT:(it + 1) * TT].rearrange(
            "(kc p) t -> p kc t", p=128)
        step = 4
        for q in range(0, nblk, step):
            n = min(step, nblk - q)
            cx.dma("pool", buf[:, blk0 + q:blk0 + q + n, :], src[:, q:q + n, :], buf.k, trk)

    def store_tile(self, buf, dram_t, row0, nblk, it, trk, blk0=0):
        cx = self.cx
        dst = dram_t.ap()[row0:row0 + nblk * 128, it * TT:(it + 1) * TT].rearrange(
            "(kc p) t -> p kc t", p=128)
        step = 4
        for q in range(0, nblk, step):
            n = min(step, nblk - q)
            cx.dma("pool", dst[:, q:q + n, :], buf[:, blk0 + q:blk0 + q + n, :], trk, buf.k)

    def rstd(self, src, blk0, nblk, n_feat, sqs, rs):
        cx = self.cx
        ps = self.next_ps()
        for b in range(nblk):
            sq = sqs[b % len(sqs)]
            cx.op("act", lambda e: e.activation(out=sq[:], in_=src[:, blk0 + b, :], func=AF.Square),
                  [src.k], [sq.k])
            cx.op("pe", lambda e: e.matmul(ps[:], self.ones_b[:], sq[:], start=(b == 0),
                                           stop=(b == nblk - 1)), [self.ones_b.k, sq.k], [ps.k])
        cx.op("act", lambda e: e.activation(out=rs[:], in_=ps[:], func=AF.Sqrt, bias=EPS,
                                            scale=1.0 / n_feat), [ps.k], [rs.k])
        cx.op("dve", lambda e: e.reciprocal(out=rs[:], in_=rs[:]), [rs.k], [rs.k])

    def norm_u(self, h, gain, u, sqs, rs):
        cx = self.cx
        self.rstd(h, 0, KC, D, sqs, rs)
        for kc in range(KC):
            cx.op("dve", lambda e: e.scalar_tensor_tensor(
                out=u[:, kc, :], in0=h[:, kc, :], scalar=self.col(gain, kc), in1=rs[:],
                op0=ALU.mult, op1=ALU.mult), [h.k, rs.k, self.cols_b.k], [u.k])

    def load_slab(self, wname, k0, nk, c0, w):
        cx = self.cx
        slab = self.slabs[self.slab_i % self.NSLAB]
        self.slab_i += 1
        src = self.wg[wname].ap()[k0 * 128:(k0 + nk) * 128, c0:c0 + w].rearrange(
            "(kc p) n -> p kc n", p=128)
        step = 4
        if not USE_CC:
            for k, v in self.wg_need[wname]:
                cx._wait("sp", k, v)
        for q in range(0, nk, step):
            n = min(step, nk - q)
            cx.dma("sp", slab[:, q:q + n, 0:w], src[:, q:q + n, :], slab.k, self.wg_k[wname] if USE_CC else None)
        return slab

    def linear_fm(self, src, nkc, wname, col0, ncols, epi, src_blk0=0):
        cx = self.cx
        ng = (ncols + 511) // 512
        nks = (nkc + 15) // 16
        for g in range(ng):
            gw = min(512, ncols - g * 512)
            nnb = (gw + 127) // 128
            pss = [self.next_ps(hold=True) for _ in range(nnb)]
            for ks in range(nks):
                nk = min(16, nkc - ks * 16)
                slab = self.load_slab(wname, ks * 16, nk, col0 + g * 512, gw)
                for nb in range(nnb):
                    bw = min(128, gw - nb * 128)
                    for kc in range(nk):
                        kk = ks * 16 + kc
                        cx.op("pe", lambda e: e.matmul(
                            pss[nb][0:bw, :], slab[:, kc, nb * 128:nb * 128 + bw],
                            src[:, src_blk0 + kk, :], start=(kk == 0), stop=(kk == nkc - 1)),
                            [slab.k, src.k], [pss[nb].k], inc=(kc == nk - 1))
            for nb in range(nnb):
                bw = min(128, gw - nb * 128)
                epi(g * 4 + nb, bw, pss[nb])
                self.release(pss[nb])

    def linear_tm(self, src, wname, col0, gw, epi):
        cx = self.cx
        slab = self.load_slab(wname, 0, KC, col0, gw)
        for ts in range(TT // 128):
            ps = self.next_ps()
            for kc in range(KC):
                cx.op("pe", lambda e: e.matmul(
                    ps[:, 0:gw], src[:, kc, ts * 128:(ts + 1) * 128], slab[:, kc, 0:gw],
                    start=(kc == 0), stop=(kc == KC - 1)), [slab.k, src.k], [ps.k], inc=(kc == KC - 1))
            epi(ts, ps)

    def phase_mlp(self, i):
        cx = self.cx
        with ExitStack() as es:
            hs = [Buf(cx, es, "h", [128, KC, TT], F32) for _ in range(2)]
            u = Buf(cx, es, "u", [128, KC, TT], BF16)
            hid = Buf(cx, es, "hid", [128, DFF // 128, TT], BF16)
            sqs = [Buf(cx, es, "sq", [128, TT], BF16) for _ in range(2)]
            rs = Buf(cx, es, "rs", [128, TT], F32)
            tmps = [Buf(cx, es, "tmp", [128, TT], F32) for _ in range(3)]
            cnt = [0]
            self.load_h(hs[0], 0)
            self.norm_u(hs[0], "norm_mlp_%d" % i, u, sqs, rs)
            for it in range(self.NT):
                h = hs[it % 2]

                def epi_up(blk, bw, ps):
                    t = tmps[cnt[0] % 3]
                    cnt[0] += 1
                    cx.op("act", lambda e: e.activation(out=t[:], in_=ps[:], func=AF.Relu), [ps.k], [t.k])
                    cx.op(self.ew(), lambda e: e.tensor_tensor(out=hid[:, blk, :], in0=t[:], in1=t[:],
                                                               op=ALU.mult), [t.k], [hid.k])
                self.linear_fm(u, KC, "w_up_%d" % i, 0, DFF, epi_up)
                if it + 1 < self.NT:
                    self.load_h(hs[(it + 1) % 2], it + 1)
                    self.norm_u(hs[(it + 1) % 2], "norm_mlp_%d" % i, u, sqs, rs)

                def epi_down(blk, bw, ps):
                    cx.op("dve", lambda e: e.tensor_tensor(out=h[:, blk, :], in0=ps[:], in1=h[:, blk, :],
                                                           op=ALU.add), [ps.k, h.k], [h.k])
                self.linear_fm(hid, DFF // 128, "w_down_%d" % i, 0, D, epi_down)
                self.store_tile(h, self.hT, 0, KC, it, self.hT_k[it])
            self.h_stored()
            cx.barrier()

    def phase_ple(self, i):
        cx = self.cx
        with ExitStack() as es:
            hs = [Buf(cx, es, "h", [128, KC, TT], F32) for _ in range(2)]
            us = [Buf(cx, es, "u", [128, KC, TT], BF16) for _ in range(2)]
            pp = Buf(cx, es, "pp", [128, KC, TT], F32)
            pf = Buf(cx, es, "pf", [128, 2, TT], F32)
            pbs = [Buf(cx, es, "pb", [128, 2, TT], BF16) for _ in range(2)]
            sqs = [Buf(cx, es, "sq", [128, TT], BF16) for _ in range(2)]
            rs = Buf(cx, es, "rs", [128, TT], F32)
            tmps = [Buf(cx, es, "tmp", [128, TT], F32) for _ in range(3)]
            cnt = [0]

            def prefetch(it):
                self.load_h(hs[it % 2], it)
                src = self.pT.ap()[i, :, it * TT:(it + 1) * TT].rearrange("(kc p) t -> p kc t", p=128)
                cx.dma("pool", pf[:], src, pf.k, None)
                cx.op("dve", lambda e: e.tensor_copy(out=pbs[it % 2][:], in_=pf[:]), [pf.k], [pbs[it % 2].k])
                self.norm_u(hs[it % 2], "norm_ple_%d" % i, us[it % 2], sqs, rs)
            prefetch(0)
            for it in range(self.NT):
                h, u, pb = hs[it % 2], us[it % 2], pbs[it % 2]

                def epi_pp(blk, bw, ps):
                    cx.op("act", lambda e: e.copy(out=pp[:, blk, :], in_=ps[:]), [ps.k], [pp.k])
                self.linear_fm(pb, 2, "w_ple_proj_%d" % i, 0, D, epi_pp)
                if it + 1 < self.NT:
                    prefetch(it + 1)

                def epi_gate(blk, bw, ps):
                    t = tmps[cnt[0] % 3]
                    cnt[0] += 1
                    cx.op("act", lambda e: e.activation(out=t[:], in_=ps[:], func=AF.Sigmoid), [ps.k], [t.k])
                    cx.op("pool", lambda e: e.tensor_tensor(out=t[:], in0=t[:], in1=pp[:, blk, :],
                                                            op=ALU.mult), [t.k, pp.k], [t.k])
                    cx.op("dve", lambda e: e.tensor_tensor(out=h[:, blk, :], in0=t[:], in1=h[:, blk, :],
                                                           op=ALU.add), [t.k, h.k], [h.k])
                self.linear_fm(u, KC, "w_ple_gate_%d" % i, 0, D, epi_gate)
                self.store_tile(h, self.hT, 0, KC, it, self.hT_k[it])
            self.h_stored()
            cx.barrier()

    def phase_out(self):
        cx = self.cx
        with ExitStack() as es:
            h = Buf(cx, es, "h", [128, KC, TT], F32)
            o = Buf(cx, es, "o", [128, KC, TT], F32)
            sqs = [Buf(cx, es, "sq", [128, TT], BF16) for _ in range(2)]
            rs = Buf(cx, es, "rs", [128, TT], F32)
            for it in range(self.NT):
                self.load_h(h, it)
                self.norm_u(h, "norm_final", o, sqs, rs)
                self.store_tile(o, self.outT, 0, KC, it, self.out_k[it])
            cx.barrier(extra=self.out_k)

    def phase_lin_mixer(self, i, j, kind):
        cx, nc = self.cx, self.nc
        if kind == "gla":
            w_in, w_out = "gla_w_in_%d" % j, "gla_w_out_%d" % j
            NG, Hg, dkb, dvb, dv = 2, 2, 2, 4, 512
            qcol = lambda g: g * 512
            kcol = lambda g: 1024 + g * 512
            vcol = lambda g: 2048 + g * 1024
            ogcol = 4096
            qscale = 256 ** -0.5
            gn = "gla_gn_%d" % j
        else:
            w_in, w_out = "hgrn_w_in_%d" % j, "hgrn_w_out_%d" % j
            NG, Hg, dkb, dvb, dv = 4, 4, 1, 1, 128
            qcol = lambda g: g * 512
            kcol = lambda g: 2048 + g * 512
            vcol = lambda g: 4096 + g * 512
            ogcol = 6144
            qscale = 128 ** -0.5
            gn = "hgrn_gn_%d" % j
        QG = 4
        VW = Hg * dv
        NQ = NG * QG
        NS0 = NQ * dv
        NS = NS0 + NQ
        sloc = nc.dram_tensor("sloc_%d" % i, [128, NS], F32)
        sg = nc.dram_tensor("sg_%d" % i, [NCORES * 128, NS], F32)
        sloc_k, sg_k = Trk("sloc"), Trk("sg")
        mask2 = self.consts_b[:, 0, :]
        NP = TT // 128
        seqpar = USE_CC

        with ExitStack() as es:
            S32 = [[Buf(cx, es, "S32", [128, dkb, dv], F32) for _ in range(Hg)] for _ in range(NG)]
            Sbf = [[Buf(cx, es, "Sbf", [128, dkb, dv], BF16) for _ in range(Hg)] for _ in range(NG)]
            eo = [Buf(cx, es, "eo", [128, QG, 9], F32) for _ in range(NG)]
            for g in range(NG):
                for hh in range(Hg):
                    cx.op("pool", lambda e: e.memset(S32[g][hh][:], 0.0), [], [S32[g][hh].k])
                    cx.op("pool", lambda e: e.memset(Sbf[g][hh][:], 0.0), [], [Sbf[g][hh].k])
                cx.op("pool", lambda e: e.memset(eo[g][:], 1.0), [], [eo[g].k])
            if kind == "gla":
                wgk2 = Buf(cx, es, "wgk2", [16, 1024], F32)
                wgk2b = Buf(cx, es, "wgk2b", [16, 1024], BF16)
                negb = Buf(cx, es, "negb", [128, 8], F32)
                glr = Buf(cx, es, "glr", [16, TT], BF16)
                cx.dma("pool", wgk2[:], self.small_d["gla_w_gk2_%d" % j].ap(), wgk2.k, None)
                cx.op("dve", lambda e: e.tensor_copy(out=wgk2b[:], in_=wgk2[:]), [wgk2.k], [wgk2b.k])
                cx.op("dve", lambda e: e.tensor_scalar(out=negb[:], in0=self.col("gla_b_gk_%d" % j, 0, 8),
                                                       scalar1=-1.0, scalar2=None, op0=ALU.mult),
                      [self.cols_b.k], [negb.k])
            else:
                lb = Buf(cx, es, "lb", [128, KC], F32)
                oml = Buf(cx, es, "oml", [128, KC], F32)
                ex = Buf(cx, es, "ex", [128, self.depth, KC], F32)
                mx = Buf(cx, es, "mx", [128, KC], F32)
                sm = Buf(cx, es, "sm", [128, KC], F32)
                lg = lambda l: self.col("hgrn_lb_%d" % l, 0, KC)
                cx.op("dve", lambda e: e.tensor_copy(out=mx[:], in_=lg(0)), [self.cols_b.k], [mx.k])
                for l in range(1, self.depth):
                    cx.op("dve", lambda e: e.tensor_tensor(out=mx[:], in0=mx[:], in1=lg(l), op=ALU.max),
                          [mx.k, self.cols_b.k], [mx.k])
                for l in range(self.depth):
                    cx.op("dve", lambda e: e.tensor_tensor(out=ex[:, l, :], in0=lg(l), in1=mx[:], op=ALU.subtract),
                          [mx.k, self.cols_b.k], [ex.k])
                cx.op("act", lambda e: e.activation(out=ex[:], in_=ex[:], func=AF.Exp), [ex.k], [ex.k])
                cx.op("dve", lambda e: e.tensor_copy(out=sm[:], in_=ex[:, 0, :]), [ex.k], [sm.k])
                cx.op("dve", lambda e: e.memset(lb[:], 0.0), [], [lb.k])
                for l in range(1, self.depth):
                    cx.op("dve", lambda e: e.tensor_tensor(out=sm[:], in0=sm[:], in1=ex[:, l, :], op=ALU.add),
                          [sm.k, ex.k], [sm.k])
                    if l <= i:
                        cx.op("dve", lambda e: e.tensor_tensor(out=lb[:], in0=lb[:], in1=ex[:, l, :], op=ALU.add),
                              [lb.k, ex.k], [lb.k])
                cx.op("dve", lambda e: e.reciprocal(out=sm[:], in_=sm[:]), [sm.k], [sm.k])
                cx.op("dve", lambda e: e.tensor_tensor(out=lb[:], in0=lb[:], in1=sm[:], op=ALU.mult),
                      [lb.k, sm.k], [lb.k])
                cx.op("dve", lambda e: e.tensor_scalar(out=oml[:], in0=lb[:], scalar1=-1.0, scalar2=1.0,
                                                       op0=ALU.mult, op1=ALU.add), [lb.k], [oml.k])
            h = Buf(cx, es, "h", [128, KC, TT], F32)
            u = Buf(cx, es, "u", [128, KC, TT], BF16)
            sqs = [Buf(cx, es, "sq", [128, TT], BF16) for _ in range(2)]
            rs = Buf(cx, es, "rs", [128, TT], F32)
            tmps = [Buf(cx, es, "tmp", [128, TT], F32) for _ in range(6)]
            bb = Buf(cx, es, "bb", [128, QG, TT], F32)
            dl = Buf(cx, es, "dl", [128, QG, 8], F32)
            kt = Buf(cx, es, "kt", [128, QG, TT], BF16)
            kdT = Buf(cx, es, "kdT", [128, QG, TT], BF16)
            qt = Buf(cx, es, "qt", [128, QG, TT], BF16)
            qpb = Buf(cx, es, "qpb", [128, QG, TT], BF16)
            kd_tok = Buf(cx, es, "kd_tok", [128, NP, QG * 128], BF16)
            v_tok = Buf(cx, es, "v_tok", [128, NP, VW], BF16)
            atts = [Buf(cx, es, "att", [128, 128], BF16) for _ in range(4)]
            osts = [Buf(cx, es, "ost", [128, 4, 128], F32) for _ in range(2)]
            tc = [0]
            ac = [0]
            oc = [0]

            def tmp():
                tc[0] += 1
                return tmps[tc[0] % len(tmps)]

            def k_finish(g, qb, src_ap, src_k):
                te = tmp()
                cx.op("act", lambda e: e.activation(out=te[:], in_=bb[:, qb, :], func=AF.Exp, scale=-1.0),
                      [bb.k], [te.k])
                cx.op("act", lambda e: e.activation(out=dl[:, qb, :], in_=bb[:, qb, 63:TT:64], func=AF.Exp),
                      [bb.k], [dl.k])
                cx.op("dve", lambda e: e.tensor_tensor(out=te[:], in0=src_ap, in1=te[:], op=ALU.mult),
                      [src_k, te.k], [te.k])
                cx.op("pool", lambda e: e.tensor_copy(out=kt[:, qb, :], in_=te[:]), [te.k], [kt.k])
                cx.op("pool", lambda e: e.tensor_tensor(
                    out=kdT[:, qb, :].rearrange("p (c t) -> p c t", t=64),
                    in0=te[:].rearrange("p (c t) -> p c t", t=64),
                    in1=dl[:, qb, :].unsqueeze(2).to_broadcast([128, 8, 64]), op=ALU.mult),
                    [te.k, dl.k], [kdT.k])

            for it in range(self.NT):
                self.load_h(h, it)
                self.norm_u(h, "norm_mix_%d" % i, u, sqs, rs)
                self.store_tile(u, self.uscr, 0, KC, it, self.u_k[it])
                if kind == "gla":
                    def epi_glr(blk, bw, ps):
                        cx.op("act", lambda e: e.copy(out=glr[:], in_=ps[0:16, :]), [ps.k], [glr.k])
                    self.linear_fm(u, KC, w_in, 6144, 16, epi_glr)
                for g in range(NG):
                    if kind == "gla":
                        for qb in range(QG):
                            gq = g * QG + qb
                            ps = self.next_ps()
                            cx.op("pe", lambda e: e.matmul(ps[:], wgk2b[0:16, gq * 128:(gq + 1) * 128], glr[:],
                                                           start=True, stop=True), [wgk2b.k, glr.k], [ps.k])
                            t1 = tmp()
                            cx.op("act", lambda e: e.activation(out=t1[:], in_=ps[:], func=AF.Exp, scale=-1.0,
                                                                bias=negb[:, gq:gq + 1]), [ps.k, negb.k], [t1.k])
                            cx.op("act", lambda e: e.activation(out=t1[:], in_=t1[:], func=AF.Ln, bias=1.0),
                                  [t1.k], [t1.k])
                            cx.op("pool", lambda e: e.tensor_scalar(out=t1[:], in0=t1[:], scalar1=-1.0 / 16.0,
                                                                    scalar2=None, op0=ALU.mult), [t1.k], [t1.k])
                            cx.op("dve", lambda e: e.tensor_tensor_scan(
                                out=bb[:, qb, :], data0=self.cmsk_b[:], data1=t1[:], initial=0.0,
                                op0=ALU.mult, op1=ALU.add), [t1.k, self.cmsk_b.k], [bb.k])

                        def epi_k(blk, bw, ps):
                            k_finish(g, blk, ps[:], ps.k)
                        self.linear_fm(u, KC, w_in, kcol(g), 512, epi_k)
                    else:
                        def epi_f(blk, bw, ps):
                            gq = g * QG + blk
                            t1, t2 = tmp(), tmp()
                            cx.op("act", lambda e: e.activation(out=t1[:], in_=ps[:], func=AF.Sigmoid), [ps.k], [t1.k])
                            cx.op("dve", lambda e: e.tensor_scalar(
                                out=t1[:], in0=t1[:], scalar1=oml[:, gq:gq + 1], scalar2=lb[:, gq:gq + 1],
                                op0=ALU.mult, op1=ALU.add), [t1.k, oml.k, lb.k], [t1.k])
                            cx.op("act", lambda e: e.activation(out=t2[:], in_=t1[:], func=AF.Ln), [t1.k], [t2.k])
                            cx.op("dve", lambda e: e.tensor_tensor_scan(
                                out=bb[:, blk, :], data0=self.cmsk_b[:], data1=t2[:], initial=0.0,
                                op0=ALU.mult, op1=ALU.add), [t2.k, self.cmsk_b.k], [bb.k])
                            cx.op("dve", lambda e: e.tensor_scalar(out=t1[:], in0=t1[:], scalar1=-1.0, scalar2=1.0,
                                                                   op0=ALU.mult, op1=ALU.add), [t1.k], [t1.k])
                            k_finish(g, blk, t1[:], t1.k)
                        self.linear_fm(u, KC, w_in, kcol(g), 512, epi_f)

                    def epi_q(blk, bw, ps):
                        te = tmp()
                        cx.op("act", lambda e: e.activation(out=te[:], in_=bb[:, blk, :], func=AF.Exp), [bb.k], [te.k])
                        if kind == "gla":
                            cx.op("dve", lambda e: e.scalar_tensor_tensor(
                                out=qt[:, blk, :], in0=ps[:], scalar=qscale, in1=te[:], op0=ALU.mult, op1=ALU.mult),
                                [ps.k, te.k], [qt.k])
                        else:
                            t2 = tmp()
                            cx.op("act", lambda e: e.activation(out=t2[:], in_=ps[:], func=AF.Silu), [ps.k], [t2.k])
                            cx.op("dve", lambda e: e.scalar_tensor_tensor(
                                out=qt[:, blk, :], in0=t2[:], scalar=qscale, in1=te[:], op0=ALU.mult, op1=ALU.mult),
                                [t2.k, te.k], [qt.k])
                    self.linear_fm(u, KC, w_in, qcol(g), 512, epi_q)

                    for c in (range(8) if seqpar else []):
                        cx.op("dve", lambda e: e.tensor_tensor(out=eo[g][:, :, c + 1], in0=eo[g][:, :, c],
                                                               in1=dl[:, :, c], op=ALU.mult), [eo[g].k, dl.k], [eo[g].k])
                    for qb in (range(QG) if seqpar else []):
                        cx.op("pool", lambda e: e.tensor_tensor(
                            out=qpb[:, qb, :].rearrange("p (c t) -> p c t", t=64),
                            in0=qt[:, qb, :].rearrange("p (c t) -> p c t", t=64),
                            in1=eo[g][:, qb, 0:8].unsqueeze(2).to_broadcast([128, 8, 64]), op=ALU.mult),
                            [qt.k, eo[g].k], [qpb.k])
                    if seqpar:
                        self.store_tile(qpb, self.qp, g * QG * 128, QG, it, self.qp_k[it])
                        cx.op("dve", lambda e: e.tensor_copy(out=eo[g][:, :, 0], in_=eo[g][:, :, 8]), [eo[g].k], [eo[g].k])

                    if it == 0 and g == 0:
                        self.dump_sb("u", u, [128, KC, TT], BF16)
                        self.dump_sb("bb", bb, [128, QG, TT])
                        self.dump_sb("kt", kt, [128, QG, TT], BF16)
                        self.dump_sb("qt", qt, [128, QG, TT], BF16)
                        self.dump_sb("kdT", kdT, [128, QG, TT], BF16)
                    for s in range(VW // 512):
                        def epi_v(ts, ps):
                            cx.op("act", lambda e: e.copy(out=v_tok[:, ts, s * 512:(s + 1) * 512], in_=ps[:]),
                                  [ps.k], [v_tok.k])
                        self.linear_tm(u, w_in, vcol(g) + s * 512, 512, epi_v)

                    for ts in range(NP):
                        ps = self.next_ps()
                        pv = ps[:, 0:256].bitcast(BF16).rearrange("p (a b) -> p a b", b=128)
                        for qb in range(QG):
                            cx.op("pe", lambda e: e.transpose(pv[:, qb, :], kdT[:, qb, ts * 128:(ts + 1) * 128],
                                                              self.ident_b[:]), [kdT.k, self.ident_b.k], [ps.k])
                        cx.op("act", lambda e: e.copy(out=kd_tok[:, ts, :].rearrange("p (a b) -> p a b", b=128),
                                                      in_=pv), [ps.k], [kd_tok.k])

                    if it == 0 and g == 0:
                        self.dump_sb("v_tok", v_tok, [128, NP, VW], BF16)
                        self.dump_sb("kd_tok", kd_tok, [128, NP, QG * 128], BF16)
                    if kind == "gla":
                        obanks = [[(hh, vb) for vb in range(4)] for hh in range(Hg)]
                    else:
                        obanks = [[(hh, 0) for hh in range(Hg)]]
                    for pr in range(NP):
                        tsl = slice(pr * 128, (pr + 1) * 128)
                        attm = {}
                        for hh in range(Hg):
                            ps = self.next_ps()
                            for jj in range(dkb):
                                qb = hh * dkb + jj
                                cx.op("pe", lambda e: e.matmul(ps[:, 0:128], kt[:, qb, tsl], qt[:, qb, tsl],
                                                               start=(jj == 0), stop=(jj == dkb - 1)),
                                      [kt.k, qt.k], [ps.k])
                            a = atts[ac[0] % 4]
                            ac[0] += 1
                            cx.op("dve", lambda e: e.tensor_tensor(out=a[:], in0=ps[:, 0:128], in1=mask2, op=ALU.mult),
                                  [ps.k, self.consts_b.k], [a.k])
                            attm[hh] = a
                        ops = []
                        for bank in obanks:
                            ps = self.next_ps(hold=True)
                            ops.append(ps)
                            for slot, (hh, vb) in enumerate(bank):
                                osl = slice(slot * 128, (slot + 1) * 128)
                                cx.op("pe", lambda e: e.matmul(
                                    ps[:, osl], v_tok[:, pr, hh * dv + vb * 128:hh * dv + (vb + 1) * 128],
                                    attm[hh][:], start=(slot == 0), stop=False, skip_group_check=True),
                                    [v_tok.k, attm[hh].k], [ps.k])
                        for c2 in range(2):
                            c = pr * 2 + c2
                            csl = slice(pr * 128 + c2 * 64, pr * 128 + (c2 + 1) * 64)
                            rows = slice(c2 * 64, (c2 + 1) * 64)
                            for bi, bank in enumerate(obanks):
                                ps = ops[bi]
                                for slot, (hh, vb) in enumerate(bank):
                                    for jj in range(dkb):
                                        qb = hh * dkb + jj
                                        cx.op("pe", lambda e: e.matmul(
                                            ps[:, slot * 128 + c2 * 64:slot * 128 + (c2 + 1) * 64],
                                            Sbf[g][hh][:, jj, vb * 128:(vb + 1) * 128], qt[:, qb, csl],
                                            start=False, stop=(jj == dkb - 1), skip_group_check=True),
                                            [Sbf[g][hh].k, qt.k], [ps.k])
                            if kind == "gla":
                                for hh in range(Hg):
                                    for jj in range(dkb):
                                        qb = hh * dkb + jj
                                        ps = self.next_ps()
                                        cx.op("pe", lambda e: e.matmul(
                                            ps[:, 0:dv], kd_tok[rows, pr, qb * 128:(qb + 1) * 128],
                                            v_tok[rows, pr, hh * dv:(hh + 1) * dv], start=True, stop=True),
                                            [kd_tok.k, v_tok.k], [ps.k])
                                        cx.op("dve", lambda e: e.scalar_tensor_tensor(
                                            out=S32[g][hh][:, jj, :], in0=S32[g][hh][:, jj, :], scalar=dl[:, qb, c:c + 1],
                                            in1=ps[:, 0:dv], op0=ALU.mult, op1=ALU.add),
                                            [S32[g][hh].k, dl.k, ps.k], [S32[g][hh].k])
                                    cx.op("act", lambda e: e.copy(out=Sbf[g][hh][:], in_=S32[g][hh][:]),
                                          [S32[g][hh].k], [Sbf[g][hh].k])
                            else:
                                ps = self.next_ps()
                                for hh in range(Hg):
                                    cx.op("pe", lambda e: e.matmul(
                                        ps[:, hh * 128:(hh + 1) * 128], kd_tok[rows, pr, hh * 128:(hh + 1) * 128],
                                        v_tok[rows, pr, hh * dv:(hh + 1) * dv], start=True, stop=True),
                                        [kd_tok.k, v_tok.k], [ps.k])
                                for hh in range(Hg):
                                    cx.op("dve", lambda e: e.scalar_tensor_tensor(
                                        out=S32[g][hh][:, 0, :], in0=S32[g][hh][:, 0, :], scalar=dl[:, hh, c:c + 1],
                                        in1=ps[:, hh * 128:(hh + 1) * 128], op0=ALU.mult, op1=ALU.add),
                                        [S32[g][hh].k, dl.k, ps.k], [S32[g][hh].k])
                                    cx.op("act", lambda e: e.copy(out=Sbf[g][hh][:], in_=S32[g][hh][:]),
                                          [S32[g][hh].k], [Sbf[g][hh].k])
                        for bi, bank in enumerate(obanks):
                            ost = osts[oc[0] % 2]
                            oc[0] += 1
                            cx.op("act", lambda e: e.copy(out=ost[:], in_=ops[bi][:].rearrange("p (a b) -> p a b", b=128)),
                                  [ops[bi].k], [ost.k])
                            hh0, vb0 = bank[0]
                            vblk0 = (g * Hg + hh0) * dvb + vb0
                            dst = self.oloc.ap()[vblk0 * 128:(vblk0 + 4) * 128,
                                                 it * TT + pr * 128:it * TT + (pr + 1) * 128].rearrange(
                                "(a p) t -> p a t", p=128)
                            cx.dma("pool", dst, ost[:], self.oloc_k[it][(oc[0] - 1) % 2], ost.k)
                            self.release(ops[bi])
            for g in (range(NG) if seqpar else []):
                for hh in range(Hg):
                    c0 = ((g * Hg + hh) * dkb) * dv
                    cx.dma("pool", sloc.ap()[:, c0:c0 + dkb * dv].rearrange("p (a b) -> p a b", b=dv),
                           S32[g][hh][:], sloc_k, S32[g][hh].k)
                eoc = Buf(cx, es, "eoc", [128, QG], F32)
                cx.op("dve", lambda e: e.tensor_copy(out=eoc[:], in_=eo[g][:, :, 0]), [eo[g].k], [eoc.k])
                cx.dma("pool", sloc.ap()[:, NS0 + g * QG:NS0 + (g + 1) * QG], eoc[:], sloc_k, eoc.k)
            if seqpar:
                cx.allgather(sg.ap(), sloc.ap(), sg_k, sloc_k, [list(range(NCORES))])

            cx.barrier(extra=[t for pair in self.oloc_k for t in pair])

        with ExitStack() as es:
            Sin = Buf(cx, es, "Sin", [128, NQ, dv], F32)
            Sinb = Buf(cx, es, "Sinb", [128, NQ, dv], BF16)
            with ExitStack() as es2:
                if not seqpar:
                    NR = 0
                else:
                    NR = NCORES
                stg = [Buf(cx, es2, "stg", [128, NS], F32) for _ in range(2)]
                dd = [Buf(cx, es2, "dd", [128, NQ], F32) for _ in range(2)]
                cx.op("pool", lambda e: e.memset(Sin[:], 0.0), [], [Sin.k])
                for r in range(NR):
                    st, d1 = stg[r % 2], dd[r % 2]
                    m = self.cmask_b[:, r:r + 1]
                    cx.dma("pool", st[:], sg.ap()[r * 128:(r + 1) * 128, :], st.k, sg_k)
                    cx.op("dve", lambda e: e.tensor_scalar(out=d1[:], in0=st[:, NS0:NS], scalar1=-1.0, scalar2=m,
                                                           op0=ALU.add, op1=ALU.mult), [st.k, self.cmask_b.k], [d1.k])
                    cx.op("dve", lambda e: e.tensor_scalar(out=d1[:], in0=d1[:], scalar1=1.0, scalar2=None,
                                                           op0=ALU.add), [d1.k], [d1.k])
                    for q in range(NQ):
                        cx.op("dve", lambda e: e.tensor_scalar(out=Sin[:, q, :], in0=Sin[:, q, :], scalar1=d1[:, q:q + 1],
                                                               scalar2=None, op0=ALU.mult), [Sin.k, d1.k], [Sin.k])
                        cx.op("dve", lambda e: e.scalar_tensor_tensor(
                            out=Sin[:, q, :], in0=st[:, q * dv:(q + 1) * dv], scalar=m, in1=Sin[:, q, :],
                            op0=ALU.mult, op1=ALU.add), [st.k, Sin.k, self.cmask_b.k], [Sin.k])
                cx.op("act", lambda e: e.copy(out=Sinb[:], in_=Sin[:]), [Sin.k], [Sinb.k])
                cx.barrier(end=False)
            h = Buf(cx, es, "h", [128, KC, TT], F32)
            u = Buf(cx, es, "u", [128, KC, TT], BF16)
            o = Buf(cx, es, "o", [128, KC, TT], F32)
            qpl = Buf(cx, es, "qpl", [128, NQ, TT], BF16)
            ogb = Buf(cx, es, "ogb", [128, KC, TT], BF16)
            sqs = [Buf(cx, es, "sq", [128, TT], BF16) for _ in range(2)]
            rss = [Buf(cx, es, "rs", [128, TT], F32) for _ in range(2)]
            tmps = [Buf(cx, es, "tmp", [128, TT], F32) for _ in range(4)]
            tc = [0]
            hpb = dv // 128
            for it in range(self.NT):
                self.load_tile(u, self.uscr, 0, KC, it, self.u_k[it])
                cx._waits("pool", self.oloc_k[it], [])
                self.load_tile(o, self.oloc, 0, KC, it, None)
                self.load_h(h, it)
                if seqpar:
                    self.load_tile(qpl, self.qp, 0, NQ, it, self.qp_k[it])
                for hd in (range(NG * Hg) if seqpar else []):
                    for vb in range(hpb):
                        ps = self.next_ps()
                        for jj in range(dkb):
                            q = hd * dkb + jj
                            cx.op("pe", lambda e: e.matmul(ps[:], Sinb[:, q, vb * 128:(vb + 1) * 128], qpl[:, q, :],
                                                           start=(jj == 0), stop=(jj == dkb - 1)), [Sinb.k, qpl.k], [ps.k])
                        blk = hd * hpb + vb
                        cx.op("dve", lambda e: e.tensor_tensor(out=o[:, blk, :], in0=ps[:], in1=o[:, blk, :], op=ALU.add),
                              [ps.k, o.k], [o.k])
                cur = {"hd": -1, "rs": None}

                def epi_og(blk, bw, ps):
                    hd = blk // hpb
                    if hd != cur["hd"]:
                        cur["hd"] = hd
                        cur["rs"] = rss[hd % 2]
                        self.rstd(o, hd * hpb, hpb, dv, sqs, cur["rs"])
                    r = cur["rs"]
                    t1, t2 = tmps[tc[0] % 4], tmps[(tc[0] + 1) % 4]
                    tc[0] += 2
                    cx.op("act", lambda e: e.activation(out=t1[:], in_=ps[:],
                                                        func=(AF.Silu if kind == "gla" else AF.Sigmoid)), [ps.k], [t1.k])
                    cx.op("dve", lambda e: e.scalar_tensor_tensor(
                        out=t2[:], in0=o[:, blk, :], scalar=self.col(gn, blk % hpb), in1=r[:],
                        op0=ALU.mult, op1=ALU.mult), [o.k, r.k, self.cols_b.k], [t2.k])
                    cx.op("pool", lambda e: e.tensor_tensor(out=ogb[:, blk, :], in0=t2[:], in1=t1[:], op=ALU.mult),
                          [t1.k, t2.k], [ogb.k])
                self.linear_fm(u, KC, w_in, ogcol, D, epi_og)

                def epi_out(blk, bw, ps):
                    cx.op("dve", lambda e: e.tensor_tensor(out=h[:, blk, :], in0=ps[:], in1=h[:, blk, :], op=ALU.add),
                          [ps.k, h.k], [h.k])
                self.linear_fm(ogb, KC, w_out, 0, D, epi_out)
                self.store_tile(h, self.hT, 0, KC, it, self.hT_k[it])
            self.h_stored()
            cx.barrier()

    def phase_ssm(self, i, j):
        cx, nc = self.cx, self.nc
        w_in, w_out = "ssm_w_in_%d" % j, "ssm_w_out_%d" % j
        mask2 = self.consts_b[:, 0, :]
        selpair = self.consts_b[:, 1, :]
        sel63 = self.consts_b[:, 2, :]
        sel127 = self.consts_b[:, 3, :]
        NP = TT // 128
        G, HG, P, NST = 8, 8, 64, 128
        GI = 2

        class View:
            def __init__(self, ap, k):
                self.ap, self.k = ap, k

            def __getitem__(self, key):
                return self.ap[key]

        def bc(ap2, n):
            return ap2.unsqueeze(2).to_broadcast([128, ap2.shape[1], n])

        with ExitStack() as es:
            big1 = Buf(cx, es, "big1", [128, KC * TT], F32)
            big2 = Buf(cx, es, "big2", [128, KC * TT], F32)
            h1 = View(big1.t[:].rearrange("p (a b) -> p a b", b=TT), big1.k)
            yT = View(big1.t[:].bitcast(BF16).rearrange("p (a b) -> p a b", b=TT), big1.k)
            u = View(big2.t[:, 0:4096].bitcast(BF16).rearrange("p (a b) -> p a b", b=TT), Trk("u"))
            BT = View(big2.t[:, 4096:6144].bitcast(BF16).rearrange("p (a b) -> p a b", b=TT), Trk("BT"))
            CT = View(big2.t[:, 6144:8192].bitcast(BF16).rearrange("p (a b) -> p a b", b=TT), Trk("CT"))
            h2 = View(big2.t[:].rearrange("p (a b) -> p a b", b=TT), big2.k)
            S32 = [Buf(cx, es, "S32", [128, HG * P], F32) for _ in range(G)]
            Sbf = [Buf(cx, es, "Sbf", [128, HG * P], BF16) for _ in range(G)]
            halo = Buf(cx, es, "halo", [128, 48, 3], F32)
            identf = Buf(cx, es, "identf", [128, 128], F32)
            onesf = Buf(cx, es, "onesf", [128, 128], F32)
            dtb = Buf(cx, es, "dtb", [128, 64], F32)
            arow = Buf(cx, es, "arow", [128, 64], F32)
            ddr = Buf(cx, es, "ddr", [128, 64], F32)
            dt_tok = Buf(cx, es, "dt_tok", [128, NP, 64], F32)
            la_tok = Buf(cx, es, "la_tok", [128, NP, 64], F32)
            b_tok = Buf(cx, es, "b_tok", [128, NP, 64], F32)
            eb_tok = Buf(cx, es, "eb_tok", [128, NP, 64], F32)
            we_tok = Buf(cx, es, "we_tok", [128, NP, 64], F32)
            Dc = Buf(cx, es, "Dc", [128, NP, 2, 64], F32)
            sqs = [Buf(cx, es, "sq", [128, TT], BF16) for _ in range(1)]
            rs = Buf(cx, es, "rs", [128, TT], F32)
            tmpc = [Buf(cx, es, "tmpc", [128, TT + 3], F32) for _ in range(2)]
            acc = Buf(cx, es, "acc", [128, TT], F32)
            xTg = Buf(cx, es, "xTg", [128, 4, TT], BF16)
            x_tok = [Buf(cx, es, "x_tok", [128, NP, 512], BF16) for _ in range(GI)]
            zg = [Buf(cx, es, "zg", [128, NP, 512], F32) for _ in range(GI)]
            rel = [Buf(cx, es, "rel", [128, 8, 128], F32) for _ in range(GI)]
            Mb = [Buf(cx, es, "Mb", [128, 8, 128], BF16) for _ in range(GI)]
            xdt = [Buf(cx, es, "xdt", [128, 512], BF16) for _ in range(GI)]
            xw = [Buf(cx, es, "xw", [128, 512], BF16) for _ in range(GI)]
            ytmp = [Buf(cx, es, "ytmp", [128, 512], F32) for _ in range(GI)]
            yn = [Buf(cx, es, "yn", [128, 512], BF16) for _ in range(GI)]
            cbm = [Buf(cx, es, "cbm", [128, 128], F32) for _ in range(GI)]
            btok = [Buf(cx, es, "btok", [128, 128], BF16) for _ in range(GI)]
            ss = [Buf(cx, es, "ss", [128, 1], F32) for _ in range(GI)]
            for g in range(G):
                cx.op("pool", lambda e: e.memset(S32[g][:], 0.0), [], [S32[g].k])
                cx.op("pool", lambda e: e.memset(Sbf[g][:], 0.0), [], [Sbf[g].k])
            cx.op("pool", lambda e: e.memset(halo[:], 0.0), [], [halo.k])
            cx.op("pool", lambda e: e.memset(onesf[:], 1.0), [], [onesf.k])
            cx.op("pool", lambda e: e.memset(identf[:], 0.0), [], [identf.k])
            cx.op("pool", lambda e: e.affine_select(
                out=identf[:], in_=identf[:], pattern=[[-1, 128]], compare_op=ALU.not_equal, fill=1.0,
                base=0, channel_multiplier=1), [identf.k], [identf.k])
            cx.dma("pool", dtb[:], self.rows_d["ssm_dt_bias_%d" % j].ap().partition_broadcast(128), dtb.k, None)
            cx.dma("pool", arow[:], self.rows_d["ssm_a_log_%d" % j].ap().partition_broadcast(128), arow.k, None)
            cx.dma("pool", ddr[:], self.rows_d["ssm_d_%d" % j].ap().partition_broadcast(128), ddr.k, None)
            cx.op("act", lambda e: e.activation(out=arow[:], in_=arow[:], func=AF.Exp), [arow.k], [arow.k])
            cx.op("dve", lambda e: e.tensor_scalar(out=arow[:], in0=arow[:], scalar1=-1.0, scalar2=None, op0=ALU.mult),
                  [arow.k], [arow.k])
            cb16 = Buf(cx, es, "cb16", [128, 4, 128], BF16)
            onesb = self.ones_b
            cx.op("dve", lambda e: e.tensor_copy(out=cb16[:], in_=self.consts_b[:]), [self.consts_b.k], [cb16.k])
            la_hi = Buf(cx, es, "la_hi", [128, NP, 64], BF16)
            la_lo = Buf(cx, es, "la_lo", [128, NP, 64], BF16)
            b_hi = Buf(cx, es, "b_hi", [128, NP, 64], BF16)
            b_lo = Buf(cx, es, "b_lo", [128, NP, 64], BF16)
            hl_f = Buf(cx, es, "hl_f", [128, NP, 64], F32)
            dg_lo = Buf(cx, es, "dg_lo", [128, 8, 128], BF16)
            dg_hi = Buf(cx, es, "dg_hi", [128, 8, 128], BF16)

            def split(src, hi, lo):
                cx.op("dve", lambda e: e.tensor_copy(out=hi[:], in_=src[:]), [src.k], [hi.k])
                cx.op("dve", lambda e: e.tensor_copy(out=hl_f[:], in_=hi[:]), [hi.k], [hl_f.k])
                cx.op("dve", lambda e: e.tensor_tensor(out=lo[:], in0=src[:], in1=hl_f[:], op=ALU.subtract),
                      [src.k, hl_f.k], [lo.k])
            cc = [0]

            def conv_epi(cb, ps, dst_ap, dst_k):
                tcv = tmpc[cc[0] % 2]
                cc[0] += 1
                cx.op("pool", lambda e: e.tensor_copy(out=tcv[:, 0:3], in_=halo[:, cb, :]), [halo.k], [tcv.k])
                cx.op("act", lambda e: e.copy(out=tcv[:, 3:TT + 3], in_=ps[:]), [ps.k], [tcv.k])
                cx.op("pool", lambda e: e.tensor_copy(out=halo[:, cb, :], in_=tcv[:, TT:TT + 3]), [tcv.k], [halo.k])
                wc = lambda t: self.col("ssm_conv_w_%d_%d" % (j, t), cb)
                cx.op("dve", lambda e: e.tensor_scalar(out=acc[:], in0=tcv[:, 0:TT], scalar1=wc(0),
                                                       scalar2=self.col("ssm_conv_b_%d" % j, cb),
                                                       op0=ALU.mult, op1=ALU.add), [tcv.k, self.cols_b.k], [acc.k])
                for t in range(1, 4):
                    cx.op("dve", lambda e: e.scalar_tensor_tensor(out=acc[:], in0=tcv[:, t:t + TT], scalar=wc(t),
                                                                  in1=acc[:], op0=ALU.mult, op1=ALU.add),
                          [tcv.k, acc.k, self.cols_b.k], [acc.k])
                cx.op("act", lambda e: e.activation(out=dst_ap, in_=acc[:], func=AF.Silu), [acc.k], [dst_k])

            for it in range(self.NT):
                self.load_h(h1, it)
                self.norm_u(h1, "norm_mix_%d" % i, u, sqs, rs)

                if SSM_STAGE < 1:
                    cx.barrier(end=False)
                    continue
                def epi_bc(blk, bw, ps):
                    if blk < 8:
                        conv_epi(32 + blk, ps, BT[:, blk, :], BT.k)
                    else:
                        conv_epi(32 + blk, ps, CT[:, blk - 8, :], CT.k)
                self.linear_fm(u, KC, w_in, 8192, 2048, epi_bc)

                if SSM_STAGE < 2:
                    cx.barrier(end=False)
                    continue
                def epi_dt(ts, ps):
                    cx.op("dve", lambda e: e.tensor_tensor(out=dt_tok[:, ts, :], in0=ps[:, 0:64], in1=dtb[:], op=ALU.add),
                          [ps.k, dtb.k], [dt_tok.k])
                self.linear_tm(u, w_in, 10240, 64, epi_dt)
                cx.op("act", lambda e: e.activation(out=dt_tok[:], in_=dt_tok[:], func=AF.Exp), [dt_tok.k], [dt_tok.k])
                cx.op("act", lambda e: e.activation(out=dt_tok[:], in_=dt_tok[:], func=AF.Ln, bias=1.0), [dt_tok.k], [dt_tok.k])
                for ts in range(NP):
                    cx.op("dve", lambda e: e.tensor_tensor(out=la_tok[:, ts, :], in0=dt_tok[:, ts, :], in1=arow[:], op=ALU.mult),
                          [dt_tok.k, arow.k], [la_tok.k])
                split(la_tok, la_hi, la_lo)
                for ts in range(NP):
                    ps = self.next_ps()
                    cx.op("pe", lambda e: e.matmul(ps[:, 0:64], cb16[:, 0, :], la_hi[:, ts, :], start=True, stop=False),
                          [cb16.k, la_hi.k], [ps.k])
                    cx.op("pe", lambda e: e.matmul(ps[:, 0:64], cb16[:, 0, :], la_lo[:, ts, :], start=False, stop=True),
                          [cb16.k, la_lo.k], [ps.k])
                    cx.op("act", lambda e: e.copy(out=b_tok[:, ts, :], in_=ps[:, 0:64]), [ps.k], [b_tok.k])
                split(b_tok, b_hi, b_lo)
                cx.op("act", lambda e: e.activation(out=eb_tok[:], in_=b_tok[:], func=AF.Exp), [b_tok.k], [eb_tok.k])
                for ts in range(NP):
                    ps = self.next_ps()
                    for si in range(3):
                        for hl, src in enumerate((b_hi, b_lo)):
                            cx.op("pe", lambda e: e.matmul(ps[:, si * 64:(si + 1) * 64], cb16[:, 1 + si, :], src[:, ts, :],
                                                           start=(si == 0 and hl == 0), stop=(hl == 1),
                                                           skip_group_check=True), [cb16.k, src.k], [ps.k])
                    cx.op("dve", lambda e: e.tensor_tensor(out=we_tok[:, ts, :], in0=ps[:, 0:64], in1=b_tok[:, ts, :],
                                                           op=ALU.subtract), [ps.k, b_tok.k], [we_tok.k])
                    cx.op("act", lambda e: e.activation(out=Dc[:, ts, :, :], in_=ps[:, 64:192].rearrange("p (a b) -> p a b", b=64),
                                                        func=AF.Exp), [ps.k], [Dc.k])
                cx.op("act", lambda e: e.activation(out=we_tok[:], in_=we_tok[:], func=AF.Exp), [we_tok.k], [we_tok.k])

                if SSM_STAGE < 3:
                    cx.barrier(end=False)
                    continue
                for gp in range(G // GI):
                    gs = tuple(range(gp * GI, (gp + 1) * GI))
                    for g in gs:
                        sl = g % GI
                        def epi_x(blk, bw, ps):
                            conv_epi(g * 4 + blk, ps, xTg[:, blk, :], xTg.k)
                        self.linear_fm(u, KC, w_in, 4096 + g * 512, 512, epi_x)
                        for ts in range(NP):
                            ps = self.next_ps()
                            pv = ps[:, 0:256].bitcast(BF16).rearrange("p (a b) -> p a b", b=128)
                            for a in range(4):
                                cx.op("pe", lambda e: e.transpose(pv[:, a, :], xTg[:, a, ts * 128:(ts + 1) * 128],
                                                                  self.ident_b[:]), [xTg.k, self.ident_b.k], [ps.k])
                            cx.op("act", lambda e: e.copy(out=x_tok[sl][:, ts, :].rearrange("p (a b) -> p a b", b=128),
                                                          in_=pv), [ps.k], [x_tok[sl].k])
                        def epi_z(ts, ps):
                            cx.op("act", lambda e: e.activation(out=zg[sl][:, ts, :], in_=ps[:], func=AF.Silu),
                                  [ps.k], [zg[sl].k])
                        self.linear_tm(u, w_in, g * 512, 512, epi_z)

                    for pr in (range(NP) if SSM_STAGE >= 4 else []):
                        tsl = slice(pr * 128, (pr + 1) * 128)
                        psy, psi = {}, {}
                        for g in gs:
                            sl = g % GI
                            hs = slice(g * 8, (g + 1) * 8)
                            ps = self.next_ps()
                            pvb = ps[:, 0:64].bitcast(BF16)
                            cx.op("pe", lambda e: e.transpose(pvb, BT[:, g, tsl], self.ident_b[:]),
                                  [BT.k, self.ident_b.k], [ps.k])
                            cx.op("act", lambda e: e.copy(out=btok[sl][:], in_=pvb), [ps.k], [btok[sl].k])
                            ps = self.next_ps()
                            cx.op("pe", lambda e: e.matmul(ps[:, 0:128], BT[:, g, tsl], CT[:, g, tsl], start=True, stop=True),
                                  [BT.k, CT.k], [ps.k])
                            cx.op("dve", lambda e: e.tensor_tensor(out=cbm[sl][:], in0=ps[:, 0:128], in1=mask2, op=ALU.mult),
                                  [ps.k, self.consts_b.k], [cbm[sl].k])
                            for dgx, bx in ((dg_hi, b_hi), (dg_lo, b_lo)):
                                cx.op("dve", lambda e: e.tensor_tensor(
                                    out=dgx[:], in0=identf[:].unsqueeze(1).to_broadcast([128, 8, 128]),
                                    in1=bc(bx[:, pr, hs], 128), op=ALU.mult), [identf.k, bx.k], [dgx.k])
                            pb0, pb1 = self.next_ps(hold=True), self.next_ps(hold=True)
                            for pbx, lo4 in ((pb0, 0), (pb1, 4)):
                                for hl, dgx in enumerate((dg_hi, dg_lo)):
                                    cx.op("pe", lambda e: e.matmul(
                                        pbx[:], onesb[:], dgx[:, lo4:lo4 + 4, :].rearrange("p a b -> p (a b)"),
                                        start=(hl == 0), stop=(hl == 1)), [onesb.k, dgx.k], [pbx.k])
                            cx.op("dve", lambda e: e.tensor_tensor(
                                out=rel[sl][:, 0:4, :], in0=pb0[:].rearrange("p (a b) -> p a b", b=128),
                                in1=bc(b_tok[:, pr, g * 8:g * 8 + 4], 128), op=ALU.subtract), [pb0.k, b_tok.k], [rel[sl].k])
                            cx.op("dve", lambda e: e.tensor_tensor(
                                out=rel[sl][:, 4:8, :], in0=pb1[:].rearrange("p (a b) -> p a b", b=128),
                                in1=bc(b_tok[:, pr, g * 8 + 4:g * 8 + 8], 128), op=ALU.subtract), [pb1.k, b_tok.k], [rel[sl].k])
                            self.release(pb0)
                            self.release(pb1)
                            cx.op("act", lambda e: e.activation(out=rel[sl][:], in_=rel[sl][:], func=AF.Relu, scale=-1.0),
                                  [rel[sl].k], [rel[sl].k])
                            cx.op("act", lambda e: e.activation(out=rel[sl][:], in_=rel[sl][:], func=AF.Exp, scale=-1.0),
                                  [rel[sl].k], [rel[sl].k])
                            cx.op("pool", lambda e: e.tensor_tensor(
                                out=Mb[sl][:], in0=rel[sl][:], in1=cbm[sl][:].unsqueeze(1).to_broadcast([128, 8, 128]),
                                op=ALU.mult), [rel[sl].k, cbm[sl].k], [Mb[sl].k])
                            cx.op("pool", lambda e: e.tensor_tensor(
                                out=xdt[sl][:].rearrange("p (a b) -> p a b", b=64),
                                in0=x_tok[sl][:, pr, :].rearrange("p (a b) -> p a b", b=64),
                                in1=bc(dt_tok[:, pr, hs], 64), op=ALU.mult), [x_tok[sl].k, dt_tok.k], [xdt[sl].k])
                            cx.op("pool", lambda e: e.tensor_tensor(
                                out=xw[sl][:].rearrange("p (a b) -> p a b", b=64),
                                in0=xdt[sl][:].rearrange("p (a b) -> p a b", b=64),
                                in1=bc(we_tok[:, pr, hs], 64), op=ALU.mult), [xdt[sl].k, we_tok.k], [xw[sl].k])
                            psy[g] = self.next_ps(hold=True)
                            for h8 in range(8):
                                cx.op("pe", lambda e: e.matmul(
                                    psy[g][:, h8 * 64:(h8 + 1) * 64], Mb[sl][:, h8, :], xdt[sl][:, h8 * 64:(h8 + 1) * 64],
                                    start=(h8 == 0), stop=True, skip_group_check=True), [Mb[sl].k, xdt[sl].k], [psy[g].k])
                            psi[g] = self.next_ps(hold=True)
                        for c2 in range(2):
                            rows = slice(c2 * 64, (c2 + 1) * 64)
                            tcs = slice(pr * 128 + c2 * 64, pr * 128 + (c2 + 1) * 64)
                            for g in gs:
                                kw = {"tile_position": (0, 64)} if c2 == 1 else {}
                                cx.op("pe", lambda e: e.matmul(psi[g][rows, :], CT[:, g, tcs], Sbf[g][:], start=True, stop=True,
                                                               skip_group_check=True, **kw), [CT.k, Sbf[g].k], [psi[g].k])
                            for g in gs:
                                sl = g % GI
                                hs = slice(g * 8, (g + 1) * 8)
                                ps = self.next_ps()
                                cx.op("pe", lambda e: e.matmul(ps[:], btok[sl][rows, :], xw[sl][rows, :], start=True, stop=True),
                                      [btok[sl].k, xw[sl].k], [ps.k])
                                cx.op("dve", lambda e: e.tensor_tensor(
                                    out=S32[g][:].rearrange("p (a b) -> p a b", b=64),
                                    in0=S32[g][:].rearrange("p (a b) -> p a b", b=64),
                                    in1=bc(Dc[:, pr, c2, hs], 64), op=ALU.mult), [S32[g].k, Dc.k], [S32[g].k])
                                cx.op("dve", lambda e: e.tensor_tensor(out=S32[g][:], in0=ps[:], in1=S32[g][:], op=ALU.add),
                                      [ps.k, S32[g].k], [S32[g].k])
                                cx.op("act", lambda e: e.copy(out=Sbf[g][:], in_=S32[g][:]), [S32[g].k], [Sbf[g].k])
                        for g in gs:
                            sl = g % GI
                            hs = slice(g * 8, (g + 1) * 8)
                            y = ytmp[sl]
                            tt = View(rel[sl].t[:, 0:4, :].rearrange("p a b -> p (a b)"), rel[sl].k)
                            cx.op("dve", lambda e: e.tensor_tensor(
                                out=y[:].rearrange("p (a b) -> p a b", b=64),
                                in0=psi[g][:].rearrange("p (a b) -> p a b", b=64),
                                in1=bc(eb_tok[:, pr, hs], 64), op=ALU.mult), [psi[g].k, eb_tok.k], [y.k])
                            cx.op("dve", lambda e: e.tensor_tensor(out=y[:], in0=psy[g][:], in1=y[:], op=ALU.add),
                                  [psy[g].k, y.k], [y.k])
                            self.release(psy[g])
                            self.release(psi[g])
                            cx.op("pool", lambda e: e.tensor_tensor(
                                out=tt[:].rearrange("p (a b) -> p a b", b=64),
                                in0=x_tok[sl][:, pr, :].rearrange("p (a b) -> p a b", b=64),
                                in1=bc(ddr[:, hs], 64), op=ALU.mult), [x_tok[sl].k, ddr.k], [tt.k])
                            cx.op("pool", lambda e: e.tensor_tensor(out=y[:], in0=y[:], in1=tt[:], op=ALU.add), [y.k, tt.k], [y.k])
                            cx.op("pool", lambda e: e.tensor_tensor(out=y[:], in0=y[:], in1=zg[sl][:, pr, :], op=ALU.mult),
                                  [y.k, zg[sl].k], [y.k])
                            cx.op("pool", lambda e: e.tensor_tensor(out=tt[:], in0=y[:], in1=y[:], op=ALU.mult), [y.k], [tt.k])
                            cx.op("dve", lambda e: e.reduce_sum(out=ss[sl][:], in_=tt[:], axis=mybir.AxisListType.X),
                                  [tt.k], [ss[sl].k])
                            cx.op("act", lambda e: e.activation(out=ss[sl][:], in_=ss[sl][:], func=AF.Sqrt, bias=EPS,
                                                                scale=1.0 / 512.0), [ss[sl].k], [ss[sl].k])
                            cx.op("dve", lambda e: e.reciprocal(out=ss[sl][:], in_=ss[sl][:]), [ss[sl].k], [ss[sl].k])
                            cx.op("dve", lambda e: e.tensor_scalar(out=yn[sl][:], in0=y[:], scalar1=ss[sl][:, 0:1], scalar2=None,
                                                                   op0=ALU.mult), [y.k, ss[sl].k], [yn[sl].k])
                            ps = self.next_ps()
                            pv = ps[:, 0:256].bitcast(BF16).rearrange("p (a b) -> p a b", b=128)
                            for a in range(4):
                                cx.op("pe", lambda e: e.transpose(pv[:, a, :], yn[sl][:, a * 128:(a + 1) * 128],
                                                                  self.ident_b[:]), [yn[sl].k, self.ident_b.k], [ps.k])
                            for a in range(4):
                                cx.op("act", lambda e: e.activation(
                                    out=yT[:, g * 4 + a, tsl], in_=pv[:, a, :], func=AF.Copy,
                                    scale=self.col("ssm_norm_%d" % j, g * 4 + a)), [ps.k, self.cols_b.k], [yT.k])
                cx.barrier(end=False)
                if SSM_STAGE < 5:
                    continue
                self.load_h(h2, it)

                def epi_out(blk, bw, ps):
                    cx.op("dve", lambda e: e.tensor_tensor(out=h2[:, blk, :], in0=ps[:], in1=h2[:, blk, :], op=ALU.add),
                          [ps.k, h2.k], [h2.k])
                self.linear_fm(yT, 32, w_out, 0, D, epi_out)
                self.store_tile(h2, self.hT, 0, KC, it, self.hT_k[it])
                cx.barrier(end=False)
            self.h_stored()
            cx.barrier()


def _weight(inputs, nm):
    base, idx = nm.rsplit("_", 1)
    table = {
        "w_up": inputs["w_up"], "w_down": inputs["w_down"],
        "w_ple_proj": inputs["w_ple_proj"], "w_ple_gate": inputs["w_ple_gate"],
        "gla_w_in": inputs["gla_w_in"], "gla_w_out": inputs["gla_w_out"],
        "hgrn_w_in": inputs["hgrn_w_in"], "hgrn_w_out": inputs["hgrn_w_out"],
        "ssm_w_in": inputs["ssm_w_in"], "ssm_w_out": inputs["ssm_w_out"],
    }
    return table[base][int(idx)]


def make_consts():
    c = np.zeros((128, 4, 128), np.float32)
    s = np.arange(128)[:, None]
    t = np.arange(128)[None, :]
    c[:, 0, :] = ((s // 64 == t // 64) & (s <= t)).astype(np.float32)
    c[:, 1, :] = (((s == 63) & (t < 64)) | ((s == 127) & (t >= 64))).astype(np.float32)
    c[:, 2, :] = (s == 63).astype(np.float32) * np.ones_like(t)
    c[:, 3, :] = (s == 127).astype(np.float32) * np.ones_like(t)
    return c


def run(inputs, depth, T, enable_mix=True, trace=False):
    x = np.asarray(inputs["x"])
    p = np.asarray(inputs["p"])
    B, L, _ = x.shape
    segs = NCORES // B
    assert L == segs * T
    prog = Prog(T, depth, enable_mix)
    nc = prog.build()
    lay, ncol, _ = col_layout(depth)
    cols = np.zeros((128, ncol), np.float32)

    def put(nm, v):
        off, n = lay[nm]
        cols[:, off:off + n] = to_cols(v)
    for i in range(depth):
        put("norm_mix_%d" % i, inputs["norm_mix"][i])
        put("norm_mlp_%d" % i, inputs["norm_mlp"][i])
        put("norm_ple_%d" % i, inputs["norm_ple"][i])
        kind, j = kind_of(i)
        if kind == 0:
            put("gla_b_gk_%d" % j, inputs["gla_b_gk"][j])
            put("gla_gn_%d" % j, inputs["gla_gn"][j])
        elif kind == 1:
            put("hgrn_gn_%d" % j, inputs["hgrn_gn"][j])
            for l in range(depth):
                put("hgrn_lb_%d" % l, inputs["hgrn_lb_logits"][l])
        else:
            for t in range(4):
                put("ssm_conv_w_%d_%d" % (j, t), inputs["ssm_conv_w"][j][t])
            put("ssm_conv_b_%d" % j, inputs["ssm_conv_b"][j])
            put("ssm_norm_%d" % j, inputs["ssm_norm"][j])
    put("norm_final", inputs["norm_final"])
    consts = make_consts()
    shared = {"cols": cols, "consts": consts}
    for i in range(depth):
        kind, j = kind_of(i)
        if kind == 0:
            shared["gla_w_gk2_%d" % j] = np.ascontiguousarray(inputs["gla_w_gk2"][j], np.float32)
        if kind == 2:
            for nm in ("ssm_dt_bias", "ssm_a_log", "ssm_d"):
                shared["%s_%d" % (nm, j)] = np.ascontiguousarray(
                    np.asarray(inputs[nm][j], np.float32).reshape(1, 64))
    in_maps = []
    for c in range(NCORES):
        b, s = c // segs, c % segs
        m = dict(shared)
        m["xT"] = np.ascontiguousarray(x[b, s * T:(s + 1) * T, :].T)
        m["pT"] = np.ascontiguousarray(np.transpose(p[:depth, b, s * T:(s + 1) * T, :], (0, 2, 1)))
        cm = np.zeros((128, 16), np.float32)
        for r in range(NCORES):
            rb, rs = r // segs, r % segs
            if rb == b and rs < s:
                cm[:, r] = 1.0
            if rb == b and rs == s - 1:
                cm[:, 8 + r] = 1.0
        m["cmask"] = cm
        for nm, K, N in big_weights(depth):
            W = _weight(inputs, nm)
            if USE_CC:
                r = K // NCORES
                m[nm] = np.ascontiguousarray(W[c * r:(c + 1) * r, :], np.float32)
            else:
                m[nm] = np.ascontiguousarray(W, np.float32)
        in_maps.append(m)
    res = run_bass_kernel_spmd(nc, in_maps, core_ids=list(range(NCORES)), trace=trace)
    out = np.empty((B, L, D), np.float32)
    for c in range(NCORES):
        b, s = c // segs, c % segs
        out[b, s * T:(s + 1) * T, :] = res.results[c]["outT"].T
    if DEBUG:
        return out, res
    if trace:
        return out, res
    return out


def kernel(**inputs):
    depth = int(np.asarray(inputs["p"]).shape[0])
    B, L, _ = np.asarray(inputs["x"]).shape
    return run(inputs, depth, L * B // NCORES)
```
